# Optimizing a Trainium2 kernel written in Bass

```python
import math
import jax, jax.numpy as jnp
from jax import lax
import numpy as np

D_MODEL = 1024
BATCH = 4
SEQ = 4096
DEPTH = 4

N_MIXERS = 2
N_A_LAYERS = (DEPTH + 1) // 2
N_B_LAYERS = DEPTH // 2

S5_GROUP = 16
S5_GROUPS = D_MODEL // S5_GROUP
S5_STATE = 64
S5_DT_MIN = 0.001
S5_DT_MAX = 0.1
N_DIRS = 2

DIL_GROUPS = ((128, 1), (512, 4), (2048, 16))
N_DIL = len(DIL_GROUPS)
HEADS_PER_GROUP = 8
HEAD_DIM = D_MODEL // HEADS_PER_GROUP
ATT_WIDTH = N_DIL * HEADS_PER_GROUP * HEAD_DIM

N_BUCKETS = 32
MAX_DISTANCE = 1024
N_BIAS_HEADS = N_DIL * HEADS_PER_GROUP

MEM_LEN = 256
MEM_HEADS = 4
MEM_HEAD_DIM = D_MODEL // MEM_HEADS

D_FF = 4 * D_MODEL
EPS = 1e-6
NEG_INF = -1e30

kernel_name = "hybrid_s5_dilated_attn_encoder"


def rms_norm(x, g):
    xf = x.astype(jnp.float32)
    y = xf * lax.rsqrt(jnp.mean(xf * xf, axis=-1, keepdims=True) + EPS)
    return (y * g.astype(jnp.float32)).astype(x.dtype)


def t5_bucket(rel):
    nb = N_BUCKETS // 2
    ret = (rel > 0).astype(np.int32) * nb
    n = np.abs(rel)
    max_exact = nb // 2
    large = max_exact + (np.log(np.maximum(n, 1).astype(np.float32) / max_exact)
                         / np.log(MAX_DISTANCE / max_exact) * (nb - max_exact)).astype(np.int32)
    large = np.minimum(large, nb - 1)
    return (ret + np.where(n < max_exact, n, large)).astype(np.int32)


def complex_mul(ar, ai, br, bi):
    return ar * br - ai * bi, ar * bi + ai * br


def s5_direction(u, lam_re, lam_im, log_dt, b_re, b_im, c_re, c_im, reverse):
    f32 = jnp.float32
    lam_re = lam_re.astype(f32); lam_im = lam_im.astype(f32)
    b_re = b_re.astype(f32); b_im = b_im.astype(f32)
    c_re = c_re.astype(f32); c_im = c_im.astype(f32)
    dt = jnp.exp(log_dt.astype(f32))[:, None]
    mag = jnp.exp(lam_re * dt)
    ang = lam_im * dt
    abar_re = mag * jnp.cos(ang)
    abar_im = mag * jnp.sin(ang)
    nr = abar_re - 1.0
    ni = abar_im
    den = lam_re * lam_re + lam_im * lam_im
    f_re = (nr * lam_re + ni * lam_im) / den
    f_im = (ni * lam_re - nr * lam_im) / den
    bbar_re, bbar_im = complex_mul(f_re[..., None], f_im[..., None], b_re, b_im)
    bu_re = jnp.einsum('bsgc,gpc->bsgp', u, bbar_re)
    bu_im = jnp.einsum('bsgc,gpc->bsgp', u, bbar_im)
    seq = u.shape[1]
    a_re = jnp.broadcast_to(abar_re, (1, seq) + abar_re.shape)
    a_im = jnp.broadcast_to(abar_im, (1, seq) + abar_im.shape)

    def combine(left, right):
        a1r, a1i, b1r, b1i = left
        a2r, a2i, b2r, b2i = right
        ar, ai = complex_mul(a2r, a2i, a1r, a1i)
        br, bi = complex_mul(a2r, a2i, b1r, b1i)
        return ar, ai, br + b2r, bi + b2i

    _, _, xr, xi = lax.associative_scan(combine, (a_re, a_im, bu_re, bu_im),
                                        reverse=reverse, axis=1)
    return jnp.einsum('bsgp,gcp->bsgc', xr, c_re) - jnp.einsum('bsgp,gcp->bsgc', xi, c_im)


def s5_mixer(h, lam_re, lam_im, log_dt, b_re, b_im, c_re, c_im, d_skip, w_glu):
    bsz, seq, _ = h.shape
    u = h.astype(jnp.float32).reshape(bsz, seq, S5_GROUPS, S5_GROUP)
    y = d_skip.astype(jnp.float32).reshape(S5_GROUPS, S5_GROUP) * u
    for di, rev in enumerate((False, True)):
        y = y + s5_direction(u, lam_re[di], lam_im[di], log_dt[di], b_re[di], b_im[di],
                             c_re[di], c_im[di], rev)
    g = jax.nn.gelu(y.reshape(bsz, seq, D_MODEL)).astype(h.dtype)
    a, b = jnp.split(g @ w_glu, 2, axis=-1)
    return a * jax.nn.sigmoid(b)


def dilated_branch(q, k, v, bias_g, window, dil):
    bsz, seq, nh, e = q.shape
    half = window // (2 * dil)
    blk = half
    sub_len = seq // dil
    nb = -(-sub_len // blk)
    lp = nb * blk

    def to_sub(t):
        return t.reshape(bsz, sub_len, dil, nh, e).transpose(0, 2, 3, 1, 4)

    qs, ks, vs = to_sub(q), to_sub(k), to_sub(v)
    qb = jnp.pad(qs, ((0, 0), (0, 0), (0, 0), (0, lp - sub_len), (0, 0))).reshape(bsz, dil, nh, nb, blk, e)

    def key_blocks(t):
        tp = jnp.pad(t, ((0, 0), (0, 0), (0, 0), (blk, lp - sub_len + blk), (0, 0)))
        tp = tp.reshape(bsz, dil, nh, nb + 2, blk, e)
        return jnp.concatenate([tp[:, :, :, :-2], tp[:, :, :, 1:-1], tp[:, :, :, 2:]], axis=4)

    kb, vb = key_blocks(ks), key_blocks(vs)

    qi = np.arange(blk)[:, None]
    kj = np.arange(3 * blk)[None, :]
    rel = kj - blk - qi
    band = np.abs(rel) <= half
    key_idx = np.arange(nb)[:, None] * blk + np.arange(3 * blk)[None, :] - blk
    key_ok = (key_idx >= 0) & (key_idx < sub_len)
    allowed = band[None] & key_ok[:, None, :]
    bucket = t5_bucket(rel * dil)
    bias = jnp.transpose(bias_g[bucket], (2, 0, 1)).astype(jnp.float32)

    logits = jnp.einsum('bdhnqe,bdhnke->bdhnqk', qb, kb).astype(jnp.float32) * (e ** -0.5)
    logits = logits + bias[None, None, :, None]
    logits = jnp.where(allowed[None, None, None], logits, NEG_INF)
    m = jnp.max(logits, axis=-1, keepdims=True)
    p = jnp.exp(logits - m)
    denom = jnp.sum(p, axis=-1, keepdims=True)
    out = jnp.einsum('bdhnqk,bdhnke->bdhnqe', p, vb.astype(jnp.float32)) / denom
    lse = (m + jnp.log(denom))[..., 0]
    out = out.reshape(bsz, dil, nh, lp, e)[:, :, :, :sub_len]
    out = out.transpose(0, 3, 1, 2, 4).reshape(bsz, seq, nh, e)
    lse = lse.reshape(bsz, dil, nh, lp)[:, :, :, :sub_len]
    lse = lse.transpose(0, 3, 1, 2).reshape(bsz, seq, nh)
    return out, lse


def dilated_attention_mixer(h, w_qkv, w_o, g_q, g_k, bias_table):
    bsz, seq, _ = h.shape
    qkv = (h @ w_qkv).reshape(bsz, seq, 3, N_DIL, HEADS_PER_GROUP, HEAD_DIM)
    q = rms_norm(qkv[:, :, 0], g_q)
    k = rms_norm(qkv[:, :, 1], g_k)
    v = qkv[:, :, 2]
    outs, lses = [], []
    for gi, (window, dil) in enumerate(DIL_GROUPS):
        bias_g = bias_table[:, gi * HEADS_PER_GROUP:(gi + 1) * HEADS_PER_GROUP]
        o, l = dilated_branch(q[:, :, gi], k[:, :, gi], v[:, :, gi], bias_g, window, dil)
        outs.append(o)
        lses.append(l)
    o = jnp.stack(outs, axis=2)
    alpha = jax.nn.softmax(jnp.stack(lses, axis=2), axis=2)
    merged = jnp.sum(alpha[..., None] * o, axis=2).reshape(bsz, seq, D_MODEL)
    return merged.astype(h.dtype) @ w_o


def memory_cross_attention(h, mem_n, w_q, w_kv, w_o, g_q, g_k):
    bsz, seq, _ = h.shape
    mlen = mem_n.shape[1]
    q = rms_norm((h @ w_q).reshape(bsz, seq, MEM_HEADS, MEM_HEAD_DIM), g_q)
    kv = (mem_n @ w_kv).reshape(bsz, mlen, 2, MEM_HEADS, MEM_HEAD_DIM)
    k = rms_norm(kv[:, :, 0], g_k)
    v = kv[:, :, 1]
    logits = jnp.einsum('bshe,bmhe->bhsm', q, k).astype(jnp.float32) * (MEM_HEAD_DIM ** -0.5)
    p = jax.nn.softmax(logits, axis=-1)
    o = jnp.einsum('bhsm,bmhe->bshe', p, v.astype(jnp.float32)).reshape(bsz, seq, D_MODEL)
    return o.astype(h.dtype) @ w_o


def sq_relu_mlp(h, w1, w2):
    return jnp.square(jax.nn.relu(h @ w1)) @ w2


def setup_inputs(seed: int = 0) -> dict:
    key = jax.random.key(seed)
    ks = jax.random.split(key, 32)
    f32 = jnp.float32
    nrm = lambda k, s, sc: jax.random.normal(k, s, f32) * sc
    gain = lambda k, s: 1.0 + 0.05 * jax.random.normal(k, s, f32)
    G, P, C = S5_GROUPS, S5_STATE, S5_GROUP
    lam_im0 = jnp.pi * jnp.arange(P, dtype=f32)
    return {
        "x": nrm(ks[0], (BATCH, SEQ, D_MODEL), 1.0),
        "mem": nrm(ks[1], (BATCH, MEM_LEN, D_MODEL), 1.0),
        "bias_table": nrm(ks[2], (N_BUCKETS, N_BIAS_HEADS), 0.2),
        "norm_mix": gain(ks[3], (DEPTH, D_MODEL)),
        "norm_xattn": gain(ks[4], (DEPTH, D_MODEL)),
        "norm_mem": gain(ks[5], (DEPTH, D_MODEL)),
        "norm_mlp": gain(ks[6], (DEPTH, D_MODEL)),
        "s5_lambda_re": -0.5 + nrm(ks[7], (N_A_LAYERS, N_DIRS, G, P), 0.01),
        "s5_lambda_im": lam_im0 + nrm(ks[8], (N_A_LAYERS, N_DIRS, G, P), 0.01),
        "s5_log_dt": jax.random.uniform(ks[9], (N_A_LAYERS, N_DIRS, G), f32,
                                        minval=math.log(S5_DT_MIN), maxval=math.log(S5_DT_MAX)),
        "s5_b_re": nrm(ks[10], (N_A_LAYERS, N_DIRS, G, P, C), (2.0 * C) ** -0.5),
        "s5_b_im": nrm(ks[11], (N_A_LAYERS, N_DIRS, G, P, C), (2.0 * C) ** -0.5),
        "s5_c_re": nrm(ks[12], (N_A_LAYERS, N_DIRS, G, C, P), (2.0 * P) ** -0.5),
        "s5_c_im": nrm(ks[13], (N_A_LAYERS, N_DIRS, G, C, P), (2.0 * P) ** -0.5),
        "s5_d": nrm(ks[14], (N_A_LAYERS, D_MODEL), 1.0),
        "s5_w_glu": nrm(ks[15], (N_A_LAYERS, D_MODEL, 2 * D_MODEL), D_MODEL ** -0.5),
        "attn_w_qkv": nrm(ks[16], (N_B_LAYERS, D_MODEL, 3 * ATT_WIDTH), D_MODEL ** -0.5),
        "attn_w_o": nrm(ks[17], (N_B_LAYERS, HEADS_PER_GROUP * HEAD_DIM, D_MODEL), D_MODEL ** -0.5),
        "attn_q_gain": gain(ks[18], (N_B_LAYERS, HEAD_DIM)),
        "attn_k_gain": gain(ks[19], (N_B_LAYERS, HEAD_DIM)),
        "xattn_w_q": nrm(ks[20], (DEPTH, D_MODEL, D_MODEL), D_MODEL ** -0.5),
        "xattn_w_kv": nrm(ks[21], (DEPTH, D_MODEL, 2 * D_MODEL), D_MODEL ** -0.5),
        "xattn_w_o": nrm(ks[22], (DEPTH, D_MODEL, D_MODEL), D_MODEL ** -0.5),
        "xattn_q_gain": gain(ks[23], (DEPTH, MEM_HEAD_DIM)),
        "xattn_k_gain": gain(ks[24], (DEPTH, MEM_HEAD_DIM)),
        "mlp_w1": nrm(ks[25], (DEPTH, D_MODEL, D_FF), D_MODEL ** -0.5),
        "mlp_w2": nrm(ks[26], (DEPTH, D_FF, D_MODEL), D_FF ** -0.5),
    }


def reference(x, mem, bias_table, norm_mix, norm_xattn, norm_mem, norm_mlp,
              s5_lambda_re, s5_lambda_im, s5_log_dt, s5_b_re, s5_b_im, s5_c_re, s5_c_im,
              s5_d, s5_w_glu, attn_w_qkv, attn_w_o, attn_q_gain, attn_k_gain,
              xattn_w_q, xattn_w_kv, xattn_w_o, xattn_q_gain, xattn_k_gain,
              mlp_w1, mlp_w2):
    h = x
    for i in range(DEPTH):
        j = i // N_MIXERS
        hn = rms_norm(h, norm_mix[i])
        if i % N_MIXERS == 0:
            mix = s5_mixer(hn, s5_lambda_re[j], s5_lambda_im[j], s5_log_dt[j],
                           s5_b_re[j], s5_b_im[j], s5_c_re[j], s5_c_im[j],
                           s5_d[j], s5_w_glu[j])
        else:
            mix = dilated_attention_mixer(hn, attn_w_qkv[j], attn_w_o[j],
                                          attn_q_gain[j], attn_k_gain[j], bias_table)
        h = h + mix
        h = h + memory_cross_attention(rms_norm(h, norm_xattn[i]), rms_norm(mem, norm_mem[i]),
                                       xattn_w_q[i], xattn_w_kv[i], xattn_w_o[i],
                                       xattn_q_gain[i], xattn_k_gain[i])
        h = h + sq_relu_mlp(rms_norm(h, norm_mlp[i]), mlp_w1[i], mlp_w2[i])
    return h
```

```python
import math
import numpy as np
from contextlib import ExitStack
import concourse.bass as bass
import concourse.mybir as mybir
from concourse.bass_utils import run_bass_kernel_spmd

F32 = mybir.dt.float32
BF16 = mybir.dt.bfloat16
AF = mybir.ActivationFunctionType
ALU = mybir.AluOpType

D = 1024
NCH = 8
SEQ = 4096
BATCH = 4
NT = 2048
EPS = 1e-6
MEMLEN = 256
DFF = 4096


class Prog:
    ENGS = ('pe', 'act', 'dve', 'pool', 'sp')
    BLK = {'pe': 'tensor', 'act': 'scalar', 'dve': 'vector', 'pool': 'gpsimd', 'sp': 'sync'}

    def __init__(self, nc):
        self.nc = nc
        self.es = ExitStack()
        self.ins = {e: [] for e in self.ENGS}
        self.last_w = {}
        self.readers = {}
        self.dma_cnt = {}
        self.log = None
        self.bar_deps = {}

    ARENA_F32 = 50688

    def use_arena(self):
        self.arena = self.es.enter_context(self.nc.sbuf_tensor('arena', [128, self.ARENA_F32], F32))
        self.aoff = 0

    def sb(self, name, shape, dt):
        if getattr(self, 'arena', None) is None:
            return self.es.enter_context(self.nc.sbuf_tensor('s_' + name, list(shape), dt))
        assert shape[0] == 128, shape
        nel = 1
        for d_ in shape[1:]:
            nel *= d_
        isz = 4 if dt == F32 else 2
        nby = (nel * isz + 63) // 64 * 64
        o4 = self.aoff // 4
        self.aoff += nby
        assert self.aoff <= self.ARENA_F32 * 4, ('arena overflow', name, self.aoff)
        v = self.arena[:, o4:o4 + nby // 4]
        if dt != F32:
            v = v.bitcast(dt)
        v = v[:, :nel]
        if len(shape) == 3:
            v = v.rearrange("p (a b) -> p a b", a=shape[1])
        elif len(shape) != 2:
            raise AssertionError(shape)
        return v

    def barrier(self):
        deps = [('d', k, c) for k, c in self.dma_cnt.items()]
        for e in self.ENGS:
            n = len(self.ins[e])
            j = n - 1
            while j >= 0 and self.ins[e][j]['dma'] is not None:
                j -= 1
            if j >= 0:
                deps.append(('e', e, j))
        self.bar_deps = {e: list(deps) for e in self.ENGS}
        self.last_w.clear()
        self.readers.clear()

    def ps(self, name, shape, dt=F32):
        return self.es.enter_context(self.nc.psum_tensor(name, list(shape), dt))

    def _deps(self, r, w):
        deps = []
        for k in r:
            t = self.last_w.get(k)
            if t is not None:
                deps.append(t)
        for k in w:
            t = self.last_w.get(k)
            if t is not None:
                deps.append(t)
            deps.extend(self.readers.get(k, ()))
        return deps

    def _commit(self, tok, r, w):
        for k in r:
            lst = self.readers.setdefault(k, [])
            lst[:] = [t for t in lst if t[:2] != tok[:2]]
            lst.append(tok)
        for k in w:
            self.last_w[k] = tok
            self.readers[k] = []

    def op(self, eng, fn, r=(), w=()):
        idx = len(self.ins[eng])
        self.ins[eng].append(dict(fn=fn, deps=self._deps(r, w) + self.bar_deps.pop(eng, []), dma=None))
        self._commit(('e', eng, idx), r, w)

    def dma(self, eng, fn, r=(), w=(), semkey=None):
        if semkey is None:
            semkey = w[0]
        c = self.dma_cnt.get(semkey, 0) + 16
        self.dma_cnt[semkey] = c
        self.ins[eng].append(dict(fn=fn, deps=self._deps(r, w) + self.bar_deps.pop(eng, []), dma=semkey))
        self._commit(('d', semkey, c), r, w)

    SAME_DIST = 4

    def _skip_same(self, e, i, d, rec):
        if d[1] != e or rec['dma'] is not None:
            return False
        if e == 'pe':
            return True
        return (i - d[2]) > self.SAME_DIST

    def finalize(self):
        nc = self.nc
        need = {e: set() for e in self.ENGS}
        for e in self.ENGS:
            for i, rec in enumerate(self.ins[e]):
                for d in rec['deps']:
                    if d[0] == 'e' and not self._skip_same(e, i, d, rec):
                        need[d[1]].add(d[2])
        cum = {}
        for e in self.ENGS:
            c = 0
            arr = []
            for i in range(len(self.ins[e])):
                if i in need[e]:
                    c += 1
                arr.append(c)
            cum[e] = arr
        esem = {e: self.es.enter_context(nc.semaphore('se_' + e)) for e in self.ENGS}
        dsem = {}
        for i, k in enumerate(self.dma_cnt):
            dsem[k] = self.es.enter_context(nc.semaphore('sd_%d' % i))
        self.stats = {e: (len(self.ins[e]), cum[e][-1] if cum[e] else 0) for e in self.ENGS}
        self.stats['ndsem'] = len(dsem)
        with nc.Block() as block:
            for e in self.ENGS:
                def body(eng, e=e):
                    waited = {}
                    for i, rec in enumerate(self.ins[e]):
                        req = {}
                        for d in rec['deps']:
                            if d[0] == 'e':
                                if self._skip_same(e, i, d, rec):
                                    continue
                                key = ('e', d[1])
                                val = cum[d[1]][d[2]]
                            else:
                                key = ('d', d[1])
                                val = d[2]
                            if val > req.get(key, 0):
                                req[key] = val
                        for key, val in req.items():
                            if waited.get(key, 0) < val:
                                sem = esem[key[1]] if key[0] == 'e' else dsem[key[1]]
                                eng.wait_ge(sem, val)
                                waited[key] = val
                                if self.log is not None:
                                    self.log.append((e, i, 'wait', key, val))
                        if self.log is not None:
                            self.log.append((e, i, 'inst', rec['dma'], cum[e][i] if i in need[e] else None))
                        inst = rec['fn'](eng)
                        if rec['dma'] is not None:
                            inst.then_inc(dsem[rec['dma']], 16)
                        elif i in need[e]:
                            inst.then_inc(esem[e], 1)
                    if e == 'sp':
                        for k, c in self.dma_cnt.items():
                            eng.wait_ge(dsem[k], c)
                getattr(block, self.BLK[e])(body)
        self.es.close()


def I_mm(out, lhsT, rhs, start, stop):
    return lambda e: e.matmul(out, lhsT, rhs, start=start, stop=stop)


def I_act(out, in_, func, **kw):
    return lambda e: e.activation(out=out, in_=in_, func=func, **kw)


def I_tt(out, in0, in1, op):
    return lambda e: e.tensor_tensor(out=out, in0=in0, in1=in1, op=op)


def I_ts(out, in0, s1, s2, op0, op1=None):
    if op1 is None:
        return lambda e: e.tensor_scalar(out=out, in0=in0, scalar1=s1, scalar2=None, op0=op0)
    return lambda e: e.tensor_scalar(out=out, in0=in0, scalar1=s1, scalar2=s2, op0=op0, op1=op1)


def I_stt(out, in0, scalar, in1, op0, op1):
    return lambda e: e.scalar_tensor_tensor(out=out, in0=in0, scalar=scalar, in1=in1, op0=op0, op1=op1)


def I_recip(out, in_):
    return lambda e: e.reciprocal(out=out, in_=in_)


def I_copy(out, in_):
    return lambda e: e.tensor_copy(out=out, in_=in_)


def I_memset(ap, c):
    return lambda e: e.memset(ap, c)


def I_dma(out, in_):
    return lambda e: e.dma_start(out=out, in_=in_)


def I_scan(out, d0, d1, init):
    return lambda e: e.tensor_tensor_scan(out=out, data0=d0, data1=d1, initial=init, op0=ALU.mult, op1=ALU.add)


class Cx:
    def __init__(self, nc, arena=False):
        self.nc = nc
        self.p = Prog(nc)
        if arena:
            self.p.use_arena()
        self.banks = [self.p.ps('bank%d' % i, [128, 512]) for i in range(8)]
        self.bi = 0
        p = self.p
        self.ones = p.sb('ones', [128, 128], BF16)
        p.op('pool', I_memset(self.ones[:], 1.0), w=['ones'])
        self.sq = [p.sb('sq%d' % i, [128, 512], BF16) for i in range(2)]
        self.sqi = 0
        self.rstd = [p.sb('rstd%d' % i, [128, 512], F32) for i in range(2)]
        self.rsi = 0
        self._rot = {}
        self.epsc = p.sb('epsc', [128, 1], F32)
        p.op('pool', I_memset(self.epsc[:], EPS), w=['epsc'])
        self.mark = getattr(p, 'aoff', 0)

    def new_stage(self):
        self.p.barrier()
        self.p.aoff = self.mark
        self._rot = {}
        for nm in ('ws', 'wsi'):
            if hasattr(self, nm):
                delattr(self, nm)

    def bank(self):
        i = self.bi
        self.bi = (i + 1) % 8
        return self.banks[i], 'bank%d' % i

    def rot(self, name, shape, dt, n=2):
        if name not in self._rot:
            self._rot[name] = [[self.p.sb('%s_%d' % (name, i), shape, dt) for i in range(n)], 0]
        tl, i = self._rot[name]
        self._rot[name][1] = (i + 1) % n
        return tl[i], '%s_%d' % (name, i)

    def next_sq(self):
        i = self.sqi
        self.sqi = 1 - i
        return self.sq[i], 'sq%d' % i

    def next_rstd(self):
        i = self.rsi
        self.rsi = 1 - i
        return self.rstd[i], 'rstd%d' % i

    def rstd_from_bank(self, bank, bk, n, dim):
        p = self.p
        rs, rk = self.next_rstd()
        p.op('act', I_act(rs[:, :n], bank[:, :n], AF.Sqrt, scale=1.0 / dim, bias=self.epsc[:, 0:1]), r=[bk, 'epsc'], w=[rk])
        p.op('dve', I_recip(rs[:, :n], rs[:, :n]), r=[rk], w=[rk])
        return rs, rk


def rmsnorm(cx, src, skey, gcols, gkey, dst, dkey, C, n0, n, dim):
    p = cx.p
    bank, bk = cx.bank()
    for c in range(C):
        sq, sk = cx.next_sq()
        p.op('act', I_act(sq[:, :n], src[:, c, n0:n0 + n], AF.Square), r=[skey(c)], w=[sk])
        p.op('pe', I_mm(bank[:, :n], cx.ones[:], sq[:, :n], c == 0, c == C - 1), r=[sk, 'ones'], w=[bk])
    rs, rk = cx.rstd_from_bank(bank, bk, n, dim)
    for c in range(C):
        p.op('dve', I_stt(dst[:, c, n0:n0 + n], src[:, c, n0:n0 + n], gcols[:, c:c + 1], rs[:, :n], ALU.mult, ALU.mult),
             r=[skey(c), gkey, rk], w=[dkey(c)])


WS_N = 3


def wslab(cx, parts):
    p = cx.p
    if not hasattr(cx, 'ws'):
        cx.ws = [p.sb('ws%d' % i, [128, 4096], BF16) for i in range(WS_N)]
        cx.wsi = 0
    i = cx.wsi
    cx.wsi = (i + 1) % WS_N
    t = cx.ws[i]
    key = 'ws%d' % i
    for src, off in parts:
        K, N = src.shape
        kc = K // 128
        dst = t[:, off:off + kc * N].rearrange("p (k n) -> p k n", k=kc)
        p.dma('pool', I_dma(dst, src.rearrange("(k p) n -> p k n", p=128)), w=[key])
    return t, key


VC = dict(gx=0, gm=8, gl=16, gq=24, gk=26, gn=28, aq=36, ak=37, dsk=38)
NVEC = 48


def tail_body(cx, A, glu, tb0, ntb, kv_ready):
    p = cx.p
    hT = cx.hT
    hk = lambda c, tb: 'h%d_%d' % (c, tb)
    hn = cx.hn
    big2 = cx.big2
    vec = cx.vec
    NB = ntb

    if glu:
        for c in range(NCH):
            for tb in range(NB):
                yt, yk = cx.rot('ytmp', [128, 512], F32)
                col0 = (tb0 + tb) * 512
                p.dma('sp', I_dma(yt[:], A['yT'][c * 128:(c + 1) * 128, col0:col0 + 512]), w=[yk])
                p.op('act', I_act(hn[:, c, tb * 512:(tb + 1) * 512], yt[:], AF.Gelu_apprx_tanh), r=[yk], w=['hn%d' % tb])
        for ns in range(2):
            wa, wak = wslab(cx, [(A['wglu'][:, ns * 512:(ns + 1) * 512], 0)])
            wb, wbk = wslab(cx, [(A['wglu'][:, 1024 + ns * 512:1024 + (ns + 1) * 512], 0)])
            for j in range(4):
                n = ns * 4 + j
                for tb in range(NB):
                    ba, bak = cx.bank()
                    bb, bbk = cx.bank()
                    for kc in range(NCH):
                        p.op('pe', I_mm(ba[:], wa[:, kc * 512 + j * 128: kc * 512 + (j + 1) * 128],
                                        hn[:, kc, tb * 512:(tb + 1) * 512], kc == 0, kc == NCH - 1),
                             r=[wak, 'hn%d' % tb], w=[bak])
                    for kc in range(NCH):
                        p.op('pe', I_mm(bb[:], wb[:, kc * 512 + j * 128: kc * 512 + (j + 1) * 128],
                                        hn[:, kc, tb * 512:(tb + 1) * 512], kc == 0, kc == NCH - 1),
                             r=[wbk, 'hn%d' % tb], w=[bbk])
                    sg, sgk = cx.rot('sg', [128, 512], F32)
                    p.op('act', I_act(sg[:], bb[:], AF.Sigmoid), r=[bbk], w=[sgk])
                    gt, gtk = cx.rot('gtmp', [128, 512], F32)
                    p.op('dve', I_tt(gt[:], ba[:], sg[:], ALU.mult), r=[bak, sgk], w=[gtk])
                    hs = hT[:, n, (tb0 + tb) * 512:(tb0 + tb + 1) * 512]
                    p.op('pool', I_tt(hs, hs, gt[:], ALU.add), r=[gtk, hk(n, tb0 + tb)], w=[hk(n, tb0 + tb)])

    if not kv_ready:
        kraw = cx.kraw
        memn = cx.memn
        for c in range(NCH):
            p.dma('sp', I_dma(kraw[:, c, :], A['memT'][c * 128:(c + 1) * 128, :]), w=['kraw'], semkey='kraw_ld')
        rmsnorm(cx, kraw, lambda c: 'kraw', vec[:, VC['gm']:VC['gm'] + 8], 'vec', memn, lambda c: 'memn', NCH, 0, MEMLEN, D)
        for hp in range(2):
            wk, wkk = wslab(cx, [(A['wkv'][:, hp * 512:(hp + 1) * 512], 0)])
            for jj in range(4):
                j = hp * 4 + jj
                bk_, bkk = cx.bank()
                for kc in range(NCH):
                    p.op('pe', I_mm(bk_[:, :MEMLEN], wk[:, kc * 512 + jj * 128: kc * 512 + (jj + 1) * 128], memn[:, kc, :],
                                    kc == 0, kc == NCH - 1), r=[wkk, 'memn'], w=[bkk])
                p.op('act', I_act(kraw[:, j, :], bk_[:, :MEMLEN], AF.Copy), r=[bkk], w=['kraw'])
        for h in range(4):
            bs, bsk = cx.bank()
            for ec in range(2):
                sq, sk = cx.next_sq()
                p.op('act', I_act(sq[:, :MEMLEN], kraw[:, 2 * h + ec, :], AF.Square), r=['kraw'], w=[sk])
                p.op('pe', I_mm(bs[:, :MEMLEN], cx.ones[:], sq[:, :MEMLEN], ec == 0, ec == 1), r=[sk, 'ones'], w=[bsk])
            rs, rk = cx.rstd_from_bank(bs, bsk, MEMLEN, 256)
            for ec in range(2):
                p.op('dve', I_stt(cx.KT[:, 2 * h + ec, :], kraw[:, 2 * h + ec, :], vec[:, VC['gk'] + ec:VC['gk'] + ec + 1],
                                  rs[:, :MEMLEN], ALU.mult, ALU.mult), r=['kraw', 'vec', rk], w=['KT'])
        for vs in range(2):
            wv, wvk = wslab(cx, [(A['wkv'][:, 1024 + vs * 512:1024 + (vs + 1) * 512], 0)])
            for mc in range(2):
                bv, bvk = cx.bank()
                for kc in range(NCH):
                    p.op('pe', I_mm(bv[:], memn[:, kc, mc * 128:(mc + 1) * 128], wv[:, kc * 512:(kc + 1) * 512],
                                    kc == 0, kc == NCH - 1), r=[wvk, 'memn'], w=[bvk])
                p.op('act', I_act(cx.V[:, mc, vs * 512:(vs + 1) * 512], bv[:], AF.Copy), r=[bvk], w=['V'])

    for tb in range(NB):
        _rmsnorm_off(cx, hT, (tb0 + tb) * 512, lambda c, tb=tb: hk(c, tb0 + tb), vec[:, VC['gx']:VC['gx'] + 8],
                     hn, tb * 512, 'hn%d' % tb)
    for hp in range(2):
        wq, wqk = wslab(cx, [(A['wq'][:, hp * 512:(hp + 1) * 512], 0)])
        for hh in range(2):
            h = 2 * hp + hh
            for tb in range(NB):
                qb = []
                for ec in range(2):
                    b, bk_ = cx.bank()
                    cc = hh * 2 + ec
                    for kc in range(NCH):
                        p.op('pe', I_mm(b[:], wq[:, kc * 512 + cc * 128: kc * 512 + (cc + 1) * 128],
                                        hn[:, kc, tb * 512:(tb + 1) * 512], kc == 0, kc == NCH - 1),
                             r=[wqk, 'hn%d' % tb], w=[bk_])
                    qb.append((b, bk_))
                bs, bsk = cx.bank()
                for ec in range(2):
                    sq, sk = cx.next_sq()
                    p.op('act', I_act(sq[:], qb[ec][0][:], AF.Square), r=[qb[ec][1]], w=[sk])
                    p.op('pe', I_mm(bs[:], cx.ones[:], sq[:], ec == 0, ec == 1), r=[sk, 'ones'], w=[bsk])
                rs, rk = cx.rstd_from_bank(bs, bsk, 512, 256)
                qn, qnk = cx.rot('qn', [128, 2, 512], BF16)
                for ec in range(2):
                    p.op('dve', I_stt(qn[:, ec, :], qb[ec][0][:], vec[:, VC['gq'] + ec:VC['gq'] + ec + 1], rs[:],
                                      ALU.mult, ALU.mult), r=[qb[ec][1], 'vec', rk], w=[qnk])
                PT, ptk = cx.rot('PT', [128, 2, 512], BF16)
                for mc in range(2):
                    bl, blk = cx.bank()
                    for ec in range(2):
                        p.op('pe', I_mm(bl[:], cx.KT[:, 2 * h + ec, mc * 128:(mc + 1) * 128], qn[:, ec, :], ec == 0, ec == 1),
                             r=['KT', qnk], w=[blk])
                    p.op('act', I_act(PT[:, mc, :], bl[:], AF.Exp, scale=1.0 / 16.0), r=[blk], w=[ptk])
                bd, bdk = cx.bank()
                for mc in range(2):
                    p.op('pe', I_mm(bd[:], cx.ones[:], PT[:, mc, :], mc == 0, mc == 1), r=['ones', ptk], w=[bdk])
                rd, rdk = cx.rot('rden', [128, 512], F32)
                p.op('dve', I_recip(rd[:], bd[:]), r=[bdk], w=[rdk])
                for ec in range(2):
                    bo, bok = cx.bank()
                    for mc in range(2):
                        p.op('pe', I_mm(bo[:], cx.V[:, mc, h * 256 + ec * 128: h * 256 + (ec + 1) * 128], PT[:, mc, :],
                                        mc == 0, mc == 1), r=['V', ptk], w=[bok])
                    p.op('dve', I_tt(big2[:, 2 * h + ec, tb * 512:(tb + 1) * 512], bo[:], rd[:], ALU.mult),
                         r=[bok, rdk], w=['big2_%d' % tb])
    for ns in range(2):
        wo, wok = wslab(cx, [(A['wo'][:, ns * 512:(ns + 1) * 512], 0)])
        for j in range(4):
            n = ns * 4 + j
            for tb in range(NB):
                b, bk_ = cx.bank()
                for kc in range(NCH):
                    p.op('pe', I_mm(b[:], wo[:, kc * 512 + j * 128: kc * 512 + (j + 1) * 128],
                                    big2[:, kc, tb * 512:(tb + 1) * 512], kc == 0, kc == NCH - 1),
                         r=[wok, 'big2_%d' % tb], w=[bk_])
                hs = hT[:, n, (tb0 + tb) * 512:(tb0 + tb + 1) * 512]
                p.op('dve', I_tt(hs, b[:], hs, ALU.add), r=[bk_, hk(n, tb0 + tb)], w=[hk(n, tb0 + tb)])

    for tb in range(NB):
        _rmsnorm_off(cx, hT, (tb0 + tb) * 512, lambda c, tb=tb: hk(c, tb0 + tb), vec[:, VC['gl']:VC['gl'] + 8],
                     hn, tb * 512, 'hn%d' % tb)
    for s in range(DFF // 256):
        ws, wsk = wslab(cx, [(A['w1'][:, s * 256:(s + 1) * 256], 0), (A['w2'][s * 256:(s + 1) * 256, :], 2048)])
        hb = s % 2
        for j in range(2):
            for tb in range(NB):
                b, bk_ = cx.bank()
                for kc in range(NCH):
                    p.op('pe', I_mm(b[:], ws[:, kc * 256 + j * 128: kc * 256 + (j + 1) * 128],
                                    hn[:, kc, tb * 512:(tb + 1) * 512], kc == 0, kc == NCH - 1),
                         r=[wsk, 'hn%d' % tb], w=[bk_])
                rt, rtk = cx.rot('rtmp', [128, 512], F32)
                p.op('act', I_act(rt[:], b[:], AF.Relu), r=[bk_], w=[rtk])
                p.op('pool', I_tt(big2[:, hb * 2 + j, tb * 512:(tb + 1) * 512], rt[:], rt[:], ALU.mult),
                     r=[rtk], w=['hid%d' % hb])
        for n in range(NCH):
            for tb in range(NB):
                b, bk_ = cx.bank()
                for j in range(2):
                    p.op('pe', I_mm(b[:], ws[:, 2048 + j * 1024 + n * 128: 2048 + j * 1024 + (n + 1) * 128],
                                    big2[:, hb * 2 + j, tb * 512:(tb + 1) * 512], j == 0, j == 1),
                         r=[wsk, 'hid%d' % hb], w=[bk_])
                hs = hT[:, n, (tb0 + tb) * 512:(tb0 + tb + 1) * 512]
                p.op('dve', I_tt(hs, b[:], hs, ALU.add), r=[bk_, hk(n, tb0 + tb)], w=[hk(n, tb0 + tb)])


def _rmsnorm_off(cx, src, s0, skey, gcols, dst, d0, dkey, n=512, C=NCH, dim=D):
    p = cx.p
    bank, bk = cx.bank()
    for c in range(C):
        sq, sk = cx.next_sq()
        p.op('act', I_act(sq[:, :n], src[:, c, s0:s0 + n], AF.Square), r=[skey(c)], w=[sk])
        p.op('pe', I_mm(bank[:, :n], cx.ones[:], sq[:, :n], c == 0, c == C - 1), r=[sk, 'ones'], w=[bk])
    rs, rk = cx.rstd_from_bank(bank, bk, n, dim)
    for c in range(C):
        p.op('dve', I_stt(dst[:, c, d0:d0 + n], src[:, c, s0:s0 + n], gcols[:, c:c + 1], rs[:, :n], ALU.mult, ALU.mult),
             r=[skey(c), 'vec', rk], w=[dkey])


def common_tiles(cx, A):
    p = cx.p
    cx.hT = p.sb('hT', [128, NCH, NT], F32)
    cx.hn = p.sb('hn', [128, NCH, 1024], BF16)
    cx.big2 = p.sb('big2', [128, NCH, 1024], BF16)
    cx.vec = p.sb('vec', [128, NVEC], F32)
    cx.kraw = p.sb('kraw', [128, NCH, MEMLEN], F32)
    cx.memn = p.sb('memn', [128, NCH, MEMLEN], BF16)
    cx.KT = p.sb('KT', [128, NCH, MEMLEN], BF16)
    cx.V = p.sb('V', [128, 2, D], BF16)
    p.dma('sp', I_dma(cx.vec[:], A['vecs'][:, :]), w=['vec'])


def load_hT(cx, src):
    p = cx.p
    for c in range(NCH):
        p.dma('sp', I_dma(cx.hT[:, c, :], src[c * 128:(c + 1) * 128, :]),
              w=['h%d_%d' % (c, tb) for tb in range(NT // 512)], semkey='hld%d' % c)


def store_hT(cx, dst):
    p = cx.p
    for c in range(NCH):
        p.dma('sp', I_dma(dst[c * 128:(c + 1) * 128, :], cx.hT[:, c, :]),
              r=['h%d_%d' % (c, tb) for tb in range(NT // 512)], w=['hout%d' % c])


def build_tail(glu, emit_hn, arena=False):
    nc = bass.Bass("TRN2", target_bir_lowering=False)
    A = {}

    def inp(name, shape, dt=F32):
        A[name] = nc.dram_tensor(name, list(shape), dt, kind="ExternalInput").ap()

    inp('hT', [D, NT])
    inp('memT', [D, MEMLEN])
    inp('vecs', [128, NVEC])
    inp('wq', [D, D])
    inp('wkv', [D, 2 * D])
    inp('wo', [D, D])
    inp('w1', [D, DFF])
    inp('w2', [DFF, D])
    if glu:
        inp('yT', [D, NT])
        inp('wglu', [D, 2 * D])
    A['hT_out'] = nc.dram_tensor('hT_out', [D, NT], F32, kind="ExternalOutput").ap()
    if emit_hn:
        A['hn_out'] = nc.dram_tensor('hn_out', [D, NT], F32, kind="ExternalOutput").ap()
    cx = Cx(nc, arena=arena)
    if arena:
        cx.new_stage()
    common_tiles(cx, A)
    load_hT(cx, A['hT'])
    for half in range(2):
        tail_body(cx, A, glu, half * 2, 2, kv_ready=(half == 1))
    store_hT(cx, A['hT_out'])
    if emit_hn:
        emit_norm(cx, A['hn_out'])
    cx.p.finalize()
    return nc, cx


def emit_norm(cx, dst):
    p = cx.p
    for tb in range(NT // 512):
        bank, bk = cx.bank()
        for c in range(NCH):
            sq, sk = cx.next_sq()
            p.op('act', I_act(sq[:], cx.hT[:, c, tb * 512:(tb + 1) * 512], AF.Square), r=['h%d_%d' % (c, tb)], w=[sk])
            p.op('pe', I_mm(bank[:], cx.ones[:], sq[:], c == 0, c == NCH - 1), r=[sk, 'ones'], w=[bk])
        rs, rk = cx.rstd_from_bank(bank, bk, 512, D)
        for c in range(NCH):
            ot, otk = cx.rot('ntmp', [128, 512], F32, n=3)
            p.op('dve', I_stt(ot[:], cx.hT[:, c, tb * 512:(tb + 1) * 512], cx.vec[:, VC['gn'] + c:VC['gn'] + c + 1], rs[:],
                              ALU.mult, ALU.mult), r=['h%d_%d' % (c, tb), 'vec', rk], w=[otk])
            p.dma('sp', I_dma(dst[c * 128:(c + 1) * 128, tb * 512:(tb + 1) * 512], ot[:]), r=[otk], w=['hnout'])


NPT = 16
SW = 512
NW = SEQ // SW
PI = math.pi


def s5_params(cx, A):
    p = cx.p
    NCOL = 2 * NPT
    T = {}
    for nm in ['lre', 'lim', 'ldt', 'dt', 'mag', 'ang', 'angc', 's1', 'c1', 'are', 'aim', 'nr', 'den', 't', 't2',
               'fre', 'fim', 'nfre', 'nfim']:
        T[nm] = p.sb('sp_' + nm, [128, NCOL], F32)
    k = 's5par'
    p.dma('sp', I_dma(T['lre'][:], A['lamre'][:, :]), w=[k], semkey='s5par_ld')
    p.dma('sp', I_dma(T['lim'][:], A['lamim'][:, :]), w=[k], semkey='s5par_ld')
    p.dma('sp', I_dma(T['ldt'][:], A['logdt'][:, :]), w=[k], semkey='s5par_ld')
    a = lambda n: T[n][:]
    p.op('act', I_act(a('dt'), a('ldt'), AF.Exp), r=[k], w=[k])
    p.op('dve', I_tt(a('t'), a('lre'), a('dt'), ALU.mult), r=[k], w=[k])
    p.op('act', I_act(a('mag'), a('t'), AF.Exp), r=[k], w=[k])
    p.op('dve', I_tt(a('ang'), a('lim'), a('dt'), ALU.mult), r=[k], w=[k])
    for _ in range(5):
        p.op('dve', I_ts(a('t'), a('ang'), PI, 2 * PI, ALU.is_gt, ALU.mult), r=[k], w=[k])
        p.op('dve', I_tt(a('ang'), a('ang'), a('t'), ALU.subtract), r=[k], w=[k])
    p.op('dve', I_ts(a('angc'), a('ang'), PI / 2, None, ALU.add), r=[k], w=[k])
    p.op('dve', I_ts(a('t'), a('angc'), PI, 2 * PI, ALU.is_gt, ALU.mult), r=[k], w=[k])
    p.op('dve', I_tt(a('angc'), a('angc'), a('t'), ALU.subtract), r=[k], w=[k])
    p.op('act', I_act(a('s1'), a('ang'), AF.Sin), r=[k], w=[k])
    p.op('act', I_act(a('c1'), a('angc'), AF.Sin), r=[k], w=[k])
    p.op('dve', I_tt(a('are'), a('mag'), a('c1'), ALU.mult), r=[k], w=[k])
    p.op('dve', I_tt(a('aim'), a('mag'), a('s1'), ALU.mult), r=[k], w=[k])
    p.op('dve', I_ts(a('nr'), a('are'), -1.0, None, ALU.add), r=[k], w=[k])
    p.op('dve', I_tt(a('den'), a('lre'), a('lre'), ALU.mult), r=[k], w=[k])
    p.op('dve', I_tt(a('t'), a('lim'), a('lim'), ALU.mult), r=[k], w=[k])
    p.op('dve', I_tt(a('den'), a('den'), a('t'), ALU.add), r=[k], w=[k])
    p.op('dve', I_recip(a('den'), a('den')), r=[k], w=[k])
    p.op('dve', I_tt(a('t'), a('nr'), a('lre'), ALU.mult), r=[k], w=[k])
    p.op('dve', I_tt(a('t2'), a('aim'), a('lim'), ALU.mult), r=[k], w=[k])
    p.op('dve', I_tt(a('t'), a('t'), a('t2'), ALU.add), r=[k], w=[k])
    p.op('dve', I_tt(a('fre'), a('t'), a('den'), ALU.mult), r=[k], w=[k])
    p.op('dve', I_tt(a('t'), a('aim'), a('lre'), ALU.mult), r=[k], w=[k])
    p.op('dve', I_tt(a('t2'), a('nr'), a('lim'), ALU.mult), r=[k], w=[k])
    p.op('dve', I_tt(a('t'), a('t'), a('t2'), ALU.subtract), r=[k], w=[k])
    p.op('dve', I_tt(a('fim'), a('t'), a('den'), ALU.mult), r=[k], w=[k])
    p.op('dve', I_ts(a('nfre'), a('fre'), -1.0, None, ALU.mult), r=[k], w=[k])
    p.op('dve', I_ts(a('nfim'), a('fim'), -1.0, None, ALU.mult), r=[k], w=[k])
    return T


def s5_body(cx, A):
    p = cx.p
    T = s5_params(cx, A)
    PK = 's5par'
    if 'dbg' in A:
        for i, nm in enumerate(['dt', 'mag', 'ang', 's1', 'c1', 'fre', 'fim', 'den']):
            p.dma('sp', I_dma(A['dbg'][:, i * 2 * NPT:(i + 1) * 2 * NPT], T[nm][:]), r=[PK], w=['dbgo'])
    ub = p.sb('ub', [128, 4, SEQ], BF16)
    for ck in range(4):
        p.dma('pool', I_dma(ub[:, ck, :], A['uT'][ck * 128:(ck + 1) * 128, :]), w=['ub%d' % ck])
    ident = p.sb('ident', [128, 128], F32)
    p.dma('sp', I_dma(ident[:], A['ident'][:, :]), w=['ident'])
    dsk = p.sb('dskc', [128, 4], F32)
    p.dma('sp', I_dma(dsk[:], A['dsk'][:, :]), w=['dskc'])
    yacc = [p.sb('yacc%d' % i, [128, SEQ], F32) for i in range(2)]
    bb_i = [0]

    def bbank():
        i = bb_i[0]
        bb_i[0] = (i + 1) % 6
        return cx.banks[i], 'bank%d' % i
    yb_i = [0]

    def ybank():
        i = 6 + yb_i[0]
        yb_i[0] = 1 - yb_i[0]
        return cx.banks[i], 'bank%d' % i

    for ck in range(4):
        ya = yacc[ck % 2]
        yk = 'yacc%d' % (ck % 2)
        dD, dDk = cx.rot('diagD', [128, 128], BF16)
        p.op('dve', I_ts(dD[:], ident[:], dsk[:, ck:ck + 1], None, ALU.mult), r=['ident', 'dskc'], w=[dDk])
        for d in range(2):
            tabs = []
            for q in range(4):
                pt = ck * 4 + q
                col = d * NPT + pt
                cosT, ck_ = cx.rot('cosT', [128, SW], F32, n=8)
                sinT, sk_ = cx.rot('sinT', [128, SW], F32, n=8)
                pw, pwk = cx.rot('pw', [128, 2, 12], F32, n=8)
                tk = 'tab%d' % ((d * 4 + q) % 8)
                p.op('dve', I_memset(cosT[:, 0:1], 1.0), w=[tk])
                p.op('dve', I_memset(sinT[:, 0:1], 0.0), w=[tk])
                p.op('dve', I_copy(pw[:, 0, 0:1], T['c1'][:, col:col + 1]), r=[PK], w=[tk])
                p.op('dve', I_copy(pw[:, 1, 0:1], T['s1'][:, col:col + 1]), r=[PK], w=[tk])
                L = 1
                lv = 0
                while L < SW:
                    pc = pw[:, 0, lv:lv + 1]
                    ps_ = pw[:, 1, lv:lv + 1]
                    tmp, tmk = cx.rot('tbtmp', [128, SW // 2], F32, n=2)
                    p.op('dve', I_ts(tmp[:, :L], sinT[:, 0:L], ps_, None, ALU.mult), r=[tk], w=[tmk])
                    p.op('dve', I_stt(cosT[:, L:2 * L], cosT[:, 0:L], pc, tmp[:, :L], ALU.mult, ALU.subtract), r=[tk, tmk], w=[tk])
                    tmp2, tmk2 = cx.rot('tbtmp', [128, SW // 2], F32, n=2)
                    p.op('dve', I_ts(tmp2[:, :L], cosT[:, 0:L], ps_, None, ALU.mult), r=[tk], w=[tmk2])
                    p.op('dve', I_stt(sinT[:, L:2 * L], sinT[:, 0:L], pc, tmp2[:, :L], ALU.mult, ALU.add), r=[tk, tmk2], w=[tk])
                    p.op('dve', I_tt(pw[:, 0, 11:12], ps_, ps_, ALU.mult), r=[tk], w=[tk])
                    p.op('dve', I_stt(pw[:, 0, lv + 1:lv + 2], pc, pc, pw[:, 0, 11:12], ALU.mult, ALU.subtract), r=[tk], w=[tk])
                    p.op('dve', I_stt(pw[:, 1, lv + 1:lv + 2], pc, 2.0, ps_, ALU.mult, ALU.mult), r=[tk], w=[tk])
                    L *= 2
                    lv += 1
                cW = pw[:, 0, lv:lv + 1]
                sW = pw[:, 1, lv:lv + 1]
                if 'dbg2' in A and ck == 0 and d == 0 and q == 0:
                    p.dma('sp', I_dma(A['dbg2'][:, 0:SW], cosT[:]), r=[tk], w=['dbgo2'])
                    p.dma('sp', I_dma(A['dbg2'][:, SW:2 * SW], sinT[:]), r=[tk], w=['dbgo2'])
                braw, brk = cx.rot('bw', [128, 2, 128], BF16, n=8)
                p.dma('pool', I_dma(braw[:, 0, :], A['Bre'][d, pt]), w=[brk])
                p.dma('pool', I_dma(braw[:, 1, :], A['Bim'][d, pt]), w=[brk])
                craw, crk = cx.rot('craw', [128, 2, 128], F32, n=4)
                p.dma('sp', I_dma(craw[:, 0, :], A['CR'][d, pt]), w=[crk])
                p.dma('sp', I_dma(craw[:, 1, :], A['CI'][d, pt]), w=[crk])
                cw, cwk = cx.rot('cw', [128, 2, 128], BF16, n=8)
                ctmp, ctk = cx.rot('ctmp', [128, 128], F32, n=2)
                fre = T['fre'][:, col:col + 1]
                nfim = T['nfim'][:, col:col + 1]
                nfre = T['nfre'][:, col:col + 1]
                p.op('dve', I_ts(ctmp[:], craw[:, 1, :], nfim, None, ALU.mult), r=[crk, PK], w=[ctk])
                p.op('dve', I_stt(cw[:, 0, :], craw[:, 0, :], fre, ctmp[:], ALU.mult, ALU.add), r=[crk, PK, ctk], w=[cwk])
                ctmp2, ctk2 = cx.rot('ctmp', [128, 128], F32, n=2)
                p.op('dve', I_ts(ctmp2[:], craw[:, 0, :], nfim, None, ALU.mult), r=[crk, PK], w=[ctk2])
                p.op('dve', I_stt(cw[:, 1, :], craw[:, 1, :], nfre, ctmp2[:], ALU.mult, ALU.add), r=[crk, PK, ctk2], w=[cwk])
                car, cak = cx.rot('carry', [128, 8], F32, n=8)
                tabs.append(dict(cos=cosT, sin=sinT, tk=tk, cW=cW, sW=sW, braw=braw, brk=brk, cw=cw, cwk=cwk,
                                 r=T['mag'][:, col:col + 1], car=car, cak=cak))
            worder = range(NW) if d == 0 else range(NW - 1, -1, -1)
            rv = (lambda ap: ap) if d == 0 else (lambda ap: ap[:, ::-1])
            for wi, w in enumerate(worder):
                win = slice(w * SW, (w + 1) * SW)
                yb, ybk = ybank()
                for q in range(4):
                    tb_ = tabs[q]
                    cosT, sinT, tk = tb_['cos'], tb_['sin'], tb_['tk']
                    bre, brek = bbank()
                    bim, bimk = bbank()
                    p.op('pe', I_mm(bre[:], tb_['braw'][:, 0, :], ub[:, ck, win], True, True), r=[tb_['brk'], 'ub%d' % ck], w=[brek])
                    p.op('pe', I_mm(bim[:], tb_['braw'][:, 1, :], ub[:, ck, win], True, True), r=[tb_['brk'], 'ub%d' % ck], w=[bimk])
                    t1, t1k = cx.rot('t1', [128, SW], F32)
                    t2, t2k = cx.rot('t2', [128, SW], F32)
                    t3, t3k = cx.rot('t3', [128, SW], F32)
                    t4, t4k = cx.rot('t4', [128, SW], F32)
                    p.op('dve', I_tt(t1[:], rv(bre[:]), cosT[:], ALU.mult), r=[brek, tk], w=[t1k])
                    p.op('dve', I_tt(t2[:], rv(bim[:]), sinT[:], ALU.mult), r=[bimk, tk], w=[t2k])
                    p.op('dve', I_tt(t3[:], rv(bim[:]), cosT[:], ALU.mult), r=[bimk, tk], w=[t3k])
                    p.op('dve', I_tt(t4[:], rv(bre[:]), sinT[:], ALU.mult), r=[brek, tk], w=[t4k])
                    wre, wrk = cx.rot('wre', [128, SW], F32)
                    wim, wik = cx.rot('wim', [128, SW], F32)
                    p.op('pool', I_tt(wre[:], t1[:], t2[:], ALU.add), r=[t1k, t2k], w=[wrk])
                    p.op('pool', I_tt(wim[:], t3[:], t4[:], ALU.subtract), r=[t3k, t4k], w=[wik])
                    zre, zrk = cx.rot('zre', [128, SW], F32)
                    zim, zik = cx.rot('zim', [128, SW], F32)
                    car, cak = tb_['car'], tb_['cak']
                    rbc = tb_['r'].to_broadcast([128, SW])
                    if wi == 0:
                        ire, iim = 0.0, 0.0
                    else:
                        ire, iim = car[:, 2:3], car[:, 3:4]
                    p.op('dve', I_scan(zre[:], rbc, wre[:], ire), r=[PK, wrk, cak], w=[zrk])
                    p.op('dve', I_scan(zim[:], rbc, wim[:], iim), r=[PK, wik, cak], w=[zik])
                    if wi < NW - 1:
                        p.op('dve', I_tt(car[:, 0:1], zim[:, SW - 1:SW], tb_['sW'], ALU.mult), r=[zik, tk], w=[cak])
                        p.op('dve', I_tt(car[:, 1:2], zre[:, SW - 1:SW], tb_['sW'], ALU.mult), r=[zrk, tk], w=[cak])
                        p.op('dve', I_stt(car[:, 2:3], zre[:, SW - 1:SW], tb_['cW'], car[:, 0:1], ALU.mult, ALU.subtract), r=[zrk, tk], w=[cak])
                        p.op('dve', I_stt(car[:, 3:4], zim[:, SW - 1:SW], tb_['cW'], car[:, 1:2], ALU.mult, ALU.add), r=[zik, tk], w=[cak])
                    u1, u1k = cx.rot('u1', [128, SW], F32)
                    u2, u2k = cx.rot('u2', [128, SW], F32)
                    u3, u3k = cx.rot('u3', [128, SW], F32)
                    u4, u4k = cx.rot('u4', [128, SW], F32)
                    p.op('pool', I_tt(u1[:], zre[:], cosT[:], ALU.mult), r=[zrk, tk], w=[u1k])
                    p.op('pool', I_tt(u2[:], zim[:], sinT[:], ALU.mult), r=[zik, tk], w=[u2k])
                    p.op('pool', I_tt(u3[:], zim[:], cosT[:], ALU.mult), r=[zik, tk], w=[u3k])
                    p.op('pool', I_tt(u4[:], zre[:], sinT[:], ALU.mult), r=[zrk, tk], w=[u4k])
                    xb, xbk = cx.rot('xb', [128, 2, SW], BF16, n=3)
                    p.op('dve', I_tt(rv(xb[:, 0, :]), u1[:], u2[:], ALU.subtract), r=[u1k, u2k], w=[xbk])
                    p.op('dve', I_tt(rv(xb[:, 1, :]), u3[:], u4[:], ALU.add), r=[u3k, u4k], w=[xbk])
                    first = (q == 0)
                    last = (q == 3) and d == 1
                    p.op('pe', I_mm(yb[:], tb_['cw'][:, 0, :], xb[:, 0, :], first, False), r=[tb_['cwk'], xbk], w=[ybk])
                    p.op('pe', I_mm(yb[:], tb_['cw'][:, 1, :], xb[:, 1, :], False, last), r=[tb_['cwk'], xbk], w=[ybk])
                if d == 0:
                    p.op('pe', I_mm(yb[:], dD[:], ub[:, ck, win], False, True), r=[dDk, 'ub%d' % ck], w=[ybk])
                    p.op('act', I_act(ya[:, win], yb[:], AF.Copy), r=[ybk], w=[yk + '_%d' % w])
                else:
                    p.op('dve', I_tt(ya[:, win], yb[:], ya[:, win], ALU.add), r=[ybk, yk + '_%d' % w], w=[yk + '_%d' % w])
        p.dma('sp', I_dma(A['yT'][ck * 128:(ck + 1) * 128, :], ya[:]), r=[yk + '_%d' % w for w in range(NW)], w=['yout%d' % ck])


def build_s5(debug=False, arena=False):
    nc = bass.Bass("TRN2", target_bir_lowering=False)
    A = {}

    def inp(name, shape, dt=F32):
        A[name] = nc.dram_tensor(name, list(shape), dt, kind="ExternalInput").ap()
    inp('uT', [512, SEQ])
    inp('Bre', [2, NPT, 128, 128])
    inp('Bim', [2, NPT, 128, 128])
    inp('CR', [2, NPT, 128, 128])
    inp('CI', [2, NPT, 128, 128])
    inp('lamre', [128, 2 * NPT])
    inp('lamim', [128, 2 * NPT])
    inp('logdt', [128, 2 * NPT])
    inp('dsk', [128, 4])
    inp('ident', [128, 128])
    A['yT'] = nc.dram_tensor('yT', [512, SEQ], F32, kind="ExternalOutput").ap()
    cx = Cx(nc, arena=arena)
    if arena:
        cx.new_stage()
    if debug:
        A['dbg'] = nc.dram_tensor('dbg', [128, 8 * 2 * NPT], F32, kind="ExternalOutput").ap()
        A['dbg2'] = nc.dram_tensor('dbg2', [128, 2 * SW], F32, kind="ExternalOutput").ap()
    s5_body(cx, A)
    cx.p.finalize()
    return nc, cx


def s5_host_inputs(inp, j, half):
    g0 = 32 * half
    Bre = np.zeros((2, NPT, 128, 128), np.float32)
    Bim = np.zeros_like(Bre)
    CR = np.zeros_like(Bre)
    CI = np.zeros_like(Bre)
    lamre = np.zeros((128, 2 * NPT), np.float32)
    lamim = np.zeros_like(lamre)
    logdt = np.zeros_like(lamre)
    for d in range(2):
        for pt in range(NPT):
            for gl in range(2):
                g = g0 + 2 * pt + gl
                r0 = (pt % 4) * 32 + gl * 16
                Bre[d, pt, r0:r0 + 16, gl * 64:(gl + 1) * 64] = inp['s5_b_re'][j, d, g].T
                Bim[d, pt, r0:r0 + 16, gl * 64:(gl + 1) * 64] = inp['s5_b_im'][j, d, g].T
                CR[d, pt, gl * 64:(gl + 1) * 64, r0:r0 + 16] = inp['s5_c_re'][j, d, g].T
                CI[d, pt, gl * 64:(gl + 1) * 64, r0:r0 + 16] = inp['s5_c_im'][j, d, g].T
                lamre[gl * 64:(gl + 1) * 64, d * NPT + pt] = inp['s5_lambda_re'][j, d, g]
                lamim[gl * 64:(gl + 1) * 64, d * NPT + pt] = inp['s5_lambda_im'][j, d, g]
                logdt[gl * 64:(gl + 1) * 64, d * NPT + pt] = inp['s5_log_dt'][j, d, g]
    dsk = np.ascontiguousarray(inp['s5_d'][j, 512 * half:512 * half + 512].reshape(4, 128).T)
    return dict(Bre=Bre, Bim=Bim, CR=CR, CI=CI, lamre=lamre, lamim=lamim, logdt=logdt, dsk=dsk,
                ident=np.eye(128, dtype=np.float32))


NEXT = 3072
GRP = [(1, 2048), (4, 512), (16, 128)]
ASCALE = 128 ** -0.5


def sub_view(ap2d, d):
    if d == 1:
        return ap2d.rearrange("p (d i) -> p d i", d=1)
    return ap2d.rearrange("p (i d) -> p d i", d=d)


def attn_body(cx, A, flip=False, src=None, dst=None):
    p = cx.p
    vec = cx.vec

    def _cols(tb):
        if src is None or not flip:
            return slice(tb * 512, (tb + 1) * 512)
        return slice(SEQ - (tb + 1) * 512, SEQ - tb * 512)
    _src = A['hT_ext'] if src is None else src
    _dst = A['hT_out'] if dst is None else dst
    rvf = (lambda ap: ap[:, ::-1]) if (flip and src is not None) else (lambda ap: ap)
    hn = p.sb('hnx', [128, NCH, NEXT], BF16)
    mT = p.sb('mT', [128, NCH, NT], BF16)
    num = p.sb('numacc', [128, NT], F32)
    den = p.sb('denacc', [128, NT], F32)
    for tb in range(NEXT // 512):
        bank, bk = cx.bank()
        for c in range(NCH):
            xt, xk = cx.rot('xin', [128, 512], F32, n=3)
            p.dma('sp', I_dma(xt[:], _src[c * 128:(c + 1) * 128, _cols(tb)]), w=[xk])
            sq, sk = cx.next_sq()
            p.op('act', I_act(sq[:], xt[:], AF.Square), r=[xk], w=[sk])
            p.op('pe', I_mm(bank[:], cx.ones[:], sq[:], c == 0, c == NCH - 1), r=[sk, 'ones'], w=[bk])
        rs, rk = cx.rstd_from_bank(bank, bk, 512, D)
        for c in range(NCH):
            xt, xk = cx.rot('xin', [128, 512], F32, n=3)
            p.dma('sp', I_dma(xt[:], _src[c * 128:(c + 1) * 128, _cols(tb)]), w=[xk])
            p.op('dve', I_stt(rvf(hn[:, c, tb * 512:(tb + 1) * 512]), xt[:], vec[:, VC['gn'] + c:VC['gn'] + c + 1], rs[:],
                              ALU.mult, ALU.mult), r=[xk, 'vec', rk], w=['hnx'])
    sb_i = [0]

    def sbank():
        i = sb_i[0]
        sb_i[0] = (i + 1) % 4
        return cx.banks[i], 'bank%d' % i
    ob_i = [0]

    def obanks():
        i = ob_i[0]
        ob_i[0] = 1 - i
        return cx.banks[4 + i], 'bank%d' % (4 + i), cx.banks[6 + i], 'bank%d' % (6 + i)

    def qknorm(bank, bk, n, gcol, dst, dkey):
        sq, sk = cx.next_sq()
        p.op('act', I_act(sq[:, :n], bank[:, :n], AF.Square), r=[bk], w=[sk])
        b2, b2k = sbank()
        p.op('pe', I_mm(b2[:, :n], cx.ones[:], sq[:, :n], True, True), r=[sk, 'ones'], w=[b2k])
        rs, rk = cx.rstd_from_bank(b2, b2k, n, 128)
        p.op('dve', I_stt(dst, bank[:, :n], vec[:, gcol:gcol + 1], rs[:, :n], ALU.mult, ALU.mult), r=[bk, 'vec', rk], w=[dkey])

    for h in range(8):
        p.op('pool', I_memset(num[:], 0.0), w=['numacc'])
        p.op('pool', I_memset(den[:], 0.0), w=['denacc'])
        bt, btk = cx.rot('biasT', [128, 3, 256], F32, n=2)
        for g in range(3):
            p.dma('sp', I_dma(bt[:, g, :], A['biasT'][g * 8 + h]), w=[btk])
        for g, (d, Lq) in enumerate(GRP):
            nto = Lq // 128
            wsl, wsk = cx.rot('wqkv', [128, NCH, 384], BF16, n=3)
            for kind in range(3):
                c0 = kind * 3072 + g * 1024 + h * 128
                p.dma('pool', I_dma(wsl[:, :, kind * 128:(kind + 1) * 128],
                                    A['wqkv'][:, c0:c0 + 128].rearrange("(k p) n -> p k n", p=128)), w=[wsk])
            qT, qk_ = cx.rot('qT', [128, NT], BF16, n=2)
            kT, kk_ = cx.rot('kT', [128, NEXT], BF16, n=2)
            vt, vk_ = cx.rot('vt', [128, 32, 128], BF16, n=2)

            for kind, dstT, dk, gcol in ((0, qT, qk_, VC['aq']), (1, kT, kk_, VC['ak'])):
                for bi in range(4):
                    b, bk = sbank()
                    for kc in range(NCH):
                        if d == 1:
                            rhs, o_ap = hn[:, kc, bi * 512:(bi + 1) * 512], b[:]
                        elif d == 4:
                            rhs, o_ap = sub_view(hn[:, kc, 0:NT], 4)[:, bi, :], b[:]
                        else:
                            rhs = sub_view(hn[:, kc, 0:NT], 16)[:, 4 * bi:4 * bi + 4, :]
                            o_ap = b[:].rearrange("p (a b) -> p a b", a=4)
                        p.op('pe', I_mm(o_ap, wsl[:, kc, kind * 128:(kind + 1) * 128], rhs, kc == 0, kc == NCH - 1),
                             r=[wsk, 'hnx'], w=[bk])
                    qknorm(b, bk, 512, gcol, dstT[:, bi * 512:(bi + 1) * 512], dk)
            nh = 64 * d
            for b0 in range(0, nh, 512):
                n = min(512, nh - b0)
                b, bk = sbank()
                for kc in range(NCH):
                    if d == 1:
                        rhs = hn[:, kc, NT:NT + 64]
                        o_ap = b[:, :64]
                    else:
                        r0 = b0 // 64
                        nr = n // 64
                        rhs = sub_view(hn[:, kc, NT:NT + 64 * d], d)[:, r0:r0 + nr, :]
                        o_ap = b[:, :n].rearrange("p (a b) -> p a b", a=nr)
                    p.op('pe', I_mm(o_ap, wsl[:, kc, 128:256], rhs, kc == 0, kc == NCH - 1), r=[wsk, 'hnx'], w=[bk])
                qknorm(b, bk, n, VC['ak'], kT[:, NT + b0:NT + b0 + n], kk_)
            for t0 in range(0, 16, 4):
                b, bk = sbank()
                for tt in range(4):
                    t = t0 + tt
                    r, m = t // nto, t % nto
                    for kc in range(NCH):
                        lhsT = sub_view(hn[:, kc, 0:NT], d)[:, r, m * 128:(m + 1) * 128]
                        p.op('pe', I_mm(b[:, tt * 128:(tt + 1) * 128], lhsT, wsl[:, kc, 256:384], kc == 0, kc == NCH - 1),
                             r=[wsk, 'hnx'], w=[bk])
                p.op('act', I_act(vt[:, t0:t0 + 4, :], b[:].rearrange("p (a b) -> p a b", a=4), AF.Copy), r=[bk], w=[vk_])
            for r0 in range(0, d, 4):
                nr = min(4, d - r0)
                b, bk = sbank()
                for rr in range(nr):
                    r = r0 + rr
                    for kc in range(NCH):
                        lhsT = sub_view(hn[:, kc, NT:NT + 64 * d], d)[:, r, :]
                        p.op('pe', I_mm(b[:64, rr * 128:(rr + 1) * 128], lhsT, wsl[:, kc, 256:384], kc == 0, kc == NCH - 1),
                             r=[wsk, 'hnx'], w=[bk])
                p.op('act', I_act(vt[:64, 16 + r0:16 + r0 + nr, :], b[:64, :nr * 128].rearrange("p (a b) -> p a b", a=nr), AF.Copy),
                     r=[bk], w=[vk_])
            for r in range(d):
                qoff = r * Lq
                ob = None
                for m in range(nto + 1):
                    halo = (m == nto)
                    nk = 64 if halo else 128
                    b0_ = 64 if m == 0 else 0
                    b1_ = 64 if halo else min(256, Lq - (128 * m - 64))
                    ktile = kT[:, NT + r * 64:NT + r * 64 + 64] if halo else kT[:, qoff + m * 128:qoff + (m + 1) * 128]
                    vtile = vt[:64, 16 + r, :] if halo else vt[:, r * nto + m, :]
                    qs = qoff + 128 * m - 64 + b0_
                    sbk, sbkk = sbank()
                    p.op('pe', I_mm(sbk[:nk, b0_:b1_], ktile, qT[:, qs:qs + (b1_ - b0_)], True, True), r=[kk_, qk_], w=[sbkk])
                    st, stk = cx.rot('stmp', [128, 256], F32, n=3)
                    p.op('dve', I_stt(st[:nk, b0_:b1_], sbk[:nk, b0_:b1_], ASCALE, bt[:nk, g, b0_:b1_], ALU.mult, ALU.add),
                         r=[sbkk, btk], w=[stk])
                    PT, ptk = cx.rot('PTa', [128, 256], BF16, n=4)
                    p.op('act', I_act(PT[:nk, b0_:b1_], st[:nk, b0_:b1_], AF.Exp), r=[stk], w=[ptk])
                    def flush(ep):
                        qlo = max(0, 512 * ep - 64)
                        qhi = min(Lq, 512 * ep + 448)
                        c0f = qlo - (512 * ep - 64)
                        wdt = qhi - qlo
                        nv = sub_view(num[:, :], d)[:, r, qlo:qhi]
                        dv = sub_view(den[:, :], d)[:, r, qlo:qhi]
                        p.op('dve', I_tt(nv, ob[0][:, c0f:c0f + wdt], nv, ALU.add), r=[ob[1], 'numacc'], w=['numacc'])
                        p.op('dve', I_tt(dv, ob[2][:, c0f:c0f + wdt], dv, ALU.add), r=[ob[3], 'denacc'], w=['denacc'])
                    if m == 0:
                        ob = obanks()
                        p.op('pe', I_mm(ob[0][:, 64:128], vtile, PT[:nk, 64:128], True, True), r=[vk_, ptk], w=[ob[1]])
                        p.op('pe', I_mm(ob[2][:, 64:128], cx.ones[:nk, :], PT[:nk, 64:128], True, True), r=['ones', ptk], w=[ob[3]])
                    else:
                        q0 = 64 + 128 * (m - 1)
                        wq_ = min(Lq, q0 + 128) - q0
                        c0 = 128 * (m % 4)
                        p.op('pe', I_mm(ob[0][:, c0:c0 + wq_], vtile, PT[:nk, 0:wq_], False, True), r=[vk_, ptk], w=[ob[1]])
                        p.op('pe', I_mm(ob[2][:, c0:c0 + wq_], cx.ones[:nk, :], PT[:nk, 0:wq_], False, True), r=['ones', ptk], w=[ob[3]])
                        if m % 4 == 3 or halo:
                            flush(m // 4)
                    if not halo:
                        q0 = 64 + 128 * m
                        wq_ = min(Lq, q0 + 128) - q0
                        if (m + 1) % 4 == 0:
                            ob = obanks()
                        c0 = 128 * ((m + 1) % 4)
                        p.op('pe', I_mm(ob[0][:, c0:c0 + wq_], vtile, PT[:nk, 128:128 + wq_], True, False), r=[vk_, ptk], w=[ob[1]])
                        p.op('pe', I_mm(ob[2][:, c0:c0 + wq_], cx.ones[:nk, :], PT[:nk, 128:128 + wq_], True, False), r=['ones', ptk], w=[ob[3]])
        p.op('dve', I_recip(den[:], den[:]), r=['denacc'], w=['denacc'])
        p.op('pool', I_tt(mT[:, h, :], num[:], den[:], ALU.mult), r=['numacc', 'denacc'], w=['mT'])
    for ns in range(2):
        wo, wok = wslab(cx, [(A['wo_a'][:, ns * 512:(ns + 1) * 512], 0)])
        for j in range(4):
            n = ns * 4 + j
            for tb in range(NT // 512):
                b, bk = sbank()
                for kc in range(NCH):
                    p.op('pe', I_mm(b[:], wo[:, kc * 512 + j * 128: kc * 512 + (j + 1) * 128], mT[:, kc, tb * 512:(tb + 1) * 512],
                                    kc == 0, kc == NCH - 1), r=[wok, 'mT'], w=[bk])
                xt, xk = cx.rot('xin', [128, 512], F32, n=3)
                p.dma('sp', I_dma(xt[:], _src[n * 128:(n + 1) * 128, _cols(tb)]), w=[xk])
                p.op('dve', I_tt(xt[:], rvf(b[:]), xt[:], ALU.add), r=[bk, xk], w=[xk])
                p.dma('sp', I_dma(_dst[n * 128:(n + 1) * 128, _cols(tb)], xt[:]), r=[xk], w=['hTout'])


def build_attn():
    nc = bass.Bass("TRN2", target_bir_lowering=False)
    A = {}

    def inp(name, shape, dt=F32):
        A[name] = nc.dram_tensor(name, list(shape), dt, kind="ExternalInput").ap()
    inp('hT_ext', [D, NEXT])
    inp('vecs', [128, NVEC])
    inp('wqkv', [D, 9216])
    inp('wo_a', [D, D])
    inp('biasT', [24, 128, 256])
    A['hT_out'] = nc.dram_tensor('hT_out', [D, NT], F32, kind="ExternalOutput").ap()
    cx = Cx(nc)
    cx.vec = cx.p.sb('vec', [128, NVEC], F32)
    cx.p.dma('sp', I_dma(cx.vec[:], A['vecs'][:, :]), w=['vec'])
    attn_body(cx, A)
    cx.p.finalize()
    return nc, cx


def t5_bucket(rel):
    nb = 16
    ret = (rel > 0).astype(np.int32) * nb
    n = np.abs(rel)
    max_exact = nb // 2
    large = max_exact + (np.log(np.maximum(n, 1).astype(np.float32) / max_exact)
                         / np.log(1024 / max_exact) * (nb - max_exact)).astype(np.int32)
    large = np.minimum(large, nb - 1)
    return (ret + np.where(n < max_exact, n, large)).astype(np.int32)


def host_bias(bias_table, flip):
    a = np.arange(128)[:, None]
    b = np.arange(256)[None, :]
    rel = a - b + 64
    out = np.full((24, 128, 256), -1e30, np.float32)
    band = np.abs(rel) <= 64
    for g, (dil, _) in enumerate(GRP):
        bk = t5_bucket((-rel if flip else rel) * dil)
        for h in range(8):
            out[g * 8 + h] = np.where(band, bias_table[bk, g * 8 + h], np.float32(-1e30))
    return out


def build_norm():
    nc = bass.Bass("TRN2", target_bir_lowering=False)
    A = {}
    A['hT'] = nc.dram_tensor('hT', [D, NT], F32, kind="ExternalInput").ap()
    A['vecs'] = nc.dram_tensor('vecs', [128, NVEC], F32, kind="ExternalInput").ap()
    A['hn_out'] = nc.dram_tensor('hn_out', [D, NT], F32, kind="ExternalOutput").ap()
    cx = Cx(nc)
    p = cx.p
    cx.hT = p.sb('hT', [128, NCH, NT], F32)
    cx.vec = p.sb('vec', [128, NVEC], F32)
    p.dma('sp', I_dma(cx.vec[:], A['vecs'][:, :]), w=['vec'])
    load_hT(cx, A['hT'])
    emit_norm(cx, A['hn_out'])
    p.finalize()
    return nc, cx


def build_fused(nlayers=4):
    nc = bass.Bass("TRN2", target_bir_lowering=False)
    shapes = {}

    def inp(name, shape, dt=F32):
        shapes[name] = list(shape)

    class Lazy(dict):
        def __missing__(self, name):
            ap = nc.dram_tensor(name, shapes[name], F32, kind="ExternalInput").ap()
            self[name] = ap
            return ap
    A = Lazy()

    def scratch(name):
        return nc.dram_tensor(name, [D, SEQ], F32, kind="Internal").ap()
    inp('xT', [D, SEQ])
    inp('memT', [D, MEMLEN])
    inp('ident', [128, 128])
    inp('v0', [128, NVEC])
    inp('biasT0', [24, 128, 256])
    inp('biasT1', [24, 128, 256])
    for i in range(4):
        inp('vecs%d' % i, [128, NVEC])
        inp('wq%d' % i, [D, D])
        inp('wkv%d' % i, [D, 2 * D])
        inp('wo%d' % i, [D, D])
        inp('w1_%d' % i, [D, DFF])
        inp('w2_%d' % i, [DFF, D])
    for j in range(2):
        inp('wglu%d' % j, [D, 2 * D])
        inp('wqkv%d' % j, [D, 9216])
        inp('woa%d' % j, [D, D])
        inp('avecs%d' % j, [128, NVEC])
        for c in range(2):
            sfx = '%d%d' % (j, c)
            for nm in ('Bre', 'Bim', 'CR', 'CI'):
                inp(nm + sfx, [2, NPT, 128, 128])
            for nm in ('lamre', 'lamim', 'logdt'):
                inp(nm + sfx, [128, 2 * NPT])
            inp('dsk' + sfx, [128, 4])
    xT = A['xT']
    outT = nc.dram_tensor('outT', [D, SEQ], F32, kind="ExternalOutput").ap()
    HN = scratch('HN')
    Y = scratch('Y')
    Hs = [xT, scratch('H1'), scratch('H1a'), scratch('H2'), scratch('H3'), scratch('H3a'), outT]
    cx = Cx(nc, arena=True)
    p = cx.p
    hv = lambda ap, half: ap[:, half * NT:(half + 1) * NT]

    cx.hT = p.sb('hT', [128, NCH, NT], F32)
    cx.vec = p.sb('vec', [128, NVEC], F32)
    p.dma('sp', I_dma(cx.vec[:], A['v0'][:, :]), w=['vec'])
    for half in range(2):
        load_hT(cx, hv(xT, half))
        emit_norm(cx, hv(HN, half))

    def s5_stage(j):
        for c in range(2):
            cx.new_stage()
            sfx = '%d%d' % (j, c)
            AA = {nm: A[nm + sfx] for nm in ('Bre', 'Bim', 'CR', 'CI', 'lamre', 'lamim', 'logdt', 'dsk')}
            AA['ident'] = A['ident']
            AA['uT'] = HN[512 * c:512 * c + 512, :]
            AA['yT'] = Y[512 * c:512 * c + 512, :]
            s5_body(cx, AA)

    def tail_stage(i, glu, src, dst, emit):
        cx.new_stage()
        AA = dict(memT=A['memT'], vecs=A['vecs%d' % i], wq=A['wq%d' % i], wkv=A['wkv%d' % i], wo=A['wo%d' % i],
                  w1=A['w1_%d' % i], w2=A['w2_%d' % i])
        if glu:
            AA['wglu'] = A['wglu%d' % (i // 2)]
        common_tiles(cx, AA)
        for half in range(2):
            load_hT(cx, hv(src, half))
            if glu:
                AA['yT'] = hv(Y, half)
            tail_body(cx, AA, glu, 0, 2, kv_ready=(half == 1))
            tail_body(cx, AA, glu, 2, 2, kv_ready=True)
            store_hT(cx, hv(dst, half))
            if emit:
                emit_norm(cx, hv(HN, half))

    def attn_stage(i, src, dst):
        j = i // 2
        for half in range(2):
            cx.new_stage()
            cx.vec = p.sb('vec', [128, NVEC], F32)
            p.dma('sp', I_dma(cx.vec[:], A['avecs%d' % j][:, :]), w=['vec'])
            AA = dict(wqkv=A['wqkv%d' % j], wo_a=A['woa%d' % j], biasT=A['biasT%d' % half])
            attn_body(cx, AA, flip=(half == 1), src=src, dst=dst)

    s5_stage(0)
    if nlayers == 0:
        cx.new_stage()
        cx.hT = p.sb('hT', [128, NCH, NT], F32)
        for half in range(2):
            load_hT(cx, hv(Y, half))
            store_hT(cx, hv(outT, half))
        p.finalize()
        cx.used = list(A.keys())
        return nc, cx
    tail_stage(0, True, Hs[0], Hs[1] if nlayers > 1 else outT, False)
    if nlayers > 1:
        attn_stage(1, Hs[1], Hs[2])
        tail_stage(1, False, Hs[2], Hs[3] if nlayers > 2 else outT, True)
    if nlayers > 2:
        s5_stage(1)
        tail_stage(2, True, Hs[3], Hs[4] if nlayers > 3 else outT, False)
    if nlayers > 3:
        attn_stage(3, Hs[4], Hs[5])
        tail_stage(3, False, Hs[5], Hs[6], False)
    p.finalize()
    cx.used = list(A.keys())
    return nc, cx


def _pc(v, C):
    return np.ascontiguousarray(np.asarray(v, np.float32).reshape(C, 128).T)


_PROGS = {}
_NL = [4]


def _prog(name):
    if name not in _PROGS:
        if name == 'norm':
            _PROGS[name] = build_norm()[0]
        elif name == 's5':
            _PROGS[name] = build_s5()[0]
        elif name == 'tail_glu':
            _PROGS[name] = build_tail(True, True)[0]
        elif name == 'tail':
            _PROGS[name] = build_tail(False, True)[0]
        elif name == 'attn':
            _PROGS[name] = build_attn()[0]
    return _PROGS[name]


def kernel_multi(**inp):
    inp = {k: np.asarray(v) for k, v in inp.items()}
    ncore = 8
    cores = list(range(ncore))
    f32 = np.float32
    loc = [np.arange(NEXT) if (k % 2 == 0) else (SEQ - 1 - np.arange(NEXT)) for k in cores]
    H = np.array(inp['x'], dtype=f32, copy=True)
    memT = [np.ascontiguousarray(inp['mem'][k // 2].T.astype(f32)) for k in cores]
    biasT = [host_bias(inp['bias_table'].astype(f32), k % 2 == 1) for k in cores]

    def own_T(arr_bsd, k):
        return np.ascontiguousarray(arr_bsd[k // 2][loc[k][:NT]].T)

    def scatter(outs, name):
        full = np.empty((BATCH, SEQ, D), f32)
        for k in cores:
            full[k // 2][loc[k][:NT]] = np.asarray(outs[k][name], f32).T
        return full

    def tail_vecs(i):
        v = np.zeros((128, NVEC), f32)
        v[:, 0:8] = _pc(inp['norm_xattn'][i], 8)
        v[:, 8:16] = _pc(inp['norm_mem'][i], 8)
        v[:, 16:24] = _pc(inp['norm_mlp'][i], 8)
        v[:, 24:26] = _pc(inp['xattn_q_gain'][i], 2)
        v[:, 26:28] = _pc(inp['xattn_k_gain'][i], 2)
        v[:, 28:36] = _pc(inp['norm_mix'][min(i + 1, 3)], 8)
        return v

    def run_tail(i, H, Y):
        v = tail_vecs(i)
        maps = []
        for k in cores:
            m = dict(hT=own_T(H, k), memT=memT[k], vecs=v, wq=inp['xattn_w_q'][i], wkv=inp['xattn_w_kv'][i],
                     wo=inp['xattn_w_o'][i], w1=inp['mlp_w1'][i], w2=inp['mlp_w2'][i])
            if Y is not None:
                m['yT'] = own_T(Y, k)
                m['wglu'] = inp['s5_w_glu'][i // 2]
            maps.append(m)
        res = run_bass_kernel_spmd(_prog('tail_glu' if Y is not None else 'tail'), maps, core_ids=cores).results
        return scatter(res, 'hT_out'), scatter(res, 'hn_out')

    def run_s5(j, HN):
        maps = []
        for k in cores:
            b, c = k // 2, k % 2
            m = s5_host_inputs(inp, j, c)
            m['uT'] = np.ascontiguousarray(HN[b][:, 512 * c:512 * c + 512].T)
            maps.append(m)
        res = run_bass_kernel_spmd(_prog('s5'), maps, core_ids=cores).results
        Y = np.empty((BATCH, SEQ, D), f32)
        for k in cores:
            b, c = k // 2, k % 2
            Y[b][:, 512 * c:512 * c + 512] = np.asarray(res[k]['yT'], f32).T
        return Y

    def run_attn(i, H):
        j = i // 2
        v = np.zeros((128, NVEC), f32)
        v[:, 28:36] = _pc(inp['norm_mix'][i], 8)
        v[:, 36] = inp['attn_q_gain'][j]
        v[:, 37] = inp['attn_k_gain'][j]
        maps = []
        for k in cores:
            maps.append(dict(hT_ext=np.ascontiguousarray(H[k // 2][loc[k]].T), vecs=v, wqkv=inp['attn_w_qkv'][j],
                             wo_a=inp['attn_w_o'][j], biasT=biasT[k]))
        res = run_bass_kernel_spmd(_prog('attn'), maps, core_ids=cores).results
        return scatter(res, 'hT_out')

    v0 = np.zeros((128, NVEC), f32)
    v0[:, 28:36] = _pc(inp['norm_mix'][0], 8)
    res = run_bass_kernel_spmd(_prog('norm'), [dict(hT=own_T(H, k), vecs=v0) for k in cores], core_ids=cores).results
    HN = scatter(res, 'hn_out')
    for i in range(4):
        if i % 2 == 0:
            Y = run_s5(i // 2, HN)
            H, HN = run_tail(i, H, Y)
        else:
            H = run_attn(i, H)
            H, HN = run_tail(i, H, None)
    return H


def tail_vecs_host(inp, i):
    v = np.zeros((128, NVEC), np.float32)
    v[:, 0:8] = _pc(inp['norm_xattn'][i], 8)
    v[:, 8:16] = _pc(inp['norm_mem'][i], 8)
    v[:, 16:24] = _pc(inp['norm_mlp'][i], 8)
    v[:, 24:26] = _pc(inp['xattn_q_gain'][i], 2)
    v[:, 26:28] = _pc(inp['xattn_k_gain'][i], 2)
    v[:, 28:36] = _pc(inp['norm_mix'][min(i + 1, 3)], 8)
    return v


def kernel(**inp):
    inp = {k: np.asarray(v) for k, v in inp.items()}
    f32 = np.float32
    nl = _NL[0]
    if ('fused', nl) not in _PROGS:
        _PROGS[('fused', nl)] = build_fused(nl)
    nc, cxf = _PROGS[('fused', nl)]
    shared = dict(ident=np.eye(128, dtype=f32),
                  biasT0=host_bias(inp['bias_table'].astype(f32), False),
                  biasT1=host_bias(inp['bias_table'].astype(f32), True))
    v0 = np.zeros((128, NVEC), f32)
    v0[:, 28:36] = _pc(inp['norm_mix'][0], 8)
    shared['v0'] = v0
    for i in range(4):
        shared['vecs%d' % i] = tail_vecs_host(inp, i)
        shared['wq%d' % i] = inp['xattn_w_q'][i]
        shared['wkv%d' % i] = inp['xattn_w_kv'][i]
        shared['wo%d' % i] = inp['xattn_w_o'][i]
        shared['w1_%d' % i] = inp['mlp_w1'][i]
        shared['w2_%d' % i] = inp['mlp_w2'][i]
    for j in range(2):
        shared['wglu%d' % j] = inp['s5_w_glu'][j]
        shared['wqkv%d' % j] = inp['attn_w_qkv'][j]
        shared['woa%d' % j] = inp['attn_w_o'][j]
        av = np.zeros((128, NVEC), f32)
        av[:, 28:36] = _pc(inp['norm_mix'][2 * j + 1], 8)
        av[:, 36] = inp['attn_q_gain'][j]
        av[:, 37] = inp['attn_k_gain'][j]
        shared['avecs%d' % j] = av
        for c in range(2):
            for nm, arr in s5_host_inputs(inp, j, c).items():
                if nm != 'ident':
                    shared[nm + '%d%d' % (j, c)] = arr
    maps = []
    for b in range(BATCH):
        m = dict(shared)
        m['xT'] = np.ascontiguousarray(inp['x'][b].T.astype(f32))
        m['memT'] = np.ascontiguousarray(inp['mem'][b].T.astype(f32))
        maps.append({k: m[k] for k in cxf.used})
    res = run_bass_kernel_spmd(nc, maps, core_ids=list(range(BATCH))).results
    out = np.empty((BATCH, SEQ, D), f32)
    for b in range(BATCH):
        out[b] = np.asarray(res[b]['outT'], f32).T
    return out
```

```python
import math
import numpy as np
from contextlib import ExitStack
import concourse.bass as bass
import concourse.mybir as mybir
from concourse.bass_utils import run_bass_kernel_spmd

F32 = mybir.dt.float32
BF16 = mybir.dt.bfloat16
AF = mybir.ActivationFunctionType
ALU = mybir.AluOpType

D = 1024
NCH = 8
SEQ = 4096
BATCH = 4
NT = 2048
EPS = 1e-6
MEMLEN = 256
DFF = 4096


class Prog:
    ENGS = ('pe', 'act', 'dve', 'pool', 'sp')
    BLK = {'pe': 'tensor', 'act': 'scalar', 'dve': 'vector', 'pool': 'gpsimd', 'sp': 'sync'}

    def __init__(self, nc):
        self.nc = nc
        self.es = ExitStack()
        self.ins = {e: [] for e in self.ENGS}
        self.last_w = {}
        self.readers = {}
        self.dma_cnt = {}
        self.log = None
        self.bar_deps = {}

    ARENA_F32 = 51712

    def use_arena(self):
        self.arena = self.es.enter_context(self.nc.sbuf_tensor('arena', [128, self.ARENA_F32], F32))
        self.aoff = 0

    def sb(self, name, shape, dt):
        if getattr(self, 'arena', None) is None:
            return self.es.enter_context(self.nc.sbuf_tensor('s_' + name, list(shape), dt))
        assert shape[0] == 128, shape
        nel = 1
        for d_ in shape[1:]:
            nel *= d_
        isz = 4 if dt == F32 else 2
        nby = (nel * isz + 63) // 64 * 64
        o4 = self.aoff // 4
        self.aoff += nby
        assert self.aoff <= self.ARENA_F32 * 4, ('arena overflow', name, self.aoff)
        v = self.arena[:, o4:o4 + nby // 4]
        if dt != F32:
            v = v.bitcast(dt)
        v = v[:, :nel]
        if len(shape) == 3:
            v = v.rearrange("p (a b) -> p a b", a=shape[1])
        elif len(shape) != 2:
            raise AssertionError(shape)
        return v

    def barrier(self):
        deps = [('d', k, c) for k, c in self.dma_cnt.items()]
        for e in self.ENGS:
            n = len(self.ins[e])
            j = n - 1
            while j >= 0 and self.ins[e][j]['dma'] is not None:
                j -= 1
            if j >= 0:
                deps.append(('e', e, j))
        self.bar_deps = {e: list(deps) for e in self.ENGS}
        self.last_w.clear()
        self.readers.clear()

    def ps(self, name, shape, dt=F32):
        return self.es.enter_context(self.nc.psum_tensor(name, list(shape), dt))

    def _deps(self, r, w):
        deps = []
        for k in r:
            t = self.last_w.get(k)
            if t is not None:
                deps.append(t)
        for k in w:
            t = self.last_w.get(k)
            if t is not None:
                deps.append(t)
            deps.extend(self.readers.get(k, ()))
        return deps

    def _commit(self, tok, r, w):
        for k in r:
            lst = self.readers.setdefault(k, [])
            lst[:] = [t for t in lst if t[:2] != tok[:2]]
            lst.append(tok)
        for k in w:
            self.last_w[k] = tok
            self.readers[k] = []

    def op(self, eng, fn, r=(), w=()):
        idx = len(self.ins[eng])
        self.ins[eng].append(dict(fn=fn, deps=self._deps(r, w) + self.bar_deps.pop(eng, []), dma=None))
        self._commit(('e', eng, idx), r, w)

    def dma(self, eng, fn, r=(), w=(), semkey=None, inc=16):
        if semkey is None:
            semkey = w[0]
        c = self.dma_cnt.get(semkey, 0) + inc
        self.dma_cnt[semkey] = c
        self.ins[eng].append(dict(fn=fn, deps=self._deps(r, w) + self.bar_deps.pop(eng, []), dma=semkey, inc=inc))
        self._commit(('d', semkey, c), r, w)

    SAME_DIST = 4

    def _skip_same(self, e, i, d, rec):
        if d[1] != e or rec['dma'] is not None:
            return False
        if e == 'pe':
            return True
        return (i - d[2]) > self.SAME_DIST

    def finalize(self):
        nc = self.nc
        need = {e: set() for e in self.ENGS}
        for e in self.ENGS:
            for i, rec in enumerate(self.ins[e]):
                for d in rec['deps']:
                    if d[0] == 'e' and not self._skip_same(e, i, d, rec):
                        need[d[1]].add(d[2])
        cum = {}
        for e in self.ENGS:
            c = 0
            arr = []
            for i in range(len(self.ins[e])):
                if i in need[e]:
                    c += 1
                arr.append(c)
            cum[e] = arr
        esem = {e: self.es.enter_context(nc.semaphore('se_' + e)) for e in self.ENGS}
        dsem = {}
        for i, k in enumerate(self.dma_cnt):
            dsem[k] = self.es.enter_context(nc.semaphore('sd_%d' % i))
        self.stats = {e: (len(self.ins[e]), cum[e][-1] if cum[e] else 0) for e in self.ENGS}
        self.stats['ndsem'] = len(dsem)
        with nc.Block() as block:
            for e in self.ENGS:
                def body(eng, e=e):
                    waited = {}
                    for i, rec in enumerate(self.ins[e]):
                        req = {}
                        for d in rec['deps']:
                            if d[0] == 'e':
                                if self._skip_same(e, i, d, rec):
                                    continue
                                key = ('e', d[1])
                                val = cum[d[1]][d[2]]
                            else:
                                key = ('d', d[1])
                                val = d[2]
                            if val > req.get(key, 0):
                                req[key] = val
                        for key, val in req.items():
                            if waited.get(key, 0) < val:
                                sem = esem[key[1]] if key[0] == 'e' else dsem[key[1]]
                                eng.wait_ge(sem, val)
                                waited[key] = val
                                if self.log is not None:
                                    self.log.append((e, i, 'wait', key, val))
                        if self.log is not None:
                            self.log.append((e, i, 'inst', rec['dma'], cum[e][i] if i in need[e] else None))
                        inst = rec['fn'](eng)
                        if rec['dma'] is not None:
                            inst.then_inc(dsem[rec['dma']], rec.get('inc', 16))
                        elif i in need[e]:
                            inst.then_inc(esem[e], 1)
                    if e == 'sp':
                        for k, c in self.dma_cnt.items():
                            eng.wait_ge(dsem[k], c)
                getattr(block, self.BLK[e])(body)
        self.es.close()


def I_mm(out, lhsT, rhs, start, stop):
    return lambda e: e.matmul(out, lhsT, rhs, start=start, stop=stop)


def I_act(out, in_, func, **kw):
    return lambda e: e.activation(out=out, in_=in_, func=func, **kw)


def I_tt(out, in0, in1, op):
    return lambda e: e.tensor_tensor(out=out, in0=in0, in1=in1, op=op)


def I_ts(out, in0, s1, s2, op0, op1=None):
    if op1 is None:
        return lambda e: e.tensor_scalar(out=out, in0=in0, scalar1=s1, scalar2=None, op0=op0)
    return lambda e: e.tensor_scalar(out=out, in0=in0, scalar1=s1, scalar2=s2, op0=op0, op1=op1)


def I_stt(out, in0, scalar, in1, op0, op1):
    return lambda e: e.scalar_tensor_tensor(out=out, in0=in0, scalar=scalar, in1=in1, op0=op0, op1=op1)


def I_recip(out, in_):
    return lambda e: e.reciprocal(out=out, in_=in_)


def I_copy(out, in_):
    return lambda e: e.tensor_copy(out=out, in_=in_)


def I_memset(ap, c):
    return lambda e: e.memset(ap, c)


def I_dma(out, in_):
    return lambda e: e.dma_start(out=out, in_=in_)


def I_scan(out, d0, d1, init):
    return lambda e: e.tensor_tensor_scan(out=out, data0=d0, data1=d1, initial=init, op0=ALU.mult, op1=ALU.add)


def mcombine(cx, out, okey, X, xk, Z, zk, ma, mb, n=512):
    p = cx.p
    tmp, tk = cx.rot('mctmp', [128, 512], F32, n=2)
    p.op('act', I_act(tmp[:, :n], X, AF.Copy, scale=ma), r=[xk, 'vec'], w=[tk])
    p.op('dve', I_stt(out, Z, mb, tmp[:, :n], ALU.mult, ALU.add), r=[zk, 'vec', tk], w=[okey])


class Cx:
    def __init__(self, nc, arena=False):
        self.nc = nc
        self.p = Prog(nc)
        if arena:
            self.p.use_arena()
        self.banks = [self.p.ps('bank%d' % i, [128, 512]) for i in range(8)]
        self.bi = 0
        p = self.p
        self.ones = p.sb('ones', [128, 128], BF16)
        p.op('pool', I_memset(self.ones[:], 1.0), w=['ones'])
        self.sq = [p.sb('sq%d' % i, [128, 512], BF16) for i in range(2)]
        self.sqi = 0
        self.rstd = [p.sb('rstd%d' % i, [128, 512], F32) for i in range(2)]
        self.rsi = 0
        self._rot = {}
        self.epsc = p.sb('epsc', [128, 1], F32)
        p.op('pool', I_memset(self.epsc[:], EPS), w=['epsc'])
        self.mark = getattr(p, 'aoff', 0)

    def new_stage(self):
        self.p.barrier()
        self.p.aoff = self.mark
        self._rot = {}
        for nm in ('ws', 'wsi'):
            if hasattr(self, nm):
                delattr(self, nm)

    def bank(self):
        i = self.bi
        self.bi = (i + 1) % 8
        return self.banks[i], 'bank%d' % i

    def rot(self, name, shape, dt, n=2):
        if name not in self._rot:
            self._rot[name] = [[self.p.sb('%s_%d' % (name, i), shape, dt) for i in range(n)], 0]
        tl, i = self._rot[name]
        self._rot[name][1] = (i + 1) % n
        return tl[i], '%s_%d' % (name, i)

    def next_sq(self):
        i = self.sqi
        self.sqi = 1 - i
        return self.sq[i], 'sq%d' % i

    def next_rstd(self):
        i = self.rsi
        self.rsi = 1 - i
        return self.rstd[i], 'rstd%d' % i

    def rstd_from_bank(self, bank, bk, n, dim):
        p = self.p
        rs, rk = self.next_rstd()
        p.op('act', I_act(rs[:, :n], bank[:, :n], AF.Sqrt, scale=1.0 / dim, bias=self.epsc[:, 0:1]), r=[bk, 'epsc'], w=[rk])
        p.op('dve', I_recip(rs[:, :n], rs[:, :n]), r=[rk], w=[rk])
        return rs, rk


def rmsnorm(cx, src, skey, gcols, gkey, dst, dkey, C, n0, n, dim):
    p = cx.p
    bank, bk = cx.bank()
    for c in range(C):
        sq, sk = cx.next_sq()
        p.op('act', I_act(sq[:, :n], src[:, c, n0:n0 + n], AF.Square), r=[skey(c)], w=[sk])
        p.op('pe', I_mm(bank[:, :n], cx.ones[:], sq[:, :n], c == 0, c == C - 1), r=[sk, 'ones'], w=[bk])
    rs, rk = cx.rstd_from_bank(bank, bk, n, dim)
    for c in range(C):
        p.op('dve', I_stt(dst[:, c, n0:n0 + n], src[:, c, n0:n0 + n], gcols[:, c:c + 1], rs[:, :n], ALU.mult, ALU.mult),
             r=[skey(c), gkey, rk], w=[dkey(c)])


WS_N = 3


def wslab(cx, parts):
    p = cx.p
    nws = getattr(cx, 'ws_n', WS_N)
    if not hasattr(cx, 'ws'):
        cx.ws = [p.sb('ws%d' % i, [128, 4096], BF16) for i in range(nws)]
        cx.wsi = 0
    i = cx.wsi
    cx.wsi = (i + 1) % nws
    t = cx.ws[i]
    key = 'ws%d' % i
    for src, off in parts:
        K, N = src.shape
        kc = K // 128
        dst = t[:, off:off + kc * N].rearrange("p (k n) -> p k n", k=kc)
        p.dma('pool', I_dma(dst, src.rearrange("(k p) n -> p k n", p=128)), w=[key])
    return t, key


VC = dict(gx=0, gm=8, gl=16, gq=24, gk=26, gn=28, aq=36, ak=37, dsk=38, m0=40, m1=41)
NVEC = 48


def tail_body(cx, A, glu, tb0, ntb, kv_ready):
    p = cx.p
    hT = cx.hT
    hk = lambda c, tb: 'h%d_%d' % (c, tb)
    hn = cx.hn
    big2 = cx.big2
    vec = cx.vec
    NB = ntb

    if glu:
        for c in range(NCH):
            for tb in range(NB):
                yt, yk = cx.rot('ytmp', [128, 512], F32)
                col0 = (tb0 + tb) * 512
                if 'yO' in A and c >= 4:
                    X, xk = cx.rot('gx', [128, 512], F32, n=2)
                    Z, zk = cx.rot('gz', [128, 512], F32, n=2)
                    p.dma('sp', I_dma(X[:], A['GS'].g_rows(0, (c - 4) * 128)[:, col0:col0 + 512]), w=[xk])
                    p.dma('sp', I_dma(Z[:], A['GS'].g_rows(1, (c - 4) * 128)[:, col0:col0 + 512]), w=[zk])
                    mcombine(cx, yt[:], yk, X[:], xk, Z[:], zk, vec[:, VC['m1']:VC['m1'] + 1], vec[:, VC['m0']:VC['m0'] + 1])
                elif 'yO' in A:
                    p.dma('sp', I_dma(yt[:], A['yO'][c * 128:(c + 1) * 128, col0:col0 + 512]), w=[yk])
                else:
                    p.dma('sp', I_dma(yt[:], A['yT'][c * 128:(c + 1) * 128, col0:col0 + 512]), w=[yk])
                p.op('act', I_act(hn[:, c, tb * 512:(tb + 1) * 512], yt[:], AF.Gelu_apprx_tanh), r=[yk], w=['hn%d' % tb])
        for ns in range(2):
            wa, wak = wslab(cx, [(A['wglu'][:, ns * 512:(ns + 1) * 512], 0)])
            wb, wbk = wslab(cx, [(A['wglu'][:, 1024 + ns * 512:1024 + (ns + 1) * 512], 0)])
            for j in range(4):
                n = ns * 4 + j
                for tb in range(NB):
                    ba, bak = cx.bank()
                    bb, bbk = cx.bank()
                    for kc in range(NCH):
                        p.op('pe', I_mm(ba[:], wa[:, kc * 512 + j * 128: kc * 512 + (j + 1) * 128],
                                        hn[:, kc, tb * 512:(tb + 1) * 512], kc == 0, kc == NCH - 1),
                             r=[wak, 'hn%d' % tb], w=[bak])
                    for kc in range(NCH):
                        p.op('pe', I_mm(bb[:], wb[:, kc * 512 + j * 128: kc * 512 + (j + 1) * 128],
                                        hn[:, kc, tb * 512:(tb + 1) * 512], kc == 0, kc == NCH - 1),
                             r=[wbk, 'hn%d' % tb], w=[bbk])
                    sg, sgk = cx.rot('sg', [128, 512], F32)
                    p.op('act', I_act(sg[:], bb[:], AF.Sigmoid), r=[bbk], w=[sgk])
                    gt, gtk = cx.rot('gtmp', [128, 512], F32)
                    p.op('dve', I_tt(gt[:], ba[:], sg[:], ALU.mult), r=[bak, sgk], w=[gtk])
                    hs = hT[:, n, (tb0 + tb) * 512:(tb0 + tb + 1) * 512]
                    p.op('pool', I_tt(hs, hs, gt[:], ALU.add), r=[gtk, hk(n, tb0 + tb)], w=[hk(n, tb0 + tb)])

    if not kv_ready:
        kraw = cx.kraw
        memn = cx.memn
        for c in range(NCH):
            p.dma('sp', I_dma(kraw[:, c, :], A['memT'][c * 128:(c + 1) * 128, :]), w=['kraw'], semkey='kraw_ld')
        rmsnorm(cx, kraw, lambda c: 'kraw', vec[:, VC['gm']:VC['gm'] + 8], 'vec', memn, lambda c: 'memn', NCH, 0, MEMLEN, D)
        for hp in range(2):
            wk, wkk = wslab(cx, [(A['wkv'][:, hp * 512:(hp + 1) * 512], 0)])
            for jj in range(4):
                j = hp * 4 + jj
                bk_, bkk = cx.bank()
                for kc in range(NCH):
                    p.op('pe', I_mm(bk_[:, :MEMLEN], wk[:, kc * 512 + jj * 128: kc * 512 + (jj + 1) * 128], memn[:, kc, :],
                                    kc == 0, kc == NCH - 1), r=[wkk, 'memn'], w=[bkk])
                p.op('act', I_act(kraw[:, j, :], bk_[:, :MEMLEN], AF.Copy), r=[bkk], w=['kraw'])
        for h in range(4):
            bs, bsk = cx.bank()
            for ec in range(2):
                sq, sk = cx.next_sq()
                p.op('act', I_act(sq[:, :MEMLEN], kraw[:, 2 * h + ec, :], AF.Square), r=['kraw'], w=[sk])
                p.op('pe', I_mm(bs[:, :MEMLEN], cx.ones[:], sq[:, :MEMLEN], ec == 0, ec == 1), r=[sk, 'ones'], w=[bsk])
            rs, rk = cx.rstd_from_bank(bs, bsk, MEMLEN, 256)
            for ec in range(2):
                p.op('dve', I_stt(cx.KT[:, 2 * h + ec, :], kraw[:, 2 * h + ec, :], vec[:, VC['gk'] + ec:VC['gk'] + ec + 1],
                                  rs[:, :MEMLEN], ALU.mult, ALU.mult), r=['kraw', 'vec', rk], w=['KT'])
        for vs in range(2):
            wv, wvk = wslab(cx, [(A['wkv'][:, 1024 + vs * 512:1024 + (vs + 1) * 512], 0)])
            for mc in range(2):
                bv, bvk = cx.bank()
                for kc in range(NCH):
                    p.op('pe', I_mm(bv[:], memn[:, kc, mc * 128:(mc + 1) * 128], wv[:, kc * 512:(kc + 1) * 512],
                                    kc == 0, kc == NCH - 1), r=[wvk, 'memn'], w=[bvk])
                p.op('act', I_act(cx.V[:, mc, vs * 512:(vs + 1) * 512], bv[:], AF.Copy), r=[bvk], w=['V'])

    for tb in range(NB):
        _rmsnorm_off(cx, hT, (tb0 + tb) * 512, lambda c, tb=tb: hk(c, tb0 + tb), vec[:, VC['gx']:VC['gx'] + 8],
                     hn, tb * 512, 'hn%d' % tb)
    for hp in range(2):
        wq, wqk = wslab(cx, [(A['wq'][:, hp * 512:(hp + 1) * 512], 0)])
        for hh in range(2):
            h = 2 * hp + hh
            for tb in range(NB):
                qb = []
                for ec in range(2):
                    b, bk_ = cx.bank()
                    cc = hh * 2 + ec
                    for kc in range(NCH):
                        p.op('pe', I_mm(b[:], wq[:, kc * 512 + cc * 128: kc * 512 + (cc + 1) * 128],
                                        hn[:, kc, tb * 512:(tb + 1) * 512], kc == 0, kc == NCH - 1),
                             r=[wqk, 'hn%d' % tb], w=[bk_])
                    qb.append((b, bk_))
                bs, bsk = cx.bank()
                for ec in range(2):
                    sq, sk = cx.next_sq()
                    p.op('act', I_act(sq[:], qb[ec][0][:], AF.Square), r=[qb[ec][1]], w=[sk])
                    p.op('pe', I_mm(bs[:], cx.ones[:], sq[:], ec == 0, ec == 1), r=[sk, 'ones'], w=[bsk])
                rs, rk = cx.rstd_from_bank(bs, bsk, 512, 256)
                qn, qnk = cx.rot('qn', [128, 2, 512], BF16)
                for ec in range(2):
                    p.op('dve', I_stt(qn[:, ec, :], qb[ec][0][:], vec[:, VC['gq'] + ec:VC['gq'] + ec + 1], rs[:],
                                      ALU.mult, ALU.mult), r=[qb[ec][1], 'vec', rk], w=[qnk])
                PT, ptk = cx.rot('PT', [128, 2, 512], BF16)
                for mc in range(2):
                    bl, blk = cx.bank()
                    for ec in range(2):
                        p.op('pe', I_mm(bl[:], cx.KT[:, 2 * h + ec, mc * 128:(mc + 1) * 128], qn[:, ec, :], ec == 0, ec == 1),
                             r=['KT', qnk], w=[blk])
                    p.op('act', I_act(PT[:, mc, :], bl[:], AF.Exp, scale=1.0 / 16.0), r=[blk], w=[ptk])
                bd, bdk = cx.bank()
                for mc in range(2):
                    p.op('pe', I_mm(bd[:], cx.ones[:], PT[:, mc, :], mc == 0, mc == 1), r=['ones', ptk], w=[bdk])
                rd, rdk = cx.rot('rden', [128, 512], F32)
                p.op('dve', I_recip(rd[:], bd[:]), r=[bdk], w=[rdk])
                for ec in range(2):
                    bo, bok = cx.bank()
                    for mc in range(2):
                        p.op('pe', I_mm(bo[:], cx.V[:, mc, h * 256 + ec * 128: h * 256 + (ec + 1) * 128], PT[:, mc, :],
                                        mc == 0, mc == 1), r=['V', ptk], w=[bok])
                    p.op('dve', I_tt(big2[:, 2 * h + ec, tb * 512:(tb + 1) * 512], bo[:], rd[:], ALU.mult),
                         r=[bok, rdk], w=['big2_%d' % tb])
    for ns in range(2):
        wo, wok = wslab(cx, [(A['wo'][:, ns * 512:(ns + 1) * 512], 0)])
        for j in range(4):
            n = ns * 4 + j
            for tb in range(NB):
                b, bk_ = cx.bank()
                for kc in range(NCH):
                    p.op('pe', I_mm(b[:], wo[:, kc * 512 + j * 128: kc * 512 + (j + 1) * 128],
                                    big2[:, kc, tb * 512:(tb + 1) * 512], kc == 0, kc == NCH - 1),
                         r=[wok, 'big2_%d' % tb], w=[bk_])
                hs = hT[:, n, (tb0 + tb) * 512:(tb0 + tb + 1) * 512]
                p.op('dve', I_tt(hs, b[:], hs, ALU.add), r=[bk_, hk(n, tb0 + tb)], w=[hk(n, tb0 + tb)])

    for tb in range(NB):
        _rmsnorm_off(cx, hT, (tb0 + tb) * 512, lambda c, tb=tb: hk(c, tb0 + tb), vec[:, VC['gl']:VC['gl'] + 8],
                     hn, tb * 512, 'hn%d' % tb)
    for s in range(DFF // 256):
        ws, wsk = wslab(cx, [(A['w1'][:, s * 256:(s + 1) * 256], 0), (A['w2'][s * 256:(s + 1) * 256, :], 2048)])
        hb = s % 2
        for j in range(2):
            for tb in range(NB):
                b, bk_ = cx.bank()
                for kc in range(NCH):
                    p.op('pe', I_mm(b[:], ws[:, kc * 256 + j * 128: kc * 256 + (j + 1) * 128],
                                    hn[:, kc, tb * 512:(tb + 1) * 512], kc == 0, kc == NCH - 1),
                         r=[wsk, 'hn%d' % tb], w=[bk_])
                rt, rtk = cx.rot('rtmp', [128, 512], F32)
                p.op('act', I_act(rt[:], b[:], AF.Relu), r=[bk_], w=[rtk])
                p.op('pool', I_tt(big2[:, hb * 2 + j, tb * 512:(tb + 1) * 512], rt[:], rt[:], ALU.mult),
                     r=[rtk], w=['hid%d' % hb])
        for n in range(NCH):
            for tb in range(NB):
                b, bk_ = cx.bank()
                for j in range(2):
                    p.op('pe', I_mm(b[:], ws[:, 2048 + j * 1024 + n * 128: 2048 + j * 1024 + (n + 1) * 128],
                                    big2[:, hb * 2 + j, tb * 512:(tb + 1) * 512], j == 0, j == 1),
                         r=[wsk, 'hid%d' % hb], w=[bk_])
                hs = hT[:, n, (tb0 + tb) * 512:(tb0 + tb + 1) * 512]
                p.op('dve', I_tt(hs, b[:], hs, ALU.add), r=[bk_, hk(n, tb0 + tb)], w=[hk(n, tb0 + tb)])


def _rmsnorm_off(cx, src, s0, skey, gcols, dst, d0, dkey, n=512, C=NCH, dim=D):
    p = cx.p
    bank, bk = cx.bank()
    for c in range(C):
        sq, sk = cx.next_sq()
        p.op('act', I_act(sq[:, :n], src[:, c, s0:s0 + n], AF.Square), r=[skey(c)], w=[sk])
        p.op('pe', I_mm(bank[:, :n], cx.ones[:], sq[:, :n], c == 0, c == C - 1), r=[sk, 'ones'], w=[bk])
    rs, rk = cx.rstd_from_bank(bank, bk, n, dim)
    for c in range(C):
        p.op('dve', I_stt(dst[:, c, d0:d0 + n], src[:, c, s0:s0 + n], gcols[:, c:c + 1], rs[:, :n], ALU.mult, ALU.mult),
             r=[skey(c), 'vec', rk], w=[dkey])


def common_tiles(cx, A):
    p = cx.p
    cx.hT = p.sb('hT', [128, NCH, NT], F32)
    cx.hn = p.sb('hn', [128, NCH, 1024], BF16)
    cx.big2 = p.sb('big2', [128, NCH, 1024], BF16)
    cx.vec = p.sb('vec', [128, NVEC], F32)
    cx.kraw = p.sb('kraw', [128, NCH, MEMLEN], F32)
    cx.memn = p.sb('memn', [128, NCH, MEMLEN], BF16)
    cx.KT = p.sb('KT', [128, NCH, MEMLEN], BF16)
    cx.V = p.sb('V', [128, 2, D], BF16)
    p.dma('sp', I_dma(cx.vec[:], A['vecs'][:, :]), w=['vec'])


def load_hT(cx, src):
    p = cx.p
    for c in range(NCH):
        p.dma('sp', I_dma(cx.hT[:, c, :], src[c * 128:(c + 1) * 128, :]),
              w=['h%d_%d' % (c, tb) for tb in range(NT // 512)], semkey='hld%d' % c)


def store_hT(cx, dst):
    p = cx.p
    for c in range(NCH):
        p.dma('sp', I_dma(dst[c * 128:(c + 1) * 128, :], cx.hT[:, c, :]),
              r=['h%d_%d' % (c, tb) for tb in range(NT // 512)], w=['hout%d' % c])


def build_tail(glu, emit_hn, arena=False):
    nc = bass.Bass("TRN2", target_bir_lowering=False)
    A = {}

    def inp(name, shape, dt=F32):
        A[name] = nc.dram_tensor(name, list(shape), dt, kind="ExternalInput").ap()

    inp('hT', [D, NT])
    inp('memT', [D, MEMLEN])
    inp('vecs', [128, NVEC])
    inp('wq', [D, D])
    inp('wkv', [D, 2 * D])
    inp('wo', [D, D])
    inp('w1', [D, DFF])
    inp('w2', [DFF, D])
    if glu:
        inp('yT', [D, NT])
        inp('wglu', [D, 2 * D])
    A['hT_out'] = nc.dram_tensor('hT_out', [D, NT], F32, kind="ExternalOutput").ap()
    if emit_hn:
        A['hn_out'] = nc.dram_tensor('hn_out', [D, NT], F32, kind="ExternalOutput").ap()
    cx = Cx(nc, arena=arena)
    if arena:
        cx.new_stage()
    common_tiles(cx, A)
    load_hT(cx, A['hT'])
    for half in range(2):
        tail_body(cx, A, glu, half * 2, 2, kv_ready=(half == 1))
    store_hT(cx, A['hT_out'])
    if emit_hn:
        emit_norm(cx, A['hn_out'])
    cx.p.finalize()
    return nc, cx


def emit_norm(cx, dst):
    p = cx.p
    for tb in range(NT // 512):
        bank, bk = cx.bank()
        for c in range(NCH):
            sq, sk = cx.next_sq()
            p.op('act', I_act(sq[:], cx.hT[:, c, tb * 512:(tb + 1) * 512], AF.Square), r=['h%d_%d' % (c, tb)], w=[sk])
            p.op('pe', I_mm(bank[:], cx.ones[:], sq[:], c == 0, c == NCH - 1), r=[sk, 'ones'], w=[bk])
        rs, rk = cx.rstd_from_bank(bank, bk, 512, D)
        for c in range(NCH):
            ot, otk = cx.rot('ntmp', [128, 512], F32, n=3)
            p.op('dve', I_stt(ot[:], cx.hT[:, c, tb * 512:(tb + 1) * 512], cx.vec[:, VC['gn'] + c:VC['gn'] + c + 1], rs[:],
                              ALU.mult, ALU.mult), r=['h%d_%d' % (c, tb), 'vec', rk], w=[otk])
            drow = dst.src_rows(c * 128) if hasattr(dst, 'src_rows') else dst[c * 128:(c + 1) * 128, :]
            p.dma('sp', I_dma(drow[:, tb * 512:(tb + 1) * 512], ot[:]), r=[otk], w=['hnout'])


NPT = 16
SW = 512
NW = SEQ // SW
PI = math.pi


def s5_params(cx, A):
    p = cx.p
    NCOL = 2 * NPT
    T = {}
    for nm in ['lre', 'lim', 'ldt', 'dt', 'mag', 'ang', 'angc', 's1', 'c1', 'are', 'aim', 'nr', 'den', 't', 't2',
               'fre', 'fim', 'nfre', 'nfim']:
        T[nm] = p.sb('sp_' + nm, [128, NCOL], F32)
    k = 's5par'
    p.dma('sp', I_dma(T['lre'][:], A['lamre'][:, :]), w=[k], semkey='s5par_ld')
    p.dma('sp', I_dma(T['lim'][:], A['lamim'][:, :]), w=[k], semkey='s5par_ld')
    p.dma('sp', I_dma(T['ldt'][:], A['logdt'][:, :]), w=[k], semkey='s5par_ld')
    a = lambda n: T[n][:]
    p.op('act', I_act(a('dt'), a('ldt'), AF.Exp), r=[k], w=[k])
    p.op('dve', I_tt(a('t'), a('lre'), a('dt'), ALU.mult), r=[k], w=[k])
    p.op('act', I_act(a('mag'), a('t'), AF.Exp), r=[k], w=[k])
    p.op('dve', I_tt(a('ang'), a('lim'), a('dt'), ALU.mult), r=[k], w=[k])
    for _ in range(5):
        p.op('dve', I_ts(a('t'), a('ang'), PI, 2 * PI, ALU.is_gt, ALU.mult), r=[k], w=[k])
        p.op('dve', I_tt(a('ang'), a('ang'), a('t'), ALU.subtract), r=[k], w=[k])
    p.op('dve', I_ts(a('angc'), a('ang'), PI / 2, None, ALU.add), r=[k], w=[k])
    p.op('dve', I_ts(a('t'), a('angc'), PI, 2 * PI, ALU.is_gt, ALU.mult), r=[k], w=[k])
    p.op('dve', I_tt(a('angc'), a('angc'), a('t'), ALU.subtract), r=[k], w=[k])
    p.op('act', I_act(a('s1'), a('ang'), AF.Sin), r=[k], w=[k])
    p.op('act', I_act(a('c1'), a('angc'), AF.Sin), r=[k], w=[k])
    p.op('dve', I_tt(a('are'), a('mag'), a('c1'), ALU.mult), r=[k], w=[k])
    p.op('dve', I_tt(a('aim'), a('mag'), a('s1'), ALU.mult), r=[k], w=[k])
    p.op('dve', I_ts(a('nr'), a('are'), -1.0, None, ALU.add), r=[k], w=[k])
    p.op('dve', I_tt(a('den'), a('lre'), a('lre'), ALU.mult), r=[k], w=[k])
    p.op('dve', I_tt(a('t'), a('lim'), a('lim'), ALU.mult), r=[k], w=[k])
    p.op('dve', I_tt(a('den'), a('den'), a('t'), ALU.add), r=[k], w=[k])
    p.op('dve', I_recip(a('den'), a('den')), r=[k], w=[k])
    p.op('dve', I_tt(a('t'), a('nr'), a('lre'), ALU.mult), r=[k], w=[k])
    p.op('dve', I_tt(a('t2'), a('aim'), a('lim'), ALU.mult), r=[k], w=[k])
    p.op('dve', I_tt(a('t'), a('t'), a('t2'), ALU.add), r=[k], w=[k])
    p.op('dve', I_tt(a('fre'), a('t'), a('den'), ALU.mult), r=[k], w=[k])
    p.op('dve', I_tt(a('t'), a('aim'), a('lre'), ALU.mult), r=[k], w=[k])
    p.op('dve', I_tt(a('t2'), a('nr'), a('lim'), ALU.mult), r=[k], w=[k])
    p.op('dve', I_tt(a('t'), a('t'), a('t2'), ALU.subtract), r=[k], w=[k])
    p.op('dve', I_tt(a('fim'), a('t'), a('den'), ALU.mult), r=[k], w=[k])
    p.op('dve', I_ts(a('nfre'), a('fre'), -1.0, None, ALU.mult), r=[k], w=[k])
    p.op('dve', I_ts(a('nfim'), a('fim'), -1.0, None, ALU.mult), r=[k], w=[k])
    return T


def s5_body(cx, A):
    p = cx.p
    T = s5_params(cx, A)
    PK = 's5par'
    if 'dbg' in A:
        for i, nm in enumerate(['dt', 'mag', 'ang', 's1', 'c1', 'fre', 'fim', 'den']):
            p.dma('sp', I_dma(A['dbg'][:, i * 2 * NPT:(i + 1) * 2 * NPT], T[nm][:]), r=[PK], w=['dbgo'])
    ub = p.sb('ub', [128, 4, SEQ], BF16)
    if 'GHN' in A:
        G = A['GHN']
        m0c = cx.vec[:, VC['m0']:VC['m0'] + 1]
        m1c = cx.vec[:, VC['m1']:VC['m1'] + 1]
        for ck in range(4):
            for w in range(NW):
                r = 0 if w < NW // 2 else 1
                if r == 0:
                    cols = slice(w * SW, (w + 1) * SW)
                else:
                    w2 = w - NW // 2
                    cols = slice(NT - (w2 + 1) * SW, NT - w2 * SW)
                X, xk = cx.rot('gx', [128, SW], F32, n=2)
                Z, zk = cx.rot('gz', [128, SW], F32, n=2)
                p.dma('sp', I_dma(X[:], G.g_rows(r, ck * 128)[:, cols]), w=[xk])
                p.dma('sp', I_dma(Z[:], G.g_rows(r, 512 + ck * 128)[:, cols]), w=[zk])
                dst = ub[:, ck, w * SW:(w + 1) * SW]
                if r == 1:
                    dst = dst[:, ::-1]
                mcombine(cx, dst, 'ub%d' % ck, X[:], xk, Z[:], zk, m0c if r == 0 else m1c, m1c if r == 0 else m0c)
    else:
        for ck in range(4):
            p.dma('pool', I_dma(ub[:, ck, :], A['uT'][ck * 128:(ck + 1) * 128, :]), w=['ub%d' % ck])
    ident = p.sb('ident', [128, 128], F32)
    p.dma('sp', I_dma(ident[:], A['ident'][:, :]), w=['ident'])
    dsk = p.sb('dskc', [128, 4], F32)
    p.dma('sp', I_dma(dsk[:], A['dsk'][:, :]), w=['dskc'])
    yacc = [p.sb('yacc%d' % i, [128, SEQ], F32) for i in range(2)]
    bb_i = [0]

    def bbank():
        i = bb_i[0]
        bb_i[0] = (i + 1) % 6
        return cx.banks[i], 'bank%d' % i
    yb_i = [0]

    def ybank():
        i = 6 + yb_i[0]
        yb_i[0] = 1 - yb_i[0]
        return cx.banks[i], 'bank%d' % i

    for ck in range(4):
        ya = yacc[ck % 2]
        yk = 'yacc%d' % (ck % 2)
        dD, dDk = cx.rot('diagD', [128, 128], BF16)
        p.op('dve', I_ts(dD[:], ident[:], dsk[:, ck:ck + 1], None, ALU.mult), r=['ident', 'dskc'], w=[dDk])
        for d in range(2):
            tabs = []
            for q in range(4):
                pt = ck * 4 + q
                col = d * NPT + pt
                cosT, ck_ = cx.rot('cosT', [128, SW], F32, n=8)
                sinT, sk_ = cx.rot('sinT', [128, SW], F32, n=8)
                pw, pwk = cx.rot('pw', [128, 2, 12], F32, n=8)
                tk = 'tab%d' % ((d * 4 + q) % 8)
                p.op('dve', I_memset(cosT[:, 0:1], 1.0), w=[tk])
                p.op('dve', I_memset(sinT[:, 0:1], 0.0), w=[tk])
                p.op('dve', I_copy(pw[:, 0, 0:1], T['c1'][:, col:col + 1]), r=[PK], w=[tk])
                p.op('dve', I_copy(pw[:, 1, 0:1], T['s1'][:, col:col + 1]), r=[PK], w=[tk])
                L = 1
                lv = 0
                while L < SW:
                    pc = pw[:, 0, lv:lv + 1]
                    ps_ = pw[:, 1, lv:lv + 1]
                    tmp, tmk = cx.rot('tbtmp', [128, SW // 2], F32, n=2)
                    p.op('dve', I_ts(tmp[:, :L], sinT[:, 0:L], ps_, None, ALU.mult), r=[tk], w=[tmk])
                    p.op('dve', I_stt(cosT[:, L:2 * L], cosT[:, 0:L], pc, tmp[:, :L], ALU.mult, ALU.subtract), r=[tk, tmk], w=[tk])
                    tmp2, tmk2 = cx.rot('tbtmp', [128, SW // 2], F32, n=2)
                    p.op('dve', I_ts(tmp2[:, :L], cosT[:, 0:L], ps_, None, ALU.mult), r=[tk], w=[tmk2])
                    p.op('dve', I_stt(sinT[:, L:2 * L], sinT[:, 0:L], pc, tmp2[:, :L], ALU.mult, ALU.add), r=[tk, tmk2], w=[tk])
                    p.op('dve', I_tt(pw[:, 0, 11:12], ps_, ps_, ALU.mult), r=[tk], w=[tk])
                    p.op('dve', I_stt(pw[:, 0, lv + 1:lv + 2], pc, pc, pw[:, 0, 11:12], ALU.mult, ALU.subtract), r=[tk], w=[tk])
                    p.op('dve', I_stt(pw[:, 1, lv + 1:lv + 2], pc, 2.0, ps_, ALU.mult, ALU.mult), r=[tk], w=[tk])
                    L *= 2
                    lv += 1
                cW = pw[:, 0, lv:lv + 1]
                sW = pw[:, 1, lv:lv + 1]
                if 'dbg2' in A and ck == 0 and d == 0 and q == 0:
                    p.dma('sp', I_dma(A['dbg2'][:, 0:SW], cosT[:]), r=[tk], w=['dbgo2'])
                    p.dma('sp', I_dma(A['dbg2'][:, SW:2 * SW], sinT[:]), r=[tk], w=['dbgo2'])
                braw, brk = cx.rot('bw', [128, 2, 128], BF16, n=8)
                p.dma('pool', I_dma(braw[:, 0, :], A['Bre'][d, pt]), w=[brk])
                p.dma('pool', I_dma(braw[:, 1, :], A['Bim'][d, pt]), w=[brk])
                craw, crk = cx.rot('craw', [128, 2, 128], F32, n=4)
                p.dma('sp', I_dma(craw[:, 0, :], A['CR'][d, pt]), w=[crk])
                p.dma('sp', I_dma(craw[:, 1, :], A['CI'][d, pt]), w=[crk])
                cw, cwk = cx.rot('cw', [128, 2, 128], BF16, n=8)
                ctmp, ctk = cx.rot('ctmp', [128, 128], F32, n=2)
                fre = T['fre'][:, col:col + 1]
                nfim = T['nfim'][:, col:col + 1]
                nfre = T['nfre'][:, col:col + 1]
                p.op('dve', I_ts(ctmp[:], craw[:, 1, :], nfim, None, ALU.mult), r=[crk, PK], w=[ctk])
                p.op('dve', I_stt(cw[:, 0, :], craw[:, 0, :], fre, ctmp[:], ALU.mult, ALU.add), r=[crk, PK, ctk], w=[cwk])
                ctmp2, ctk2 = cx.rot('ctmp', [128, 128], F32, n=2)
                p.op('dve', I_ts(ctmp2[:], craw[:, 0, :], nfim, None, ALU.mult), r=[crk, PK], w=[ctk2])
                p.op('dve', I_stt(cw[:, 1, :], craw[:, 1, :], nfre, ctmp2[:], ALU.mult, ALU.add), r=[crk, PK, ctk2], w=[cwk])
                car, cak = cx.rot('carry', [128, 8], F32, n=8)
                tabs.append(dict(cos=cosT, sin=sinT, tk=tk, cW=cW, sW=sW, braw=braw, brk=brk, cw=cw, cwk=cwk,
                                 r=T['mag'][:, col:col + 1], car=car, cak=cak))
            worder = range(NW) if d == 0 else range(NW - 1, -1, -1)
            rv = (lambda ap: ap) if d == 0 else (lambda ap: ap[:, ::-1])
            for wi, w in enumerate(worder):
                win = slice(w * SW, (w + 1) * SW)
                yb, ybk = ybank()
                for q in range(4):
                    tb_ = tabs[q]
                    cosT, sinT, tk = tb_['cos'], tb_['sin'], tb_['tk']
                    bre, brek = bbank()
                    bim, bimk = bbank()
                    p.op('pe', I_mm(bre[:], tb_['braw'][:, 0, :], ub[:, ck, win], True, True), r=[tb_['brk'], 'ub%d' % ck], w=[brek])
                    p.op('pe', I_mm(bim[:], tb_['braw'][:, 1, :], ub[:, ck, win], True, True), r=[tb_['brk'], 'ub%d' % ck], w=[bimk])
                    t1, t1k = cx.rot('t1', [128, SW], F32)
                    t2, t2k = cx.rot('t2', [128, SW], F32)
                    t3, t3k = cx.rot('t3', [128, SW], F32)
                    t4, t4k = cx.rot('t4', [128, SW], F32)
                    p.op('dve', I_tt(t1[:], rv(bre[:]), cosT[:], ALU.mult), r=[brek, tk], w=[t1k])
                    p.op('dve', I_tt(t2[:], rv(bim[:]), sinT[:], ALU.mult), r=[bimk, tk], w=[t2k])
                    p.op('dve', I_tt(t3[:], rv(bim[:]), cosT[:], ALU.mult), r=[bimk, tk], w=[t3k])
                    p.op('dve', I_tt(t4[:], rv(bre[:]), sinT[:], ALU.mult), r=[brek, tk], w=[t4k])
                    wre, wrk = cx.rot('wre', [128, SW], F32)
                    wim, wik = cx.rot('wim', [128, SW], F32)
                    p.op('pool', I_tt(wre[:], t1[:], t2[:], ALU.add), r=[t1k, t2k], w=[wrk])
                    p.op('pool', I_tt(wim[:], t3[:], t4[:], ALU.subtract), r=[t3k, t4k], w=[wik])
                    zre, zrk = cx.rot('zre', [128, SW], F32)
                    zim, zik = cx.rot('zim', [128, SW], F32)
                    car, cak = tb_['car'], tb_['cak']
                    rbc = tb_['r'].to_broadcast([128, SW])
                    if wi == 0:
                        ire, iim = 0.0, 0.0
                    else:
                        ire, iim = car[:, 2:3], car[:, 3:4]
                    p.op('dve', I_scan(zre[:], rbc, wre[:], ire), r=[PK, wrk, cak], w=[zrk])
                    p.op('dve', I_scan(zim[:], rbc, wim[:], iim), r=[PK, wik, cak], w=[zik])
                    if wi < NW - 1:
                        p.op('dve', I_tt(car[:, 0:1], zim[:, SW - 1:SW], tb_['sW'], ALU.mult), r=[zik, tk], w=[cak])
                        p.op('dve', I_tt(car[:, 1:2], zre[:, SW - 1:SW], tb_['sW'], ALU.mult), r=[zrk, tk], w=[cak])
                        p.op('dve', I_stt(car[:, 2:3], zre[:, SW - 1:SW], tb_['cW'], car[:, 0:1], ALU.mult, ALU.subtract), r=[zrk, tk], w=[cak])
                        p.op('dve', I_stt(car[:, 3:4], zim[:, SW - 1:SW], tb_['cW'], car[:, 1:2], ALU.mult, ALU.add), r=[zik, tk], w=[cak])
                    u1, u1k = cx.rot('u1', [128, SW], F32)
                    u2, u2k = cx.rot('u2', [128, SW], F32)
                    u3, u3k = cx.rot('u3', [128, SW], F32)
                    u4, u4k = cx.rot('u4', [128, SW], F32)
                    p.op('pool', I_tt(u1[:], zre[:], cosT[:], ALU.mult), r=[zrk, tk], w=[u1k])
                    p.op('pool', I_tt(u2[:], zim[:], sinT[:], ALU.mult), r=[zik, tk], w=[u2k])
                    p.op('pool', I_tt(u3[:], zim[:], cosT[:], ALU.mult), r=[zik, tk], w=[u3k])
                    p.op('pool', I_tt(u4[:], zre[:], sinT[:], ALU.mult), r=[zrk, tk], w=[u4k])
                    xb, xbk = cx.rot('xb', [128, 2, SW], BF16, n=3)
                    p.op('dve', I_tt(rv(xb[:, 0, :]), u1[:], u2[:], ALU.subtract), r=[u1k, u2k], w=[xbk])
                    p.op('dve', I_tt(rv(xb[:, 1, :]), u3[:], u4[:], ALU.add), r=[u3k, u4k], w=[xbk])
                    first = (q == 0)
                    last = (q == 3) and d == 1
                    p.op('pe', I_mm(yb[:], tb_['cw'][:, 0, :], xb[:, 0, :], first, False), r=[tb_['cwk'], xbk], w=[ybk])
                    p.op('pe', I_mm(yb[:], tb_['cw'][:, 1, :], xb[:, 1, :], False, last), r=[tb_['cwk'], xbk], w=[ybk])
                if d == 0:
                    p.op('pe', I_mm(yb[:], dD[:], ub[:, ck, win], False, True), r=[dDk, 'ub%d' % ck], w=[ybk])
                    p.op('act', I_act(ya[:, win], yb[:], AF.Copy), r=[ybk], w=[yk + '_%d' % w])
                else:
                    p.op('dve', I_tt(ya[:, win], yb[:], ya[:, win], ALU.add), r=[ybk, yk + '_%d' % w], w=[yk + '_%d' % w])
        if 'yO' in A:
            m0c = cx.vec[:, VC['m0']:VC['m0'] + 1]
            m1c = cx.vec[:, VC['m1']:VC['m1'] + 1]
            for hb in range(NT // SW):
                A1 = ya[:, hb * SW:(hb + 1) * SW]
                B1 = ya[:, SEQ - (hb + 1) * SW:SEQ - hb * SW][:, ::-1]
                ka = yk + '_%d' % hb
                kb = yk + '_%d' % (NW - 1 - hb)
                ot, otk = cx.rot('yo_t', [128, SW], F32, n=2)
                mcombine(cx, ot[:], otk, A1, ka, B1, kb, m0c, m1c)
                p.dma('sp', I_dma(A['yO'][ck * 128:(ck + 1) * 128, hb * SW:(hb + 1) * SW], ot[:]), r=[otk], w=['yout%d' % ck])
                st_, stk_ = cx.rot('yo_t', [128, SW], F32, n=2)
                mcombine(cx, st_[:], stk_, A1, ka, B1, kb, m1c, m0c)
                p.dma('sp', I_dma(A['yS'].src_rows(ck * 128)[:, hb * SW:(hb + 1) * SW], st_[:]), r=[stk_], w=['yout%d' % ck])
        else:
            p.dma('sp', I_dma(A['yT'][ck * 128:(ck + 1) * 128, :], ya[:]), r=[yk + '_%d' % w for w in range(NW)], w=['yout%d' % ck])


def build_s5(debug=False, arena=False):
    nc = bass.Bass("TRN2", target_bir_lowering=False)
    A = {}

    def inp(name, shape, dt=F32):
        A[name] = nc.dram_tensor(name, list(shape), dt, kind="ExternalInput").ap()
    inp('uT', [512, SEQ])
    inp('Bre', [2, NPT, 128, 128])
    inp('Bim', [2, NPT, 128, 128])
    inp('CR', [2, NPT, 128, 128])
    inp('CI', [2, NPT, 128, 128])
    inp('lamre', [128, 2 * NPT])
    inp('lamim', [128, 2 * NPT])
    inp('logdt', [128, 2 * NPT])
    inp('dsk', [128, 4])
    inp('ident', [128, 128])
    A['yT'] = nc.dram_tensor('yT', [512, SEQ], F32, kind="ExternalOutput").ap()
    cx = Cx(nc, arena=arena)
    if arena:
        cx.new_stage()
    if debug:
        A['dbg'] = nc.dram_tensor('dbg', [128, 8 * 2 * NPT], F32, kind="ExternalOutput").ap()
        A['dbg2'] = nc.dram_tensor('dbg2', [128, 2 * SW], F32, kind="ExternalOutput").ap()
    s5_body(cx, A)
    cx.p.finalize()
    return nc, cx


def s5_host_inputs(inp, j, half):
    g0 = 32 * half
    Bre = np.zeros((2, NPT, 128, 128), np.float32)
    Bim = np.zeros_like(Bre)
    CR = np.zeros_like(Bre)
    CI = np.zeros_like(Bre)
    lamre = np.zeros((128, 2 * NPT), np.float32)
    lamim = np.zeros_like(lamre)
    logdt = np.zeros_like(lamre)
    for d in range(2):
        for pt in range(NPT):
            for gl in range(2):
                g = g0 + 2 * pt + gl
                r0 = (pt % 4) * 32 + gl * 16
                Bre[d, pt, r0:r0 + 16, gl * 64:(gl + 1) * 64] = inp['s5_b_re'][j, d, g].T
                Bim[d, pt, r0:r0 + 16, gl * 64:(gl + 1) * 64] = inp['s5_b_im'][j, d, g].T
                CR[d, pt, gl * 64:(gl + 1) * 64, r0:r0 + 16] = inp['s5_c_re'][j, d, g].T
                CI[d, pt, gl * 64:(gl + 1) * 64, r0:r0 + 16] = inp['s5_c_im'][j, d, g].T
                lamre[gl * 64:(gl + 1) * 64, d * NPT + pt] = inp['s5_lambda_re'][j, d, g]
                lamim[gl * 64:(gl + 1) * 64, d * NPT + pt] = inp['s5_lambda_im'][j, d, g]
                logdt[gl * 64:(gl + 1) * 64, d * NPT + pt] = inp['s5_log_dt'][j, d, g]
    dsk = np.ascontiguousarray(inp['s5_d'][j, 512 * half:512 * half + 512].reshape(4, 128).T)
    return dict(Bre=Bre, Bim=Bim, CR=CR, CI=CI, lamre=lamre, lamim=lamim, logdt=logdt, dsk=dsk,
                ident=np.eye(128, dtype=np.float32))


NEXT = 3072
GRP = [(1, 2048), (4, 512), (16, 128)]
ASCALE = 128 ** -0.5


def sub_view(ap2d, d):
    if d == 1:
        return ap2d.rearrange("p (d i) -> p d i", d=1)
    return ap2d.rearrange("p (i d) -> p d i", d=d)


def attn_body(cx, A, flip=False, src=None, dst=None, gh=None):
    p = cx.p
    vec = cx.vec

    def load_blk(xt, xk, c, tb):
        if gh is None or tb < NT // 512:
            p.dma('sp', I_dma(xt[:], _src[c * 128:(c + 1) * 128, _cols(tb)]), w=[xk])
            return
        hb = tb - NT // 512
        cols = slice(1024 - 512 * (hb + 1), 1024 - 512 * hb)
        pc = (c + 4) % 8
        X, xk2 = cx.rot('gx', [128, 512], F32, n=2)
        Z, zk2 = cx.rot('gz', [128, 512], F32, n=2)
        p.dma('sp', I_dma(X[:], gh.g_rows(0, pc * 128)[:, cols]), w=[xk2])
        p.dma('sp', I_dma(Z[:], gh.g_rows(1, pc * 128)[:, cols]), w=[zk2])
        mcombine(cx, xt[:], xk, X[:], xk2, Z[:], zk2, vec[:, VC['m1']:VC['m1'] + 1], vec[:, VC['m0']:VC['m0'] + 1])

    def _cols(tb):
        if src is None or not flip:
            return slice(tb * 512, (tb + 1) * 512)
        return slice(SEQ - (tb + 1) * 512, SEQ - tb * 512)
    _src = A['hT_ext'] if src is None else src
    _dst = A['hT_out'] if dst is None else dst
    rvf = (lambda ap: ap[:, ::-1]) if (flip and src is not None) else (lambda ap: ap)
    hn = p.sb('hnx', [128, NCH, NEXT], BF16)
    mT = p.sb('mT', [128, NCH, NT], BF16)
    num = p.sb('numacc', [128, NT], F32)
    den = p.sb('denacc', [128, NT], F32)
    for tb in range(NEXT // 512):
        bank, bk = cx.bank()
        for c in range(NCH):
            xt, xk = cx.rot('xin', [128, 512], F32, n=2)
            load_blk(xt, xk, c, tb)
            sq, sk = cx.next_sq()
            p.op('act', I_act(sq[:], xt[:], AF.Square), r=[xk], w=[sk])
            p.op('pe', I_mm(bank[:], cx.ones[:], sq[:], c == 0, c == NCH - 1), r=[sk, 'ones'], w=[bk])
        rs, rk = cx.rstd_from_bank(bank, bk, 512, D)
        for c in range(NCH):
            xt, xk = cx.rot('xin', [128, 512], F32, n=2)
            load_blk(xt, xk, c, tb)
            rv_ = (lambda ap: ap[:, ::-1]) if (gh is not None and tb >= NT // 512) else rvf
            p.op('dve', I_stt(rv_(hn[:, c, tb * 512:(tb + 1) * 512]), xt[:], vec[:, VC['gn'] + c:VC['gn'] + c + 1], rs[:],
                              ALU.mult, ALU.mult), r=[xk, 'vec', rk], w=['hnx'])
    sb_i = [0]

    def sbank():
        i = sb_i[0]
        sb_i[0] = (i + 1) % 4
        return cx.banks[i], 'bank%d' % i
    ob_i = [0]

    def obanks():
        i = ob_i[0]
        ob_i[0] = 1 - i
        return cx.banks[4 + i], 'bank%d' % (4 + i), cx.banks[6 + i], 'bank%d' % (6 + i)

    def qknorm(bank, bk, n, gcol, dst, dkey):
        sq, sk = cx.next_sq()
        p.op('act', I_act(sq[:, :n], bank[:, :n], AF.Square), r=[bk], w=[sk])
        b2, b2k = sbank()
        p.op('pe', I_mm(b2[:, :n], cx.ones[:], sq[:, :n], True, True), r=[sk, 'ones'], w=[b2k])
        rs, rk = cx.rstd_from_bank(b2, b2k, n, 128)
        p.op('dve', I_stt(dst, bank[:, :n], vec[:, gcol:gcol + 1], rs[:, :n], ALU.mult, ALU.mult), r=[bk, 'vec', rk], w=[dkey])

    for h in range(8):
        p.op('pool', I_memset(num[:], 0.0), w=['numacc'])
        p.op('pool', I_memset(den[:], 0.0), w=['denacc'])
        bt, btk = cx.rot('biasT', [128, 3, 256], F32, n=2)
        for g in range(3):
            p.dma('sp', I_dma(bt[:, g, :], A['biasT'][g * 8 + h]), w=[btk])
        for g, (d, Lq) in enumerate(GRP):
            nto = Lq // 128
            wsl, wsk = cx.rot('wqkv', [128, NCH, 384], BF16, n=3)
            for kind in range(3):
                c0 = kind * 3072 + g * 1024 + h * 128
                p.dma('pool', I_dma(wsl[:, :, kind * 128:(kind + 1) * 128],
                                    A['wqkv'][:, c0:c0 + 128].rearrange("(k p) n -> p k n", p=128)), w=[wsk])
            qT, qk_ = cx.rot('qT', [128, NT], BF16, n=2)
            kT, kk_ = cx.rot('kT', [128, NEXT], BF16, n=2)
            vt, vk_ = cx.rot('vt', [128, 32, 128], BF16, n=2)

            for kind, dstT, dk, gcol in ((0, qT, qk_, VC['aq']), (1, kT, kk_, VC['ak'])):
                for bi in range(4):
                    b, bk = sbank()
                    for kc in range(NCH):
                        if d == 1:
                            rhs, o_ap = hn[:, kc, bi * 512:(bi + 1) * 512], b[:]
                        elif d == 4:
                            rhs, o_ap = sub_view(hn[:, kc, 0:NT], 4)[:, bi, :], b[:]
                        else:
                            rhs = sub_view(hn[:, kc, 0:NT], 16)[:, 4 * bi:4 * bi + 4, :]
                            o_ap = b[:].rearrange("p (a b) -> p a b", a=4)
                        p.op('pe', I_mm(o_ap, wsl[:, kc, kind * 128:(kind + 1) * 128], rhs, kc == 0, kc == NCH - 1),
                             r=[wsk, 'hnx'], w=[bk])
                    qknorm(b, bk, 512, gcol, dstT[:, bi * 512:(bi + 1) * 512], dk)
            nh = 64 * d
            for b0 in range(0, nh, 512):
                n = min(512, nh - b0)
                b, bk = sbank()
                for kc in range(NCH):
                    if d == 1:
                        rhs = hn[:, kc, NT:NT + 64]
                        o_ap = b[:, :64]
                    else:
                        r0 = b0 // 64
                        nr = n // 64
                        rhs = sub_view(hn[:, kc, NT:NT + 64 * d], d)[:, r0:r0 + nr, :]
                        o_ap = b[:, :n].rearrange("p (a b) -> p a b", a=nr)
                    p.op('pe', I_mm(o_ap, wsl[:, kc, 128:256], rhs, kc == 0, kc == NCH - 1), r=[wsk, 'hnx'], w=[bk])
                qknorm(b, bk, n, VC['ak'], kT[:, NT + b0:NT + b0 + n], kk_)
            for t0 in range(0, 16, 4):
                b, bk = sbank()
                for tt in range(4):
                    t = t0 + tt
                    r, m = t // nto, t % nto
                    for kc in range(NCH):
                        lhsT = sub_view(hn[:, kc, 0:NT], d)[:, r, m * 128:(m + 1) * 128]
                        p.op('pe', I_mm(b[:, tt * 128:(tt + 1) * 128], lhsT, wsl[:, kc, 256:384], kc == 0, kc == NCH - 1),
                             r=[wsk, 'hnx'], w=[bk])
                p.op('act', I_act(vt[:, t0:t0 + 4, :], b[:].rearrange("p (a b) -> p a b", a=4), AF.Copy), r=[bk], w=[vk_])
            for r0 in range(0, d, 4):
                nr = min(4, d - r0)
                b, bk = sbank()
                for rr in range(nr):
                    r = r0 + rr
                    for kc in range(NCH):
                        lhsT = sub_view(hn[:, kc, NT:NT + 64 * d], d)[:, r, :]
                        p.op('pe', I_mm(b[:64, rr * 128:(rr + 1) * 128], lhsT, wsl[:, kc, 256:384], kc == 0, kc == NCH - 1),
                             r=[wsk, 'hnx'], w=[bk])
                p.op('act', I_act(vt[:64, 16 + r0:16 + r0 + nr, :], b[:64, :nr * 128].rearrange("p (a b) -> p a b", a=nr), AF.Copy),
                     r=[bk], w=[vk_])
            for r in range(d):
                qoff = r * Lq
                ob = None
                for m in range(nto + 1):
                    halo = (m == nto)
                    nk = 64 if halo else 128
                    b0_ = 64 if m == 0 else 0
                    b1_ = 64 if halo else min(256, Lq - (128 * m - 64))
                    ktile = kT[:, NT + r * 64:NT + r * 64 + 64] if halo else kT[:, qoff + m * 128:qoff + (m + 1) * 128]
                    vtile = vt[:64, 16 + r, :] if halo else vt[:, r * nto + m, :]
                    qs = qoff + 128 * m - 64 + b0_
                    sbk, sbkk = sbank()
                    p.op('pe', I_mm(sbk[:nk, b0_:b1_], ktile, qT[:, qs:qs + (b1_ - b0_)], True, True), r=[kk_, qk_], w=[sbkk])
                    st, stk = cx.rot('stmp', [128, 256], F32, n=3)
                    p.op('dve', I_stt(st[:nk, b0_:b1_], sbk[:nk, b0_:b1_], ASCALE, bt[:nk, g, b0_:b1_], ALU.mult, ALU.add),
                         r=[sbkk, btk], w=[stk])
                    PT, ptk = cx.rot('PTa', [128, 256], BF16, n=4)
                    p.op('act', I_act(PT[:nk, b0_:b1_], st[:nk, b0_:b1_], AF.Exp), r=[stk], w=[ptk])
                    def flush(ep):
                        qlo = max(0, 512 * ep - 64)
                        qhi = min(Lq, 512 * ep + 448)
                        c0f = qlo - (512 * ep - 64)
                        wdt = qhi - qlo
                        nv = sub_view(num[:, :], d)[:, r, qlo:qhi]
                        dv = sub_view(den[:, :], d)[:, r, qlo:qhi]
                        p.op('dve', I_tt(nv, ob[0][:, c0f:c0f + wdt], nv, ALU.add), r=[ob[1], 'numacc'], w=['numacc'])
                        p.op('dve', I_tt(dv, ob[2][:, c0f:c0f + wdt], dv, ALU.add), r=[ob[3], 'denacc'], w=['denacc'])
                    if m == 0:
                        ob = obanks()
                        p.op('pe', I_mm(ob[0][:, 64:128], vtile, PT[:nk, 64:128], True, True), r=[vk_, ptk], w=[ob[1]])
                        p.op('pe', I_mm(ob[2][:, 64:128], cx.ones[:nk, :], PT[:nk, 64:128], True, True), r=['ones', ptk], w=[ob[3]])
                    else:
                        q0 = 64 + 128 * (m - 1)
                        wq_ = min(Lq, q0 + 128) - q0
                        c0 = 128 * (m % 4)
                        p.op('pe', I_mm(ob[0][:, c0:c0 + wq_], vtile, PT[:nk, 0:wq_], False, True), r=[vk_, ptk], w=[ob[1]])
                        p.op('pe', I_mm(ob[2][:, c0:c0 + wq_], cx.ones[:nk, :], PT[:nk, 0:wq_], False, True), r=['ones', ptk], w=[ob[3]])
                        if m % 4 == 3 or halo:
                            flush(m // 4)
                    if not halo:
                        q0 = 64 + 128 * m
                        wq_ = min(Lq, q0 + 128) - q0
                        if (m + 1) % 4 == 0:
                            ob = obanks()
                        c0 = 128 * ((m + 1) % 4)
                        p.op('pe', I_mm(ob[0][:, c0:c0 + wq_], vtile, PT[:nk, 128:128 + wq_], True, False), r=[vk_, ptk], w=[ob[1]])
                        p.op('pe', I_mm(ob[2][:, c0:c0 + wq_], cx.ones[:nk, :], PT[:nk, 128:128 + wq_], True, False), r=['ones', ptk], w=[ob[3]])
        p.op('dve', I_recip(den[:], den[:]), r=['denacc'], w=['denacc'])
        p.op('pool', I_tt(mT[:, h, :], num[:], den[:], ALU.mult), r=['numacc', 'denacc'], w=['mT'])
    for ns in range(2):
        wo, wok = wslab(cx, [(A['wo_a'][:, ns * 512:(ns + 1) * 512], 0)])
        for j in range(4):
            n = ns * 4 + j
            for tb in range(NT // 512):
                b, bk = sbank()
                for kc in range(NCH):
                    p.op('pe', I_mm(b[:], wo[:, kc * 512 + j * 128: kc * 512 + (j + 1) * 128], mT[:, kc, tb * 512:(tb + 1) * 512],
                                    kc == 0, kc == NCH - 1), r=[wok, 'mT'], w=[bk])
                xt, xk = cx.rot('xin', [128, 512], F32, n=2)
                p.dma('sp', I_dma(xt[:], _src[n * 128:(n + 1) * 128, _cols(tb)]), w=[xk])
                p.op('dve', I_tt(xt[:], rvf(b[:]), xt[:], ALU.add), r=[bk, xk], w=[xk])
                p.dma('sp', I_dma(_dst[n * 128:(n + 1) * 128, _cols(tb)], xt[:]), r=[xk], w=['hTout'])


def build_attn():
    nc = bass.Bass("TRN2", target_bir_lowering=False)
    A = {}

    def inp(name, shape, dt=F32):
        A[name] = nc.dram_tensor(name, list(shape), dt, kind="ExternalInput").ap()
    inp('hT_ext', [D, NEXT])
    inp('vecs', [128, NVEC])
    inp('wqkv', [D, 9216])
    inp('wo_a', [D, D])
    inp('biasT', [24, 128, 256])
    A['hT_out'] = nc.dram_tensor('hT_out', [D, NT], F32, kind="ExternalOutput").ap()
    cx = Cx(nc)
    cx.vec = cx.p.sb('vec', [128, NVEC], F32)
    cx.p.dma('sp', I_dma(cx.vec[:], A['vecs'][:, :]), w=['vec'])
    attn_body(cx, A)
    cx.p.finalize()
    return nc, cx


def t5_bucket(rel):
    nb = 16
    ret = (rel > 0).astype(np.int32) * nb
    n = np.abs(rel)
    max_exact = nb // 2
    large = max_exact + (np.log(np.maximum(n, 1).astype(np.float32) / max_exact)
                         / np.log(1024 / max_exact) * (nb - max_exact)).astype(np.int32)
    large = np.minimum(large, nb - 1)
    return (ret + np.where(n < max_exact, n, large)).astype(np.int32)


def host_bias(bias_table, flip):
    a = np.arange(128)[:, None]
    b = np.arange(256)[None, :]
    rel = a - b + 64
    out = np.full((24, 128, 256), -1e30, np.float32)
    band = np.abs(rel) <= 64
    for g, (dil, _) in enumerate(GRP):
        bk = t5_bucket((-rel if flip else rel) * dil)
        for h in range(8):
            out[g * 8 + h] = np.where(band, bias_table[bk, g * 8 + h], np.float32(-1e30))
    return out


def build_norm():
    nc = bass.Bass("TRN2", target_bir_lowering=False)
    A = {}
    A['hT'] = nc.dram_tensor('hT', [D, NT], F32, kind="ExternalInput").ap()
    A['vecs'] = nc.dram_tensor('vecs', [128, NVEC], F32, kind="ExternalInput").ap()
    A['hn_out'] = nc.dram_tensor('hn_out', [D, NT], F32, kind="ExternalOutput").ap()
    cx = Cx(nc)
    p = cx.p
    cx.hT = p.sb('hT', [128, NCH, NT], F32)
    cx.vec = p.sb('vec', [128, NVEC], F32)
    p.dma('sp', I_dma(cx.vec[:], A['vecs'][:, :]), w=['vec'])
    load_hT(cx, A['hT'])
    emit_norm(cx, A['hn_out'])
    p.finalize()
    return nc, cx


def build_fused(nlayers=4):
    nc = bass.Bass("TRN2", target_bir_lowering=False)
    shapes = {}

    def inp(name, shape, dt=F32):
        shapes[name] = list(shape)

    class Lazy(dict):
        def __missing__(self, name):
            ap = nc.dram_tensor(name, shapes[name], F32, kind="ExternalInput").ap()
            self[name] = ap
            return ap
    A = Lazy()

    def scratch(name):
        return nc.dram_tensor(name, [D, SEQ], F32, kind="Internal").ap()
    inp('xT', [D, SEQ])
    inp('memT', [D, MEMLEN])
    inp('ident', [128, 128])
    inp('v0', [128, NVEC])
    inp('biasT0', [24, 128, 256])
    inp('biasT1', [24, 128, 256])
    for i in range(4):
        inp('vecs%d' % i, [128, NVEC])
        inp('wq%d' % i, [D, D])
        inp('wkv%d' % i, [D, 2 * D])
        inp('wo%d' % i, [D, D])
        inp('w1_%d' % i, [D, DFF])
        inp('w2_%d' % i, [DFF, D])
    for j in range(2):
        inp('wglu%d' % j, [D, 2 * D])
        inp('wqkv%d' % j, [D, 9216])
        inp('woa%d' % j, [D, D])
        inp('avecs%d' % j, [128, NVEC])
        for c in range(2):
            sfx = '%d%d' % (j, c)
            for nm in ('Bre', 'Bim', 'CR', 'CI'):
                inp(nm + sfx, [2, NPT, 128, 128])
            for nm in ('lamre', 'lamim', 'logdt'):
                inp(nm + sfx, [128, 2 * NPT])
            inp('dsk' + sfx, [128, 4])
    xT = A['xT']
    outT = nc.dram_tensor('outT', [D, SEQ], F32, kind="ExternalOutput").ap()
    HN = scratch('HN')
    Y = scratch('Y')
    Hs = [xT, scratch('H1'), scratch('H1a'), scratch('H2'), scratch('H3'), scratch('H3a'), outT]
    cx = Cx(nc, arena=True)
    p = cx.p
    hv = lambda ap, half: ap[:, half * NT:(half + 1) * NT]

    cx.hT = p.sb('hT', [128, NCH, NT], F32)
    cx.vec = p.sb('vec', [128, NVEC], F32)
    p.dma('sp', I_dma(cx.vec[:], A['v0'][:, :]), w=['vec'])
    for half in range(2):
        load_hT(cx, hv(xT, half))
        emit_norm(cx, hv(HN, half))

    def s5_stage(j):
        for c in range(2):
            cx.new_stage()
            sfx = '%d%d' % (j, c)
            AA = {nm: A[nm + sfx] for nm in ('Bre', 'Bim', 'CR', 'CI', 'lamre', 'lamim', 'logdt', 'dsk')}
            AA['ident'] = A['ident']
            AA['uT'] = HN[512 * c:512 * c + 512, :]
            AA['yT'] = Y[512 * c:512 * c + 512, :]
            s5_body(cx, AA)

    def tail_stage(i, glu, src, dst, emit):
        cx.new_stage()
        AA = dict(memT=A['memT'], vecs=A['vecs%d' % i], wq=A['wq%d' % i], wkv=A['wkv%d' % i], wo=A['wo%d' % i],
                  w1=A['w1_%d' % i], w2=A['w2_%d' % i])
        if glu:
            AA['wglu'] = A['wglu%d' % (i // 2)]
        common_tiles(cx, AA)
        for half in range(2):
            load_hT(cx, hv(src, half))
            if glu:
                AA['yT'] = hv(Y, half)
            tail_body(cx, AA, glu, 0, 2, kv_ready=(half == 1))
            tail_body(cx, AA, glu, 2, 2, kv_ready=True)
            store_hT(cx, hv(dst, half))
            if emit:
                emit_norm(cx, hv(HN, half))

    def attn_stage(i, src, dst):
        j = i // 2
        for half in range(2):
            cx.new_stage()
            cx.vec = p.sb('vec', [128, NVEC], F32)
            p.dma('sp', I_dma(cx.vec[:], A['avecs%d' % j][:, :]), w=['vec'])
            AA = dict(wqkv=A['wqkv%d' % j], wo_a=A['woa%d' % j], biasT=A['biasT%d' % half])
            attn_body(cx, AA, flip=(half == 1), src=src, dst=dst)

    s5_stage(0)
    if nlayers == 0:
        cx.new_stage()
        cx.hT = p.sb('hT', [128, NCH, NT], F32)
        for half in range(2):
            load_hT(cx, hv(Y, half))
            store_hT(cx, hv(outT, half))
        p.finalize()
        cx.used = list(A.keys())
        return nc, cx
    tail_stage(0, True, Hs[0], Hs[1] if nlayers > 1 else outT, False)
    if nlayers > 1:
        attn_stage(1, Hs[1], Hs[2])
        tail_stage(1, False, Hs[2], Hs[3] if nlayers > 2 else outT, True)
    if nlayers > 2:
        s5_stage(1)
        tail_stage(2, True, Hs[3], Hs[4] if nlayers > 3 else outT, False)
    if nlayers > 3:
        attn_stage(3, Hs[4], Hs[5])
        tail_stage(3, False, Hs[5], Hs[6], False)
    p.finalize()
    cx.used = list(A.keys())
    return nc, cx


RG2 = [[0, 1], [2, 3], [4, 5], [6, 7]]


class GBuf:
    def __init__(self, nc, name, rows, cols, chunk_rows):
        self.cr = chunk_rows
        self.n = rows // chunk_rows
        self.src = [nc.dram_tensor('%s_s%d' % (name, q), [chunk_rows, cols], F32, kind="Internal").ap() for q in range(self.n)]
        self.dst = [nc.dram_tensor('%s_g%d' % (name, q), [2 * chunk_rows, cols], F32, kind="Internal").ap() for q in range(self.n)]

    def src_rows(self, r0, nrows=128):
        q = r0 // self.cr
        o = r0 - q * self.cr
        return self.src[q][o:o + nrows, :]

    def g_rows(self, rank, r0, nrows=128):
        q = r0 // self.cr
        o = rank * self.cr + r0 - q * self.cr
        return self.dst[q][o:o + nrows, :]


def build_fused8():
    nc = bass.Bass("TRN2", target_bir_lowering=False, num_devices=8)
    shapes = {}

    def inp(name, shape):
        shapes[name] = list(shape)

    class Lazy(dict):
        def __missing__(self, name):
            ap = nc.dram_tensor(name, shapes[name], F32, kind="ExternalInput").ap()
            self[name] = ap
            return ap
    A = Lazy()

    def scratch(name, shape):
        return nc.dram_tensor(name, list(shape), F32, kind="Internal").ap()
    inp('xT', [D, NT])
    inp('memT', [D, MEMLEN])
    inp('ident', [128, 128])
    inp('v0', [128, NVEC])
    inp('biasT', [24, 128, 256])
    for i in range(4):
        inp('vecs%d' % i, [128, NVEC])
        inp('wq%d' % i, [D, D])
        inp('wkv%d' % i, [D, 2 * D])
        inp('wo%d' % i, [D, D])
        inp('w1_%d' % i, [D, DFF])
        inp('w2_%d' % i, [DFF, D])
    for j in range(2):
        inp('wglu%d' % j, [D, 2 * D])
        inp('wqkv%d' % j, [D, 9216])
        inp('woa%d' % j, [D, D])
        inp('avecs%d' % j, [128, NVEC])
        for nm in ('Bre', 'Bim', 'CR', 'CI'):
            inp(nm + '%d' % j, [2, NPT, 128, 128])
        for nm in ('lamre', 'lamim', 'logdt'):
            inp(nm + '%d' % j, [128, 2 * NPT])
        inp('dsk%d' % j, [128, 4])
    outT = nc.dram_tensor('outT', [D, NT], F32, kind="ExternalOutput").ap()
    HNb = GBuf(nc, 'HN', D, NT, 256)
    HN = GHN = HNb
    yO = scratch('yO', [512, NT])
    ySb = GBuf(nc, 'yS', 512, NT, 256)
    yS = GS = ySb
    Hhb = GBuf(nc, 'Hh', D, 1024, 512)
    Hh = GH = Hhb
    Hs = [A['xT']] + [scratch(n, [D, NT]) for n in ('H1', 'H1a', 'H2', 'H3', 'H3a')] + [outT]
    cx = Cx(nc, arena=True)
    p = cx.p

    def allgather(gb, _unused=None):
        cx.new_stage()
        for q in range(gb.n):
            p.dma('pool', lambda e, q=q: e.collective_compute("AllGather", ALU.bypass, replica_groups=RG2,
                                                               ins=[gb.src[q][:, :]], outs=[gb.dst[q][:, :]]),
                  w=['cc'], semkey='cc', inc=1)

    cx.hT = p.sb('hT', [128, NCH, NT], F32)
    cx.vec = p.sb('vec', [128, NVEC], F32)
    p.dma('sp', I_dma(cx.vec[:], A['v0'][:, :]), w=['vec'])
    load_hT(cx, A['xT'])
    emit_norm(cx, HN)
    allgather(HN, GHN)

    def s5_stage(j):
        cx.new_stage()
        AA = {nm: A[nm + '%d' % j] for nm in ('Bre', 'Bim', 'CR', 'CI', 'lamre', 'lamim', 'logdt', 'dsk')}
        AA['ident'] = A['ident']
        AA['GHN'] = GHN
        AA['yO'] = yO
        AA['yS'] = yS
        cx.vec = p.sb('vec', [128, NVEC], F32)
        p.dma('sp', I_dma(cx.vec[:], A['v0'][:, :]), w=['vec'])
        s5_body(cx, AA)
        allgather(yS, GS)

    def tail_stage(i, glu, src, dst, emit, halo):
        cx.new_stage()
        AA = dict(memT=A['memT'], vecs=A['vecs%d' % i], wq=A['wq%d' % i], wkv=A['wkv%d' % i], wo=A['wo%d' % i],
                  w1=A['w1_%d' % i], w2=A['w2_%d' % i])
        if glu:
            AA['wglu'] = A['wglu%d' % (i // 2)]
            AA['yO'] = yO
            AA['GS'] = GS
        common_tiles(cx, AA)
        load_hT(cx, src)
        tail_body(cx, AA, glu, 0, 2, kv_ready=False)
        tail_body(cx, AA, glu, 2, 2, kv_ready=True)
        store_hT(cx, dst)
        if halo:
            for c in range(NCH):
                p.dma('sp', I_dma(Hh.src_rows(c * 128), cx.hT[:, c, 1024:2048]),
                      r=['h%d_%d' % (c, tb) for tb in (2, 3)], w=['hhout'])
            allgather(Hh, GH)
        if emit:
            emit_norm(cx, HN)
            allgather(HN, GHN)

    def attn_stage(i, src, dst):
        j = i // 2
        cx.new_stage()
        cx.vec = p.sb('vec', [128, NVEC], F32)
        p.dma('sp', I_dma(cx.vec[:], A['avecs%d' % j][:, :]), w=['vec'])
        AA = dict(wqkv=A['wqkv%d' % j], wo_a=A['woa%d' % j], biasT=A['biasT'])
        cx.ws_n = 2
        attn_body(cx, AA, flip=False, src=src, dst=dst, gh=GH)
        cx.ws_n = WS_N

    s5_stage(0)
    tail_stage(0, True, Hs[0], Hs[1], False, True)
    attn_stage(1, Hs[1], Hs[2])
    tail_stage(1, False, Hs[2], Hs[3], True, False)
    s5_stage(1)
    tail_stage(2, True, Hs[3], Hs[4], False, True)
    attn_stage(3, Hs[4], Hs[5])
    tail_stage(3, False, Hs[5], Hs[6], False, False)
    p.finalize()
    cx.used = list(A.keys())
    return nc, cx


def _pc(v, C):
    return np.ascontiguousarray(np.asarray(v, np.float32).reshape(C, 128).T)


_PROGS = {}
_NL = [4]


def _prog(name):
    if name not in _PROGS:
        if name == 'norm':
            _PROGS[name] = build_norm()[0]
        elif name == 's5':
            _PROGS[name] = build_s5()[0]
        elif name == 'tail_glu':
            _PROGS[name] = build_tail(True, True)[0]
        elif name == 'tail':
            _PROGS[name] = build_tail(False, True)[0]
        elif name == 'attn':
            _PROGS[name] = build_attn()[0]
    return _PROGS[name]


def kernel_multi(**inp):
    inp = {k: np.asarray(v) for k, v in inp.items()}
    ncore = 8
    cores = list(range(ncore))
    f32 = np.float32
    loc = [np.arange(NEXT) if (k % 2 == 0) else (SEQ - 1 - np.arange(NEXT)) for k in cores]
    H = np.array(inp['x'], dtype=f32, copy=True)
    memT = [np.ascontiguousarray(inp['mem'][k // 2].T.astype(f32)) for k in cores]
    biasT = [host_bias(inp['bias_table'].astype(f32), k % 2 == 1) for k in cores]

    def own_T(arr_bsd, k):
        return np.ascontiguousarray(arr_bsd[k // 2][loc[k][:NT]].T)

    def scatter(outs, name):
        full = np.empty((BATCH, SEQ, D), f32)
        for k in cores:
            full[k // 2][loc[k][:NT]] = np.asarray(outs[k][name], f32).T
        return full

    def tail_vecs(i):
        v = np.zeros((128, NVEC), f32)
        v[:, 0:8] = _pc(inp['norm_xattn'][i], 8)
        v[:, 8:16] = _pc(inp['norm_mem'][i], 8)
        v[:, 16:24] = _pc(inp['norm_mlp'][i], 8)
        v[:, 24:26] = _pc(inp['xattn_q_gain'][i], 2)
        v[:, 26:28] = _pc(inp['xattn_k_gain'][i], 2)
        v[:, 28:36] = _pc(inp['norm_mix'][min(i + 1, 3)], 8)
        return v

    def run_tail(i, H, Y):
        v = tail_vecs(i)
        maps = []
        for k in cores:
            m = dict(hT=own_T(H, k), memT=memT[k], vecs=v, wq=inp['xattn_w_q'][i], wkv=inp['xattn_w_kv'][i],
                     wo=inp['xattn_w_o'][i], w1=inp['mlp_w1'][i], w2=inp['mlp_w2'][i])
            if Y is not None:
                m['yT'] = own_T(Y, k)
                m['wglu'] = inp['s5_w_glu'][i // 2]
            maps.append(m)
        res = run_bass_kernel_spmd(_prog('tail_glu' if Y is not None else 'tail'), maps, core_ids=cores).results
        return scatter(res, 'hT_out'), scatter(res, 'hn_out')

    def run_s5(j, HN):
        maps = []
        for k in cores:
            b, c = k // 2, k % 2
            m = s5_host_inputs(inp, j, c)
            m['uT'] = np.ascontiguousarray(HN[b][:, 512 * c:512 * c + 512].T)
            maps.append(m)
        res = run_bass_kernel_spmd(_prog('s5'), maps, core_ids=cores).results
        Y = np.empty((BATCH, SEQ, D), f32)
        for k in cores:
            b, c = k // 2, k % 2
            Y[b][:, 512 * c:512 * c + 512] = np.asarray(res[k]['yT'], f32).T
        return Y

    def run_attn(i, H):
        j = i // 2
        v = np.zeros((128, NVEC), f32)
        v[:, 28:36] = _pc(inp['norm_mix'][i], 8)
        v[:, 36] = inp['attn_q_gain'][j]
        v[:, 37] = inp['attn_k_gain'][j]
        maps = []
        for k in cores:
            maps.append(dict(hT_ext=np.ascontiguousarray(H[k // 2][loc[k]].T), vecs=v, wqkv=inp['attn_w_qkv'][j],
                             wo_a=inp['attn_w_o'][j], biasT=biasT[k]))
        res = run_bass_kernel_spmd(_prog('attn'), maps, core_ids=cores).results
        return scatter(res, 'hT_out')

    v0 = np.zeros((128, NVEC), f32)
    v0[:, 28:36] = _pc(inp['norm_mix'][0], 8)
    res = run_bass_kernel_spmd(_prog('norm'), [dict(hT=own_T(H, k), vecs=v0) for k in cores], core_ids=cores).results
    HN = scatter(res, 'hn_out')
    for i in range(4):
        if i % 2 == 0:
            Y = run_s5(i // 2, HN)
            H, HN = run_tail(i, H, Y)
        else:
            H = run_attn(i, H)
            H, HN = run_tail(i, H, None)
    return H


def tail_vecs_host(inp, i):
    v = np.zeros((128, NVEC), np.float32)
    v[:, 0:8] = _pc(inp['norm_xattn'][i], 8)
    v[:, 8:16] = _pc(inp['norm_mem'][i], 8)
    v[:, 16:24] = _pc(inp['norm_mlp'][i], 8)
    v[:, 24:26] = _pc(inp['xattn_q_gain'][i], 2)
    v[:, 26:28] = _pc(inp['xattn_k_gain'][i], 2)
    v[:, 28:36] = _pc(inp['norm_mix'][min(i + 1, 3)], 8)
    return v


def kernel_fused4(**inp):
    inp = {k: np.asarray(v) for k, v in inp.items()}
    f32 = np.float32
    nl = _NL[0]
    if ('fused', nl) not in _PROGS:
        _PROGS[('fused', nl)] = build_fused(nl)
    nc, cxf = _PROGS[('fused', nl)]
    shared = dict(ident=np.eye(128, dtype=f32),
                  biasT0=host_bias(inp['bias_table'].astype(f32), False),
                  biasT1=host_bias(inp['bias_table'].astype(f32), True))
    v0 = np.zeros((128, NVEC), f32)
    v0[:, 28:36] = _pc(inp['norm_mix'][0], 8)
    shared['v0'] = v0
    for i in range(4):
        shared['vecs%d' % i] = tail_vecs_host(inp, i)
        shared['wq%d' % i] = inp['xattn_w_q'][i]
        shared['wkv%d' % i] = inp['xattn_w_kv'][i]
        shared['wo%d' % i] = inp['xattn_w_o'][i]
        shared['w1_%d' % i] = inp['mlp_w1'][i]
        shared['w2_%d' % i] = inp['mlp_w2'][i]
    for j in range(2):
        shared['wglu%d' % j] = inp['s5_w_glu'][j]
        shared['wqkv%d' % j] = inp['attn_w_qkv'][j]
        shared['woa%d' % j] = inp['attn_w_o'][j]
        av = np.zeros((128, NVEC), f32)
        av[:, 28:36] = _pc(inp['norm_mix'][2 * j + 1], 8)
        av[:, 36] = inp['attn_q_gain'][j]
        av[:, 37] = inp['attn_k_gain'][j]
        shared['avecs%d' % j] = av
        for c in range(2):
            for nm, arr in s5_host_inputs(inp, j, c).items():
                if nm != 'ident':
                    shared[nm + '%d%d' % (j, c)] = arr
    maps = []
    for b in range(BATCH):
        m = dict(shared)
        m['xT'] = np.ascontiguousarray(inp['x'][b].T.astype(f32))
        m['memT'] = np.ascontiguousarray(inp['mem'][b].T.astype(f32))
        maps.append({k: m[k] for k in cxf.used})
    res = run_bass_kernel_spmd(nc, maps, core_ids=list(range(BATCH))).results
    out = np.empty((BATCH, SEQ, D), f32)
    for b in range(BATCH):
        out[b] = np.asarray(res[b]['outT'], f32).T
    return out


def _sw(a, axis):
    return np.roll(a, 512, axis=axis)


def kernel(**inp):
    inp = {k: np.asarray(v, np.float32) for k, v in inp.items()}
    f32 = np.float32
    if 'fused8' not in _PROGS:
        _PROGS['fused8'] = build_fused8()
    nc, cxf = _PROGS['fused8']
    ident = np.eye(128, dtype=f32)
    per_c = []
    for c in range(2):
        sw = (lambda a, axis: _sw(a, axis)) if c == 1 else (lambda a, axis: a)
        g = {}
        gi = {k: (sw(inp[k], 1) if k in ('norm_mix', 'norm_xattn', 'norm_mem', 'norm_mlp') else inp[k]) for k in inp}
        g['ident'] = ident
        g['biasT'] = host_bias(inp['bias_table'], c == 1)
        v0 = np.zeros((128, NVEC), f32)
        v0[:, 28:36] = _pc(gi['norm_mix'][0], 8)
        v0[:, 40 + c] = 1.0
        g['v0'] = v0
        for i in range(4):
            v = tail_vecs_host(gi, i)
            v[:, 40 + c] = 1.0
            g['vecs%d' % i] = v
            g['wq%d' % i] = np.ascontiguousarray(sw(inp['xattn_w_q'][i], 0))
            g['wkv%d' % i] = np.ascontiguousarray(sw(inp['xattn_w_kv'][i], 0))
            g['wo%d' % i] = np.ascontiguousarray(sw(inp['xattn_w_o'][i], 1))
            g['w1_%d' % i] = np.ascontiguousarray(sw(inp['mlp_w1'][i], 0))
            g['w2_%d' % i] = np.ascontiguousarray(sw(inp['mlp_w2'][i], 1))
        for j in range(2):
            wg = sw(inp['s5_w_glu'][j], 0).reshape(D, 2, D)
            g['wglu%d' % j] = np.ascontiguousarray(sw(wg, 2).reshape(D, 2 * D))
            g['wqkv%d' % j] = np.ascontiguousarray(sw(inp['attn_w_qkv'][j], 0))
            g['woa%d' % j] = np.ascontiguousarray(sw(inp['attn_w_o'][j], 1))
            av = np.zeros((128, NVEC), f32)
            av[:, 28:36] = _pc(gi['norm_mix'][2 * j + 1], 8)
            av[:, 36] = inp['attn_q_gain'][j]
            av[:, 37] = inp['attn_k_gain'][j]
            av[:, 40 + c] = 1.0
            g['avecs%d' % j] = av
            for nm, arr in s5_host_inputs(inp, j, c).items():
                if nm != 'ident':
                    g[nm + '%d' % j] = arr
        per_c.append(g)
    maps = []
    for k in range(8):
        b, c = k // 2, k % 2
        m = dict(per_c[c])
        xb = inp['x'][b]
        if c == 0:
            m['xT'] = np.ascontiguousarray(xb[:NT].T)
            m['memT'] = np.ascontiguousarray(inp['mem'][b].T)
        else:
            m['xT'] = np.ascontiguousarray(_sw(xb[::-1][:NT], 1).T)
            m['memT'] = np.ascontiguousarray(_sw(inp['mem'][b], 1).T)
        maps.append({kk: m[kk] for kk in cxf.used})
    res = run_bass_kernel_spmd(nc, maps, core_ids=list(range(8))).results
    out = np.empty((BATCH, SEQ, D), f32)
    for k in range(8):
        b, c = k // 2, k % 2
        o = np.asarray(res[k]['outT'], f32).T
        if c == 0:
            out[b, :NT] = o
        else:
            out[b, NT:] = _sw(o, 1)[::-1]
    return out
```

```python
import math
import numpy as np
from contextlib import ExitStack
import concourse.bass as bass
import concourse.mybir as mybir
from concourse.bass_utils import run_bass_kernel_spmd

F32 = mybir.dt.float32
BF16 = mybir.dt.bfloat16
AF = mybir.ActivationFunctionType
ALU = mybir.AluOpType

D = 1024
NCH = 8
SEQ = 4096
BATCH = 4
NT = 2048
EPS = 1e-6
MEMLEN = 256
DFF = 4096


class Prog:
    ENGS = ('pe', 'act', 'dve', 'pool', 'sp')
    BLK = {'pe': 'tensor', 'act': 'scalar', 'dve': 'vector', 'pool': 'gpsimd', 'sp': 'sync'}

    def __init__(self, nc):
        self.nc = nc
        self.es = ExitStack()
        self.ins = {e: [] for e in self.ENGS}
        self.last_w = {}
        self.readers = {}
        self.dma_cnt = {}
        self.log = None
        self.bar_deps = {}

    ARENA_F32 = 51712

    def use_arena(self):
        self.arena = self.es.enter_context(self.nc.sbuf_tensor('arena', [128, self.ARENA_F32], F32))
        self.aoff = 0

    def sb(self, name, shape, dt):
        if getattr(self, 'arena', None) is None:
            return self.es.enter_context(self.nc.sbuf_tensor('s_' + name, list(shape), dt))
        assert shape[0] == 128, shape
        nel = 1
        for d_ in shape[1:]:
            nel *= d_
        isz = 4 if dt == F32 else 2
        nby = (nel * isz + 63) // 64 * 64
        o4 = self.aoff // 4
        self.aoff += nby
        assert self.aoff <= self.ARENA_F32 * 4, ('arena overflow', name, self.aoff)
        v = self.arena[:, o4:o4 + nby // 4]
        if dt != F32:
            v = v.bitcast(dt)
        v = v[:, :nel]
        if len(shape) == 3:
            v = v.rearrange("p (a b) -> p a b", a=shape[1])
        elif len(shape) != 2:
            raise AssertionError(shape)
        return v

    def barrier(self):
        deps = [('d', k, c) for k, c in self.dma_cnt.items()]
        for e in self.ENGS:
            n = len(self.ins[e])
            j = n - 1
            while j >= 0 and self.ins[e][j]['dma'] is not None:
                j -= 1
            if j >= 0:
                deps.append(('e', e, j))
        self.bar_deps = {e: list(deps) for e in self.ENGS}
        self.last_w.clear()
        self.readers.clear()

    def ps(self, name, shape, dt=F32):
        return self.es.enter_context(self.nc.psum_tensor(name, list(shape), dt))

    def _deps(self, r, w):
        deps = []
        for k in r:
            t = self.last_w.get(k)
            if t is not None:
                deps.append(t)
        for k in w:
            t = self.last_w.get(k)
            if t is not None:
                deps.append(t)
            deps.extend(self.readers.get(k, ()))
        return deps

    def _commit(self, tok, r, w):
        for k in r:
            lst = self.readers.setdefault(k, [])
            lst[:] = [t for t in lst if t[:2] != tok[:2]]
            lst.append(tok)
        for k in w:
            self.last_w[k] = tok
            self.readers[k] = []

    def op(self, eng, fn, r=(), w=()):
        idx = len(self.ins[eng])
        self.ins[eng].append(dict(fn=fn, deps=self._deps(r, w) + self.bar_deps.pop(eng, []), dma=None))
        self._commit(('e', eng, idx), r, w)

    def dma(self, eng, fn, r=(), w=(), semkey=None, inc=16):
        if semkey is None:
            semkey = w[0]
        c = self.dma_cnt.get(semkey, 0) + inc
        self.dma_cnt[semkey] = c
        self.ins[eng].append(dict(fn=fn, deps=self._deps(r, w) + self.bar_deps.pop(eng, []), dma=semkey, inc=inc))
        self._commit(('d', semkey, c), r, w)

    SAME_DIST = 4

    def _skip_same(self, e, i, d, rec):
        if d[1] != e or rec['dma'] is not None:
            return False
        if e == 'pe':
            return True
        return (i - d[2]) > self.SAME_DIST

    def finalize(self):
        nc = self.nc
        need = {e: set() for e in self.ENGS}
        for e in self.ENGS:
            for i, rec in enumerate(self.ins[e]):
                for d in rec['deps']:
                    if d[0] == 'e' and not self._skip_same(e, i, d, rec):
                        need[d[1]].add(d[2])
        cum = {}
        for e in self.ENGS:
            c = 0
            arr = []
            for i in range(len(self.ins[e])):
                if i in need[e]:
                    c += 1
                arr.append(c)
            cum[e] = arr
        esem = {e: self.es.enter_context(nc.semaphore('se_' + e)) for e in self.ENGS}
        dsem = {}
        for i, k in enumerate(self.dma_cnt):
            dsem[k] = self.es.enter_context(nc.semaphore('sd_%d' % i))
        self.stats = {e: (len(self.ins[e]), cum[e][-1] if cum[e] else 0) for e in self.ENGS}
        self.stats['ndsem'] = len(dsem)
        with nc.Block() as block:
            for e in self.ENGS:
                def body(eng, e=e):
                    waited = {}
                    for i, rec in enumerate(self.ins[e]):
                        req = {}
                        for d in rec['deps']:
                            if d[0] == 'e':
                                if self._skip_same(e, i, d, rec):
                                    continue
                                key = ('e', d[1])
                                val = cum[d[1]][d[2]]
                            else:
                                key = ('d', d[1])
                                val = d[2]
                            if val > req.get(key, 0):
                                req[key] = val
                        for key, val in req.items():
                            if waited.get(key, 0) < val:
                                sem = esem[key[1]] if key[0] == 'e' else dsem[key[1]]
                                eng.wait_ge(sem, val)
                                waited[key] = val
                                if self.log is not None:
                                    self.log.append((e, i, 'wait', key, val))
                        if self.log is not None:
                            self.log.append((e, i, 'inst', rec['dma'], cum[e][i] if i in need[e] else None))
                        inst = rec['fn'](eng)
                        if rec['dma'] is not None:
                            inst.then_inc(dsem[rec['dma']], rec.get('inc', 16))
                        elif i in need[e]:
                            inst.then_inc(esem[e], 1)
                    if e == 'sp':
                        for k, c in self.dma_cnt.items():
                            eng.wait_ge(dsem[k], c)
                getattr(block, self.BLK[e])(body)
        self.es.close()


def I_mm(out, lhsT, rhs, start, stop):
    return lambda e: e.matmul(out, lhsT, rhs, start=start, stop=stop)


def I_act(out, in_, func, **kw):
    return lambda e: e.activation(out=out, in_=in_, func=func, **kw)


def I_tt(out, in0, in1, op):
    return lambda e: e.tensor_tensor(out=out, in0=in0, in1=in1, op=op)


def I_ts(out, in0, s1, s2, op0, op1=None):
    if op1 is None:
        return lambda e: e.tensor_scalar(out=out, in0=in0, scalar1=s1, scalar2=None, op0=op0)
    return lambda e: e.tensor_scalar(out=out, in0=in0, scalar1=s1, scalar2=s2, op0=op0, op1=op1)


def I_stt(out, in0, scalar, in1, op0, op1):
    return lambda e: e.scalar_tensor_tensor(out=out, in0=in0, scalar=scalar, in1=in1, op0=op0, op1=op1)


def I_recip(out, in_):
    return lambda e: e.reciprocal(out=out, in_=in_)


def I_copy(out, in_):
    return lambda e: e.tensor_copy(out=out, in_=in_)


def I_memset(ap, c):
    return lambda e: e.memset(ap, c)


def I_dma(out, in_):
    return lambda e: e.dma_start(out=out, in_=in_)


def I_scan(out, d0, d1, init):
    return lambda e: e.tensor_tensor_scan(out=out, data0=d0, data1=d1, initial=init, op0=ALU.mult, op1=ALU.add)


def mcombine(cx, out, okey, X, xk, Z, zk, ma, mb, n=512):
    p = cx.p
    tmp, tk = cx.rot('mctmp', [128, 512], F32, n=2)
    p.op('act', I_act(tmp[:, :n], X, AF.Copy, scale=ma), r=[xk, 'vec'], w=[tk])
    p.op('dve', I_stt(out, Z, mb, tmp[:, :n], ALU.mult, ALU.add), r=[zk, 'vec', tk], w=[okey])


class Cx:
    def __init__(self, nc, arena=False):
        self.nc = nc
        self.p = Prog(nc)
        if arena:
            self.p.use_arena()
        self.banks = [self.p.ps('bank%d' % i, [128, 512]) for i in range(8)]
        self.bi = 0
        p = self.p
        self.ones = p.sb('ones', [128, 128], BF16)
        p.op('pool', I_memset(self.ones[:], 1.0), w=['ones'])
        self.sq = [p.sb('sq%d' % i, [128, 512], BF16) for i in range(2)]
        self.sqi = 0
        self.rstd = [p.sb('rstd%d' % i, [128, 512], F32) for i in range(2)]
        self.rsi = 0
        self._rot = {}
        self.epsc = p.sb('epsc', [128, 1], F32)
        p.op('pool', I_memset(self.epsc[:], EPS), w=['epsc'])
        self.mark = getattr(p, 'aoff', 0)

    def new_stage(self):
        self.p.barrier()
        self.p.aoff = self.mark
        self._rot = {}
        for nm in ('ws', 'wsi'):
            if hasattr(self, nm):
                delattr(self, nm)

    def bank(self):
        i = self.bi
        self.bi = (i + 1) % 8
        return self.banks[i], 'bank%d' % i

    def rot(self, name, shape, dt, n=2):
        if name not in self._rot:
            self._rot[name] = [[self.p.sb('%s_%d' % (name, i), shape, dt) for i in range(n)], 0]
        tl, i = self._rot[name]
        self._rot[name][1] = (i + 1) % n
        return tl[i], '%s_%d' % (name, i)

    def next_sq(self):
        i = self.sqi
        self.sqi = 1 - i
        return self.sq[i], 'sq%d' % i

    def next_rstd(self):
        i = self.rsi
        self.rsi = 1 - i
        return self.rstd[i], 'rstd%d' % i

    def rstd_from_bank(self, bank, bk, n, dim):
        p = self.p
        rs, rk = self.next_rstd()
        p.op('act', I_act(rs[:, :n], bank[:, :n], AF.Sqrt, scale=1.0 / dim, bias=self.epsc[:, 0:1]), r=[bk, 'epsc'], w=[rk])
        p.op('dve', I_recip(rs[:, :n], rs[:, :n]), r=[rk], w=[rk])
        return rs, rk


def rmsnorm(cx, src, skey, gcols, gkey, dst, dkey, C, n0, n, dim):
    p = cx.p
    bank, bk = cx.bank()
    for c in range(C):
        sq, sk = cx.next_sq()
        p.op('act', I_act(sq[:, :n], src[:, c, n0:n0 + n], AF.Square), r=[skey(c)], w=[sk])
        p.op('pe', I_mm(bank[:, :n], cx.ones[:], sq[:, :n], c == 0, c == C - 1), r=[sk, 'ones'], w=[bk])
    rs, rk = cx.rstd_from_bank(bank, bk, n, dim)
    for c in range(C):
        p.op('dve', I_stt(dst[:, c, n0:n0 + n], src[:, c, n0:n0 + n], gcols[:, c:c + 1], rs[:, :n], ALU.mult, ALU.mult),
             r=[skey(c), gkey, rk], w=[dkey(c)])


WS_N = 3


def wslab(cx, parts):
    p = cx.p
    nws = getattr(cx, 'ws_n', WS_N)
    if not hasattr(cx, 'ws'):
        cx.ws = [p.sb('ws%d' % i, [128, 4096], BF16) for i in range(nws)]
        cx.wsi = 0
    i = cx.wsi
    cx.wsi = (i + 1) % nws
    t = cx.ws[i]
    key = 'ws%d' % i
    for src, off in parts:
        K, N = src.shape
        kc = K // 128
        dst = t[:, off:off + kc * N].rearrange("p (k n) -> p k n", k=kc)
        p.dma('pool', I_dma(dst, src.rearrange("(k p) n -> p k n", p=128)), w=[key])
    return t, key


VC = dict(gx=0, gm=8, gl=16, gq=24, gk=26, gn=28, aq=36, ak=37, dsk=38, m0=40, m1=41)
NVEC = 48


def tail_body(cx, A, glu, tb0, ntb, kv_ready):
    p = cx.p
    hT = cx.hT
    hk = lambda c, tb: 'h%d_%d' % (c, tb)
    hn = cx.hn
    big2 = cx.big2
    vec = cx.vec
    NB = ntb

    if glu:
        for c in range(NCH):
            for tb in range(NB):
                yt, yk = cx.rot('ytmp', [128, 512], F32)
                col0 = (tb0 + tb) * 512
                if 'yO' in A and c >= 4:
                    X, xk = cx.rot('gx', [128, 512], F32, n=2)
                    Z, zk = cx.rot('gz', [128, 512], F32, n=2)
                    p.dma('sp', I_dma(X[:], A['GS'].g_rows(0, (c - 4) * 128)[:, col0:col0 + 512]), w=[xk])
                    p.dma('sp', I_dma(Z[:], A['GS'].g_rows(1, (c - 4) * 128)[:, col0:col0 + 512]), w=[zk])
                    mcombine(cx, yt[:], yk, X[:], xk, Z[:], zk, vec[:, VC['m1']:VC['m1'] + 1], vec[:, VC['m0']:VC['m0'] + 1])
                elif 'yO' in A:
                    p.dma('sp', I_dma(yt[:], A['yO'][c * 128:(c + 1) * 128, col0:col0 + 512]), w=[yk])
                else:
                    p.dma('sp', I_dma(yt[:], A['yT'][c * 128:(c + 1) * 128, col0:col0 + 512]), w=[yk])
                p.op('act', I_act(hn[:, c, tb * 512:(tb + 1) * 512], yt[:], AF.Gelu_apprx_tanh), r=[yk], w=['hn%d' % tb])
        for ns in range(2):
            wa, wak = wslab(cx, [(A['wglu'][:, ns * 512:(ns + 1) * 512], 0)])
            wb, wbk = wslab(cx, [(A['wglu'][:, 1024 + ns * 512:1024 + (ns + 1) * 512], 0)])
            for j in range(4):
                n = ns * 4 + j
                for tb in range(NB):
                    ba, bak = cx.bank()
                    bb, bbk = cx.bank()
                    for kc in range(NCH):
                        p.op('pe', I_mm(ba[:], wa[:, kc * 512 + j * 128: kc * 512 + (j + 1) * 128],
                                        hn[:, kc, tb * 512:(tb + 1) * 512], kc == 0, kc == NCH - 1),
                             r=[wak, 'hn%d' % tb], w=[bak])
                    for kc in range(NCH):
                        p.op('pe', I_mm(bb[:], wb[:, kc * 512 + j * 128: kc * 512 + (j + 1) * 128],
                                        hn[:, kc, tb * 512:(tb + 1) * 512], kc == 0, kc == NCH - 1),
                             r=[wbk, 'hn%d' % tb], w=[bbk])
                    sg, sgk = cx.rot('sg', [128, 512], F32)
                    p.op('act', I_act(sg[:], bb[:], AF.Sigmoid), r=[bbk], w=[sgk])
                    gt, gtk = cx.rot('gtmp', [128, 512], F32)
                    p.op('dve', I_tt(gt[:], ba[:], sg[:], ALU.mult), r=[bak, sgk], w=[gtk])
                    hs = hT[:, n, (tb0 + tb) * 512:(tb0 + tb + 1) * 512]
                    p.op('pool', I_tt(hs, hs, gt[:], ALU.add), r=[gtk, hk(n, tb0 + tb)], w=[hk(n, tb0 + tb)])

    if not kv_ready:
        kraw = cx.kraw
        memn = cx.memn
        for c in range(NCH):
            p.dma('sp', I_dma(kraw[:, c, :], A['memT'][c * 128:(c + 1) * 128, :]), w=['kraw'], semkey='kraw_ld')
        rmsnorm(cx, kraw, lambda c: 'kraw', vec[:, VC['gm']:VC['gm'] + 8], 'vec', memn, lambda c: 'memn', NCH, 0, MEMLEN, D)
        for hp in range(2):
            wk, wkk = wslab(cx, [(A['wkv'][:, hp * 512:(hp + 1) * 512], 0)])
            for jj in range(4):
                j = hp * 4 + jj
                bk_, bkk = cx.bank()
                for kc in range(NCH):
                    p.op('pe', I_mm(bk_[:, :MEMLEN], wk[:, kc * 512 + jj * 128: kc * 512 + (jj + 1) * 128], memn[:, kc, :],
                                    kc == 0, kc == NCH - 1), r=[wkk, 'memn'], w=[bkk])
                p.op('act', I_act(kraw[:, j, :], bk_[:, :MEMLEN], AF.Copy), r=[bkk], w=['kraw'])
        for h in range(4):
            bs, bsk = cx.bank()
            for ec in range(2):
                sq, sk = cx.next_sq()
                p.op('act', I_act(sq[:, :MEMLEN], kraw[:, 2 * h + ec, :], AF.Square), r=['kraw'], w=[sk])
                p.op('pe', I_mm(bs[:, :MEMLEN], cx.ones[:], sq[:, :MEMLEN], ec == 0, ec == 1), r=[sk, 'ones'], w=[bsk])
            rs, rk = cx.rstd_from_bank(bs, bsk, MEMLEN, 256)
            for ec in range(2):
                p.op('dve', I_stt(cx.KT[:, 2 * h + ec, :], kraw[:, 2 * h + ec, :], vec[:, VC['gk'] + ec:VC['gk'] + ec + 1],
                                  rs[:, :MEMLEN], ALU.mult, ALU.mult), r=['kraw', 'vec', rk], w=['KT'])
        for vs in range(2):
            wv, wvk = wslab(cx, [(A['wkv'][:, 1024 + vs * 512:1024 + (vs + 1) * 512], 0)])
            for mc in range(2):
                bv, bvk = cx.bank()
                for kc in range(NCH):
                    p.op('pe', I_mm(bv[:], memn[:, kc, mc * 128:(mc + 1) * 128], wv[:, kc * 512:(kc + 1) * 512],
                                    kc == 0, kc == NCH - 1), r=[wvk, 'memn'], w=[bvk])
                p.op('act', I_act(cx.V[:, mc, vs * 512:(vs + 1) * 512], bv[:], AF.Copy), r=[bvk], w=['V'])

    for tb in range(NB):
        _rmsnorm_off(cx, hT, (tb0 + tb) * 512, lambda c, tb=tb: hk(c, tb0 + tb), vec[:, VC['gx']:VC['gx'] + 8],
                     hn, tb * 512, 'hn%d' % tb)
    for hp in range(2):
        wq, wqk = wslab(cx, [(A['wq'][:, hp * 512:(hp + 1) * 512], 0)])
        for hh in range(2):
            h = 2 * hp + hh
            for tb in range(NB):
                qb = []
                for ec in range(2):
                    b, bk_ = cx.bank()
                    cc = hh * 2 + ec
                    for kc in range(NCH):
                        p.op('pe', I_mm(b[:], wq[:, kc * 512 + cc * 128: kc * 512 + (cc + 1) * 128],
                                        hn[:, kc, tb * 512:(tb + 1) * 512], kc == 0, kc == NCH - 1),
                             r=[wqk, 'hn%d' % tb], w=[bk_])
                    qb.append((b, bk_))
                bs, bsk = cx.bank()
                for ec in range(2):
                    sq, sk = cx.next_sq()
                    p.op('act', I_act(sq[:], qb[ec][0][:], AF.Square), r=[qb[ec][1]], w=[sk])
                    p.op('pe', I_mm(bs[:], cx.ones[:], sq[:], ec == 0, ec == 1), r=[sk, 'ones'], w=[bsk])
                rs, rk = cx.rstd_from_bank(bs, bsk, 512, 256)
                qn, qnk = cx.rot('qn', [128, 2, 512], BF16)
                for ec in range(2):
                    p.op('dve', I_stt(qn[:, ec, :], qb[ec][0][:], vec[:, VC['gq'] + ec:VC['gq'] + ec + 1], rs[:],
                                      ALU.mult, ALU.mult), r=[qb[ec][1], 'vec', rk], w=[qnk])
                PT, ptk = cx.rot('PT', [128, 2, 512], BF16)
                for mc in range(2):
                    bl, blk = cx.bank()
                    for ec in range(2):
                        p.op('pe', I_mm(bl[:], cx.KT[:, 2 * h + ec, mc * 128:(mc + 1) * 128], qn[:, ec, :], ec == 0, ec == 1),
                             r=['KT', qnk], w=[blk])
                    p.op('act', I_act(PT[:, mc, :], bl[:], AF.Exp, scale=1.0 / 16.0), r=[blk], w=[ptk])
                bd, bdk = cx.bank()
                for mc in range(2):
                    p.op('pe', I_mm(bd[:], cx.ones[:], PT[:, mc, :], mc == 0, mc == 1), r=['ones', ptk], w=[bdk])
                rd, rdk = cx.rot('rden', [128, 512], F32)
                p.op('dve', I_recip(rd[:], bd[:]), r=[bdk], w=[rdk])
                for ec in range(2):
                    bo, bok = cx.bank()
                    for mc in range(2):
                        p.op('pe', I_mm(bo[:], cx.V[:, mc, h * 256 + ec * 128: h * 256 + (ec + 1) * 128], PT[:, mc, :],
                                        mc == 0, mc == 1), r=['V', ptk], w=[bok])
                    p.op('dve', I_tt(big2[:, 2 * h + ec, tb * 512:(tb + 1) * 512], bo[:], rd[:], ALU.mult),
                         r=[bok, rdk], w=['big2_%d' % tb])
    for ns in range(2):
        wo, wok = wslab(cx, [(A['wo'][:, ns * 512:(ns + 1) * 512], 0)])
        for j in range(4):
            n = ns * 4 + j
            for tb in range(NB):
                b, bk_ = cx.bank()
                for kc in range(NCH):
                    p.op('pe', I_mm(b[:], wo[:, kc * 512 + j * 128: kc * 512 + (j + 1) * 128],
                                    big2[:, kc, tb * 512:(tb + 1) * 512], kc == 0, kc == NCH - 1),
                         r=[wok, 'big2_%d' % tb], w=[bk_])
                hs = hT[:, n, (tb0 + tb) * 512:(tb0 + tb + 1) * 512]
                p.op('dve', I_tt(hs, b[:], hs, ALU.add), r=[bk_, hk(n, tb0 + tb)], w=[hk(n, tb0 + tb)])

    for tb in range(NB):
        _rmsnorm_off(cx, hT, (tb0 + tb) * 512, lambda c, tb=tb: hk(c, tb0 + tb), vec[:, VC['gl']:VC['gl'] + 8],
                     hn, tb * 512, 'hn%d' % tb)
    for s in range(DFF // 256):
        ws, wsk = wslab(cx, [(A['w1'][:, s * 256:(s + 1) * 256], 0), (A['w2'][s * 256:(s + 1) * 256, :], 2048)])
        hb = s % 2
        for j in range(2):
            for tb in range(NB):
                b, bk_ = cx.bank()
                for kc in range(NCH):
                    p.op('pe', I_mm(b[:], ws[:, kc * 256 + j * 128: kc * 256 + (j + 1) * 128],
                                    hn[:, kc, tb * 512:(tb + 1) * 512], kc == 0, kc == NCH - 1),
                         r=[wsk, 'hn%d' % tb], w=[bk_])
                rt, rtk = cx.rot('rtmp', [128, 512], F32)
                p.op('act', I_act(rt[:], b[:], AF.Relu), r=[bk_], w=[rtk])
                p.op('pool', I_tt(big2[:, hb * 2 + j, tb * 512:(tb + 1) * 512], rt[:], rt[:], ALU.mult),
                     r=[rtk], w=['hid%d' % hb])
        for n in range(NCH):
            for tb in range(NB):
                b, bk_ = cx.bank()
                for j in range(2):
                    p.op('pe', I_mm(b[:], ws[:, 2048 + j * 1024 + n * 128: 2048 + j * 1024 + (n + 1) * 128],
                                    big2[:, hb * 2 + j, tb * 512:(tb + 1) * 512], j == 0, j == 1),
                         r=[wsk, 'hid%d' % hb], w=[bk_])
                hs = hT[:, n, (tb0 + tb) * 512:(tb0 + tb + 1) * 512]
                p.op('dve', I_tt(hs, b[:], hs, ALU.add), r=[bk_, hk(n, tb0 + tb)], w=[hk(n, tb0 + tb)])


def _rmsnorm_off(cx, src, s0, skey, gcols, dst, d0, dkey, n=512, C=NCH, dim=D):
    p = cx.p
    bank, bk = cx.bank()
    for c in range(C):
        sq, sk = cx.next_sq()
        p.op('act', I_act(sq[:, :n], src[:, c, s0:s0 + n], AF.Square), r=[skey(c)], w=[sk])
        p.op('pe', I_mm(bank[:, :n], cx.ones[:], sq[:, :n], c == 0, c == C - 1), r=[sk, 'ones'], w=[bk])
    rs, rk = cx.rstd_from_bank(bank, bk, n, dim)
    for c in range(C):
        p.op('dve', I_stt(dst[:, c, d0:d0 + n], src[:, c, s0:s0 + n], gcols[:, c:c + 1], rs[:, :n], ALU.mult, ALU.mult),
             r=[skey(c), 'vec', rk], w=[dkey])


def common_tiles(cx, A):
    p = cx.p
    cx.hT = p.sb('hT', [128, NCH, NT], F32)
    cx.hn = p.sb('hn', [128, NCH, 1024], BF16)
    cx.big2 = p.sb('big2', [128, NCH, 1024], BF16)
    cx.vec = p.sb('vec', [128, NVEC], F32)
    cx.kraw = p.sb('kraw', [128, NCH, MEMLEN], F32)
    cx.memn = p.sb('memn', [128, NCH, MEMLEN], BF16)
    cx.KT = p.sb('KT', [128, NCH, MEMLEN], BF16)
    cx.V = p.sb('V', [128, 2, D], BF16)
    p.dma('sp', I_dma(cx.vec[:], A['vecs'][:, :]), w=['vec'])


def load_hT(cx, src):
    p = cx.p
    for c in range(NCH):
        p.dma('sp', I_dma(cx.hT[:, c, :], src[c * 128:(c + 1) * 128, :]),
              w=['h%d_%d' % (c, tb) for tb in range(NT // 512)], semkey='hld%d' % c)


def store_hT(cx, dst):
    p = cx.p
    for c in range(NCH):
        p.dma('sp', I_dma(dst[c * 128:(c + 1) * 128, :], cx.hT[:, c, :]),
              r=['h%d_%d' % (c, tb) for tb in range(NT // 512)], w=['hout%d' % c])


def build_tail(glu, emit_hn, arena=False):
    nc = bass.Bass("TRN2", target_bir_lowering=False)
    A = {}

    def inp(name, shape, dt=F32):
        A[name] = nc.dram_tensor(name, list(shape), dt, kind="ExternalInput").ap()

    inp('hT', [D, NT])
    inp('memT', [D, MEMLEN])
    inp('vecs', [128, NVEC])
    inp('wq', [D, D])
    inp('wkv', [D, 2 * D])
    inp('wo', [D, D])
    inp('w1', [D, DFF])
    inp('w2', [DFF, D])
    if glu:
        inp('yT', [D, NT])
        inp('wglu', [D, 2 * D])
    A['hT_out'] = nc.dram_tensor('hT_out', [D, NT], F32, kind="ExternalOutput").ap()
    if emit_hn:
        A['hn_out'] = nc.dram_tensor('hn_out', [D, NT], F32, kind="ExternalOutput").ap()
    cx = Cx(nc, arena=arena)
    if arena:
        cx.new_stage()
    common_tiles(cx, A)
    load_hT(cx, A['hT'])
    for half in range(2):
        tail_body(cx, A, glu, half * 2, 2, kv_ready=(half == 1))
    store_hT(cx, A['hT_out'])
    if emit_hn:
        emit_norm(cx, A['hn_out'])
    cx.p.finalize()
    return nc, cx


def emit_norm(cx, dst):
    p = cx.p
    for tb in range(NT // 512):
        bank, bk = cx.bank()
        for c in range(NCH):
            sq, sk = cx.next_sq()
            p.op('act', I_act(sq[:], cx.hT[:, c, tb * 512:(tb + 1) * 512], AF.Square), r=['h%d_%d' % (c, tb)], w=[sk])
            p.op('pe', I_mm(bank[:], cx.ones[:], sq[:], c == 0, c == NCH - 1), r=[sk, 'ones'], w=[bk])
        rs, rk = cx.rstd_from_bank(bank, bk, 512, D)
        for c in range(NCH):
            ot, otk = cx.rot('ntmp', [128, 512], F32, n=3)
            p.op('dve', I_stt(ot[:], cx.hT[:, c, tb * 512:(tb + 1) * 512], cx.vec[:, VC['gn'] + c:VC['gn'] + c + 1], rs[:],
                              ALU.mult, ALU.mult), r=['h%d_%d' % (c, tb), 'vec', rk], w=[otk])
            drow = dst.src_rows(c * 128) if hasattr(dst, 'src_rows') else dst[c * 128:(c + 1) * 128, :]
            p.dma('sp', I_dma(drow[:, tb * 512:(tb + 1) * 512], ot[:]), r=[otk], w=['hnout'])


NPT = 16
SW = 512
NW = SEQ // SW
PI = math.pi


def s5_params(cx, A):
    p = cx.p
    NCOL = 2 * NPT
    T = {}
    for nm in ['lre', 'lim', 'ldt', 'dt', 'mag', 'ang', 'angc', 's1', 'c1', 'are', 'aim', 'nr', 'den', 't', 't2',
               'fre', 'fim', 'nfre', 'nfim']:
        T[nm] = p.sb('sp_' + nm, [128, NCOL], F32)
    k = 's5par'
    p.dma('sp', I_dma(T['lre'][:], A['lamre'][:, :]), w=[k], semkey='s5par_ld')
    p.dma('sp', I_dma(T['lim'][:], A['lamim'][:, :]), w=[k], semkey='s5par_ld')
    p.dma('sp', I_dma(T['ldt'][:], A['logdt'][:, :]), w=[k], semkey='s5par_ld')
    a = lambda n: T[n][:]
    p.op('act', I_act(a('dt'), a('ldt'), AF.Exp), r=[k], w=[k])
    p.op('dve', I_tt(a('t'), a('lre'), a('dt'), ALU.mult), r=[k], w=[k])
    p.op('act', I_act(a('mag'), a('t'), AF.Exp), r=[k], w=[k])
    p.op('dve', I_tt(a('ang'), a('lim'), a('dt'), ALU.mult), r=[k], w=[k])
    for _ in range(5):
        p.op('dve', I_ts(a('t'), a('ang'), PI, 2 * PI, ALU.is_gt, ALU.mult), r=[k], w=[k])
        p.op('dve', I_tt(a('ang'), a('ang'), a('t'), ALU.subtract), r=[k], w=[k])
    p.op('dve', I_ts(a('angc'), a('ang'), PI / 2, None, ALU.add), r=[k], w=[k])
    p.op('dve', I_ts(a('t'), a('angc'), PI, 2 * PI, ALU.is_gt, ALU.mult), r=[k], w=[k])
    p.op('dve', I_tt(a('angc'), a('angc'), a('t'), ALU.subtract), r=[k], w=[k])
    p.op('act', I_act(a('s1'), a('ang'), AF.Sin), r=[k], w=[k])
    p.op('act', I_act(a('c1'), a('angc'), AF.Sin), r=[k], w=[k])
    p.op('dve', I_tt(a('are'), a('mag'), a('c1'), ALU.mult), r=[k], w=[k])
    p.op('dve', I_tt(a('aim'), a('mag'), a('s1'), ALU.mult), r=[k], w=[k])
    p.op('dve', I_ts(a('nr'), a('are'), -1.0, None, ALU.add), r=[k], w=[k])
    p.op('dve', I_tt(a('den'), a('lre'), a('lre'), ALU.mult), r=[k], w=[k])
    p.op('dve', I_tt(a('t'), a('lim'), a('lim'), ALU.mult), r=[k], w=[k])
    p.op('dve', I_tt(a('den'), a('den'), a('t'), ALU.add), r=[k], w=[k])
    p.op('dve', I_recip(a('den'), a('den')), r=[k], w=[k])
    p.op('dve', I_tt(a('t'), a('nr'), a('lre'), ALU.mult), r=[k], w=[k])
    p.op('dve', I_tt(a('t2'), a('aim'), a('lim'), ALU.mult), r=[k], w=[k])
    p.op('dve', I_tt(a('t'), a('t'), a('t2'), ALU.add), r=[k], w=[k])
    p.op('dve', I_tt(a('fre'), a('t'), a('den'), ALU.mult), r=[k], w=[k])
    p.op('dve', I_tt(a('t'), a('aim'), a('lre'), ALU.mult), r=[k], w=[k])
    p.op('dve', I_tt(a('t2'), a('nr'), a('lim'), ALU.mult), r=[k], w=[k])
    p.op('dve', I_tt(a('t'), a('t'), a('t2'), ALU.subtract), r=[k], w=[k])
    p.op('dve', I_tt(a('fim'), a('t'), a('den'), ALU.mult), r=[k], w=[k])
    p.op('dve', I_ts(a('nfre'), a('fre'), -1.0, None, ALU.mult), r=[k], w=[k])
    p.op('dve', I_ts(a('nfim'), a('fim'), -1.0, None, ALU.mult), r=[k], w=[k])
    nlv = int(math.log2(SW))
    T['pwc'] = p.sb('sp_pwc', [128, nlv + 1, NCOL], F32)
    T['pws'] = p.sb('sp_pws', [128, nlv + 1, NCOL], F32)
    T['npws'] = p.sb('sp_npws', [128, NCOL], F32)
    p.op('dve', I_copy(T['pwc'][:, 0, :], a('c1')), r=[k], w=[k])
    p.op('dve', I_copy(T['pws'][:, 0, :], a('s1')), r=[k], w=[k])
    for lv in range(nlv):
        c_ = T['pwc'][:, lv, :]
        s_ = T['pws'][:, lv, :]
        p.op('dve', I_tt(a('t'), s_, s_, ALU.mult), r=[k], w=[k])
        p.op('dve', I_tt(a('t2'), c_, c_, ALU.mult), r=[k], w=[k])
        p.op('dve', I_tt(T['pwc'][:, lv + 1, :], a('t2'), a('t'), ALU.subtract), r=[k], w=[k])
        p.op('dve', I_stt(T['pws'][:, lv + 1, :], c_, 2.0, s_, ALU.mult, ALU.mult), r=[k], w=[k])
    p.op('dve', I_ts(T['npws'][:], T['pws'][:, nlv, :], -1.0, None, ALU.mult), r=[k], w=[k])
    return T


def s5_body(cx, A):
    p = cx.p
    T = s5_params(cx, A)
    PK = 's5par'
    if 'dbg' in A:
        for i, nm in enumerate(['dt', 'mag', 'ang', 's1', 'c1', 'fre', 'fim', 'den']):
            p.dma('sp', I_dma(A['dbg'][:, i * 2 * NPT:(i + 1) * 2 * NPT], T[nm][:]), r=[PK], w=['dbgo'])
    ub = p.sb('ub', [128, 4, SEQ], BF16)
    if 'GHN' in A:
        G = A['GHN']
        m0c = cx.vec[:, VC['m0']:VC['m0'] + 1]
        m1c = cx.vec[:, VC['m1']:VC['m1'] + 1]
        for ck in range(4):
            for w in range(NW):
                r = 0 if w < NW // 2 else 1
                if r == 0:
                    cols = slice(w * SW, (w + 1) * SW)
                else:
                    w2 = w - NW // 2
                    cols = slice(NT - (w2 + 1) * SW, NT - w2 * SW)
                X, xk = cx.rot('gx', [128, SW], F32, n=2)
                Z, zk = cx.rot('gz', [128, SW], F32, n=2)
                p.dma('sp', I_dma(X[:], G.g_rows(r, ck * 128)[:, cols]), w=[xk])
                p.dma('sp', I_dma(Z[:], G.g_rows(r, 512 + ck * 128)[:, cols]), w=[zk])
                dst = ub[:, ck, w * SW:(w + 1) * SW]
                if r == 1:
                    dst = dst[:, ::-1]
                mcombine(cx, dst, 'ub%d' % ck, X[:], xk, Z[:], zk, m0c if r == 0 else m1c, m1c if r == 0 else m0c)
    else:
        for ck in range(4):
            p.dma('pool', I_dma(ub[:, ck, :], A['uT'][ck * 128:(ck + 1) * 128, :]), w=['ub%d' % ck])
    ident = p.sb('ident', [128, 128], F32)
    p.dma('sp', I_dma(ident[:], A['ident'][:, :]), w=['ident'])
    dsk = p.sb('dskc', [128, 4], F32)
    p.dma('sp', I_dma(dsk[:], A['dsk'][:, :]), w=['dskc'])
    yacc = [p.sb('yacc%d' % i, [128, SEQ], F32) for i in range(2)]
    bb_i = [0]

    def bbank():
        i = bb_i[0]
        bb_i[0] = (i + 1) % 6
        return cx.banks[i], 'bank%d' % i
    yb_i = [0]

    def ybank():
        i = 6 + yb_i[0]
        yb_i[0] = 1 - yb_i[0]
        return cx.banks[i], 'bank%d' % i

    for ck in range(4):
        ya = yacc[ck % 2]
        yk = 'yacc%d' % (ck % 2)
        dD, dDk = cx.rot('diagD', [128, 128], BF16)
        p.op('dve', I_ts(dD[:], ident[:], dsk[:, ck:ck + 1], None, ALU.mult), r=['ident', 'dskc'], w=[dDk])
        for d in range(2):
            tabs = []
            col0 = d * NPT + ck * 4
            cos4, c4k = cx.rot('cos4', [128, 4, SW], F32, n=2)
            sin4, s4k = cx.rot('sin4', [128, 4, SW], F32, n=2)
            tk = c4k
            p.op('dve', I_memset(cos4[:, :, 0:1], 1.0), w=[tk])
            p.op('dve', I_memset(sin4[:, :, 0:1], 0.0), w=[tk])
            L = 1
            lv = 0
            while L < SW:
                pcb = T['pwc'][:, lv, col0:col0 + 4].unsqueeze(2).to_broadcast([128, 4, L])
                psb = T['pws'][:, lv, col0:col0 + 4].unsqueeze(2).to_broadcast([128, 4, L])
                ta, tak = cx.rot('tbA', [128, 4, SW // 2], F32, n=1)
                tb2, tbk = cx.rot('tbB', [128, 4, SW // 2], F32, n=1)
                p.op('dve', I_tt(ta[:, :, :L], sin4[:, :, 0:L], psb, ALU.mult), r=[tk, PK], w=[tak])
                p.op('dve', I_tt(tb2[:, :, :L], cos4[:, :, 0:L], pcb, ALU.mult), r=[tk, PK], w=[tbk])
                p.op('dve', I_tt(cos4[:, :, L:2 * L], tb2[:, :, :L], ta[:, :, :L], ALU.subtract), r=[tak, tbk], w=[tk])
                p.op('dve', I_tt(ta[:, :, :L], cos4[:, :, 0:L], psb, ALU.mult), r=[tk, PK], w=[tak])
                p.op('dve', I_tt(tb2[:, :, :L], sin4[:, :, 0:L], pcb, ALU.mult), r=[tk, PK], w=[tbk])
                p.op('dve', I_tt(sin4[:, :, L:2 * L], tb2[:, :, :L], ta[:, :, :L], ALU.add), r=[tak, tbk], w=[tk])
                L *= 2
                lv += 1
            for q in range(4):
                pt = ck * 4 + q
                col = d * NPT + pt
                cosT = cos4[:, q, :]
                sinT = sin4[:, q, :]
                cW = T['pwc'][:, lv, col:col + 1]
                sW = T['pws'][:, lv, col:col + 1]
                nsW = T['npws'][:, col:col + 1]
                braw, brk = cx.rot('bw', [128, 2, 128], BF16, n=8)
                p.dma('pool', I_dma(braw[:, 0, :], A['Bre'][d, pt]), w=[brk])
                p.dma('pool', I_dma(braw[:, 1, :], A['Bim'][d, pt]), w=[brk])
                craw, crk = cx.rot('craw', [128, 2, 128], F32, n=2)
                p.dma('sp', I_dma(craw[:, 0, :], A['CR'][d, pt]), w=[crk])
                p.dma('sp', I_dma(craw[:, 1, :], A['CI'][d, pt]), w=[crk])
                cw, cwk = cx.rot('cw', [128, 3, 128], BF16, n=8)
                ctmp, ctk = cx.rot('ctmp', [128, 128], F32, n=2)
                fre = T['fre'][:, col:col + 1]
                nfim = T['nfim'][:, col:col + 1]
                nfre = T['nfre'][:, col:col + 1]
                p.op('dve', I_ts(ctmp[:], craw[:, 1, :], nfim, None, ALU.mult), r=[crk, PK], w=[ctk])
                p.op('dve', I_stt(cw[:, 0, :], craw[:, 0, :], fre, ctmp[:], ALU.mult, ALU.add), r=[crk, PK, ctk], w=[cwk])
                ctmp2, ctk2 = cx.rot('ctmp', [128, 128], F32, n=2)
                p.op('dve', I_ts(ctmp2[:], craw[:, 0, :], nfim, None, ALU.mult), r=[crk, PK], w=[ctk2])
                p.op('dve', I_stt(cw[:, 1, :], craw[:, 1, :], nfre, ctmp2[:], ALU.mult, ALU.add), r=[crk, PK, ctk2], w=[cwk])
                ctmp3, ctk3 = cx.rot('ctmp', [128, 128], F32, n=2)
                p.op('dve', I_ts(ctmp3[:], craw[:, 1, :], T['fim'][:, col:col + 1], None, ALU.mult), r=[crk, PK], w=[ctk3])
                p.op('dve', I_stt(cw[:, 2, :], craw[:, 0, :], nfre, ctmp3[:], ALU.mult, ALU.add), r=[crk, PK, ctk3], w=[cwk])
                car, cak = cx.rot('carry', [128, 8], F32, n=8)
                tabs.append(dict(cos=cosT, sin=sinT, tk=tk, cW=cW, sW=sW, nsW=nsW, braw=braw, brk=brk, cw=cw, cwk=cwk,
                                 r=T['mag'][:, col:col + 1], car=car, cak=cak))
            worder = range(NW) if d == 0 else range(NW - 1, -1, -1)
            rv = (lambda ap: ap) if d == 0 else (lambda ap: ap[:, ::-1])
            units = [dict(wi=wi, w=w, q=q) for wi, w in enumerate(worder) for q in range(4)]
            ybs = {}

            def P01(u):
                tb_ = tabs[u['q']]
                win = slice(u['w'] * SW, (u['w'] + 1) * SW)
                bre, brek = bbank()
                bim, bimk = bbank()
                p.op('pe', I_mm(bre[:], tb_['braw'][:, 0, :], ub[:, ck, win], True, True), r=[tb_['brk'], 'ub%d' % ck], w=[brek])
                p.op('pe', I_mm(bim[:], tb_['braw'][:, 1, :], ub[:, ck, win], True, True), r=[tb_['brk'], 'ub%d' % ck], w=[bimk])
                cosT, sinT, tk = tb_['cos'], tb_['sin'], tb_['tk']
                t1, t1k = cx.rot('t1', [128, SW], F32)
                t2, t2k = cx.rot('t2', [128, SW], F32)
                t3, t3k = cx.rot('t3', [128, SW], F32)
                t4, t4k = cx.rot('t4', [128, SW], F32)
                p.op('dve', I_tt(t1[:], rv(bre[:]), cosT, ALU.mult), r=[brek, tk], w=[t1k])
                p.op('dve', I_tt(t2[:], rv(bim[:]), sinT, ALU.mult), r=[bimk, tk], w=[t2k])
                p.op('dve', I_tt(t3[:], rv(bim[:]), cosT, ALU.mult), r=[bimk, tk], w=[t3k])
                p.op('dve', I_tt(t4[:], rv(bre[:]), sinT, ALU.mult), r=[brek, tk], w=[t4k])
                u.update(t=(t1, t1k, t2, t2k, t3, t3k, t4, t4k))

            def P2(u):
                t1, t1k, t2, t2k, t3, t3k, t4, t4k = u['t']
                wre, wrk = cx.rot('wre', [128, SW], F32)
                wim, wik = cx.rot('wim', [128, SW], F32)
                p.op('pool', I_tt(wre[:], t1[:], t2[:], ALU.add), r=[t1k, t2k], w=[wrk])
                p.op('pool', I_tt(wim[:], t3[:], t4[:], ALU.subtract), r=[t3k, t4k], w=[wik])
                u.update(wv=(wre, wrk, wim, wik))

            def P3(u):
                tb_ = tabs[u['q']]
                tk = tb_['tk']
                wre, wrk, wim, wik = u['wv']
                zre, zrk = cx.rot('zre', [128, SW], F32)
                zim, zik = cx.rot('zim', [128, SW], F32)
                car, cak = tb_['car'], tb_['cak']
                rbc = tb_['r'].to_broadcast([128, SW])
                if u['wi'] == 0:
                    ire, iim = 0.0, 0.0
                else:
                    ire, iim = car[:, 2:3], car[:, 3:4]
                p.op('dve', I_scan(zre[:], rbc, wre[:], ire), r=[PK, wrk, cak], w=[zrk])
                p.op('dve', I_scan(zim[:], rbc, wim[:], iim), r=[PK, wik, cak], w=[zik])
                if u['wi'] < NW - 1:
                    p.op('act', I_act(car[:, 0:1], zim[:, SW - 1:SW], AF.Copy, scale=tb_['nsW']), r=[zik, PK], w=[cak])
                    p.op('act', I_act(car[:, 1:2], zre[:, SW - 1:SW], AF.Copy, scale=tb_['sW']), r=[zrk, PK], w=[cak])
                    p.op('act', I_act(car[:, 2:3], zre[:, SW - 1:SW], AF.Identity, scale=tb_['cW'], bias=car[:, 0:1]), r=[zrk, PK], w=[cak])
                    p.op('act', I_act(car[:, 3:4], zim[:, SW - 1:SW], AF.Identity, scale=tb_['cW'], bias=car[:, 1:2]), r=[zik, PK], w=[cak])
                u.update(z=(zre, zrk, zim, zik))

            def P4(u):
                tb_ = tabs[u['q']]
                cosT, sinT, tk = tb_['cos'], tb_['sin'], tb_['tk']
                zre, zrk, zim, zik = u['z']
                u1, u1k = cx.rot('u1', [128, SW], BF16, n=3)
                u2, u2k = cx.rot('u2', [128, SW], BF16, n=3)
                u3, u3k = cx.rot('u3', [128, SW], BF16, n=3)
                u4, u4k = cx.rot('u4', [128, SW], BF16, n=3)
                p.op('pool', I_tt(rv(u1[:]), zre[:], cosT, ALU.mult), r=[zrk, tk], w=[u1k])
                p.op('pool', I_tt(rv(u2[:]), zim[:], sinT, ALU.mult), r=[zik, tk], w=[u2k])
                p.op('pool', I_tt(rv(u3[:]), zim[:], cosT, ALU.mult), r=[zik, tk], w=[u3k])
                p.op('dve', I_tt(rv(u4[:]), zre[:], sinT, ALU.mult), r=[zrk, tk], w=[u4k])
                u.update(uu=(u1, u1k, u2, u2k, u3, u3k, u4, u4k))

            def P5(u):
                pass

            def P6(u):
                tb_ = tabs[u['q']]
                w, q = u['w'], u['q']
                win = slice(w * SW, (w + 1) * SW)
                if q == 0:
                    ybs[w] = ybank()
                yb, ybk = ybs[w]
                u1, u1k, u2, u2k, u3, u3k, u4, u4k = u['uu']
                first = (q == 0)
                last = (q == 3) and d == 1
                p.op('pe', I_mm(yb[:], tb_['cw'][:, 0, :], u1[:], first, False), r=[tb_['cwk'], u1k], w=[ybk])
                p.op('pe', I_mm(yb[:], tb_['cw'][:, 2, :], u2[:], False, False), r=[tb_['cwk'], u2k], w=[ybk])
                p.op('pe', I_mm(yb[:], tb_['cw'][:, 1, :], u3[:], False, False), r=[tb_['cwk'], u3k], w=[ybk])
                p.op('pe', I_mm(yb[:], tb_['cw'][:, 1, :], u4[:], False, last), r=[tb_['cwk'], u4k], w=[ybk])
                if q == 3:
                    if d == 0:
                        p.op('pe', I_mm(yb[:], dD[:], ub[:, ck, win], False, True), r=[dDk, 'ub%d' % ck], w=[ybk])
                        p.op('act', I_act(ya[:, win], yb[:], AF.Copy), r=[ybk], w=[yk + '_%d' % w])
                    else:
                        p.op('dve', I_tt(ya[:, win], yb[:], ya[:, win], ALU.add), r=[ybk, yk + '_%d' % w], w=[yk + '_%d' % w])
            nu = len(units)
            for step in range(nu + 2):
                if step < nu:
                    P01(units[step])
                    P2(units[step])
                if 0 <= step - 1 < nu:
                    P3(units[step - 1])
                    P4(units[step - 1])
                if 0 <= step - 2 < nu:
                    P5(units[step - 2])
                    P6(units[step - 2])
        if 'yO' in A:
            m0c = cx.vec[:, VC['m0']:VC['m0'] + 1]
            m1c = cx.vec[:, VC['m1']:VC['m1'] + 1]
            for hb in range(NT // SW):
                A1 = ya[:, hb * SW:(hb + 1) * SW]
                B1 = ya[:, SEQ - (hb + 1) * SW:SEQ - hb * SW][:, ::-1]
                ka = yk + '_%d' % hb
                kb = yk + '_%d' % (NW - 1 - hb)
                ot, otk = cx.rot('yo_t', [128, SW], F32, n=2)
                mcombine(cx, ot[:], otk, A1, ka, B1, kb, m0c, m1c)
                p.dma('sp', I_dma(A['yO'][ck * 128:(ck + 1) * 128, hb * SW:(hb + 1) * SW], ot[:]), r=[otk], w=['yout%d' % ck])
                st_, stk_ = cx.rot('yo_t', [128, SW], F32, n=2)
                mcombine(cx, st_[:], stk_, A1, ka, B1, kb, m1c, m0c)
                p.dma('sp', I_dma(A['yS'].src_rows(ck * 128)[:, hb * SW:(hb + 1) * SW], st_[:]), r=[stk_], w=['yout%d' % ck])
        else:
            p.dma('sp', I_dma(A['yT'][ck * 128:(ck + 1) * 128, :], ya[:]), r=[yk + '_%d' % w for w in range(NW)], w=['yout%d' % ck])


def build_s5(debug=False, arena=False):
    nc = bass.Bass("TRN2", target_bir_lowering=False)
    A = {}

    def inp(name, shape, dt=F32):
        A[name] = nc.dram_tensor(name, list(shape), dt, kind="ExternalInput").ap()
    inp('uT', [512, SEQ])
    inp('Bre', [2, NPT, 128, 128])
    inp('Bim', [2, NPT, 128, 128])
    inp('CR', [2, NPT, 128, 128])
    inp('CI', [2, NPT, 128, 128])
    inp('lamre', [128, 2 * NPT])
    inp('lamim', [128, 2 * NPT])
    inp('logdt', [128, 2 * NPT])
    inp('dsk', [128, 4])
    inp('ident', [128, 128])
    A['yT'] = nc.dram_tensor('yT', [512, SEQ], F32, kind="ExternalOutput").ap()
    cx = Cx(nc, arena=arena)
    if arena:
        cx.new_stage()
    if debug:
        A['dbg'] = nc.dram_tensor('dbg', [128, 8 * 2 * NPT], F32, kind="ExternalOutput").ap()
        A['dbg2'] = nc.dram_tensor('dbg2', [128, 2 * SW], F32, kind="ExternalOutput").ap()
    s5_body(cx, A)
    cx.p.finalize()
    return nc, cx


def s5_host_inputs(inp, j, half):
    g0 = 32 * half
    Bre = np.zeros((2, NPT, 128, 128), np.float32)
    Bim = np.zeros_like(Bre)
    CR = np.zeros_like(Bre)
    CI = np.zeros_like(Bre)
    lamre = np.zeros((128, 2 * NPT), np.float32)
    lamim = np.zeros_like(lamre)
    logdt = np.zeros_like(lamre)
    for d in range(2):
        for pt in range(NPT):
            for gl in range(2):
                g = g0 + 2 * pt + gl
                r0 = (pt % 4) * 32 + gl * 16
                Bre[d, pt, r0:r0 + 16, gl * 64:(gl + 1) * 64] = inp['s5_b_re'][j, d, g].T
                Bim[d, pt, r0:r0 + 16, gl * 64:(gl + 1) * 64] = inp['s5_b_im'][j, d, g].T
                CR[d, pt, gl * 64:(gl + 1) * 64, r0:r0 + 16] = inp['s5_c_re'][j, d, g].T
                CI[d, pt, gl * 64:(gl + 1) * 64, r0:r0 + 16] = inp['s5_c_im'][j, d, g].T
                lamre[gl * 64:(gl + 1) * 64, d * NPT + pt] = inp['s5_lambda_re'][j, d, g]
                lamim[gl * 64:(gl + 1) * 64, d * NPT + pt] = inp['s5_lambda_im'][j, d, g]
                logdt[gl * 64:(gl + 1) * 64, d * NPT + pt] = inp['s5_log_dt'][j, d, g]
    dsk = np.ascontiguousarray(inp['s5_d'][j, 512 * half:512 * half + 512].reshape(4, 128).T)
    return dict(Bre=Bre, Bim=Bim, CR=CR, CI=CI, lamre=lamre, lamim=lamim, logdt=logdt, dsk=dsk,
                ident=np.eye(128, dtype=np.float32))


NEXT = 3072
GRP = [(1, 2048), (4, 512), (16, 128)]
ASCALE = 128 ** -0.5


def sub_view(ap2d, d):
    if d == 1:
        return ap2d.rearrange("p (d i) -> p d i", d=1)
    return ap2d.rearrange("p (i d) -> p d i", d=d)


def attn_body(cx, A, flip=False, src=None, dst=None, gh=None):
    p = cx.p
    vec = cx.vec

    def load_blk(xt, xk, c, tb):
        if gh is None or tb < NT // 512:
            p.dma('sp', I_dma(xt[:], _src[c * 128:(c + 1) * 128, _cols(tb)]), w=[xk])
            return
        hb = tb - NT // 512
        cols = slice(1024 - 512 * (hb + 1), 1024 - 512 * hb)
        pc = (c + 4) % 8
        X, xk2 = cx.rot('gx', [128, 512], F32, n=2)
        Z, zk2 = cx.rot('gz', [128, 512], F32, n=2)
        p.dma('sp', I_dma(X[:], gh.g_rows(0, pc * 128)[:, cols]), w=[xk2])
        p.dma('sp', I_dma(Z[:], gh.g_rows(1, pc * 128)[:, cols]), w=[zk2])
        mcombine(cx, xt[:], xk, X[:], xk2, Z[:], zk2, vec[:, VC['m1']:VC['m1'] + 1], vec[:, VC['m0']:VC['m0'] + 1])

    def _cols(tb):
        if src is None or not flip:
            return slice(tb * 512, (tb + 1) * 512)
        return slice(SEQ - (tb + 1) * 512, SEQ - tb * 512)
    _src = A['hT_ext'] if src is None else src
    _dst = A['hT_out'] if dst is None else dst
    rvf = (lambda ap: ap[:, ::-1]) if (flip and src is not None) else (lambda ap: ap)
    hn = p.sb('hnx', [128, NCH, NEXT], BF16)
    mT = p.sb('mT', [128, NCH, NT], BF16)
    num = p.sb('numacc', [128, NT], F32)
    den = p.sb('denacc', [128, NT], F32)
    for tb in range(NEXT // 512):
        bank, bk = cx.bank()
        for c in range(NCH):
            xt, xk = cx.rot('xin', [128, 512], F32, n=2)
            load_blk(xt, xk, c, tb)
            sq, sk = cx.next_sq()
            p.op('act', I_act(sq[:], xt[:], AF.Square), r=[xk], w=[sk])
            p.op('pe', I_mm(bank[:], cx.ones[:], sq[:], c == 0, c == NCH - 1), r=[sk, 'ones'], w=[bk])
        rs, rk = cx.rstd_from_bank(bank, bk, 512, D)
        for c in range(NCH):
            xt, xk = cx.rot('xin', [128, 512], F32, n=2)
            load_blk(xt, xk, c, tb)
            rv_ = (lambda ap: ap[:, ::-1]) if (gh is not None and tb >= NT // 512) else rvf
            p.op('dve', I_stt(rv_(hn[:, c, tb * 512:(tb + 1) * 512]), xt[:], vec[:, VC['gn'] + c:VC['gn'] + c + 1], rs[:],
                              ALU.mult, ALU.mult), r=[xk, 'vec', rk], w=['hnx'])
    sb_i = [0]

    def sbank():
        i = sb_i[0]
        sb_i[0] = (i + 1) % 4
        return cx.banks[i], 'bank%d' % i
    ob_i = [0]

    def obanks():
        i = ob_i[0]
        ob_i[0] = 1 - i
        return cx.banks[4 + i], 'bank%d' % (4 + i), cx.banks[6 + i], 'bank%d' % (6 + i)

    def qknorm(bank, bk, n, gcol, dst, dkey):
        sq, sk = cx.next_sq()
        p.op('act', I_act(sq[:, :n], bank[:, :n], AF.Square), r=[bk], w=[sk])
        b2, b2k = sbank()
        p.op('pe', I_mm(b2[:, :n], cx.ones[:], sq[:, :n], True, True), r=[sk, 'ones'], w=[b2k])
        rs, rk = cx.rstd_from_bank(b2, b2k, n, 128)
        p.op('dve', I_stt(dst, bank[:, :n], vec[:, gcol:gcol + 1], rs[:, :n], ALU.mult, ALU.mult), r=[bk, 'vec', rk], w=[dkey])

    for h in range(8):
        p.op('pool', I_memset(num[:], 0.0), w=['numacc'])
        p.op('pool', I_memset(den[:], 0.0), w=['denacc'])
        bt, btk = cx.rot('biasT', [128, 3, 256], F32, n=2)
        for g in range(3):
            p.dma('sp', I_dma(bt[:, g, :], A['biasT'][g * 8 + h]), w=[btk])
        for g, (d, Lq) in enumerate(GRP):
            nto = Lq // 128
            wsl, wsk = cx.rot('wqkv', [128, NCH, 384], BF16, n=3)
            for kind in range(3):
                c0 = kind * 3072 + g * 1024 + h * 128
                p.dma('pool', I_dma(wsl[:, :, kind * 128:(kind + 1) * 128],
                                    A['wqkv'][:, c0:c0 + 128].rearrange("(k p) n -> p k n", p=128)), w=[wsk])
            qT, qk_ = cx.rot('qT', [128, NT], BF16, n=2)
            kT, kk_ = cx.rot('kT', [128, NEXT], BF16, n=2)
            vt, vk_ = cx.rot('vt', [128, 32, 128], BF16, n=2)

            for kind, dstT, dk, gcol in ((0, qT, qk_, VC['aq']), (1, kT, kk_, VC['ak'])):
                for bi in range(4):
                    b, bk = sbank()
                    for kc in range(NCH):
                        if d == 1:
                            rhs, o_ap = hn[:, kc, bi * 512:(bi + 1) * 512], b[:]
                        elif d == 4:
                            rhs, o_ap = sub_view(hn[:, kc, 0:NT], 4)[:, bi, :], b[:]
                        else:
                            rhs = sub_view(hn[:, kc, 0:NT], 16)[:, 4 * bi:4 * bi + 4, :]
                            o_ap = b[:].rearrange("p (a b) -> p a b", a=4)
                        p.op('pe', I_mm(o_ap, wsl[:, kc, kind * 128:(kind + 1) * 128], rhs, kc == 0, kc == NCH - 1),
                             r=[wsk, 'hnx'], w=[bk])
                    qknorm(b, bk, 512, gcol, dstT[:, bi * 512:(bi + 1) * 512], dk)
            nh = 64 * d
            for b0 in range(0, nh, 512):
                n = min(512, nh - b0)
                b, bk = sbank()
                for kc in range(NCH):
                    if d == 1:
                        rhs = hn[:, kc, NT:NT + 64]
                        o_ap = b[:, :64]
                    else:
                        r0 = b0 // 64
                        nr = n // 64
                        rhs = sub_view(hn[:, kc, NT:NT + 64 * d], d)[:, r0:r0 + nr, :]
                        o_ap = b[:, :n].rearrange("p (a b) -> p a b", a=nr)
                    p.op('pe', I_mm(o_ap, wsl[:, kc, 128:256], rhs, kc == 0, kc == NCH - 1), r=[wsk, 'hnx'], w=[bk])
                qknorm(b, bk, n, VC['ak'], kT[:, NT + b0:NT + b0 + n], kk_)
            for t0 in range(0, 16, 4):
                b, bk = sbank()
                for tt in range(4):
                    t = t0 + tt
                    r, m = t // nto, t % nto
                    for kc in range(NCH):
                        lhsT = sub_view(hn[:, kc, 0:NT], d)[:, r, m * 128:(m + 1) * 128]
                        p.op('pe', I_mm(b[:, tt * 128:(tt + 1) * 128], lhsT, wsl[:, kc, 256:384], kc == 0, kc == NCH - 1),
                             r=[wsk, 'hnx'], w=[bk])
                p.op('act', I_act(vt[:, t0:t0 + 4, :], b[:].rearrange("p (a b) -> p a b", a=4), AF.Copy), r=[bk], w=[vk_])
            for r0 in range(0, d, 4):
                nr = min(4, d - r0)
                b, bk = sbank()
                for rr in range(nr):
                    r = r0 + rr
                    for kc in range(NCH):
                        lhsT = sub_view(hn[:, kc, NT:NT + 64 * d], d)[:, r, :]
                        p.op('pe', I_mm(b[:64, rr * 128:(rr + 1) * 128], lhsT, wsl[:, kc, 256:384], kc == 0, kc == NCH - 1),
                             r=[wsk, 'hnx'], w=[bk])
                p.op('act', I_act(vt[:64, 16 + r0:16 + r0 + nr, :], b[:64, :nr * 128].rearrange("p (a b) -> p a b", a=nr), AF.Copy),
                     r=[bk], w=[vk_])
            for r in range(d):
                qoff = r * Lq
                ob = None
                for m in range(nto + 1):
                    halo = (m == nto)
                    nk = 64 if halo else 128
                    b0_ = 64 if m == 0 else 0
                    b1_ = 64 if halo else min(256, Lq - (128 * m - 64))
                    ktile = kT[:, NT + r * 64:NT + r * 64 + 64] if halo else kT[:, qoff + m * 128:qoff + (m + 1) * 128]
                    vtile = vt[:64, 16 + r, :] if halo else vt[:, r * nto + m, :]
                    qs = qoff + 128 * m - 64 + b0_
                    sbk, sbkk = sbank()
                    p.op('pe', I_mm(sbk[:nk, b0_:b1_], ktile, qT[:, qs:qs + (b1_ - b0_)], True, True), r=[kk_, qk_], w=[sbkk])
                    st, stk = cx.rot('stmp', [128, 256], F32, n=3)
                    p.op('dve', I_stt(st[:nk, b0_:b1_], sbk[:nk, b0_:b1_], ASCALE, bt[:nk, g, b0_:b1_], ALU.mult, ALU.add),
                         r=[sbkk, btk], w=[stk])
                    PT, ptk = cx.rot('PTa', [128, 256], BF16, n=4)
                    p.op('act', I_act(PT[:nk, b0_:b1_], st[:nk, b0_:b1_], AF.Exp), r=[stk], w=[ptk])
                    def flush(ep):
                        qlo = max(0, 512 * ep - 64)
                        qhi = min(Lq, 512 * ep + 448)
                        c0f = qlo - (512 * ep - 64)
                        wdt = qhi - qlo
                        nv = sub_view(num[:, :], d)[:, r, qlo:qhi]
                        dv = sub_view(den[:, :], d)[:, r, qlo:qhi]
                        p.op('dve', I_tt(nv, ob[0][:, c0f:c0f + wdt], nv, ALU.add), r=[ob[1], 'numacc'], w=['numacc'])
                        p.op('dve', I_tt(dv, ob[2][:, c0f:c0f + wdt], dv, ALU.add), r=[ob[3], 'denacc'], w=['denacc'])
                    if m == 0:
                        ob = obanks()
                        p.op('pe', I_mm(ob[0][:, 64:128], vtile, PT[:nk, 64:128], True, True), r=[vk_, ptk], w=[ob[1]])
                        p.op('pe', I_mm(ob[2][:, 64:128], cx.ones[:nk, :], PT[:nk, 64:128], True, True), r=['ones', ptk], w=[ob[3]])
                    else:
                        q0 = 64 + 128 * (m - 1)
                        wq_ = min(Lq, q0 + 128) - q0
                        c0 = 128 * (m % 4)
                        p.op('pe', I_mm(ob[0][:, c0:c0 + wq_], vtile, PT[:nk, 0:wq_], False, True), r=[vk_, ptk], w=[ob[1]])
                        p.op('pe', I_mm(ob[2][:, c0:c0 + wq_], cx.ones[:nk, :], PT[:nk, 0:wq_], False, True), r=['ones', ptk], w=[ob[3]])
                        if m % 4 == 3 or halo:
                            flush(m // 4)
                    if not halo:
                        q0 = 64 + 128 * m
                        wq_ = min(Lq, q0 + 128) - q0
                        if (m + 1) % 4 == 0:
                            ob = obanks()
                        c0 = 128 * ((m + 1) % 4)
                        p.op('pe', I_mm(ob[0][:, c0:c0 + wq_], vtile, PT[:nk, 128:128 + wq_], True, False), r=[vk_, ptk], w=[ob[1]])
                        p.op('pe', I_mm(ob[2][:, c0:c0 + wq_], cx.ones[:nk, :], PT[:nk, 128:128 + wq_], True, False), r=['ones', ptk], w=[ob[3]])
        p.op('dve', I_recip(den[:], den[:]), r=['denacc'], w=['denacc'])
        p.op('pool', I_tt(mT[:, h, :], num[:], den[:], ALU.mult), r=['numacc', 'denacc'], w=['mT'])
    for ns in range(2):
        wo, wok = wslab(cx, [(A['wo_a'][:, ns * 512:(ns + 1) * 512], 0)])
        for j in range(4):
            n = ns * 4 + j
            for tb in range(NT // 512):
                b, bk = sbank()
                for kc in range(NCH):
                    p.op('pe', I_mm(b[:], wo[:, kc * 512 + j * 128: kc * 512 + (j + 1) * 128], mT[:, kc, tb * 512:(tb + 1) * 512],
                                    kc == 0, kc == NCH - 1), r=[wok, 'mT'], w=[bk])
                xt, xk = cx.rot('xin', [128, 512], F32, n=2)
                p.dma('sp', I_dma(xt[:], _src[n * 128:(n + 1) * 128, _cols(tb)]), w=[xk])
                p.op('dve', I_tt(xt[:], rvf(b[:]), xt[:], ALU.add), r=[bk, xk], w=[xk])
                p.dma('sp', I_dma(_dst[n * 128:(n + 1) * 128, _cols(tb)], xt[:]), r=[xk], w=['hTout'])


def build_attn():
    nc = bass.Bass("TRN2", target_bir_lowering=False)
    A = {}

    def inp(name, shape, dt=F32):
        A[name] = nc.dram_tensor(name, list(shape), dt, kind="ExternalInput").ap()
    inp('hT_ext', [D, NEXT])
    inp('vecs', [128, NVEC])
    inp('wqkv', [D, 9216])
    inp('wo_a', [D, D])
    inp('biasT', [24, 128, 256])
    A['hT_out'] = nc.dram_tensor('hT_out', [D, NT], F32, kind="ExternalOutput").ap()
    cx = Cx(nc)
    cx.vec = cx.p.sb('vec', [128, NVEC], F32)
    cx.p.dma('sp', I_dma(cx.vec[:], A['vecs'][:, :]), w=['vec'])
    attn_body(cx, A)
    cx.p.finalize()
    return nc, cx


def t5_bucket(rel):
    nb = 16
    ret = (rel > 0).astype(np.int32) * nb
    n = np.abs(rel)
    max_exact = nb // 2
    large = max_exact + (np.log(np.maximum(n, 1).astype(np.float32) / max_exact)
                         / np.log(1024 / max_exact) * (nb - max_exact)).astype(np.int32)
    large = np.minimum(large, nb - 1)
    return (ret + np.where(n < max_exact, n, large)).astype(np.int32)


def host_bias(bias_table, flip):
    a = np.arange(128)[:, None]
    b = np.arange(256)[None, :]
    rel = a - b + 64
    out = np.full((24, 128, 256), -1e30, np.float32)
    band = np.abs(rel) <= 64
    for g, (dil, _) in enumerate(GRP):
        bk = t5_bucket((-rel if flip else rel) * dil)
        for h in range(8):
            out[g * 8 + h] = np.where(band, bias_table[bk, g * 8 + h], np.float32(-1e30))
    return out


def build_norm():
    nc = bass.Bass("TRN2", target_bir_lowering=False)
    A = {}
    A['hT'] = nc.dram_tensor('hT', [D, NT], F32, kind="ExternalInput").ap()
    A['vecs'] = nc.dram_tensor('vecs', [128, NVEC], F32, kind="ExternalInput").ap()
    A['hn_out'] = nc.dram_tensor('hn_out', [D, NT], F32, kind="ExternalOutput").ap()
    cx = Cx(nc)
    p = cx.p
    cx.hT = p.sb('hT', [128, NCH, NT], F32)
    cx.vec = p.sb('vec', [128, NVEC], F32)
    p.dma('sp', I_dma(cx.vec[:], A['vecs'][:, :]), w=['vec'])
    load_hT(cx, A['hT'])
    emit_norm(cx, A['hn_out'])
    p.finalize()
    return nc, cx


def build_fused(nlayers=4):
    nc = bass.Bass("TRN2", target_bir_lowering=False)
    shapes = {}

    def inp(name, shape, dt=F32):
        shapes[name] = list(shape)

    class Lazy(dict):
        def __missing__(self, name):
            ap = nc.dram_tensor(name, shapes[name], F32, kind="ExternalInput").ap()
            self[name] = ap
            return ap
    A = Lazy()

    def scratch(name):
        return nc.dram_tensor(name, [D, SEQ], F32, kind="Internal").ap()
    inp('xT', [D, SEQ])
    inp('memT', [D, MEMLEN])
    inp('ident', [128, 128])
    inp('v0', [128, NVEC])
    inp('biasT0', [24, 128, 256])
    inp('biasT1', [24, 128, 256])
    for i in range(4):
        inp('vecs%d' % i, [128, NVEC])
        inp('wq%d' % i, [D, D])
        inp('wkv%d' % i, [D, 2 * D])
        inp('wo%d' % i, [D, D])
        inp('w1_%d' % i, [D, DFF])
        inp('w2_%d' % i, [DFF, D])
    for j in range(2):
        inp('wglu%d' % j, [D, 2 * D])
        inp('wqkv%d' % j, [D, 9216])
        inp('woa%d' % j, [D, D])
        inp('avecs%d' % j, [128, NVEC])
        for c in range(2):
            sfx = '%d%d' % (j, c)
            for nm in ('Bre', 'Bim', 'CR', 'CI'):
                inp(nm + sfx, [2, NPT, 128, 128])
            for nm in ('lamre', 'lamim', 'logdt'):
                inp(nm + sfx, [128, 2 * NPT])
            inp('dsk' + sfx, [128, 4])
    xT = A['xT']
    outT = nc.dram_tensor('outT', [D, SEQ], F32, kind="ExternalOutput").ap()
    HN = scratch('HN')
    Y = scratch('Y')
    Hs = [xT, scratch('H1'), scratch('H1a'), scratch('H2'), scratch('H3'), scratch('H3a'), outT]
    cx = Cx(nc, arena=True)
    p = cx.p
    hv = lambda ap, half: ap[:, half * NT:(half + 1) * NT]

    cx.hT = p.sb('hT', [128, NCH, NT], F32)
    cx.vec = p.sb('vec', [128, NVEC], F32)
    p.dma('sp', I_dma(cx.vec[:], A['v0'][:, :]), w=['vec'])
    for half in range(2):
        load_hT(cx, hv(xT, half))
        emit_norm(cx, hv(HN, half))

    def s5_stage(j):
        for c in range(2):
            cx.new_stage()
            sfx = '%d%d' % (j, c)
            AA = {nm: A[nm + sfx] for nm in ('Bre', 'Bim', 'CR', 'CI', 'lamre', 'lamim', 'logdt', 'dsk')}
            AA['ident'] = A['ident']
            AA['uT'] = HN[512 * c:512 * c + 512, :]
            AA['yT'] = Y[512 * c:512 * c + 512, :]
            s5_body(cx, AA)

    def tail_stage(i, glu, src, dst, emit):
        cx.new_stage()
        AA = dict(memT=A['memT'], vecs=A['vecs%d' % i], wq=A['wq%d' % i], wkv=A['wkv%d' % i], wo=A['wo%d' % i],
                  w1=A['w1_%d' % i], w2=A['w2_%d' % i])
        if glu:
            AA['wglu'] = A['wglu%d' % (i // 2)]
        common_tiles(cx, AA)
        for half in range(2):
            load_hT(cx, hv(src, half))
            if glu:
                AA['yT'] = hv(Y, half)
            tail_body(cx, AA, glu, 0, 2, kv_ready=(half == 1))
            tail_body(cx, AA, glu, 2, 2, kv_ready=True)
            store_hT(cx, hv(dst, half))
            if emit:
                emit_norm(cx, hv(HN, half))

    def attn_stage(i, src, dst):
        j = i // 2
        for half in range(2):
            cx.new_stage()
            cx.vec = p.sb('vec', [128, NVEC], F32)
            p.dma('sp', I_dma(cx.vec[:], A['avecs%d' % j][:, :]), w=['vec'])
            AA = dict(wqkv=A['wqkv%d' % j], wo_a=A['woa%d' % j], biasT=A['biasT%d' % half])
            attn_body(cx, AA, flip=(half == 1), src=src, dst=dst)

    s5_stage(0)
    if nlayers == 0:
        cx.new_stage()
        cx.hT = p.sb('hT', [128, NCH, NT], F32)
        for half in range(2):
            load_hT(cx, hv(Y, half))
            store_hT(cx, hv(outT, half))
        p.finalize()
        cx.used = list(A.keys())
        return nc, cx
    tail_stage(0, True, Hs[0], Hs[1] if nlayers > 1 else outT, False)
    if nlayers > 1:
        attn_stage(1, Hs[1], Hs[2])
        tail_stage(1, False, Hs[2], Hs[3] if nlayers > 2 else outT, True)
    if nlayers > 2:
        s5_stage(1)
        tail_stage(2, True, Hs[3], Hs[4] if nlayers > 3 else outT, False)
    if nlayers > 3:
        attn_stage(3, Hs[4], Hs[5])
        tail_stage(3, False, Hs[5], Hs[6], False)
    p.finalize()
    cx.used = list(A.keys())
    return nc, cx


RG2 = [[0, 1], [2, 3], [4, 5], [6, 7]]


class GBuf:
    def __init__(self, nc, name, rows, cols, chunk_rows):
        self.cr = chunk_rows
        self.n = rows // chunk_rows
        self.src = [nc.dram_tensor('%s_s%d' % (name, q), [chunk_rows, cols], F32, kind="Internal").ap() for q in range(self.n)]
        self.dst = [nc.dram_tensor('%s_g%d' % (name, q), [2 * chunk_rows, cols], F32, kind="Internal").ap() for q in range(self.n)]

    def src_rows(self, r0, nrows=128):
        q = r0 // self.cr
        o = r0 - q * self.cr
        return self.src[q][o:o + nrows, :]

    def g_rows(self, rank, r0, nrows=128):
        q = r0 // self.cr
        o = rank * self.cr + r0 - q * self.cr
        return self.dst[q][o:o + nrows, :]


def build_fused8():
    nc = bass.Bass("TRN2", target_bir_lowering=False, num_devices=8)
    shapes = {}

    def inp(name, shape):
        shapes[name] = list(shape)

    class Lazy(dict):
        def __missing__(self, name):
            ap = nc.dram_tensor(name, shapes[name], F32, kind="ExternalInput").ap()
            self[name] = ap
            return ap
    A = Lazy()

    def scratch(name, shape):
        return nc.dram_tensor(name, list(shape), F32, kind="Internal").ap()
    inp('xT', [D, NT])
    inp('memT', [D, MEMLEN])
    inp('ident', [128, 128])
    inp('v0', [128, NVEC])
    inp('biasT', [24, 128, 256])
    for i in range(4):
        inp('vecs%d' % i, [128, NVEC])
        inp('wq%d' % i, [D, D])
        inp('wkv%d' % i, [D, 2 * D])
        inp('wo%d' % i, [D, D])
        inp('w1_%d' % i, [D, DFF])
        inp('w2_%d' % i, [DFF, D])
    for j in range(2):
        inp('wglu%d' % j, [D, 2 * D])
        inp('wqkv%d' % j, [D, 9216])
        inp('woa%d' % j, [D, D])
        inp('avecs%d' % j, [128, NVEC])
        for nm in ('Bre', 'Bim', 'CR', 'CI'):
            inp(nm + '%d' % j, [2, NPT, 128, 128])
        for nm in ('lamre', 'lamim', 'logdt'):
            inp(nm + '%d' % j, [128, 2 * NPT])
        inp('dsk%d' % j, [128, 4])
    outT = nc.dram_tensor('outT', [D, NT], F32, kind="ExternalOutput").ap()
    HNb = GBuf(nc, 'HN', D, NT, 256)
    HN = GHN = HNb
    yO = scratch('yO', [512, NT])
    ySb = GBuf(nc, 'yS', 512, NT, 256)
    yS = GS = ySb
    Hhb = GBuf(nc, 'Hh', D, 1024, 512)
    Hh = GH = Hhb
    Hs = [A['xT']] + [scratch(n, [D, NT]) for n in ('H1', 'H1a', 'H2', 'H3', 'H3a')] + [outT]
    cx = Cx(nc, arena=True)
    p = cx.p

    def allgather(gb, _unused=None):
        cx.new_stage()
        for q in range(gb.n):
            p.dma('pool', lambda e, q=q: e.collective_compute("AllGather", ALU.bypass, replica_groups=RG2,
                                                               ins=[gb.src[q][:, :]], outs=[gb.dst[q][:, :]]),
                  w=['cc'], semkey='cc', inc=1)

    cx.hT = p.sb('hT', [128, NCH, NT], F32)
    cx.vec = p.sb('vec', [128, NVEC], F32)
    p.dma('sp', I_dma(cx.vec[:], A['v0'][:, :]), w=['vec'])
    load_hT(cx, A['xT'])
    emit_norm(cx, HN)
    allgather(HN, GHN)

    def s5_stage(j):
        cx.new_stage()
        AA = {nm: A[nm + '%d' % j] for nm in ('Bre', 'Bim', 'CR', 'CI', 'lamre', 'lamim', 'logdt', 'dsk')}
        AA['ident'] = A['ident']
        AA['GHN'] = GHN
        AA['yO'] = yO
        AA['yS'] = yS
        cx.vec = p.sb('vec', [128, NVEC], F32)
        p.dma('sp', I_dma(cx.vec[:], A['v0'][:, :]), w=['vec'])
        s5_body(cx, AA)
        allgather(yS, GS)

    def tail_stage(i, glu, src, dst, emit, halo):
        cx.new_stage()
        AA = dict(memT=A['memT'], vecs=A['vecs%d' % i], wq=A['wq%d' % i], wkv=A['wkv%d' % i], wo=A['wo%d' % i],
                  w1=A['w1_%d' % i], w2=A['w2_%d' % i])
        if glu:
            AA['wglu'] = A['wglu%d' % (i // 2)]
            AA['yO'] = yO
            AA['GS'] = GS
        common_tiles(cx, AA)
        load_hT(cx, src)
        tail_body(cx, AA, glu, 0, 2, kv_ready=False)
        tail_body(cx, AA, glu, 2, 2, kv_ready=True)
        store_hT(cx, dst)
        if halo:
            for c in range(NCH):
                p.dma('sp', I_dma(Hh.src_rows(c * 128), cx.hT[:, c, 1024:2048]),
                      r=['h%d_%d' % (c, tb) for tb in (2, 3)], w=['hhout'])
            allgather(Hh, GH)
        if emit:
            emit_norm(cx, HN)
            allgather(HN, GHN)

    def attn_stage(i, src, dst):
        j = i // 2
        cx.new_stage()
        cx.vec = p.sb('vec', [128, NVEC], F32)
        p.dma('sp', I_dma(cx.vec[:], A['avecs%d' % j][:, :]), w=['vec'])
        AA = dict(wqkv=A['wqkv%d' % j], wo_a=A['woa%d' % j], biasT=A['biasT'])
        cx.ws_n = 2
        attn_body(cx, AA, flip=False, src=src, dst=dst, gh=GH)
        cx.ws_n = WS_N

    s5_stage(0)
    tail_stage(0, True, Hs[0], Hs[1], False, True)
    attn_stage(1, Hs[1], Hs[2])
    tail_stage(1, False, Hs[2], Hs[3], True, False)
    s5_stage(1)
    tail_stage(2, True, Hs[3], Hs[4], False, True)
    attn_stage(3, Hs[4], Hs[5])
    tail_stage(3, False, Hs[5], Hs[6], False, False)
    p.finalize()
    cx.used = list(A.keys())
    return nc, cx


def _pc(v, C):
    return np.ascontiguousarray(np.asarray(v, np.float32).reshape(C, 128).T)


_PROGS = {}
_NL = [4]


def _prog(name):
    if name not in _PROGS:
        if name == 'norm':
            _PROGS[name] = build_norm()[0]
        elif name == 's5':
            _PROGS[name] = build_s5()[0]
        elif name == 'tail_glu':
            _PROGS[name] = build_tail(True, True)[0]
        elif name == 'tail':
            _PROGS[name] = build_tail(False, True)[0]
        elif name == 'attn':
            _PROGS[name] = build_attn()[0]
    return _PROGS[name]


def kernel_multi(**inp):
    inp = {k: np.asarray(v) for k, v in inp.items()}
    ncore = 8
    cores = list(range(ncore))
    f32 = np.float32
    loc = [np.arange(NEXT) if (k % 2 == 0) else (SEQ - 1 - np.arange(NEXT)) for k in cores]
    H = np.array(inp['x'], dtype=f32, copy=True)
    memT = [np.ascontiguousarray(inp['mem'][k // 2].T.astype(f32)) for k in cores]
    biasT = [host_bias(inp['bias_table'].astype(f32), k % 2 == 1) for k in cores]

    def own_T(arr_bsd, k):
        return np.ascontiguousarray(arr_bsd[k // 2][loc[k][:NT]].T)

    def scatter(outs, name):
        full = np.empty((BATCH, SEQ, D), f32)
        for k in cores:
            full[k // 2][loc[k][:NT]] = np.asarray(outs[k][name], f32).T
        return full

    def tail_vecs(i):
        v = np.zeros((128, NVEC), f32)
        v[:, 0:8] = _pc(inp['norm_xattn'][i], 8)
        v[:, 8:16] = _pc(inp['norm_mem'][i], 8)
        v[:, 16:24] = _pc(inp['norm_mlp'][i], 8)
        v[:, 24:26] = _pc(inp['xattn_q_gain'][i], 2)
        v[:, 26:28] = _pc(inp['xattn_k_gain'][i], 2)
        v[:, 28:36] = _pc(inp['norm_mix'][min(i + 1, 3)], 8)
        return v

    def run_tail(i, H, Y):
        v = tail_vecs(i)
        maps = []
        for k in cores:
            m = dict(hT=own_T(H, k), memT=memT[k], vecs=v, wq=inp['xattn_w_q'][i], wkv=inp['xattn_w_kv'][i],
                     wo=inp['xattn_w_o'][i], w1=inp['mlp_w1'][i], w2=inp['mlp_w2'][i])
            if Y is not None:
                m['yT'] = own_T(Y, k)
                m['wglu'] = inp['s5_w_glu'][i // 2]
            maps.append(m)
        res = run_bass_kernel_spmd(_prog('tail_glu' if Y is not None else 'tail'), maps, core_ids=cores).results
        return scatter(res, 'hT_out'), scatter(res, 'hn_out')

    def run_s5(j, HN):
        maps = []
        for k in cores:
            b, c = k // 2, k % 2
            m = s5_host_inputs(inp, j, c)
            m['uT'] = np.ascontiguousarray(HN[b][:, 512 * c:512 * c + 512].T)
            maps.append(m)
        res = run_bass_kernel_spmd(_prog('s5'), maps, core_ids=cores).results
        Y = np.empty((BATCH, SEQ, D), f32)
        for k in cores:
            b, c = k // 2, k % 2
            Y[b][:, 512 * c:512 * c + 512] = np.asarray(res[k]['yT'], f32).T
        return Y

    def run_attn(i, H):
        j = i // 2
        v = np.zeros((128, NVEC), f32)
        v[:, 28:36] = _pc(inp['norm_mix'][i], 8)
        v[:, 36] = inp['attn_q_gain'][j]
        v[:, 37] = inp['attn_k_gain'][j]
        maps = []
        for k in cores:
            maps.append(dict(hT_ext=np.ascontiguousarray(H[k // 2][loc[k]].T), vecs=v, wqkv=inp['attn_w_qkv'][j],
                             wo_a=inp['attn_w_o'][j], biasT=biasT[k]))
        res = run_bass_kernel_spmd(_prog('attn'), maps, core_ids=cores).results
        return scatter(res, 'hT_out')

    v0 = np.zeros((128, NVEC), f32)
    v0[:, 28:36] = _pc(inp['norm_mix'][0], 8)
    res = run_bass_kernel_spmd(_prog('norm'), [dict(hT=own_T(H, k), vecs=v0) for k in cores], core_ids=cores).results
    HN = scatter(res, 'hn_out')
    for i in range(4):
        if i % 2 == 0:
            Y = run_s5(i // 2, HN)
            H, HN = run_tail(i, H, Y)
        else:
            H = run_attn(i, H)
            H, HN = run_tail(i, H, None)
    return H


def tail_vecs_host(inp, i):
    v = np.zeros((128, NVEC), np.float32)
    v[:, 0:8] = _pc(inp['norm_xattn'][i], 8)
    v[:, 8:16] = _pc(inp['norm_mem'][i], 8)
    v[:, 16:24] = _pc(inp['norm_mlp'][i], 8)
    v[:, 24:26] = _pc(inp['xattn_q_gain'][i], 2)
    v[:, 26:28] = _pc(inp['xattn_k_gain'][i], 2)
    v[:, 28:36] = _pc(inp['norm_mix'][min(i + 1, 3)], 8)
    return v


def kernel_fused4(**inp):
    inp = {k: np.asarray(v) for k, v in inp.items()}
    f32 = np.float32
    nl = _NL[0]
    if ('fused', nl) not in _PROGS:
        _PROGS[('fused', nl)] = build_fused(nl)
    nc, cxf = _PROGS[('fused', nl)]
    shared = dict(ident=np.eye(128, dtype=f32),
                  biasT0=host_bias(inp['bias_table'].astype(f32), False),
                  biasT1=host_bias(inp['bias_table'].astype(f32), True))
    v0 = np.zeros((128, NVEC), f32)
    v0[:, 28:36] = _pc(inp['norm_mix'][0], 8)
    shared['v0'] = v0
    for i in range(4):
        shared['vecs%d' % i] = tail_vecs_host(inp, i)
        shared['wq%d' % i] = inp['xattn_w_q'][i]
        shared['wkv%d' % i] = inp['xattn_w_kv'][i]
        shared['wo%d' % i] = inp['xattn_w_o'][i]
        shared['w1_%d' % i] = inp['mlp_w1'][i]
        shared['w2_%d' % i] = inp['mlp_w2'][i]
    for j in range(2):
        shared['wglu%d' % j] = inp['s5_w_glu'][j]
        shared['wqkv%d' % j] = inp['attn_w_qkv'][j]
        shared['woa%d' % j] = inp['attn_w_o'][j]
        av = np.zeros((128, NVEC), f32)
        av[:, 28:36] = _pc(inp['norm_mix'][2 * j + 1], 8)
        av[:, 36] = inp['attn_q_gain'][j]
        av[:, 37] = inp['attn_k_gain'][j]
        shared['avecs%d' % j] = av
        for c in range(2):
            for nm, arr in s5_host_inputs(inp, j, c).items():
                if nm != 'ident':
                    shared[nm + '%d%d' % (j, c)] = arr
    maps = []
    for b in range(BATCH):
        m = dict(shared)
        m['xT'] = np.ascontiguousarray(inp['x'][b].T.astype(f32))
        m['memT'] = np.ascontiguousarray(inp['mem'][b].T.astype(f32))
        maps.append({k: m[k] for k in cxf.used})
    res = run_bass_kernel_spmd(nc, maps, core_ids=list(range(BATCH))).results
    out = np.empty((BATCH, SEQ, D), f32)
    for b in range(BATCH):
        out[b] = np.asarray(res[b]['outT'], f32).T
    return out


def _sw(a, axis):
    return np.roll(a, 512, axis=axis)


def kernel(**inp):
    inp = {k: np.asarray(v, np.float32) for k, v in inp.items()}
    f32 = np.float32
    if 'fused8' not in _PROGS:
        _PROGS['fused8'] = build_fused8()
    nc, cxf = _PROGS['fused8']
    ident = np.eye(128, dtype=f32)
    per_c = []
    for c in range(2):
        sw = (lambda a, axis: _sw(a, axis)) if c == 1 else (lambda a, axis: a)
        g = {}
        gi = {k: (sw(inp[k], 1) if k in ('norm_mix', 'norm_xattn', 'norm_mem', 'norm_mlp') else inp[k]) for k in inp}
        g['ident'] = ident
        g['biasT'] = host_bias(inp['bias_table'], c == 1)
        v0 = np.zeros((128, NVEC), f32)
        v0[:, 28:36] = _pc(gi['norm_mix'][0], 8)
        v0[:, 40 + c] = 1.0
        g['v0'] = v0
        for i in range(4):
            v = tail_vecs_host(gi, i)
            v[:, 40 + c] = 1.0
            g['vecs%d' % i] = v
            g['wq%d' % i] = np.ascontiguousarray(sw(inp['xattn_w_q'][i], 0))
            g['wkv%d' % i] = np.ascontiguousarray(sw(inp['xattn_w_kv'][i], 0))
            g['wo%d' % i] = np.ascontiguousarray(sw(inp['xattn_w_o'][i], 1))
            g['w1_%d' % i] = np.ascontiguousarray(sw(inp['mlp_w1'][i], 0))
            g['w2_%d' % i] = np.ascontiguousarray(sw(inp['mlp_w2'][i], 1))
        for j in range(2):
            wg = sw(inp['s5_w_glu'][j], 0).reshape(D, 2, D)
            g['wglu%d' % j] = np.ascontiguousarray(sw(wg, 2).reshape(D, 2 * D))
            g['wqkv%d' % j] = np.ascontiguousarray(sw(inp['attn_w_qkv'][j], 0))
            g['woa%d' % j] = np.ascontiguousarray(sw(inp['attn_w_o'][j], 1))
            av = np.zeros((128, NVEC), f32)
            av[:, 28:36] = _pc(gi['norm_mix'][2 * j + 1], 8)
            av[:, 36] = inp['attn_q_gain'][j]
            av[:, 37] = inp['attn_k_gain'][j]
            av[:, 40 + c] = 1.0
            g['avecs%d' % j] = av
            for nm, arr in s5_host_inputs(inp, j, c).items():
                if nm != 'ident':
                    g[nm + '%d' % j] = arr
        per_c.append(g)
    maps = []
    for k in range(8):
        b, c = k // 2, k % 2
        m = dict(per_c[c])
        xb = inp['x'][b]
        if c == 0:
            m['xT'] = np.ascontiguousarray(xb[:NT].T)
            m['memT'] = np.ascontiguousarray(inp['mem'][b].T)
        else:
            m['xT'] = np.ascontiguousarray(_sw(xb[::-1][:NT], 1).T)
            m['memT'] = np.ascontiguousarray(_sw(inp['mem'][b], 1).T)
        maps.append({kk: m[kk] for kk in cxf.used})
    res = run_bass_kernel_spmd(nc, maps, core_ids=list(range(8))).results
    out = np.empty((BATCH, SEQ, D), f32)
    for k in range(8):
        b, c = k // 2, k % 2
        o = np.asarray(res[k]['outT'], f32).T
        if c == 0:
            out[b, :NT] = o
        else:
            out[b, NT:] = _sw(o, 1)[::-1]
    return out
```

```python
import math
import numpy as np
from contextlib import ExitStack
import concourse.bass as bass
import concourse.mybir as mybir
from concourse.bass_utils import run_bass_kernel_spmd

F32 = mybir.dt.float32
BF16 = mybir.dt.bfloat16
AF = mybir.ActivationFunctionType
ALU = mybir.AluOpType

D = 1024
NCH = 8
SEQ = 4096
BATCH = 4
NT = 2048
EPS = 1e-6
MEMLEN = 256
DFF = 4096


class Prog:
    ENGS = ('pe', 'act', 'dve', 'pool', 'sp')
    BLK = {'pe': 'tensor', 'act': 'scalar', 'dve': 'vector', 'pool': 'gpsimd', 'sp': 'sync'}

    def __init__(self, nc):
        self.nc = nc
        self.es = ExitStack()
        self.ins = {e: [] for e in self.ENGS}
        self.last_w = {}
        self.readers = {}
        self.dma_cnt = {}
        self.log = None
        self.bar_deps = {}

    ARENA_F32 = 51712

    def use_arena(self):
        self.arena = self.es.enter_context(self.nc.sbuf_tensor('arena', [128, self.ARENA_F32], F32))
        self.aoff = 0

    def sb(self, name, shape, dt):
        if getattr(self, 'arena', None) is None:
            return self.es.enter_context(self.nc.sbuf_tensor('s_' + name, list(shape), dt))
        assert shape[0] == 128, shape
        nel = 1
        for d_ in shape[1:]:
            nel *= d_
        isz = 4 if dt == F32 else 2
        nby = (nel * isz + 63) // 64 * 64
        o4 = self.aoff // 4
        self.aoff += nby
        assert self.aoff <= self.ARENA_F32 * 4, ('arena overflow', name, self.aoff)
        v = self.arena[:, o4:o4 + nby // 4]
        if dt != F32:
            v = v.bitcast(dt)
        v = v[:, :nel]
        if len(shape) == 3:
            v = v.rearrange("p (a b) -> p a b", a=shape[1])
        elif len(shape) != 2:
            raise AssertionError(shape)
        return v

    def barrier(self):
        deps = [('d', k, c) for k, c in self.dma_cnt.items()]
        for e in self.ENGS:
            n = len(self.ins[e])
            j = n - 1
            while j >= 0 and self.ins[e][j]['dma'] is not None:
                j -= 1
            if j >= 0:
                deps.append(('e', e, j))
        self.bar_deps = {e: list(deps) for e in self.ENGS}
        self.last_w.clear()
        self.readers.clear()

    def ps(self, name, shape, dt=F32):
        return self.es.enter_context(self.nc.psum_tensor(name, list(shape), dt))

    def _deps(self, r, w):
        deps = []
        for k in r:
            t = self.last_w.get(k)
            if t is not None:
                deps.append(t)
        for k in w:
            t = self.last_w.get(k)
            if t is not None:
                deps.append(t)
            deps.extend(self.readers.get(k, ()))
        return deps

    def _commit(self, tok, r, w):
        for k in r:
            lst = self.readers.setdefault(k, [])
            lst[:] = [t for t in lst if t[:2] != tok[:2]]
            lst.append(tok)
        for k in w:
            self.last_w[k] = tok
            self.readers[k] = []

    def op(self, eng, fn, r=(), w=()):
        idx = len(self.ins[eng])
        self.ins[eng].append(dict(fn=fn, deps=self._deps(r, w) + self.bar_deps.pop(eng, []), dma=None))
        self._commit(('e', eng, idx), r, w)

    def dma(self, eng, fn, r=(), w=(), semkey=None, inc=16):
        if semkey is None:
            semkey = w[0]
        c = self.dma_cnt.get(semkey, 0) + inc
        self.dma_cnt[semkey] = c
        self.ins[eng].append(dict(fn=fn, deps=self._deps(r, w) + self.bar_deps.pop(eng, []), dma=semkey, inc=inc))
        self._commit(('d', semkey, c), r, w)

    SAME_DIST = 4

    def _skip_same(self, e, i, d, rec):
        if d[1] != e or rec['dma'] is not None:
            return False
        if e == 'pe':
            return True
        return (i - d[2]) > self.SAME_DIST

    def finalize(self):
        nc = self.nc
        need = {e: set() for e in self.ENGS}
        for e in self.ENGS:
            for i, rec in enumerate(self.ins[e]):
                for d in rec['deps']:
                    if d[0] == 'e' and not self._skip_same(e, i, d, rec):
                        need[d[1]].add(d[2])
        cum = {}
        for e in self.ENGS:
            c = 0
            arr = []
            for i in range(len(self.ins[e])):
                if i in need[e]:
                    c += 1
                arr.append(c)
            cum[e] = arr
        esem = {e: self.es.enter_context(nc.semaphore('se_' + e)) for e in self.ENGS}
        dsem = {}
        for i, k in enumerate(self.dma_cnt):
            dsem[k] = self.es.enter_context(nc.semaphore('sd_%d' % i))
        self.stats = {e: (len(self.ins[e]), cum[e][-1] if cum[e] else 0) for e in self.ENGS}
        self.stats['ndsem'] = len(dsem)
        with nc.Block() as block:
            for e in self.ENGS:
                def body(eng, e=e):
                    waited = {}
                    for i, rec in enumerate(self.ins[e]):
                        req = {}
                        for d in rec['deps']:
                            if d[0] == 'e':
                                if self._skip_same(e, i, d, rec):
                                    continue
                                key = ('e', d[1])
                                val = cum[d[1]][d[2]]
                            else:
                                key = ('d', d[1])
                                val = d[2]
                            if val > req.get(key, 0):
                                req[key] = val
                        for key, val in req.items():
                            if waited.get(key, 0) < val:
                                sem = esem[key[1]] if key[0] == 'e' else dsem[key[1]]
                                eng.wait_ge(sem, val)
                                waited[key] = val
                                if self.log is not None:
                                    self.log.append((e, i, 'wait', key, val))
                        if self.log is not None:
                            self.log.append((e, i, 'inst', rec['dma'], cum[e][i] if i in need[e] else None))
                        inst = rec['fn'](eng)
                        if rec['dma'] is not None:
                            inst.then_inc(dsem[rec['dma']], rec.get('inc', 16))
                        elif i in need[e]:
                            inst.then_inc(esem[e], 1)
                    if e == 'sp':
                        for k, c in self.dma_cnt.items():
                            eng.wait_ge(dsem[k], c)
                getattr(block, self.BLK[e])(body)
        self.es.close()


def I_mm(out, lhsT, rhs, start, stop):
    return lambda e: e.matmul(out, lhsT, rhs, start=start, stop=stop)


def I_act(out, in_, func, **kw):
    return lambda e: e.activation(out=out, in_=in_, func=func, **kw)


def I_tt(out, in0, in1, op):
    return lambda e: e.tensor_tensor(out=out, in0=in0, in1=in1, op=op)


def I_ts(out, in0, s1, s2, op0, op1=None):
    if op1 is None:
        return lambda e: e.tensor_scalar(out=out, in0=in0, scalar1=s1, scalar2=None, op0=op0)
    return lambda e: e.tensor_scalar(out=out, in0=in0, scalar1=s1, scalar2=s2, op0=op0, op1=op1)


def I_stt(out, in0, scalar, in1, op0, op1):
    return lambda e: e.scalar_tensor_tensor(out=out, in0=in0, scalar=scalar, in1=in1, op0=op0, op1=op1)


def I_recip(out, in_):
    return lambda e: e.reciprocal(out=out, in_=in_)


def I_copy(out, in_):
    return lambda e: e.tensor_copy(out=out, in_=in_)


def I_memset(ap, c):
    return lambda e: e.memset(ap, c)


def I_dma(out, in_):
    return lambda e: e.dma_start(out=out, in_=in_)


def I_scan(out, d0, d1, init):
    return lambda e: e.tensor_tensor_scan(out=out, data0=d0, data1=d1, initial=init, op0=ALU.mult, op1=ALU.add)


def mcombine(cx, out, okey, X, xk, Z, zk, ma, mb, n=512):
    p = cx.p
    tmp, tk = cx.rot('mctmp', [128, 512], F32, n=2)
    p.op('act', I_act(tmp[:, :n], X, AF.Copy, scale=ma), r=[xk, 'vec'], w=[tk])
    p.op('dve', I_stt(out, Z, mb, tmp[:, :n], ALU.mult, ALU.add), r=[zk, 'vec', tk], w=[okey])


class Cx:
    def __init__(self, nc, arena=False):
        self.nc = nc
        self.p = Prog(nc)
        if arena:
            self.p.use_arena()
        self.banks = [self.p.ps('bank%d' % i, [128, 512]) for i in range(8)]
        self.bi = 0
        p = self.p
        self.ones = p.sb('ones', [128, 128], BF16)
        p.op('pool', I_memset(self.ones[:], 1.0), w=['ones'])
        self.sq = [p.sb('sq%d' % i, [128, 512], BF16) for i in range(2)]
        self.sqi = 0
        self.rstd = [p.sb('rstd%d' % i, [128, 512], F32) for i in range(2)]
        self.rsi = 0
        self._rot = {}
        self.epsc = p.sb('epsc', [128, 1], F32)
        p.op('pool', I_memset(self.epsc[:], EPS), w=['epsc'])
        self.mark = getattr(p, 'aoff', 0)

    def new_stage(self):
        self.p.barrier()
        self.p.aoff = self.mark
        self._rot = {}
        for nm in ('ws', 'wsi'):
            if hasattr(self, nm):
                delattr(self, nm)

    def bank(self):
        i = self.bi
        self.bi = (i + 1) % 8
        return self.banks[i], 'bank%d' % i

    def rot(self, name, shape, dt, n=2):
        if name not in self._rot:
            self._rot[name] = [[self.p.sb('%s_%d' % (name, i), shape, dt) for i in range(n)], 0]
        tl, i = self._rot[name]
        self._rot[name][1] = (i + 1) % n
        return tl[i], '%s_%d' % (name, i)

    def next_sq(self):
        i = self.sqi
        self.sqi = 1 - i
        return self.sq[i], 'sq%d' % i

    def next_rstd(self):
        i = self.rsi
        self.rsi = 1 - i
        return self.rstd[i], 'rstd%d' % i

    def rstd_from_bank(self, bank, bk, n, dim):
        p = self.p
        rs, rk = self.next_rstd()
        p.op('act', I_act(rs[:, :n], bank[:, :n], AF.Ln, scale=1.0 / dim, bias=self.epsc[:, 0:1]), r=[bk, 'epsc'], w=[rk])
        p.op('act', I_act(rs[:, :n], rs[:, :n], AF.Exp, scale=-0.5), r=[rk], w=[rk])
        return rs, rk


def rmsnorm(cx, src, skey, gcols, gkey, dst, dkey, C, n0, n, dim):
    p = cx.p
    bank, bk = cx.bank()
    for c in range(C):
        sq, sk = cx.next_sq()
        p.op('act', I_act(sq[:, :n], src[:, c, n0:n0 + n], AF.Square), r=[skey(c)], w=[sk])
        p.op('pe', I_mm(bank[:, :n], cx.ones[:], sq[:, :n], c == 0, c == C - 1), r=[sk, 'ones'], w=[bk])
    rs, rk = cx.rstd_from_bank(bank, bk, n, dim)
    for c in range(C):
        p.op('dve', I_stt(dst[:, c, n0:n0 + n], src[:, c, n0:n0 + n], gcols[:, c:c + 1], rs[:, :n], ALU.mult, ALU.mult),
             r=[skey(c), gkey, rk], w=[dkey(c)])


WS_N = 3


def wslab(cx, parts):
    p = cx.p
    nws = getattr(cx, 'ws_n', WS_N)
    if not hasattr(cx, 'ws'):
        cx.ws = [p.sb('ws%d' % i, [128, 4096], BF16) for i in range(nws)]
        cx.wsi = 0
    i = cx.wsi
    cx.wsi = (i + 1) % nws
    t = cx.ws[i]
    key = 'ws%d' % i
    for src, off in parts:
        K, N = src.shape
        kc = K // 128
        dst = t[:, off:off + kc * N].rearrange("p (k n) -> p k n", k=kc)
        p.dma('pool', I_dma(dst, src.rearrange("(k p) n -> p k n", p=128)), w=[key])
    return t, key


class SlabStream:
    def __init__(self, cx, specs):
        self.cx, self.specs, self.loaded, self.i = cx, specs, [], 0

    def get(self):
        while len(self.loaded) < min(len(self.specs), self.i + 2):
            self.loaded.append(wslab(self.cx, self.specs[len(self.loaded)]))
        r = self.loaded[self.i]
        self.i += 1
        return r


VC = dict(gx=0, gm=8, gl=16, gq=24, gk=26, gn=28, aq=36, ak=37, dsk=38, m0=40, m1=41)
NVEC = 48


def tail_body(cx, A, glu, tb0, ntb, kv_ready):
    p = cx.p
    hT = cx.hT
    hk = lambda c, tb: 'h%d_%d' % (c, tb)
    hn = cx.hn
    big2 = cx.big2
    vec = cx.vec
    NB = ntb
    specs = []
    if glu:
        for ns in range(2):
            specs.append([(A['wglu'][:, ns * 512:(ns + 1) * 512], 0)])
            specs.append([(A['wglu'][:, 1024 + ns * 512:1024 + (ns + 1) * 512], 0)])
    if not kv_ready:
        for hp in range(2):
            specs.append([(A['wkv'][:, hp * 512:(hp + 1) * 512], 0)])
        for vs in range(2):
            specs.append([(A['wkv'][:, 1024 + vs * 512:1024 + (vs + 1) * 512], 0)])
    for hp in range(2):
        specs.append([(A['wq'][:, hp * 512:(hp + 1) * 512], 0)])
    for ns in range(2):
        specs.append([(A['wo'][:, ns * 512:(ns + 1) * 512], 0)])
    for s in range(DFF // 256):
        specs.append([(A['w1'][:, s * 256:(s + 1) * 256], 0), (A['w2'][s * 256:(s + 1) * 256, :], 2048)])
    ss = SlabStream(cx, specs)

    if glu:
        for c in range(NCH):
            for tb in range(NB):
                yt, yk = cx.rot('ytmp', [128, 512], F32)
                col0 = (tb0 + tb) * 512
                if 'yO' in A and c >= 4:
                    X, xk = cx.rot('gx', [128, 512], F32, n=2)
                    Z, zk = cx.rot('gz', [128, 512], F32, n=2)
                    p.dma('sp', I_dma(X[:], A['GS'].g_rows(0, (c - 4) * 128)[:, col0:col0 + 512]), w=[xk])
                    p.dma('sp', I_dma(Z[:], A['GS'].g_rows(1, (c - 4) * 128)[:, col0:col0 + 512]), w=[zk])
                    mcombine(cx, yt[:], yk, X[:], xk, Z[:], zk, vec[:, VC['m1']:VC['m1'] + 1], vec[:, VC['m0']:VC['m0'] + 1])
                elif 'yO' in A:
                    p.dma('sp', I_dma(yt[:], A['yO'][c * 128:(c + 1) * 128, col0:col0 + 512]), w=[yk])
                else:
                    p.dma('sp', I_dma(yt[:], A['yT'][c * 128:(c + 1) * 128, col0:col0 + 512]), w=[yk])
                p.op('act', I_act(hn[:, c, tb * 512:(tb + 1) * 512], yt[:], AF.Gelu_apprx_tanh), r=[yk], w=['hn%d' % tb])
        for ns in range(2):
            wa, wak = ss.get()
            wb, wbk = ss.get()
            for j in range(4):
                n = ns * 4 + j
                for tb in range(NB):
                    ba, bak = cx.bank()
                    bb, bbk = cx.bank()
                    for kc in range(NCH):
                        p.op('pe', I_mm(ba[:], wa[:, kc * 512 + j * 128: kc * 512 + (j + 1) * 128],
                                        hn[:, kc, tb * 512:(tb + 1) * 512], kc == 0, kc == NCH - 1),
                             r=[wak, 'hn%d' % tb], w=[bak])
                    for kc in range(NCH):
                        p.op('pe', I_mm(bb[:], wb[:, kc * 512 + j * 128: kc * 512 + (j + 1) * 128],
                                        hn[:, kc, tb * 512:(tb + 1) * 512], kc == 0, kc == NCH - 1),
                             r=[wbk, 'hn%d' % tb], w=[bbk])
                    sg, sgk = cx.rot('sg', [128, 512], F32)
                    p.op('act', I_act(sg[:], bb[:], AF.Sigmoid), r=[bbk], w=[sgk])
                    gt, gtk = cx.rot('gtmp', [128, 512], F32)
                    p.op('dve', I_tt(gt[:], ba[:], sg[:], ALU.mult), r=[bak, sgk], w=[gtk])
                    hs = hT[:, n, (tb0 + tb) * 512:(tb0 + tb + 1) * 512]
                    p.op('pool', I_tt(hs, hs, gt[:], ALU.add), r=[gtk, hk(n, tb0 + tb)], w=[hk(n, tb0 + tb)])

    if not kv_ready:
        kraw = cx.kraw
        memn = cx.memn
        for c in range(NCH):
            p.dma('sp', I_dma(kraw[:, c, :], A['memT'][c * 128:(c + 1) * 128, :]), w=['kraw'], semkey='kraw_ld')
        rmsnorm(cx, kraw, lambda c: 'kraw', vec[:, VC['gm']:VC['gm'] + 8], 'vec', memn, lambda c: 'memn', NCH, 0, MEMLEN, D)
        for hp in range(2):
            wk, wkk = ss.get()
            for jj in range(4):
                j = hp * 4 + jj
                bk_, bkk = cx.bank()
                for kc in range(NCH):
                    p.op('pe', I_mm(bk_[:, :MEMLEN], wk[:, kc * 512 + jj * 128: kc * 512 + (jj + 1) * 128], memn[:, kc, :],
                                    kc == 0, kc == NCH - 1), r=[wkk, 'memn'], w=[bkk])
                p.op('act', I_act(kraw[:, j, :], bk_[:, :MEMLEN], AF.Copy), r=[bkk], w=['kraw'])
        for h in range(4):
            bs, bsk = cx.bank()
            for ec in range(2):
                sq, sk = cx.next_sq()
                p.op('act', I_act(sq[:, :MEMLEN], kraw[:, 2 * h + ec, :], AF.Square), r=['kraw'], w=[sk])
                p.op('pe', I_mm(bs[:, :MEMLEN], cx.ones[:], sq[:, :MEMLEN], ec == 0, ec == 1), r=[sk, 'ones'], w=[bsk])
            rs, rk = cx.rstd_from_bank(bs, bsk, MEMLEN, 256)
            for ec in range(2):
                p.op('dve', I_stt(cx.KT[:, 2 * h + ec, :], kraw[:, 2 * h + ec, :], vec[:, VC['gk'] + ec:VC['gk'] + ec + 1],
                                  rs[:, :MEMLEN], ALU.mult, ALU.mult), r=['kraw', 'vec', rk], w=['KT'])
        for vs in range(2):
            wv, wvk = ss.get()
            for mc in range(2):
                bv, bvk = cx.bank()
                for kc in range(NCH):
                    p.op('pe', I_mm(bv[:], memn[:, kc, mc * 128:(mc + 1) * 128], wv[:, kc * 512:(kc + 1) * 512],
                                    kc == 0, kc == NCH - 1), r=[wvk, 'memn'], w=[bvk])
                p.op('act', I_act(cx.V[:, mc, vs * 512:(vs + 1) * 512], bv[:], AF.Copy), r=[bvk], w=['V'])

    for tb in range(NB):
        _rmsnorm_off(cx, hT, (tb0 + tb) * 512, lambda c, tb=tb: hk(c, tb0 + tb), vec[:, VC['gx']:VC['gx'] + 8],
                     hn, tb * 512, 'hn%d' % tb)
    for hp in range(2):
        wq, wqk = ss.get()
        for hh in range(2):
            h = 2 * hp + hh
            for tb in range(NB):
                qb = []
                for ec in range(2):
                    b, bk_ = cx.bank()
                    cc = hh * 2 + ec
                    for kc in range(NCH):
                        p.op('pe', I_mm(b[:], wq[:, kc * 512 + cc * 128: kc * 512 + (cc + 1) * 128],
                                        hn[:, kc, tb * 512:(tb + 1) * 512], kc == 0, kc == NCH - 1),
                             r=[wqk, 'hn%d' % tb], w=[bk_])
                    qb.append((b, bk_))
                bs, bsk = cx.bank()
                for ec in range(2):
                    sq, sk = cx.next_sq()
                    p.op('act', I_act(sq[:], qb[ec][0][:], AF.Square), r=[qb[ec][1]], w=[sk])
                    p.op('pe', I_mm(bs[:], cx.ones[:], sq[:], ec == 0, ec == 1), r=[sk, 'ones'], w=[bsk])
                rs, rk = cx.rstd_from_bank(bs, bsk, 512, 256)
                qn, qnk = cx.rot('qn', [128, 2, 512], BF16)
                for ec in range(2):
                    p.op('dve', I_stt(qn[:, ec, :], qb[ec][0][:], vec[:, VC['gq'] + ec:VC['gq'] + ec + 1], rs[:],
                                      ALU.mult, ALU.mult), r=[qb[ec][1], 'vec', rk], w=[qnk])
                PT, ptk = cx.rot('PT', [128, 2, 512], BF16)
                for mc in range(2):
                    bl, blk = cx.bank()
                    for ec in range(2):
                        p.op('pe', I_mm(bl[:], cx.KT[:, 2 * h + ec, mc * 128:(mc + 1) * 128], qn[:, ec, :], ec == 0, ec == 1),
                             r=['KT', qnk], w=[blk])
                    p.op('act', I_act(PT[:, mc, :], bl[:], AF.Exp, scale=1.0 / 16.0), r=[blk], w=[ptk])
                bd, bdk = cx.bank()
                for mc in range(2):
                    p.op('pe', I_mm(bd[:], cx.ones[:], PT[:, mc, :], mc == 0, mc == 1), r=['ones', ptk], w=[bdk])
                rd, rdk = cx.rot('rden', [128, 512], F32)
                p.op('act', I_act(rd[:], bd[:], AF.Ln), r=[bdk], w=[rdk])
                p.op('act', I_act(rd[:], rd[:], AF.Exp, scale=-1.0), r=[rdk], w=[rdk])
                for ec in range(2):
                    bo, bok = cx.bank()
                    for mc in range(2):
                        p.op('pe', I_mm(bo[:], cx.V[:, mc, h * 256 + ec * 128: h * 256 + (ec + 1) * 128], PT[:, mc, :],
                                        mc == 0, mc == 1), r=['V', ptk], w=[bok])
                    p.op('dve', I_tt(big2[:, 2 * h + ec, tb * 512:(tb + 1) * 512], bo[:], rd[:], ALU.mult),
                         r=[bok, rdk], w=['big2_%d' % tb])
    for ns in range(2):
        wo, wok = ss.get()
        for j in range(4):
            n = ns * 4 + j
            for tb in range(NB):
                b, bk_ = cx.bank()
                for kc in range(NCH):
                    p.op('pe', I_mm(b[:], wo[:, kc * 512 + j * 128: kc * 512 + (j + 1) * 128],
                                    big2[:, kc, tb * 512:(tb + 1) * 512], kc == 0, kc == NCH - 1),
                         r=[wok, 'big2_%d' % tb], w=[bk_])
                hs = hT[:, n, (tb0 + tb) * 512:(tb0 + tb + 1) * 512]
                p.op('dve', I_tt(hs, b[:], hs, ALU.add), r=[bk_, hk(n, tb0 + tb)], w=[hk(n, tb0 + tb)])

    for tb in range(NB):
        _rmsnorm_off(cx, hT, (tb0 + tb) * 512, lambda c, tb=tb: hk(c, tb0 + tb), vec[:, VC['gl']:VC['gl'] + 8],
                     hn, tb * 512, 'hn%d' % tb)
    for s in range(DFF // 256):
        ws, wsk = ss.get()
        hb = s % 2
        for j in range(2):
            for tb in range(NB):
                b, bk_ = cx.bank()
                for kc in range(NCH):
                    p.op('pe', I_mm(b[:], ws[:, kc * 256 + j * 128: kc * 256 + (j + 1) * 128],
                                    hn[:, kc, tb * 512:(tb + 1) * 512], kc == 0, kc == NCH - 1),
                         r=[wsk, 'hn%d' % tb], w=[bk_])
                rt, rtk = cx.rot('rtmp', [128, 512], F32)
                p.op('act', I_act(rt[:], b[:], AF.Relu), r=[bk_], w=[rtk])
                p.op('pool', I_tt(big2[:, hb * 2 + j, tb * 512:(tb + 1) * 512], rt[:], rt[:], ALU.mult),
                     r=[rtk], w=['hid%d' % hb])
        for n in range(NCH):
            for tb in range(NB):
                b, bk_ = cx.bank()
                for j in range(2):
                    p.op('pe', I_mm(b[:], ws[:, 2048 + j * 1024 + n * 128: 2048 + j * 1024 + (n + 1) * 128],
                                    big2[:, hb * 2 + j, tb * 512:(tb + 1) * 512], j == 0, j == 1),
                         r=[wsk, 'hid%d' % hb], w=[bk_])
                hs = hT[:, n, (tb0 + tb) * 512:(tb0 + tb + 1) * 512]
                p.op('dve', I_tt(hs, b[:], hs, ALU.add), r=[bk_, hk(n, tb0 + tb)], w=[hk(n, tb0 + tb)])


def _rmsnorm_off(cx, src, s0, skey, gcols, dst, d0, dkey, n=512, C=NCH, dim=D):
    p = cx.p
    bank, bk = cx.bank()
    for c in range(C):
        sq, sk = cx.next_sq()
        p.op('act', I_act(sq[:, :n], src[:, c, s0:s0 + n], AF.Square), r=[skey(c)], w=[sk])
        p.op('pe', I_mm(bank[:, :n], cx.ones[:], sq[:, :n], c == 0, c == C - 1), r=[sk, 'ones'], w=[bk])
    rs, rk = cx.rstd_from_bank(bank, bk, n, dim)
    for c in range(C):
        p.op('dve', I_stt(dst[:, c, d0:d0 + n], src[:, c, s0:s0 + n], gcols[:, c:c + 1], rs[:, :n], ALU.mult, ALU.mult),
             r=[skey(c), 'vec', rk], w=[dkey])


def common_tiles(cx, A):
    p = cx.p
    cx.hT = p.sb('hT', [128, NCH, NT], F32)
    cx.hn = p.sb('hn', [128, NCH, 1024], BF16)
    cx.big2 = p.sb('big2', [128, NCH, 1024], BF16)
    cx.vec = p.sb('vec', [128, NVEC], F32)
    cx.kraw = p.sb('kraw', [128, NCH, MEMLEN], F32)
    cx.memn = p.sb('memn', [128, NCH, MEMLEN], BF16)
    cx.KT = p.sb('KT', [128, NCH, MEMLEN], BF16)
    cx.V = p.sb('V', [128, 2, D], BF16)
    p.dma('sp', I_dma(cx.vec[:], A['vecs'][:, :]), w=['vec'])


def load_hT(cx, src):
    p = cx.p
    for c in range(NCH):
        p.dma('sp', I_dma(cx.hT[:, c, :], src[c * 128:(c + 1) * 128, :]),
              w=['h%d_%d' % (c, tb) for tb in range(NT // 512)], semkey='hld%d' % c)


def store_hT(cx, dst):
    p = cx.p
    for c in range(NCH):
        p.dma('sp', I_dma(dst[c * 128:(c + 1) * 128, :], cx.hT[:, c, :]),
              r=['h%d_%d' % (c, tb) for tb in range(NT // 512)], w=['hout%d' % c])


def build_tail(glu, emit_hn, arena=False):
    nc = bass.Bass("TRN2", target_bir_lowering=False)
    A = {}

    def inp(name, shape, dt=F32):
        A[name] = nc.dram_tensor(name, list(shape), dt, kind="ExternalInput").ap()

    inp('hT', [D, NT])
    inp('memT', [D, MEMLEN])
    inp('vecs', [128, NVEC])
    inp('wq', [D, D])
    inp('wkv', [D, 2 * D])
    inp('wo', [D, D])
    inp('w1', [D, DFF])
    inp('w2', [DFF, D])
    if glu:
        inp('yT', [D, NT])
        inp('wglu', [D, 2 * D])
    A['hT_out'] = nc.dram_tensor('hT_out', [D, NT], F32, kind="ExternalOutput").ap()
    if emit_hn:
        A['hn_out'] = nc.dram_tensor('hn_out', [D, NT], F32, kind="ExternalOutput").ap()
    cx = Cx(nc, arena=arena)
    if arena:
        cx.new_stage()
    common_tiles(cx, A)
    load_hT(cx, A['hT'])
    for half in range(2):
        tail_body(cx, A, glu, half * 2, 2, kv_ready=(half == 1))
    store_hT(cx, A['hT_out'])
    if emit_hn:
        emit_norm(cx, A['hn_out'])
    cx.p.finalize()
    return nc, cx


def emit_norm(cx, dst):
    p = cx.p
    for tb in range(NT // 512):
        bank, bk = cx.bank()
        for c in range(NCH):
            sq, sk = cx.next_sq()
            p.op('act', I_act(sq[:], cx.hT[:, c, tb * 512:(tb + 1) * 512], AF.Square), r=['h%d_%d' % (c, tb)], w=[sk])
            p.op('pe', I_mm(bank[:], cx.ones[:], sq[:], c == 0, c == NCH - 1), r=[sk, 'ones'], w=[bk])
        rs, rk = cx.rstd_from_bank(bank, bk, 512, D)
        for c in range(NCH):
            ot, otk = cx.rot('ntmp', [128, 512], F32, n=3)
            p.op('dve', I_stt(ot[:], cx.hT[:, c, tb * 512:(tb + 1) * 512], cx.vec[:, VC['gn'] + c:VC['gn'] + c + 1], rs[:],
                              ALU.mult, ALU.mult), r=['h%d_%d' % (c, tb), 'vec', rk], w=[otk])
            drow = dst.src_rows(c * 128) if hasattr(dst, 'src_rows') else dst[c * 128:(c + 1) * 128, :]
            p.dma('sp', I_dma(drow[:, tb * 512:(tb + 1) * 512], ot[:]), r=[otk], w=['hnout'])


NPT = 16
SW = 512
NW = SEQ // SW
PI = math.pi


def s5_params(cx, A):
    p = cx.p
    NCOL = 2 * NPT
    T = {}
    for nm in ['lre', 'lim', 'ldt', 'dt', 'mag', 'ang', 'angc', 's1', 'c1', 'are', 'aim', 'nr', 'den', 't', 't2',
               'fre', 'fim', 'nfre', 'nfim']:
        T[nm] = p.sb('sp_' + nm, [128, NCOL], F32)
    k = 's5par'
    p.dma('sp', I_dma(T['lre'][:], A['lamre'][:, :]), w=[k], semkey='s5par_ld')
    p.dma('sp', I_dma(T['lim'][:], A['lamim'][:, :]), w=[k], semkey='s5par_ld')
    p.dma('sp', I_dma(T['ldt'][:], A['logdt'][:, :]), w=[k], semkey='s5par_ld')
    a = lambda n: T[n][:]
    p.op('act', I_act(a('dt'), a('ldt'), AF.Exp), r=[k], w=[k])
    p.op('dve', I_tt(a('t'), a('lre'), a('dt'), ALU.mult), r=[k], w=[k])
    p.op('act', I_act(a('mag'), a('t'), AF.Exp), r=[k], w=[k])
    p.op('dve', I_tt(a('ang'), a('lim'), a('dt'), ALU.mult), r=[k], w=[k])
    for _ in range(5):
        p.op('dve', I_ts(a('t'), a('ang'), PI, 2 * PI, ALU.is_gt, ALU.mult), r=[k], w=[k])
        p.op('dve', I_tt(a('ang'), a('ang'), a('t'), ALU.subtract), r=[k], w=[k])
    p.op('dve', I_ts(a('angc'), a('ang'), PI / 2, None, ALU.add), r=[k], w=[k])
    p.op('dve', I_ts(a('t'), a('angc'), PI, 2 * PI, ALU.is_gt, ALU.mult), r=[k], w=[k])
    p.op('dve', I_tt(a('angc'), a('angc'), a('t'), ALU.subtract), r=[k], w=[k])
    p.op('act', I_act(a('s1'), a('ang'), AF.Sin), r=[k], w=[k])
    p.op('act', I_act(a('c1'), a('angc'), AF.Sin), r=[k], w=[k])
    p.op('dve', I_tt(a('are'), a('mag'), a('c1'), ALU.mult), r=[k], w=[k])
    p.op('dve', I_tt(a('aim'), a('mag'), a('s1'), ALU.mult), r=[k], w=[k])
    p.op('dve', I_ts(a('nr'), a('are'), -1.0, None, ALU.add), r=[k], w=[k])
    p.op('dve', I_tt(a('den'), a('lre'), a('lre'), ALU.mult), r=[k], w=[k])
    p.op('dve', I_tt(a('t'), a('lim'), a('lim'), ALU.mult), r=[k], w=[k])
    p.op('dve', I_tt(a('den'), a('den'), a('t'), ALU.add), r=[k], w=[k])
    p.op('dve', I_recip(a('den'), a('den')), r=[k], w=[k])
    p.op('dve', I_tt(a('t'), a('nr'), a('lre'), ALU.mult), r=[k], w=[k])
    p.op('dve', I_tt(a('t2'), a('aim'), a('lim'), ALU.mult), r=[k], w=[k])
    p.op('dve', I_tt(a('t'), a('t'), a('t2'), ALU.add), r=[k], w=[k])
    p.op('dve', I_tt(a('fre'), a('t'), a('den'), ALU.mult), r=[k], w=[k])
    p.op('dve', I_tt(a('t'), a('aim'), a('lre'), ALU.mult), r=[k], w=[k])
    p.op('dve', I_tt(a('t2'), a('nr'), a('lim'), ALU.mult), r=[k], w=[k])
    p.op('dve', I_tt(a('t'), a('t'), a('t2'), ALU.subtract), r=[k], w=[k])
    p.op('dve', I_tt(a('fim'), a('t'), a('den'), ALU.mult), r=[k], w=[k])
    p.op('dve', I_ts(a('nfre'), a('fre'), -1.0, None, ALU.mult), r=[k], w=[k])
    p.op('dve', I_ts(a('nfim'), a('fim'), -1.0, None, ALU.mult), r=[k], w=[k])
    nlv = int(math.log2(SW))
    T['pwc'] = p.sb('sp_pwc', [128, nlv + 1, NCOL], F32)
    T['pws'] = p.sb('sp_pws', [128, nlv + 1, NCOL], F32)
    T['npws'] = p.sb('sp_npws', [128, NCOL], F32)
    p.op('dve', I_copy(T['pwc'][:, 0, :], a('c1')), r=[k], w=[k])
    p.op('dve', I_copy(T['pws'][:, 0, :], a('s1')), r=[k], w=[k])
    for lv in range(nlv):
        c_ = T['pwc'][:, lv, :]
        s_ = T['pws'][:, lv, :]
        p.op('dve', I_tt(a('t'), s_, s_, ALU.mult), r=[k], w=[k])
        p.op('dve', I_tt(a('t2'), c_, c_, ALU.mult), r=[k], w=[k])
        p.op('dve', I_tt(T['pwc'][:, lv + 1, :], a('t2'), a('t'), ALU.subtract), r=[k], w=[k])
        p.op('dve', I_stt(T['pws'][:, lv + 1, :], c_, 2.0, s_, ALU.mult, ALU.mult), r=[k], w=[k])
    p.op('dve', I_ts(T['npws'][:], T['pws'][:, nlv, :], -1.0, None, ALU.mult), r=[k], w=[k])
    return T


def s5_body(cx, A):
    p = cx.p
    T = s5_params(cx, A)
    PK = 's5par'
    if 'dbg' in A:
        for i, nm in enumerate(['dt', 'mag', 'ang', 's1', 'c1', 'fre', 'fim', 'den']):
            p.dma('sp', I_dma(A['dbg'][:, i * 2 * NPT:(i + 1) * 2 * NPT], T[nm][:]), r=[PK], w=['dbgo'])
    ub = p.sb('ub', [128, 4, SEQ], BF16)
    if 'GHN' in A:
        G = A['GHN']
        m0c = cx.vec[:, VC['m0']:VC['m0'] + 1]
        m1c = cx.vec[:, VC['m1']:VC['m1'] + 1]
        for ck in range(4):
            for w in range(NW):
                r = 0 if w < NW // 2 else 1
                if r == 0:
                    cols = slice(w * SW, (w + 1) * SW)
                else:
                    w2 = w - NW // 2
                    cols = slice(NT - (w2 + 1) * SW, NT - w2 * SW)
                X, xk = cx.rot('gx', [128, SW], F32, n=2)
                Z, zk = cx.rot('gz', [128, SW], F32, n=2)
                p.dma('sp', I_dma(X[:], G.g_rows(r, ck * 128)[:, cols]), w=[xk])
                p.dma('sp', I_dma(Z[:], G.g_rows(r, 512 + ck * 128)[:, cols]), w=[zk])
                dst = ub[:, ck, w * SW:(w + 1) * SW]
                if r == 1:
                    dst = dst[:, ::-1]
                mcombine(cx, dst, 'ub%d' % ck, X[:], xk, Z[:], zk, m0c if r == 0 else m1c, m1c if r == 0 else m0c)
    else:
        for ck in range(4):
            p.dma('pool', I_dma(ub[:, ck, :], A['uT'][ck * 128:(ck + 1) * 128, :]), w=['ub%d' % ck])
    ident = p.sb('ident', [128, 128], F32)
    p.dma('sp', I_dma(ident[:], A['ident'][:, :]), w=['ident'])
    dsk = p.sb('dskc', [128, 4], F32)
    p.dma('sp', I_dma(dsk[:], A['dsk'][:, :]), w=['dskc'])
    yacc = [p.sb('yacc%d' % i, [128, SEQ], F32) for i in range(2)]
    bb_i = [0]

    def bbank():
        i = bb_i[0]
        bb_i[0] = (i + 1) % 6
        return cx.banks[i], 'bank%d' % i
    yb_i = [0]

    def ybank():
        i = 6 + yb_i[0]
        yb_i[0] = 1 - yb_i[0]
        return cx.banks[i], 'bank%d' % i

    for ck in range(4):
        ya = yacc[ck % 2]
        yk = 'yacc%d' % (ck % 2)
        dD, dDk = cx.rot('diagD', [128, 128], BF16)
        p.op('dve', I_ts(dD[:], ident[:], dsk[:, ck:ck + 1], None, ALU.mult), r=['ident', 'dskc'], w=[dDk])
        for d in range(2):
            tabs = []
            col0 = d * NPT + ck * 4
            cos4, c4k = cx.rot('cos4', [128, 4, SW], F32, n=2)
            sin4, s4k = cx.rot('sin4', [128, 4, SW], F32, n=2)
            tk = c4k
            p.op('dve', I_memset(cos4[:, :, 0:1], 1.0), w=[tk])
            p.op('dve', I_memset(sin4[:, :, 0:1], 0.0), w=[tk])
            L = 1
            lv = 0
            while L < SW:
                pcb = T['pwc'][:, lv, col0:col0 + 4].unsqueeze(2).to_broadcast([128, 4, L])
                psb = T['pws'][:, lv, col0:col0 + 4].unsqueeze(2).to_broadcast([128, 4, L])
                ta, tak = cx.rot('tbA', [128, 4, SW // 2], F32, n=1)
                tb2, tbk = cx.rot('tbB', [128, 4, SW // 2], F32, n=1)
                p.op('dve', I_tt(ta[:, :, :L], sin4[:, :, 0:L], psb, ALU.mult), r=[tk, PK], w=[tak])
                p.op('dve', I_tt(tb2[:, :, :L], cos4[:, :, 0:L], pcb, ALU.mult), r=[tk, PK], w=[tbk])
                p.op('dve', I_tt(cos4[:, :, L:2 * L], tb2[:, :, :L], ta[:, :, :L], ALU.subtract), r=[tak, tbk], w=[tk])
                p.op('dve', I_tt(ta[:, :, :L], cos4[:, :, 0:L], psb, ALU.mult), r=[tk, PK], w=[tak])
                p.op('dve', I_tt(tb2[:, :, :L], sin4[:, :, 0:L], pcb, ALU.mult), r=[tk, PK], w=[tbk])
                p.op('dve', I_tt(sin4[:, :, L:2 * L], tb2[:, :, :L], ta[:, :, :L], ALU.add), r=[tak, tbk], w=[tk])
                L *= 2
                lv += 1
            for q in range(4):
                pt = ck * 4 + q
                col = d * NPT + pt
                cosT = cos4[:, q, :]
                sinT = sin4[:, q, :]
                cW = T['pwc'][:, lv, col:col + 1]
                sW = T['pws'][:, lv, col:col + 1]
                nsW = T['npws'][:, col:col + 1]
                braw, brk = cx.rot('bw', [128, 2, 128], BF16, n=8)
                p.dma('pool', I_dma(braw[:, 0, :], A['Bre'][d, pt]), w=[brk])
                p.dma('pool', I_dma(braw[:, 1, :], A['Bim'][d, pt]), w=[brk])
                craw, crk = cx.rot('craw', [128, 2, 128], F32, n=2)
                p.dma('sp', I_dma(craw[:, 0, :], A['CR'][d, pt]), w=[crk])
                p.dma('sp', I_dma(craw[:, 1, :], A['CI'][d, pt]), w=[crk])
                cw, cwk = cx.rot('cw', [128, 3, 128], BF16, n=8)
                ctmp, ctk = cx.rot('ctmp', [128, 128], F32, n=2)
                fre = T['fre'][:, col:col + 1]
                nfim = T['nfim'][:, col:col + 1]
                nfre = T['nfre'][:, col:col + 1]
                p.op('dve', I_ts(ctmp[:], craw[:, 1, :], nfim, None, ALU.mult), r=[crk, PK], w=[ctk])
                p.op('dve', I_stt(cw[:, 0, :], craw[:, 0, :], fre, ctmp[:], ALU.mult, ALU.add), r=[crk, PK, ctk], w=[cwk])
                ctmp2, ctk2 = cx.rot('ctmp', [128, 128], F32, n=2)
                p.op('dve', I_ts(ctmp2[:], craw[:, 0, :], nfim, None, ALU.mult), r=[crk, PK], w=[ctk2])
                p.op('dve', I_stt(cw[:, 1, :], craw[:, 1, :], nfre, ctmp2[:], ALU.mult, ALU.add), r=[crk, PK, ctk2], w=[cwk])
                ctmp3, ctk3 = cx.rot('ctmp', [128, 128], F32, n=2)
                p.op('dve', I_ts(ctmp3[:], craw[:, 1, :], T['fim'][:, col:col + 1], None, ALU.mult), r=[crk, PK], w=[ctk3])
                p.op('dve', I_stt(cw[:, 2, :], craw[:, 0, :], nfre, ctmp3[:], ALU.mult, ALU.add), r=[crk, PK, ctk3], w=[cwk])
                car, cak = cx.rot('carry', [128, 8], F32, n=8)
                tabs.append(dict(cos=cosT, sin=sinT, tk=tk, cW=cW, sW=sW, nsW=nsW, braw=braw, brk=brk, cw=cw, cwk=cwk,
                                 r=T['mag'][:, col:col + 1], car=car, cak=cak))
            worder = range(NW) if d == 0 else range(NW - 1, -1, -1)
            rv = (lambda ap: ap) if d == 0 else (lambda ap: ap[:, ::-1])
            units = [dict(wi=wi, w=w, q=q) for wi, w in enumerate(worder) for q in range(4)]
            ybs = {}

            def P01(u):
                tb_ = tabs[u['q']]
                win = slice(u['w'] * SW, (u['w'] + 1) * SW)
                bre, brek = bbank()
                bim, bimk = bbank()
                p.op('pe', I_mm(bre[:], tb_['braw'][:, 0, :], ub[:, ck, win], True, True), r=[tb_['brk'], 'ub%d' % ck], w=[brek])
                p.op('pe', I_mm(bim[:], tb_['braw'][:, 1, :], ub[:, ck, win], True, True), r=[tb_['brk'], 'ub%d' % ck], w=[bimk])
                cosT, sinT, tk = tb_['cos'], tb_['sin'], tb_['tk']
                t1, t1k = cx.rot('t1', [128, SW], F32)
                t2, t2k = cx.rot('t2', [128, SW], F32)
                t3, t3k = cx.rot('t3', [128, SW], F32)
                t4, t4k = cx.rot('t4', [128, SW], F32)
                p.op('dve', I_tt(t1[:], rv(bre[:]), cosT, ALU.mult), r=[brek, tk], w=[t1k])
                p.op('dve', I_tt(t2[:], rv(bim[:]), sinT, ALU.mult), r=[bimk, tk], w=[t2k])
                p.op('dve', I_tt(t3[:], rv(bim[:]), cosT, ALU.mult), r=[bimk, tk], w=[t3k])
                p.op('dve', I_tt(t4[:], rv(bre[:]), sinT, ALU.mult), r=[brek, tk], w=[t4k])
                u.update(t=(t1, t1k, t2, t2k, t3, t3k, t4, t4k))

            def P2(u):
                t1, t1k, t2, t2k, t3, t3k, t4, t4k = u['t']
                wre, wrk = cx.rot('wre', [128, SW], F32)
                wim, wik = cx.rot('wim', [128, SW], F32)
                p.op('pool', I_tt(wre[:], t1[:], t2[:], ALU.add), r=[t1k, t2k], w=[wrk])
                p.op('pool', I_tt(wim[:], t3[:], t4[:], ALU.subtract), r=[t3k, t4k], w=[wik])
                u.update(wv=(wre, wrk, wim, wik))

            def P3(u):
                tb_ = tabs[u['q']]
                tk = tb_['tk']
                wre, wrk, wim, wik = u['wv']
                zre, zrk = cx.rot('zre', [128, SW], F32)
                zim, zik = cx.rot('zim', [128, SW], F32)
                car, cak = tb_['car'], tb_['cak']
                rbc = tb_['r'].to_broadcast([128, SW])
                if u['wi'] == 0:
                    ire, iim = 0.0, 0.0
                else:
                    ire, iim = car[:, 2:3], car[:, 3:4]
                p.op('dve', I_scan(zre[:], rbc, wre[:], ire), r=[PK, wrk, cak], w=[zrk])
                p.op('dve', I_scan(zim[:], rbc, wim[:], iim), r=[PK, wik, cak], w=[zik])
                if u['wi'] < NW - 1:
                    p.op('act', I_act(car[:, 0:1], zim[:, SW - 1:SW], AF.Copy, scale=tb_['nsW']), r=[zik, PK], w=[cak])
                    p.op('act', I_act(car[:, 1:2], zre[:, SW - 1:SW], AF.Copy, scale=tb_['sW']), r=[zrk, PK], w=[cak])
                    p.op('act', I_act(car[:, 2:3], zre[:, SW - 1:SW], AF.Identity, scale=tb_['cW'], bias=car[:, 0:1]), r=[zrk, PK], w=[cak])
                    p.op('act', I_act(car[:, 3:4], zim[:, SW - 1:SW], AF.Identity, scale=tb_['cW'], bias=car[:, 1:2]), r=[zik, PK], w=[cak])
                u.update(z=(zre, zrk, zim, zik))

            def P4(u):
                tb_ = tabs[u['q']]
                cosT, sinT, tk = tb_['cos'], tb_['sin'], tb_['tk']
                zre, zrk, zim, zik = u['z']
                u1, u1k = cx.rot('u1', [128, SW], BF16, n=3)
                u2, u2k = cx.rot('u2', [128, SW], BF16, n=3)
                u3, u3k = cx.rot('u3', [128, SW], BF16, n=3)
                u4, u4k = cx.rot('u4', [128, SW], BF16, n=3)
                p.op('pool', I_tt(rv(u1[:]), zre[:], cosT, ALU.mult), r=[zrk, tk], w=[u1k])
                p.op('pool', I_tt(rv(u2[:]), zim[:], sinT, ALU.mult), r=[zik, tk], w=[u2k])
                p.op('pool', I_tt(rv(u3[:]), zim[:], cosT, ALU.mult), r=[zik, tk], w=[u3k])
                p.op('dve', I_tt(rv(u4[:]), zre[:], sinT, ALU.mult), r=[zrk, tk], w=[u4k])
                u.update(uu=(u1, u1k, u2, u2k, u3, u3k, u4, u4k))

            def P5(u):
                pass

            def P6(u):
                tb_ = tabs[u['q']]
                w, q = u['w'], u['q']
                win = slice(w * SW, (w + 1) * SW)
                if q == 0:
                    ybs[w] = ybank()
                yb, ybk = ybs[w]
                u1, u1k, u2, u2k, u3, u3k, u4, u4k = u['uu']
                first = (q == 0)
                last = (q == 3) and d == 1
                p.op('pe', I_mm(yb[:], tb_['cw'][:, 0, :], u1[:], first, False), r=[tb_['cwk'], u1k], w=[ybk])
                p.op('pe', I_mm(yb[:], tb_['cw'][:, 2, :], u2[:], False, False), r=[tb_['cwk'], u2k], w=[ybk])
                p.op('pe', I_mm(yb[:], tb_['cw'][:, 1, :], u3[:], False, False), r=[tb_['cwk'], u3k], w=[ybk])
                p.op('pe', I_mm(yb[:], tb_['cw'][:, 1, :], u4[:], False, last), r=[tb_['cwk'], u4k], w=[ybk])
                if q == 3:
                    if d == 0:
                        p.op('pe', I_mm(yb[:], dD[:], ub[:, ck, win], False, True), r=[dDk, 'ub%d' % ck], w=[ybk])
                        p.op('act', I_act(ya[:, win], yb[:], AF.Copy), r=[ybk], w=[yk + '_%d' % w])
                    else:
                        p.op('dve', I_tt(ya[:, win], yb[:], ya[:, win], ALU.add), r=[ybk, yk + '_%d' % w], w=[yk + '_%d' % w])
            nu = len(units)
            for step in range(nu + 2):
                if step < nu:
                    P01(units[step])
                    P2(units[step])
                if 0 <= step - 1 < nu:
                    P3(units[step - 1])
                    P4(units[step - 1])
                if 0 <= step - 2 < nu:
                    P5(units[step - 2])
                    P6(units[step - 2])
        if 'yO' in A:
            m0c = cx.vec[:, VC['m0']:VC['m0'] + 1]
            m1c = cx.vec[:, VC['m1']:VC['m1'] + 1]
            for hb in range(NT // SW):
                A1 = ya[:, hb * SW:(hb + 1) * SW]
                B1 = ya[:, SEQ - (hb + 1) * SW:SEQ - hb * SW][:, ::-1]
                ka = yk + '_%d' % hb
                kb = yk + '_%d' % (NW - 1 - hb)
                ot, otk = cx.rot('yo_t', [128, SW], F32, n=2)
                mcombine(cx, ot[:], otk, A1, ka, B1, kb, m0c, m1c)
                p.dma('sp', I_dma(A['yO'][ck * 128:(ck + 1) * 128, hb * SW:(hb + 1) * SW], ot[:]), r=[otk], w=['yout%d' % ck])
                st_, stk_ = cx.rot('yo_t', [128, SW], F32, n=2)
                mcombine(cx, st_[:], stk_, A1, ka, B1, kb, m1c, m0c)
                p.dma('sp', I_dma(A['yS'].src_rows(ck * 128)[:, hb * SW:(hb + 1) * SW], st_[:]), r=[stk_], w=['yout%d' % ck])
        else:
            p.dma('sp', I_dma(A['yT'][ck * 128:(ck + 1) * 128, :], ya[:]), r=[yk + '_%d' % w for w in range(NW)], w=['yout%d' % ck])


def build_s5(debug=False, arena=False):
    nc = bass.Bass("TRN2", target_bir_lowering=False)
    A = {}

    def inp(name, shape, dt=F32):
        A[name] = nc.dram_tensor(name, list(shape), dt, kind="ExternalInput").ap()
    inp('uT', [512, SEQ])
    inp('Bre', [2, NPT, 128, 128])
    inp('Bim', [2, NPT, 128, 128])
    inp('CR', [2, NPT, 128, 128])
    inp('CI', [2, NPT, 128, 128])
    inp('lamre', [128, 2 * NPT])
    inp('lamim', [128, 2 * NPT])
    inp('logdt', [128, 2 * NPT])
    inp('dsk', [128, 4])
    inp('ident', [128, 128])
    A['yT'] = nc.dram_tensor('yT', [512, SEQ], F32, kind="ExternalOutput").ap()
    cx = Cx(nc, arena=arena)
    if arena:
        cx.new_stage()
    if debug:
        A['dbg'] = nc.dram_tensor('dbg', [128, 8 * 2 * NPT], F32, kind="ExternalOutput").ap()
        A['dbg2'] = nc.dram_tensor('dbg2', [128, 2 * SW], F32, kind="ExternalOutput").ap()
    s5_body(cx, A)
    cx.p.finalize()
    return nc, cx


def s5_host_inputs(inp, j, half):
    g0 = 32 * half
    Bre = np.zeros((2, NPT, 128, 128), np.float32)
    Bim = np.zeros_like(Bre)
    CR = np.zeros_like(Bre)
    CI = np.zeros_like(Bre)
    lamre = np.zeros((128, 2 * NPT), np.float32)
    lamim = np.zeros_like(lamre)
    logdt = np.zeros_like(lamre)
    for d in range(2):
        for pt in range(NPT):
            for gl in range(2):
                g = g0 + 2 * pt + gl
                r0 = (pt % 4) * 32 + gl * 16
                Bre[d, pt, r0:r0 + 16, gl * 64:(gl + 1) * 64] = inp['s5_b_re'][j, d, g].T
                Bim[d, pt, r0:r0 + 16, gl * 64:(gl + 1) * 64] = inp['s5_b_im'][j, d, g].T
                CR[d, pt, gl * 64:(gl + 1) * 64, r0:r0 + 16] = inp['s5_c_re'][j, d, g].T
                CI[d, pt, gl * 64:(gl + 1) * 64, r0:r0 + 16] = inp['s5_c_im'][j, d, g].T
                lamre[gl * 64:(gl + 1) * 64, d * NPT + pt] = inp['s5_lambda_re'][j, d, g]
                lamim[gl * 64:(gl + 1) * 64, d * NPT + pt] = inp['s5_lambda_im'][j, d, g]
                logdt[gl * 64:(gl + 1) * 64, d * NPT + pt] = inp['s5_log_dt'][j, d, g]
    dsk = np.ascontiguousarray(inp['s5_d'][j, 512 * half:512 * half + 512].reshape(4, 128).T)
    return dict(Bre=Bre, Bim=Bim, CR=CR, CI=CI, lamre=lamre, lamim=lamim, logdt=logdt, dsk=dsk,
                ident=np.eye(128, dtype=np.float32))


NEXT = 3072
GRP = [(1, 2048), (4, 512), (16, 128)]
ASCALE = 128 ** -0.5


def sub_view(ap2d, d):
    if d == 1:
        return ap2d.rearrange("p (d i) -> p d i", d=1)
    return ap2d.rearrange("p (i d) -> p d i", d=d)


def attn_body(cx, A, flip=False, src=None, dst=None, gh=None):
    p = cx.p
    vec = cx.vec

    def load_blk(xt, xk, c, tb):
        if gh is None or tb < NT // 512:
            p.dma('sp', I_dma(xt[:], _src[c * 128:(c + 1) * 128, _cols(tb)]), w=[xk])
            return
        hb = tb - NT // 512
        cols = slice(1024 - 512 * (hb + 1), 1024 - 512 * hb)
        pc = (c + 4) % 8
        X, xk2 = cx.rot('gx', [128, 512], F32, n=2)
        Z, zk2 = cx.rot('gz', [128, 512], F32, n=2)
        p.dma('sp', I_dma(X[:], gh.g_rows(0, pc * 128)[:, cols]), w=[xk2])
        p.dma('sp', I_dma(Z[:], gh.g_rows(1, pc * 128)[:, cols]), w=[zk2])
        mcombine(cx, xt[:], xk, X[:], xk2, Z[:], zk2, vec[:, VC['m1']:VC['m1'] + 1], vec[:, VC['m0']:VC['m0'] + 1])

    def _cols(tb):
        if src is None or not flip:
            return slice(tb * 512, (tb + 1) * 512)
        return slice(SEQ - (tb + 1) * 512, SEQ - tb * 512)
    _src = A['hT_ext'] if src is None else src
    _dst = A['hT_out'] if dst is None else dst
    rvf = (lambda ap: ap[:, ::-1]) if (flip and src is not None) else (lambda ap: ap)
    hn = p.sb('hnx', [128, NCH, NEXT], BF16)
    mT = p.sb('mT', [128, NCH, NT], BF16)
    num = p.sb('numacc', [128, NT], F32)
    den = p.sb('denacc', [128, NT], F32)
    for tb in range(NEXT // 512):
        bank, bk = cx.bank()
        for c in range(NCH):
            xt, xk = cx.rot('xin', [128, 512], F32, n=2)
            load_blk(xt, xk, c, tb)
            sq, sk = cx.next_sq()
            p.op('act', I_act(sq[:], xt[:], AF.Square), r=[xk], w=[sk])
            p.op('pe', I_mm(bank[:], cx.ones[:], sq[:], c == 0, c == NCH - 1), r=[sk, 'ones'], w=[bk])
        rs, rk = cx.rstd_from_bank(bank, bk, 512, D)
        for c in range(NCH):
            xt, xk = cx.rot('xin', [128, 512], F32, n=2)
            load_blk(xt, xk, c, tb)
            rv_ = (lambda ap: ap[:, ::-1]) if (gh is not None and tb >= NT // 512) else rvf
            p.op('dve', I_stt(rv_(hn[:, c, tb * 512:(tb + 1) * 512]), xt[:], vec[:, VC['gn'] + c:VC['gn'] + c + 1], rs[:],
                              ALU.mult, ALU.mult), r=[xk, 'vec', rk], w=['hnx'])
    sb_i = [0]

    def sbank():
        i = sb_i[0]
        sb_i[0] = (i + 1) % 4
        return cx.banks[i], 'bank%d' % i
    ob_i = [0]

    def obanks():
        i = ob_i[0]
        ob_i[0] = 1 - i
        return cx.banks[4 + i], 'bank%d' % (4 + i), cx.banks[6 + i], 'bank%d' % (6 + i)

    def qknorm(bank, bk, n, gcol, dst, dkey):
        sq, sk = cx.next_sq()
        p.op('act', I_act(sq[:, :n], bank[:, :n], AF.Square), r=[bk], w=[sk])
        b2, b2k = sbank()
        p.op('pe', I_mm(b2[:, :n], cx.ones[:], sq[:, :n], True, True), r=[sk, 'ones'], w=[b2k])
        rs, rk = cx.rstd_from_bank(b2, b2k, n, 128)
        p.op('dve', I_stt(dst, bank[:, :n], vec[:, gcol:gcol + 1], rs[:, :n], ALU.mult, ALU.mult), r=[bk, 'vec', rk], w=[dkey])

    wq_loaded = {}
    bias_loaded = {}

    def load_w(h_, g_):
        if (h_, g_) in wq_loaded or h_ >= 8:
            return
        wsl_, wsk_ = cx.rot('wqkv', [128, NCH, 384], BF16, n=3)
        for kind in range(3):
            c0 = kind * 3072 + g_ * 1024 + h_ * 128
            p.dma('pool', I_dma(wsl_[:, :, kind * 128:(kind + 1) * 128],
                                A['wqkv'][:, c0:c0 + 128].rearrange("(k p) n -> p k n", p=128)), w=[wsk_])
        wq_loaded[(h_, g_)] = (wsl_, wsk_)

    def load_bias(h_):
        if h_ in bias_loaded or h_ >= 8:
            return
        bt_, btk_ = cx.rot('biasT', [128, 3, 256], F32, n=2)
        for g_ in range(3):
            p.dma('sp', I_dma(bt_[:, g_, :], A['biasT'][g_ * 8 + h_]), w=[btk_])
        bias_loaded[h_] = (bt_, btk_)

    for h in range(8):
        p.op('pool', I_memset(num[:], 0.0), w=['numacc'])
        p.op('pool', I_memset(den[:], 0.0), w=['denacc'])
        load_bias(h)
        bt, btk = bias_loaded[h]
        for g, (d, Lq) in enumerate(GRP):
            nto = Lq // 128
            load_w(h, g)
            wsl, wsk = wq_loaded[(h, g)]
            load_w(h + (g + 1) // 3, (g + 1) % 3)
            if g == 0:
                load_bias(h + 1)
            qT, qk_ = cx.rot('qT', [128, NT], BF16, n=2)
            kT, kk_ = cx.rot('kT', [128, NEXT], BF16, n=2)
            vt, vk_ = cx.rot('vt', [128, 32, 128], BF16, n=2)

            for kind, dstT, dk, gcol in ((0, qT, qk_, VC['aq']), (1, kT, kk_, VC['ak'])):
                for bi in range(4):
                    b, bk = sbank()
                    for kc in range(NCH):
                        if d == 1:
                            rhs, o_ap = hn[:, kc, bi * 512:(bi + 1) * 512], b[:]
                        elif d == 4:
                            rhs, o_ap = sub_view(hn[:, kc, 0:NT], 4)[:, bi, :], b[:]
                        else:
                            rhs = sub_view(hn[:, kc, 0:NT], 16)[:, 4 * bi:4 * bi + 4, :]
                            o_ap = b[:].rearrange("p (a b) -> p a b", a=4)
                        p.op('pe', I_mm(o_ap, wsl[:, kc, kind * 128:(kind + 1) * 128], rhs, kc == 0, kc == NCH - 1),
                             r=[wsk, 'hnx'], w=[bk])
                    qknorm(b, bk, 512, gcol, dstT[:, bi * 512:(bi + 1) * 512], dk)
            nh = 64 * d
            for b0 in range(0, nh, 512):
                n = min(512, nh - b0)
                b, bk = sbank()
                for kc in range(NCH):
                    if d == 1:
                        rhs = hn[:, kc, NT:NT + 64]
                        o_ap = b[:, :64]
                    else:
                        r0 = b0 // 64
                        nr = n // 64
                        rhs = sub_view(hn[:, kc, NT:NT + 64 * d], d)[:, r0:r0 + nr, :]
                        o_ap = b[:, :n].rearrange("p (a b) -> p a b", a=nr)
                    p.op('pe', I_mm(o_ap, wsl[:, kc, 128:256], rhs, kc == 0, kc == NCH - 1), r=[wsk, 'hnx'], w=[bk])
                qknorm(b, bk, n, VC['ak'], kT[:, NT + b0:NT + b0 + n], kk_)
            for t0 in range(0, 16, 4):
                b, bk = sbank()
                for tt in range(4):
                    t = t0 + tt
                    r, m = t // nto, t % nto
                    for kc in range(NCH):
                        lhsT = sub_view(hn[:, kc, 0:NT], d)[:, r, m * 128:(m + 1) * 128]
                        p.op('pe', I_mm(b[:, tt * 128:(tt + 1) * 128], lhsT, wsl[:, kc, 256:384], kc == 0, kc == NCH - 1),
                             r=[wsk, 'hnx'], w=[bk])
                p.op('act', I_act(vt[:, t0:t0 + 4, :], b[:].rearrange("p (a b) -> p a b", a=4), AF.Copy), r=[bk], w=[vk_])
            for r0 in range(0, d, 4):
                nr = min(4, d - r0)
                b, bk = sbank()
                for rr in range(nr):
                    r = r0 + rr
                    for kc in range(NCH):
                        lhsT = sub_view(hn[:, kc, NT:NT + 64 * d], d)[:, r, :]
                        p.op('pe', I_mm(b[:64, rr * 128:(rr + 1) * 128], lhsT, wsl[:, kc, 256:384], kc == 0, kc == NCH - 1),
                             r=[wsk, 'hnx'], w=[bk])
                p.op('act', I_act(vt[:64, 16 + r0:16 + r0 + nr, :], b[:64, :nr * 128].rearrange("p (a b) -> p a b", a=nr), AF.Copy),
                     r=[bk], w=[vk_])
            for r in range(d):
                qoff = r * Lq
                ob = None
                for m in range(nto + 1):
                    halo = (m == nto)
                    nk = 64 if halo else 128
                    b0_ = 64 if m == 0 else 0
                    b1_ = 64 if halo else min(256, Lq - (128 * m - 64))
                    ktile = kT[:, NT + r * 64:NT + r * 64 + 64] if halo else kT[:, qoff + m * 128:qoff + (m + 1) * 128]
                    vtile = vt[:64, 16 + r, :] if halo else vt[:, r * nto + m, :]
                    qs = qoff + 128 * m - 64 + b0_
                    sbk, sbkk = sbank()
                    p.op('pe', I_mm(sbk[:nk, b0_:b1_], ktile, qT[:, qs:qs + (b1_ - b0_)], True, True), r=[kk_, qk_], w=[sbkk])
                    st, stk = cx.rot('stmp', [128, 256], F32, n=3)
                    p.op('dve', I_stt(st[:nk, b0_:b1_], sbk[:nk, b0_:b1_], ASCALE, bt[:nk, g, b0_:b1_], ALU.mult, ALU.add),
                         r=[sbkk, btk], w=[stk])
                    PT, ptk = cx.rot('PTa', [128, 256], BF16, n=4)
                    p.op('act', I_act(PT[:nk, b0_:b1_], st[:nk, b0_:b1_], AF.Exp), r=[stk], w=[ptk])
                    def flush(ep):
                        qlo = max(0, 512 * ep - 64)
                        qhi = min(Lq, 512 * ep + 448)
                        c0f = qlo - (512 * ep - 64)
                        wdt = qhi - qlo
                        nv = sub_view(num[:, :], d)[:, r, qlo:qhi]
                        dv = sub_view(den[:, :], d)[:, r, qlo:qhi]
                        p.op('dve', I_tt(nv, ob[0][:, c0f:c0f + wdt], nv, ALU.add), r=[ob[1], 'numacc'], w=['numacc'])
                        p.op('dve', I_tt(dv, ob[2][:, c0f:c0f + wdt], dv, ALU.add), r=[ob[3], 'denacc'], w=['denacc'])
                    if m == 0:
                        ob = obanks()
                        p.op('pe', I_mm(ob[0][:, 64:128], vtile, PT[:nk, 64:128], True, True), r=[vk_, ptk], w=[ob[1]])
                        p.op('pe', I_mm(ob[2][:, 64:128], cx.ones[:nk, :], PT[:nk, 64:128], True, True), r=['ones', ptk], w=[ob[3]])
                    else:
                        q0 = 64 + 128 * (m - 1)
                        wq_ = min(Lq, q0 + 128) - q0
                        c0 = 128 * (m % 4)
                        p.op('pe', I_mm(ob[0][:, c0:c0 + wq_], vtile, PT[:nk, 0:wq_], False, True), r=[vk_, ptk], w=[ob[1]])
                        p.op('pe', I_mm(ob[2][:, c0:c0 + wq_], cx.ones[:nk, :], PT[:nk, 0:wq_], False, True), r=['ones', ptk], w=[ob[3]])
                        if m % 4 == 3 or halo:
                            flush(m // 4)
                    if not halo:
                        q0 = 64 + 128 * m
                        wq_ = min(Lq, q0 + 128) - q0
                        if (m + 1) % 4 == 0:
                            ob = obanks()
                        c0 = 128 * ((m + 1) % 4)
                        p.op('pe', I_mm(ob[0][:, c0:c0 + wq_], vtile, PT[:nk, 128:128 + wq_], True, False), r=[vk_, ptk], w=[ob[1]])
                        p.op('pe', I_mm(ob[2][:, c0:c0 + wq_], cx.ones[:nk, :], PT[:nk, 128:128 + wq_], True, False), r=['ones', ptk], w=[ob[3]])
        p.op('act', I_act(den[:], den[:], AF.Ln), r=['denacc'], w=['denacc'])
        p.op('act', I_act(den[:], den[:], AF.Exp, scale=-1.0), r=['denacc'], w=['denacc'])
        p.op('pool', I_tt(mT[:, h, :], num[:], den[:], ALU.mult), r=['numacc', 'denacc'], w=['mT'])
    for ns in range(2):
        wo, wok = wslab(cx, [(A['wo_a'][:, ns * 512:(ns + 1) * 512], 0)])
        for j in range(4):
            n = ns * 4 + j
            for tb in range(NT // 512):
                b, bk = sbank()
                for kc in range(NCH):
                    p.op('pe', I_mm(b[:], wo[:, kc * 512 + j * 128: kc * 512 + (j + 1) * 128], mT[:, kc, tb * 512:(tb + 1) * 512],
                                    kc == 0, kc == NCH - 1), r=[wok, 'mT'], w=[bk])
                xt, xk = cx.rot('xin', [128, 512], F32, n=2)
                p.dma('sp', I_dma(xt[:], _src[n * 128:(n + 1) * 128, _cols(tb)]), w=[xk])
                p.op('dve', I_tt(xt[:], rvf(b[:]), xt[:], ALU.add), r=[bk, xk], w=[xk])
                p.dma('sp', I_dma(_dst[n * 128:(n + 1) * 128, _cols(tb)], xt[:]), r=[xk], w=['hTout'])


def build_attn():
    nc = bass.Bass("TRN2", target_bir_lowering=False)
    A = {}

    def inp(name, shape, dt=F32):
        A[name] = nc.dram_tensor(name, list(shape), dt, kind="ExternalInput").ap()
    inp('hT_ext', [D, NEXT])
    inp('vecs', [128, NVEC])
    inp('wqkv', [D, 9216])
    inp('wo_a', [D, D])
    inp('biasT', [24, 128, 256])
    A['hT_out'] = nc.dram_tensor('hT_out', [D, NT], F32, kind="ExternalOutput").ap()
    cx = Cx(nc)
    cx.vec = cx.p.sb('vec', [128, NVEC], F32)
    cx.p.dma('sp', I_dma(cx.vec[:], A['vecs'][:, :]), w=['vec'])
    attn_body(cx, A)
    cx.p.finalize()
    return nc, cx


def t5_bucket(rel):
    nb = 16
    ret = (rel > 0).astype(np.int32) * nb
    n = np.abs(rel)
    max_exact = nb // 2
    large = max_exact + (np.log(np.maximum(n, 1).astype(np.float32) / max_exact)
                         / np.log(1024 / max_exact) * (nb - max_exact)).astype(np.int32)
    large = np.minimum(large, nb - 1)
    return (ret + np.where(n < max_exact, n, large)).astype(np.int32)


def host_bias(bias_table, flip):
    a = np.arange(128)[:, None]
    b = np.arange(256)[None, :]
    rel = a - b + 64
    out = np.full((24, 128, 256), -1e30, np.float32)
    band = np.abs(rel) <= 64
    for g, (dil, _) in enumerate(GRP):
        bk = t5_bucket((-rel if flip else rel) * dil)
        for h in range(8):
            out[g * 8 + h] = np.where(band, bias_table[bk, g * 8 + h], np.float32(-1e30))
    return out


def build_norm():
    nc = bass.Bass("TRN2", target_bir_lowering=False)
    A = {}
    A['hT'] = nc.dram_tensor('hT', [D, NT], F32, kind="ExternalInput").ap()
    A['vecs'] = nc.dram_tensor('vecs', [128, NVEC], F32, kind="ExternalInput").ap()
    A['hn_out'] = nc.dram_tensor('hn_out', [D, NT], F32, kind="ExternalOutput").ap()
    cx = Cx(nc)
    p = cx.p
    cx.hT = p.sb('hT', [128, NCH, NT], F32)
    cx.vec = p.sb('vec', [128, NVEC], F32)
    p.dma('sp', I_dma(cx.vec[:], A['vecs'][:, :]), w=['vec'])
    load_hT(cx, A['hT'])
    emit_norm(cx, A['hn_out'])
    p.finalize()
    return nc, cx


def build_fused(nlayers=4):
    nc = bass.Bass("TRN2", target_bir_lowering=False)
    shapes = {}

    def inp(name, shape, dt=F32):
        shapes[name] = list(shape)

    class Lazy(dict):
        def __missing__(self, name):
            ap = nc.dram_tensor(name, shapes[name], F32, kind="ExternalInput").ap()
            self[name] = ap
            return ap
    A = Lazy()

    def scratch(name):
        return nc.dram_tensor(name, [D, SEQ], F32, kind="Internal").ap()
    inp('xT', [D, SEQ])
    inp('memT', [D, MEMLEN])
    inp('ident', [128, 128])
    inp('v0', [128, NVEC])
    inp('biasT0', [24, 128, 256])
    inp('biasT1', [24, 128, 256])
    for i in range(4):
        inp('vecs%d' % i, [128, NVEC])
        inp('wq%d' % i, [D, D])
        inp('wkv%d' % i, [D, 2 * D])
        inp('wo%d' % i, [D, D])
        inp('w1_%d' % i, [D, DFF])
        inp('w2_%d' % i, [DFF, D])
    for j in range(2):
        inp('wglu%d' % j, [D, 2 * D])
        inp('wqkv%d' % j, [D, 9216])
        inp('woa%d' % j, [D, D])
        inp('avecs%d' % j, [128, NVEC])
        for c in range(2):
            sfx = '%d%d' % (j, c)
            for nm in ('Bre', 'Bim', 'CR', 'CI'):
                inp(nm + sfx, [2, NPT, 128, 128])
            for nm in ('lamre', 'lamim', 'logdt'):
                inp(nm + sfx, [128, 2 * NPT])
            inp('dsk' + sfx, [128, 4])
    xT = A['xT']
    outT = nc.dram_tensor('outT', [D, SEQ], F32, kind="ExternalOutput").ap()
    HN = scratch('HN')
    Y = scratch('Y')
    Hs = [xT, scratch('H1'), scratch('H1a'), scratch('H2'), scratch('H3'), scratch('H3a'), outT]
    cx = Cx(nc, arena=True)
    p = cx.p
    hv = lambda ap, half: ap[:, half * NT:(half + 1) * NT]

    cx.hT = p.sb('hT', [128, NCH, NT], F32)
    cx.vec = p.sb('vec', [128, NVEC], F32)
    p.dma('sp', I_dma(cx.vec[:], A['v0'][:, :]), w=['vec'])
    for half in range(2):
        load_hT(cx, hv(xT, half))
        emit_norm(cx, hv(HN, half))

    def s5_stage(j):
        for c in range(2):
            cx.new_stage()
            sfx = '%d%d' % (j, c)
            AA = {nm: A[nm + sfx] for nm in ('Bre', 'Bim', 'CR', 'CI', 'lamre', 'lamim', 'logdt', 'dsk')}
            AA['ident'] = A['ident']
            AA['uT'] = HN[512 * c:512 * c + 512, :]
            AA['yT'] = Y[512 * c:512 * c + 512, :]
            s5_body(cx, AA)

    def tail_stage(i, glu, src, dst, emit):
        cx.new_stage()
        AA = dict(memT=A['memT'], vecs=A['vecs%d' % i], wq=A['wq%d' % i], wkv=A['wkv%d' % i], wo=A['wo%d' % i],
                  w1=A['w1_%d' % i], w2=A['w2_%d' % i])
        if glu:
            AA['wglu'] = A['wglu%d' % (i // 2)]
        common_tiles(cx, AA)
        for half in range(2):
            load_hT(cx, hv(src, half))
            if glu:
                AA['yT'] = hv(Y, half)
            tail_body(cx, AA, glu, 0, 2, kv_ready=(half == 1))
            tail_body(cx, AA, glu, 2, 2, kv_ready=True)
            store_hT(cx, hv(dst, half))
            if emit:
                emit_norm(cx, hv(HN, half))

    def attn_stage(i, src, dst):
        j = i // 2
        for half in range(2):
            cx.new_stage()
            cx.vec = p.sb('vec', [128, NVEC], F32)
            p.dma('sp', I_dma(cx.vec[:], A['avecs%d' % j][:, :]), w=['vec'])
            AA = dict(wqkv=A['wqkv%d' % j], wo_a=A['woa%d' % j], biasT=A['biasT%d' % half])
            attn_body(cx, AA, flip=(half == 1), src=src, dst=dst)

    s5_stage(0)
    if nlayers == 0:
        cx.new_stage()
        cx.hT = p.sb('hT', [128, NCH, NT], F32)
        for half in range(2):
            load_hT(cx, hv(Y, half))
            store_hT(cx, hv(outT, half))
        p.finalize()
        cx.used = list(A.keys())
        return nc, cx
    tail_stage(0, True, Hs[0], Hs[1] if nlayers > 1 else outT, False)
    if nlayers > 1:
        attn_stage(1, Hs[1], Hs[2])
        tail_stage(1, False, Hs[2], Hs[3] if nlayers > 2 else outT, True)
    if nlayers > 2:
        s5_stage(1)
        tail_stage(2, True, Hs[3], Hs[4] if nlayers > 3 else outT, False)
    if nlayers > 3:
        attn_stage(3, Hs[4], Hs[5])
        tail_stage(3, False, Hs[5], Hs[6], False)
    p.finalize()
    cx.used = list(A.keys())
    return nc, cx


RG2 = [[0, 1], [2, 3], [4, 5], [6, 7]]


class GBuf:
    def __init__(self, nc, name, rows, cols, chunk_rows):
        self.cr = chunk_rows
        self.n = rows // chunk_rows
        self.src = [nc.dram_tensor('%s_s%d' % (name, q), [chunk_rows, cols], F32, kind="Internal").ap() for q in range(self.n)]
        self.dst = [nc.dram_tensor('%s_g%d' % (name, q), [2 * chunk_rows, cols], F32, kind="Internal").ap() for q in range(self.n)]

    def src_rows(self, r0, nrows=128):
        q = r0 // self.cr
        o = r0 - q * self.cr
        return self.src[q][o:o + nrows, :]

    def g_rows(self, rank, r0, nrows=128):
        q = r0 // self.cr
        o = rank * self.cr + r0 - q * self.cr
        return self.dst[q][o:o + nrows, :]


def build_fused8():
    nc = bass.Bass("TRN2", target_bir_lowering=False, num_devices=8)
    shapes = {}

    def inp(name, shape):
        shapes[name] = list(shape)

    class Lazy(dict):
        def __missing__(self, name):
            ap = nc.dram_tensor(name, shapes[name], F32, kind="ExternalInput").ap()
            self[name] = ap
            return ap
    A = Lazy()

    def scratch(name, shape):
        return nc.dram_tensor(name, list(shape), F32, kind="Internal").ap()
    inp('xT', [D, NT])
    inp('memT', [D, MEMLEN])
    inp('ident', [128, 128])
    inp('v0', [128, NVEC])
    inp('biasT', [24, 128, 256])
    for i in range(4):
        inp('vecs%d' % i, [128, NVEC])
        inp('wq%d' % i, [D, D])
        inp('wkv%d' % i, [D, 2 * D])
        inp('wo%d' % i, [D, D])
        inp('w1_%d' % i, [D, DFF])
        inp('w2_%d' % i, [DFF, D])
    for j in range(2):
        inp('wglu%d' % j, [D, 2 * D])
        inp('wqkv%d' % j, [D, 9216])
        inp('woa%d' % j, [D, D])
        inp('avecs%d' % j, [128, NVEC])
        for nm in ('Bre', 'Bim', 'CR', 'CI'):
            inp(nm + '%d' % j, [2, NPT, 128, 128])
        for nm in ('lamre', 'lamim', 'logdt'):
            inp(nm + '%d' % j, [128, 2 * NPT])
        inp('dsk%d' % j, [128, 4])
    outT = nc.dram_tensor('outT', [D, NT], F32, kind="ExternalOutput").ap()
    HNb = GBuf(nc, 'HN', D, NT, 256)
    HN = GHN = HNb
    yO = scratch('yO', [512, NT])
    ySb = GBuf(nc, 'yS', 512, NT, 256)
    yS = GS = ySb
    Hhb = GBuf(nc, 'Hh', D, 1024, 512)
    Hh = GH = Hhb
    Hs = [A['xT']] + [scratch(n, [D, NT]) for n in ('H1', 'H1a', 'H2', 'H3', 'H3a')] + [outT]
    cx = Cx(nc, arena=True)
    p = cx.p

    def allgather(gb, _unused=None):
        cx.new_stage()
        for q in range(gb.n):
            p.dma('pool', lambda e, q=q: e.collective_compute("AllGather", ALU.bypass, replica_groups=RG2,
                                                               ins=[gb.src[q][:, :]], outs=[gb.dst[q][:, :]]),
                  w=['cc'], semkey='cc', inc=1)

    cx.hT = p.sb('hT', [128, NCH, NT], F32)
    cx.vec = p.sb('vec', [128, NVEC], F32)
    p.dma('sp', I_dma(cx.vec[:], A['v0'][:, :]), w=['vec'])
    load_hT(cx, A['xT'])
    emit_norm(cx, HN)
    allgather(HN, GHN)

    def s5_stage(j):
        cx.new_stage()
        AA = {nm: A[nm + '%d' % j] for nm in ('Bre', 'Bim', 'CR', 'CI', 'lamre', 'lamim', 'logdt', 'dsk')}
        AA['ident'] = A['ident']
        AA['GHN'] = GHN
        AA['yO'] = yO
        AA['yS'] = yS
        cx.vec = p.sb('vec', [128, NVEC], F32)
        p.dma('sp', I_dma(cx.vec[:], A['v0'][:, :]), w=['vec'])
        s5_body(cx, AA)
        allgather(yS, GS)

    def tail_stage(i, glu, src, dst, emit, halo):
        cx.new_stage()
        AA = dict(memT=A['memT'], vecs=A['vecs%d' % i], wq=A['wq%d' % i], wkv=A['wkv%d' % i], wo=A['wo%d' % i],
                  w1=A['w1_%d' % i], w2=A['w2_%d' % i])
        if glu:
            AA['wglu'] = A['wglu%d' % (i // 2)]
            AA['yO'] = yO
            AA['GS'] = GS
        common_tiles(cx, AA)
        load_hT(cx, src)
        tail_body(cx, AA, glu, 0, 2, kv_ready=False)
        tail_body(cx, AA, glu, 2, 2, kv_ready=True)
        store_hT(cx, dst)
        if halo:
            for c in range(NCH):
                p.dma('sp', I_dma(Hh.src_rows(c * 128), cx.hT[:, c, 1024:2048]),
                      r=['h%d_%d' % (c, tb) for tb in (2, 3)], w=['hhout'])
            allgather(Hh, GH)
        if emit:
            emit_norm(cx, HN)
            allgather(HN, GHN)

    def attn_stage(i, src, dst):
        j = i // 2
        cx.new_stage()
        cx.vec = p.sb('vec', [128, NVEC], F32)
        p.dma('sp', I_dma(cx.vec[:], A['avecs%d' % j][:, :]), w=['vec'])
        AA = dict(wqkv=A['wqkv%d' % j], wo_a=A['woa%d' % j], biasT=A['biasT'])
        cx.ws_n = 2
        attn_body(cx, AA, flip=False, src=src, dst=dst, gh=GH)
        cx.ws_n = WS_N

    s5_stage(0)
    tail_stage(0, True, Hs[0], Hs[1], False, True)
    attn_stage(1, Hs[1], Hs[2])
    tail_stage(1, False, Hs[2], Hs[3], True, False)
    s5_stage(1)
    tail_stage(2, True, Hs[3], Hs[4], False, True)
    attn_stage(3, Hs[4], Hs[5])
    tail_stage(3, False, Hs[5], Hs[6], False, False)
    p.finalize()
    cx.used = list(A.keys())
    return nc, cx


def _pc(v, C):
    return np.ascontiguousarray(np.asarray(v, np.float32).reshape(C, 128).T)


_PROGS = {}
_NL = [4]


def _prog(name):
    if name not in _PROGS:
        if name == 'norm':
            _PROGS[name] = build_norm()[0]
        elif name == 's5':
            _PROGS[name] = build_s5()[0]
        elif name == 'tail_glu':
            _PROGS[name] = build_tail(True, True)[0]
        elif name == 'tail':
            _PROGS[name] = build_tail(False, True)[0]
        elif name == 'attn':
            _PROGS[name] = build_attn()[0]
    return _PROGS[name]


def kernel_multi(**inp):
    inp = {k: np.asarray(v) for k, v in inp.items()}
    ncore = 8
    cores = list(range(ncore))
    f32 = np.float32
    loc = [np.arange(NEXT) if (k % 2 == 0) else (SEQ - 1 - np.arange(NEXT)) for k in cores]
    H = np.array(inp['x'], dtype=f32, copy=True)
    memT = [np.ascontiguousarray(inp['mem'][k // 2].T.astype(f32)) for k in cores]
    biasT = [host_bias(inp['bias_table'].astype(f32), k % 2 == 1) for k in cores]

    def own_T(arr_bsd, k):
        return np.ascontiguousarray(arr_bsd[k // 2][loc[k][:NT]].T)

    def scatter(outs, name):
        full = np.empty((BATCH, SEQ, D), f32)
        for k in cores:
            full[k // 2][loc[k][:NT]] = np.asarray(outs[k][name], f32).T
        return full

    def tail_vecs(i):
        v = np.zeros((128, NVEC), f32)
        v[:, 0:8] = _pc(inp['norm_xattn'][i], 8)
        v[:, 8:16] = _pc(inp['norm_mem'][i], 8)
        v[:, 16:24] = _pc(inp['norm_mlp'][i], 8)
        v[:, 24:26] = _pc(inp['xattn_q_gain'][i], 2)
        v[:, 26:28] = _pc(inp['xattn_k_gain'][i], 2)
        v[:, 28:36] = _pc(inp['norm_mix'][min(i + 1, 3)], 8)
        return v

    def run_tail(i, H, Y):
        v = tail_vecs(i)
        maps = []
        for k in cores:
            m = dict(hT=own_T(H, k), memT=memT[k], vecs=v, wq=inp['xattn_w_q'][i], wkv=inp['xattn_w_kv'][i],
                     wo=inp['xattn_w_o'][i], w1=inp['mlp_w1'][i], w2=inp['mlp_w2'][i])
            if Y is not None:
                m['yT'] = own_T(Y, k)
                m['wglu'] = inp['s5_w_glu'][i // 2]
            maps.append(m)
        res = run_bass_kernel_spmd(_prog('tail_glu' if Y is not None else 'tail'), maps, core_ids=cores).results
        return scatter(res, 'hT_out'), scatter(res, 'hn_out')

    def run_s5(j, HN):
        maps = []
        for k in cores:
            b, c = k // 2, k % 2
            m = s5_host_inputs(inp, j, c)
            m['uT'] = np.ascontiguousarray(HN[b][:, 512 * c:512 * c + 512].T)
            maps.append(m)
        res = run_bass_kernel_spmd(_prog('s5'), maps, core_ids=cores).results
        Y = np.empty((BATCH, SEQ, D), f32)
        for k in cores:
            b, c = k // 2, k % 2
            Y[b][:, 512 * c:512 * c + 512] = np.asarray(res[k]['yT'], f32).T
        return Y

    def run_attn(i, H):
        j = i // 2
        v = np.zeros((128, NVEC), f32)
        v[:, 28:36] = _pc(inp['norm_mix'][i], 8)
        v[:, 36] = inp['attn_q_gain'][j]
        v[:, 37] = inp['attn_k_gain'][j]
        maps = []
        for k in cores:
            maps.append(dict(hT_ext=np.ascontiguousarray(H[k // 2][loc[k]].T), vecs=v, wqkv=inp['attn_w_qkv'][j],
                             wo_a=inp['attn_w_o'][j], biasT=biasT[k]))
        res = run_bass_kernel_spmd(_prog('attn'), maps, core_ids=cores).results
        return scatter(res, 'hT_out')

    v0 = np.zeros((128, NVEC), f32)
    v0[:, 28:36] = _pc(inp['norm_mix'][0], 8)
    res = run_bass_kernel_spmd(_prog('norm'), [dict(hT=own_T(H, k), vecs=v0) for k in cores], core_ids=cores).results
    HN = scatter(res, 'hn_out')
    for i in range(4):
        if i % 2 == 0:
            Y = run_s5(i // 2, HN)
            H, HN = run_tail(i, H, Y)
        else:
            H = run_attn(i, H)
            H, HN = run_tail(i, H, None)
    return H


def tail_vecs_host(inp, i):
    v = np.zeros((128, NVEC), np.float32)
    v[:, 0:8] = _pc(inp['norm_xattn'][i], 8)
    v[:, 8:16] = _pc(inp['norm_mem'][i], 8)
    v[:, 16:24] = _pc(inp['norm_mlp'][i], 8)
    v[:, 24:26] = _pc(inp['xattn_q_gain'][i], 2)
    v[:, 26:28] = _pc(inp['xattn_k_gain'][i], 2)
    v[:, 28:36] = _pc(inp['norm_mix'][min(i + 1, 3)], 8)
    return v


def kernel_fused4(**inp):
    inp = {k: np.asarray(v) for k, v in inp.items()}
    f32 = np.float32
    nl = _NL[0]
    if ('fused', nl) not in _PROGS:
        _PROGS[('fused', nl)] = build_fused(nl)
    nc, cxf = _PROGS[('fused', nl)]
    shared = dict(ident=np.eye(128, dtype=f32),
                  biasT0=host_bias(inp['bias_table'].astype(f32), False),
                  biasT1=host_bias(inp['bias_table'].astype(f32), True))
    v0 = np.zeros((128, NVEC), f32)
    v0[:, 28:36] = _pc(inp['norm_mix'][0], 8)
    shared['v0'] = v0
    for i in range(4):
        shared['vecs%d' % i] = tail_vecs_host(inp, i)
        shared['wq%d' % i] = inp['xattn_w_q'][i]
        shared['wkv%d' % i] = inp['xattn_w_kv'][i]
        shared['wo%d' % i] = inp['xattn_w_o'][i]
        shared['w1_%d' % i] = inp['mlp_w1'][i]
        shared['w2_%d' % i] = inp['mlp_w2'][i]
    for j in range(2):
        shared['wglu%d' % j] = inp['s5_w_glu'][j]
        shared['wqkv%d' % j] = inp['attn_w_qkv'][j]
        shared['woa%d' % j] = inp['attn_w_o'][j]
        av = np.zeros((128, NVEC), f32)
        av[:, 28:36] = _pc(inp['norm_mix'][2 * j + 1], 8)
        av[:, 36] = inp['attn_q_gain'][j]
        av[:, 37] = inp['attn_k_gain'][j]
        shared['avecs%d' % j] = av
        for c in range(2):
            for nm, arr in s5_host_inputs(inp, j, c).items():
                if nm != 'ident':
                    shared[nm + '%d%d' % (j, c)] = arr
    maps = []
    for b in range(BATCH):
        m = dict(shared)
        m['xT'] = np.ascontiguousarray(inp['x'][b].T.astype(f32))
        m['memT'] = np.ascontiguousarray(inp['mem'][b].T.astype(f32))
        maps.append({k: m[k] for k in cxf.used})
    res = run_bass_kernel_spmd(nc, maps, core_ids=list(range(BATCH))).results
    out = np.empty((BATCH, SEQ, D), f32)
    for b in range(BATCH):
        out[b] = np.asarray(res[b]['outT'], f32).T
    return out


def _sw(a, axis):
    return np.roll(a, 512, axis=axis)


def kernel(**inp):
    inp = {k: np.asarray(v, np.float32) for k, v in inp.items()}
    f32 = np.float32
    if 'fused8' not in _PROGS:
        _PROGS['fused8'] = build_fused8()
    nc, cxf = _PROGS['fused8']
    ident = np.eye(128, dtype=f32)
    per_c = []
    for c in range(2):
        sw = (lambda a, axis: _sw(a, axis)) if c == 1 else (lambda a, axis: a)
        g = {}
        gi = {k: (sw(inp[k], 1) if k in ('norm_mix', 'norm_xattn', 'norm_mem', 'norm_mlp') else inp[k]) for k in inp}
        g['ident'] = ident
        g['biasT'] = host_bias(inp['bias_table'], c == 1)
        v0 = np.zeros((128, NVEC), f32)
        v0[:, 28:36] = _pc(gi['norm_mix'][0], 8)
        v0[:, 40 + c] = 1.0
        g['v0'] = v0
        for i in range(4):
            v = tail_vecs_host(gi, i)
            v[:, 40 + c] = 1.0
            g['vecs%d' % i] = v
            g['wq%d' % i] = np.ascontiguousarray(sw(inp['xattn_w_q'][i], 0))
            g['wkv%d' % i] = np.ascontiguousarray(sw(inp['xattn_w_kv'][i], 0))
            g['wo%d' % i] = np.ascontiguousarray(sw(inp['xattn_w_o'][i], 1))
            g['w1_%d' % i] = np.ascontiguousarray(sw(inp['mlp_w1'][i], 0))
            g['w2_%d' % i] = np.ascontiguousarray(sw(inp['mlp_w2'][i], 1))
        for j in range(2):
            wg = sw(inp['s5_w_glu'][j], 0).reshape(D, 2, D)
            g['wglu%d' % j] = np.ascontiguousarray(sw(wg, 2).reshape(D, 2 * D))
            g['wqkv%d' % j] = np.ascontiguousarray(sw(inp['attn_w_qkv'][j], 0))
            g['woa%d' % j] = np.ascontiguousarray(sw(inp['attn_w_o'][j], 1))
            av = np.zeros((128, NVEC), f32)
            av[:, 28:36] = _pc(gi['norm_mix'][2 * j + 1], 8)
            av[:, 36] = inp['attn_q_gain'][j]
            av[:, 37] = inp['attn_k_gain'][j]
            av[:, 40 + c] = 1.0
            g['avecs%d' % j] = av
            for nm, arr in s5_host_inputs(inp, j, c).items():
                if nm != 'ident':
                    g[nm + '%d' % j] = arr
        per_c.append(g)
    maps = []
    for k in range(8):
        b, c = k // 2, k % 2
        m = dict(per_c[c])
        xb = inp['x'][b]
        if c == 0:
            m['xT'] = np.ascontiguousarray(xb[:NT].T)
            m['memT'] = np.ascontiguousarray(inp['mem'][b].T)
        else:
            m['xT'] = np.ascontiguousarray(_sw(xb[::-1][:NT], 1).T)
            m['memT'] = np.ascontiguousarray(_sw(inp['mem'][b], 1).T)
        maps.append({kk: m[kk] for kk in cxf.used})
    res = run_bass_kernel_spmd(nc, maps, core_ids=list(range(8))).results
    out = np.empty((BATCH, SEQ, D), f32)
    for k in range(8):
        b, c = k // 2, k % 2
        o = np.asarray(res[k]['outT'], f32).T
        if c == 0:
            out[b, :NT] = o
        else:
            out[b, NT:] = _sw(o, 1)[::-1]
    return out
```

```python
import math
import numpy as np
from contextlib import ExitStack
import concourse.bass as bass
import concourse.mybir as mybir
from concourse.bass_utils import run_bass_kernel_spmd

F32 = mybir.dt.float32
BF16 = mybir.dt.bfloat16
AF = mybir.ActivationFunctionType
ALU = mybir.AluOpType

D = 1024
NCH = 8
SEQ = 4096
BATCH = 4
NT = 2048
EPS = 1e-6
MEMLEN = 256
DFF = 4096


class Prog:
    ENGS = ('pe', 'act', 'dve', 'pool', 'sp')
    BLK = {'pe': 'tensor', 'act': 'scalar', 'dve': 'vector', 'pool': 'gpsimd', 'sp': 'sync'}

    def __init__(self, nc):
        self.nc = nc
        self.es = ExitStack()
        self.ins = {e: [] for e in self.ENGS}
        self.last_w = {}
        self.readers = {}
        self.dma_cnt = {}
        self.log = None
        self.bar_deps = {}

    ARENA_F32 = 51712

    def use_arena(self):
        self.arena = self.es.enter_context(self.nc.sbuf_tensor('arena', [128, self.ARENA_F32], F32))
        self.aoff = 0

    def sb(self, name, shape, dt):
        if getattr(self, 'arena', None) is None:
            return self.es.enter_context(self.nc.sbuf_tensor('s_' + name, list(shape), dt))
        assert shape[0] == 128, shape
        nel = 1
        for d_ in shape[1:]:
            nel *= d_
        isz = 4 if dt == F32 else 2
        nby = (nel * isz + 63) // 64 * 64
        o4 = self.aoff // 4
        self.aoff += nby
        assert self.aoff <= self.ARENA_F32 * 4, ('arena overflow', name, self.aoff)
        v = self.arena[:, o4:o4 + nby // 4]
        if dt != F32:
            v = v.bitcast(dt)
        v = v[:, :nel]
        if len(shape) == 3:
            v = v.rearrange("p (a b) -> p a b", a=shape[1])
        elif len(shape) != 2:
            raise AssertionError(shape)
        return v

    def barrier(self):
        deps = [('d', k, c) for k, c in self.dma_cnt.items()]
        for e in self.ENGS:
            n = len(self.ins[e])
            j = n - 1
            while j >= 0 and self.ins[e][j]['dma'] is not None:
                j -= 1
            if j >= 0:
                deps.append(('e', e, j))
        self.bar_deps = {e: list(deps) for e in self.ENGS}
        self.last_w.clear()
        self.readers.clear()

    def ps(self, name, shape, dt=F32):
        return self.es.enter_context(self.nc.psum_tensor(name, list(shape), dt))

    def _deps(self, r, w):
        deps = []
        for k in r:
            t = self.last_w.get(k)
            if t is not None:
                deps.append(t)
        for k in w:
            t = self.last_w.get(k)
            if t is not None:
                deps.append(t)
            deps.extend(self.readers.get(k, ()))
        return deps

    def _commit(self, tok, r, w):
        for k in r:
            lst = self.readers.setdefault(k, [])
            lst[:] = [t for t in lst if t[:2] != tok[:2]]
            lst.append(tok)
        for k in w:
            self.last_w[k] = tok
            self.readers[k] = []

    def op(self, eng, fn, r=(), w=()):
        idx = len(self.ins[eng])
        self.ins[eng].append(dict(fn=fn, deps=self._deps(r, w) + self.bar_deps.pop(eng, []), dma=None))
        self._commit(('e', eng, idx), r, w)

    def dma(self, eng, fn, r=(), w=(), semkey=None, inc=16):
        if semkey is None:
            semkey = w[0]
        c = self.dma_cnt.get(semkey, 0) + inc
        self.dma_cnt[semkey] = c
        self.ins[eng].append(dict(fn=fn, deps=self._deps(r, w) + self.bar_deps.pop(eng, []), dma=semkey, inc=inc))
        self._commit(('d', semkey, c), r, w)

    SAME_DIST = 4

    def _skip_same(self, e, i, d, rec):
        if d[1] != e or rec['dma'] is not None:
            return False
        if e == 'pe':
            return True
        return (i - d[2]) > self.SAME_DIST

    def finalize(self):
        nc = self.nc
        need = {e: set() for e in self.ENGS}
        for e in self.ENGS:
            for i, rec in enumerate(self.ins[e]):
                for d in rec['deps']:
                    if d[0] == 'e' and not self._skip_same(e, i, d, rec):
                        need[d[1]].add(d[2])
        cum = {}
        for e in self.ENGS:
            c = 0
            arr = []
            for i in range(len(self.ins[e])):
                if i in need[e]:
                    c += 1
                arr.append(c)
            cum[e] = arr
        esem = {e: self.es.enter_context(nc.semaphore('se_' + e)) for e in self.ENGS}
        dsem = {}
        for i, k in enumerate(self.dma_cnt):
            dsem[k] = self.es.enter_context(nc.semaphore('sd_%d' % i))
        self.stats = {e: (len(self.ins[e]), cum[e][-1] if cum[e] else 0) for e in self.ENGS}
        self.stats['ndsem'] = len(dsem)
        with nc.Block() as block:
            for e in self.ENGS:
                def body(eng, e=e):
                    waited = {}
                    for i, rec in enumerate(self.ins[e]):
                        req = {}
                        for d in rec['deps']:
                            if d[0] == 'e':
                                if self._skip_same(e, i, d, rec):
                                    continue
                                key = ('e', d[1])
                                val = cum[d[1]][d[2]]
                            else:
                                key = ('d', d[1])
                                val = d[2]
                            if val > req.get(key, 0):
                                req[key] = val
                        for key, val in req.items():
                            if waited.get(key, 0) < val:
                                sem = esem[key[1]] if key[0] == 'e' else dsem[key[1]]
                                eng.wait_ge(sem, val)
                                waited[key] = val
                                if self.log is not None:
                                    self.log.append((e, i, 'wait', key, val))
                        if self.log is not None:
                            self.log.append((e, i, 'inst', rec['dma'], cum[e][i] if i in need[e] else None))
                        inst = rec['fn'](eng)
                        if rec['dma'] is not None:
                            inst.then_inc(dsem[rec['dma']], rec.get('inc', 16))
                        elif i in need[e]:
                            inst.then_inc(esem[e], 1)
                    if e == 'sp':
                        for k, c in self.dma_cnt.items():
                            eng.wait_ge(dsem[k], c)
                getattr(block, self.BLK[e])(body)
        self.es.close()


def I_mm(out, lhsT, rhs, start, stop):
    return lambda e: e.matmul(out, lhsT, rhs, start=start, stop=stop)


def I_act(out, in_, func, **kw):
    return lambda e: e.activation(out=out, in_=in_, func=func, **kw)


def I_tt(out, in0, in1, op):
    return lambda e: e.tensor_tensor(out=out, in0=in0, in1=in1, op=op)


def I_ts(out, in0, s1, s2, op0, op1=None):
    if op1 is None:
        return lambda e: e.tensor_scalar(out=out, in0=in0, scalar1=s1, scalar2=None, op0=op0)
    return lambda e: e.tensor_scalar(out=out, in0=in0, scalar1=s1, scalar2=s2, op0=op0, op1=op1)


def I_stt(out, in0, scalar, in1, op0, op1):
    return lambda e: e.scalar_tensor_tensor(out=out, in0=in0, scalar=scalar, in1=in1, op0=op0, op1=op1)


def I_recip(out, in_):
    return lambda e: e.reciprocal(out=out, in_=in_)


def I_copy(out, in_):
    return lambda e: e.tensor_copy(out=out, in_=in_)


def I_memset(ap, c):
    return lambda e: e.memset(ap, c)


def I_dma(out, in_):
    return lambda e: e.dma_start(out=out, in_=in_)


def I_scan(out, d0, d1, init):
    return lambda e: e.tensor_tensor_scan(out=out, data0=d0, data1=d1, initial=init, op0=ALU.mult, op1=ALU.add)


def mcombine(cx, out, okey, X, xk, Z, zk, ma, mb, n=512):
    p = cx.p
    tmp, tk = cx.rot('mctmp', [128, 512], F32, n=2)
    p.op('act', I_act(tmp[:, :n], X, AF.Copy, scale=ma), r=[xk, 'vec'], w=[tk])
    p.op('dve', I_stt(out, Z, mb, tmp[:, :n], ALU.mult, ALU.add), r=[zk, 'vec', tk], w=[okey])


class Cx:
    def __init__(self, nc, arena=False):
        self.nc = nc
        self.p = Prog(nc)
        if arena:
            self.p.use_arena()
        self.banks = [self.p.ps('bank%d' % i, [128, 512]) for i in range(8)]
        self.bi = 0
        p = self.p
        self.ones = p.sb('ones', [128, 128], BF16)
        p.op('pool', I_memset(self.ones[:], 1.0), w=['ones'])
        self.sq = [p.sb('sq%d' % i, [128, 512], BF16) for i in range(2)]
        self.sqi = 0
        self.rstd = [p.sb('rstd%d' % i, [128, 512], F32) for i in range(2)]
        self.rsi = 0
        self._rot = {}
        self.epsc = p.sb('epsc', [128, 1], F32)
        p.op('pool', I_memset(self.epsc[:], EPS), w=['epsc'])
        self.mark = getattr(p, 'aoff', 0)

    def new_stage(self):
        self.p.barrier()
        self.p.aoff = self.mark
        self._rot = {}
        for nm in ('ws', 'wsi'):
            if hasattr(self, nm):
                delattr(self, nm)

    def bank(self):
        i = self.bi
        self.bi = (i + 1) % 8
        return self.banks[i], 'bank%d' % i

    def rot(self, name, shape, dt, n=2):
        if name not in self._rot:
            self._rot[name] = [[self.p.sb('%s_%d' % (name, i), shape, dt) for i in range(n)], 0]
        tl, i = self._rot[name]
        self._rot[name][1] = (i + 1) % n
        return tl[i], '%s_%d' % (name, i)

    def next_sq(self):
        i = self.sqi
        self.sqi = 1 - i
        return self.sq[i], 'sq%d' % i

    def next_rstd(self):
        i = self.rsi
        self.rsi = 1 - i
        return self.rstd[i], 'rstd%d' % i

    def rstd_from_bank(self, bank, bk, n, dim):
        p = self.p
        rs, rk = self.next_rstd()
        p.op('act', I_act(rs[:, :n], bank[:, :n], AF.Ln, scale=1.0 / dim, bias=self.epsc[:, 0:1]), r=[bk, 'epsc'], w=[rk])
        p.op('act', I_act(rs[:, :n], rs[:, :n], AF.Exp, scale=-0.5), r=[rk], w=[rk])
        return rs, rk


def rmsnorm(cx, src, skey, gcols, gkey, dst, dkey, C, n0, n, dim):
    p = cx.p
    bank, bk = cx.bank()
    for c in range(C):
        sq, sk = cx.next_sq()
        p.op('act', I_act(sq[:, :n], src[:, c, n0:n0 + n], AF.Square), r=[skey(c)], w=[sk])
        p.op('pe', I_mm(bank[:, :n], cx.ones[:], sq[:, :n], c == 0, c == C - 1), r=[sk, 'ones'], w=[bk])
    rs, rk = cx.rstd_from_bank(bank, bk, n, dim)
    for c in range(C):
        p.op('dve', I_stt(dst[:, c, n0:n0 + n], src[:, c, n0:n0 + n], gcols[:, c:c + 1], rs[:, :n], ALU.mult, ALU.mult),
             r=[skey(c), gkey, rk], w=[dkey(c)])


WS_N = 3


def wslab(cx, parts):
    p = cx.p
    nws = getattr(cx, 'ws_n', WS_N)
    if not hasattr(cx, 'ws'):
        cx.ws = [p.sb('ws%d' % i, [128, 4096], BF16) for i in range(nws)]
        cx.wsi = 0
    i = cx.wsi
    cx.wsi = (i + 1) % nws
    t = cx.ws[i]
    key = 'ws%d' % i
    for src, off in parts:
        K, N = src.shape
        kc = K // 128
        dst = t[:, off:off + kc * N].rearrange("p (k n) -> p k n", k=kc)
        p.dma('pool', I_dma(dst, src.rearrange("(k p) n -> p k n", p=128)), w=[key])
    return t, key


class SlabStream:
    def __init__(self, cx, specs):
        self.cx, self.specs, self.loaded, self.i = cx, specs, [], 0

    def get(self):
        while len(self.loaded) < min(len(self.specs), self.i + 2):
            self.loaded.append(wslab(self.cx, self.specs[len(self.loaded)]))
        r = self.loaded[self.i]
        self.i += 1
        return r


VC = dict(gx=0, gm=8, gl=16, gq=24, gk=26, gn=28, aq=36, ak=37, dsk=38, m0=40, m1=41)
NVEC = 48


def tail_body(cx, A, glu, tb0, ntb, kv_ready):
    p = cx.p
    hT = cx.hT
    hk = lambda c, tb: 'h%d_%d' % (c, tb)
    hn = cx.hn
    big2 = cx.big2
    vec = cx.vec
    NB = ntb
    specs = []
    if glu:
        for ns in range(2):
            specs.append([(A['wglu'][:, ns * 512:(ns + 1) * 512], 0)])
            specs.append([(A['wglu'][:, 1024 + ns * 512:1024 + (ns + 1) * 512], 0)])
    if not kv_ready:
        for hp in range(2):
            specs.append([(A['wkv'][:, hp * 512:(hp + 1) * 512], 0)])
        for vs in range(2):
            specs.append([(A['wkv'][:, 1024 + vs * 512:1024 + (vs + 1) * 512], 0)])
    for hp in range(2):
        specs.append([(A['wq'][:, hp * 512:(hp + 1) * 512], 0)])
    for ns in range(2):
        specs.append([(A['wo'][:, ns * 512:(ns + 1) * 512], 0)])
    for s in range(DFF // 256):
        specs.append([(A['w1'][:, s * 256:(s + 1) * 256], 0), (A['w2'][s * 256:(s + 1) * 256, :], 2048)])
    ss = SlabStream(cx, specs)

    if glu:
        for c in range(NCH):
            for tb in range(NB):
                yt, yk = cx.rot('ytmp', [128, 512], F32)
                col0 = (tb0 + tb) * 512
                if 'yO' in A and c >= 4:
                    X, xk = cx.rot('gx', [128, 512], F32, n=2)
                    Z, zk = cx.rot('gz', [128, 512], F32, n=2)
                    p.dma('sp', I_dma(X[:], A['GS'].g_rows(0, (c - 4) * 128)[:, col0:col0 + 512]), w=[xk])
                    p.dma('sp', I_dma(Z[:], A['GS'].g_rows(1, (c - 4) * 128)[:, col0:col0 + 512]), w=[zk])
                    mcombine(cx, yt[:], yk, X[:], xk, Z[:], zk, vec[:, VC['m1']:VC['m1'] + 1], vec[:, VC['m0']:VC['m0'] + 1])
                elif 'yO' in A:
                    p.dma('sp', I_dma(yt[:], A['yO'][c * 128:(c + 1) * 128, col0:col0 + 512]), w=[yk])
                else:
                    p.dma('sp', I_dma(yt[:], A['yT'][c * 128:(c + 1) * 128, col0:col0 + 512]), w=[yk])
                p.op('act', I_act(hn[:, c, tb * 512:(tb + 1) * 512], yt[:], AF.Gelu_apprx_tanh), r=[yk], w=['hn%d' % tb])
        for ns in range(2):
            wa, wak = ss.get()
            wb, wbk = ss.get()
            for j in range(4):
                n = ns * 4 + j
                for tb in range(NB):
                    ba, bak = cx.bank()
                    bb, bbk = cx.bank()
                    for kc in range(NCH):
                        p.op('pe', I_mm(ba[:], wa[:, kc * 512 + j * 128: kc * 512 + (j + 1) * 128],
                                        hn[:, kc, tb * 512:(tb + 1) * 512], kc == 0, kc == NCH - 1),
                             r=[wak, 'hn%d' % tb], w=[bak])
                    for kc in range(NCH):
                        p.op('pe', I_mm(bb[:], wb[:, kc * 512 + j * 128: kc * 512 + (j + 1) * 128],
                                        hn[:, kc, tb * 512:(tb + 1) * 512], kc == 0, kc == NCH - 1),
                             r=[wbk, 'hn%d' % tb], w=[bbk])
                    sg, sgk = cx.rot('sg', [128, 512], F32)
                    p.op('act', I_act(sg[:], bb[:], AF.Sigmoid), r=[bbk], w=[sgk])
                    gt, gtk = cx.rot('gtmp', [128, 512], F32)
                    p.op('dve', I_tt(gt[:], ba[:], sg[:], ALU.mult), r=[bak, sgk], w=[gtk])
                    hs = hT[:, n, (tb0 + tb) * 512:(tb0 + tb + 1) * 512]
                    p.op('pool', I_tt(hs, hs, gt[:], ALU.add), r=[gtk, hk(n, tb0 + tb)], w=[hk(n, tb0 + tb)])

    if not kv_ready:
        kraw = cx.kraw
        memn = cx.memn
        for c in range(NCH):
            p.dma('sp', I_dma(kraw[:, c, :], A['memT'][c * 128:(c + 1) * 128, :]), w=['kraw'], semkey='kraw_ld')
        rmsnorm(cx, kraw, lambda c: 'kraw', vec[:, VC['gm']:VC['gm'] + 8], 'vec', memn, lambda c: 'memn', NCH, 0, MEMLEN, D)
        for hp in range(2):
            wk, wkk = ss.get()
            for jj in range(4):
                j = hp * 4 + jj
                bk_, bkk = cx.bank()
                for kc in range(NCH):
                    p.op('pe', I_mm(bk_[:, :MEMLEN], wk[:, kc * 512 + jj * 128: kc * 512 + (jj + 1) * 128], memn[:, kc, :],
                                    kc == 0, kc == NCH - 1), r=[wkk, 'memn'], w=[bkk])
                p.op('act', I_act(kraw[:, j, :], bk_[:, :MEMLEN], AF.Copy), r=[bkk], w=['kraw'])
        for h in range(4):
            bs, bsk = cx.bank()
            for ec in range(2):
                sq, sk = cx.next_sq()
                p.op('act', I_act(sq[:, :MEMLEN], kraw[:, 2 * h + ec, :], AF.Square), r=['kraw'], w=[sk])
                p.op('pe', I_mm(bs[:, :MEMLEN], cx.ones[:], sq[:, :MEMLEN], ec == 0, ec == 1), r=[sk, 'ones'], w=[bsk])
            rs, rk = cx.rstd_from_bank(bs, bsk, MEMLEN, 256)
            for ec in range(2):
                p.op('dve', I_stt(cx.KT[:, 2 * h + ec, :], kraw[:, 2 * h + ec, :], vec[:, VC['gk'] + ec:VC['gk'] + ec + 1],
                                  rs[:, :MEMLEN], ALU.mult, ALU.mult), r=['kraw', 'vec', rk], w=['KT'])
        for vs in range(2):
            wv, wvk = ss.get()
            for mc in range(2):
                bv, bvk = cx.bank()
                for kc in range(NCH):
                    p.op('pe', I_mm(bv[:], memn[:, kc, mc * 128:(mc + 1) * 128], wv[:, kc * 512:(kc + 1) * 512],
                                    kc == 0, kc == NCH - 1), r=[wvk, 'memn'], w=[bvk])
                p.op('act', I_act(cx.V[:, mc, vs * 512:(vs + 1) * 512], bv[:], AF.Copy), r=[bvk], w=['V'])

    for tb in range(NB):
        _rmsnorm_off(cx, hT, (tb0 + tb) * 512, lambda c, tb=tb: hk(c, tb0 + tb), vec[:, VC['gx']:VC['gx'] + 8],
                     hn, tb * 512, 'hn%d' % tb)
    its = [(hp, hh, tb) for hp in range(2) for hh in range(2) for tb in range(NB)]
    BK = lambda i: (cx.banks[i], 'bank%d' % i)
    wq_cur = {}
    stx = {}

    def X_phase(i):
        hp, hh, tb = its[i]
        if hp not in wq_cur:
            wq_cur[hp] = ss.get()
        wq, wqk = wq_cur[hp]
        qb = []
        for ec in range(2):
            b, bk_ = BK((i % 2) * 2 + ec)
            cc = hh * 2 + ec
            for kc in range(NCH):
                p.op('pe', I_mm(b[:], wq[:, kc * 512 + cc * 128: kc * 512 + (cc + 1) * 128],
                                hn[:, kc, tb * 512:(tb + 1) * 512], kc == 0, kc == NCH - 1),
                     r=[wqk, 'hn%d' % tb], w=[bk_])
            qb.append((b, bk_))
        stx[i] = dict(qb=qb)

    def Y_phase(i):
        qb = stx[i]['qb']
        bs, bsk = BK(4)
        for ec in range(2):
            sq, sk = cx.next_sq()
            p.op('act', I_act(sq[:], qb[ec][0][:], AF.Square), r=[qb[ec][1]], w=[sk])
            p.op('pe', I_mm(bs[:], cx.ones[:], sq[:], ec == 0, ec == 1), r=[sk, 'ones'], w=[bsk])
        rs, rk = cx.rstd_from_bank(bs, bsk, 512, 256)
        qn, qnk = cx.rot('qn', [128, 2, 512], BF16)
        for ec in range(2):
            p.op('dve', I_stt(qn[:, ec, :], qb[ec][0][:], vec[:, VC['gq'] + ec:VC['gq'] + ec + 1], rs[:],
                              ALU.mult, ALU.mult), r=[qb[ec][1], 'vec', rk], w=[qnk])
        stx[i]['qn'] = (qn, qnk)

    def Z_phase(i):
        hp, hh, tb = its[i]
        h = 2 * hp + hh
        qn, qnk = stx[i]['qn']
        PT, ptk = cx.rot('PT', [128, 2, 512], BF16)
        for mc in range(2):
            bl, blk = BK(5 + mc)
            for ec in range(2):
                p.op('pe', I_mm(bl[:], cx.KT[:, 2 * h + ec, mc * 128:(mc + 1) * 128], qn[:, ec, :], ec == 0, ec == 1),
                     r=['KT', qnk], w=[blk])
            p.op('act', I_act(PT[:, mc, :], bl[:], AF.Exp, scale=1.0 / 16.0), r=[blk], w=[ptk])
        bd, bdk = BK(7)
        for mc in range(2):
            p.op('pe', I_mm(bd[:], cx.ones[:], PT[:, mc, :], mc == 0, mc == 1), r=['ones', ptk], w=[bdk])
        rd, rdk = cx.rot('rden', [128, 512], F32)
        p.op('act', I_act(rd[:], bd[:], AF.Ln), r=[bdk], w=[rdk])
        p.op('act', I_act(rd[:], rd[:], AF.Exp, scale=-1.0), r=[rdk], w=[rdk])
        for ec in range(2):
            bo, bok = BK(5 + ec)
            for mc in range(2):
                p.op('pe', I_mm(bo[:], cx.V[:, mc, h * 256 + ec * 128: h * 256 + (ec + 1) * 128], PT[:, mc, :],
                                mc == 0, mc == 1), r=['V', ptk], w=[bok])
            p.op('dve', I_tt(big2[:, 2 * h + ec, tb * 512:(tb + 1) * 512], bo[:], rd[:], ALU.mult),
                 r=[bok, rdk], w=['big2_%d' % tb])
        del stx[i]
    nit = len(its)
    for step in range(nit + 2):
        if step < nit:
            X_phase(step)
        if 0 <= step - 1 < nit:
            Y_phase(step - 1)
        if 0 <= step - 2 < nit:
            Z_phase(step - 2)
    for ns in range(2):
        wo, wok = ss.get()
        for j in range(4):
            n = ns * 4 + j
            for tb in range(NB):
                b, bk_ = cx.bank()
                for kc in range(NCH):
                    p.op('pe', I_mm(b[:], wo[:, kc * 512 + j * 128: kc * 512 + (j + 1) * 128],
                                    big2[:, kc, tb * 512:(tb + 1) * 512], kc == 0, kc == NCH - 1),
                         r=[wok, 'big2_%d' % tb], w=[bk_])
                hs = hT[:, n, (tb0 + tb) * 512:(tb0 + tb + 1) * 512]
                p.op('dve', I_tt(hs, b[:], hs, ALU.add), r=[bk_, hk(n, tb0 + tb)], w=[hk(n, tb0 + tb)])

    for tb in range(NB):
        _rmsnorm_off(cx, hT, (tb0 + tb) * 512, lambda c, tb=tb: hk(c, tb0 + tb), vec[:, VC['gl']:VC['gl'] + 8],
                     hn, tb * 512, 'hn%d' % tb)
    def mlp_w1(s, ws, wsk):
        hb = s % 2
        for j in range(2):
            for tb in range(NB):
                b, bk_ = cx.bank()
                for kc in range(NCH):
                    p.op('pe', I_mm(b[:], ws[:, kc * 256 + j * 128: kc * 256 + (j + 1) * 128],
                                    hn[:, kc, tb * 512:(tb + 1) * 512], kc == 0, kc == NCH - 1),
                         r=[wsk, 'hn%d' % tb], w=[bk_])
                rt, rtk = cx.rot('rtmp', [128, 512], F32)
                p.op('act', I_act(rt[:], b[:], AF.Relu), r=[bk_], w=[rtk])
                p.op('pool', I_tt(big2[:, hb * 2 + j, tb * 512:(tb + 1) * 512], rt[:], rt[:], ALU.mult),
                     r=[rtk], w=['hid%d' % hb])

    def mlp_w2(s, ws, wsk):
        hb = s % 2
        for n in range(NCH):
            for tb in range(NB):
                b, bk_ = cx.bank()
                for j in range(2):
                    p.op('pe', I_mm(b[:], ws[:, 2048 + j * 1024 + n * 128: 2048 + j * 1024 + (n + 1) * 128],
                                    big2[:, hb * 2 + j, tb * 512:(tb + 1) * 512], j == 0, j == 1),
                         r=[wsk, 'hid%d' % hb], w=[bk_])
                hs = hT[:, n, (tb0 + tb) * 512:(tb0 + tb + 1) * 512]
                p.op('dve', I_tt(hs, b[:], hs, ALU.add), r=[bk_, hk(n, tb0 + tb)], w=[hk(n, tb0 + tb)])
    nsl = DFF // 256
    prev = None
    for s in range(nsl):
        ws, wsk = ss.get()
        mlp_w1(s, ws, wsk)
        if prev is not None:
            mlp_w2(*prev)
        prev = (s, ws, wsk)
    mlp_w2(*prev)


def _rmsnorm_off(cx, src, s0, skey, gcols, dst, d0, dkey, n=512, C=NCH, dim=D):
    p = cx.p
    bank, bk = cx.bank()
    for c in range(C):
        sq, sk = cx.next_sq()
        p.op('act', I_act(sq[:, :n], src[:, c, s0:s0 + n], AF.Square), r=[skey(c)], w=[sk])
        p.op('pe', I_mm(bank[:, :n], cx.ones[:], sq[:, :n], c == 0, c == C - 1), r=[sk, 'ones'], w=[bk])
    rs, rk = cx.rstd_from_bank(bank, bk, n, dim)
    for c in range(C):
        p.op('dve', I_stt(dst[:, c, d0:d0 + n], src[:, c, s0:s0 + n], gcols[:, c:c + 1], rs[:, :n], ALU.mult, ALU.mult),
             r=[skey(c), 'vec', rk], w=[dkey])


def common_tiles(cx, A):
    p = cx.p
    cx.hT = p.sb('hT', [128, NCH, NT], F32)
    cx.hn = p.sb('hn', [128, NCH, 1024], BF16)
    cx.big2 = p.sb('big2', [128, NCH, 1024], BF16)
    cx.vec = p.sb('vec', [128, NVEC], F32)
    cx.kraw = p.sb('kraw', [128, NCH, MEMLEN], F32)
    cx.memn = p.sb('memn', [128, NCH, MEMLEN], BF16)
    cx.KT = p.sb('KT', [128, NCH, MEMLEN], BF16)
    cx.V = p.sb('V', [128, 2, D], BF16)
    p.dma('sp', I_dma(cx.vec[:], A['vecs'][:, :]), w=['vec'])


def load_hT(cx, src):
    p = cx.p
    for c in range(NCH):
        p.dma('sp', I_dma(cx.hT[:, c, :], src[c * 128:(c + 1) * 128, :]),
              w=['h%d_%d' % (c, tb) for tb in range(NT // 512)], semkey='hld%d' % c)


def store_hT(cx, dst):
    p = cx.p
    for c in range(NCH):
        p.dma('sp', I_dma(dst[c * 128:(c + 1) * 128, :], cx.hT[:, c, :]),
              r=['h%d_%d' % (c, tb) for tb in range(NT // 512)], w=['hout%d' % c])


def build_tail(glu, emit_hn, arena=False):
    nc = bass.Bass("TRN2", target_bir_lowering=False)
    A = {}

    def inp(name, shape, dt=F32):
        A[name] = nc.dram_tensor(name, list(shape), dt, kind="ExternalInput").ap()

    inp('hT', [D, NT])
    inp('memT', [D, MEMLEN])
    inp('vecs', [128, NVEC])
    inp('wq', [D, D])
    inp('wkv', [D, 2 * D])
    inp('wo', [D, D])
    inp('w1', [D, DFF])
    inp('w2', [DFF, D])
    if glu:
        inp('yT', [D, NT])
        inp('wglu', [D, 2 * D])
    A['hT_out'] = nc.dram_tensor('hT_out', [D, NT], F32, kind="ExternalOutput").ap()
    if emit_hn:
        A['hn_out'] = nc.dram_tensor('hn_out', [D, NT], F32, kind="ExternalOutput").ap()
    cx = Cx(nc, arena=arena)
    if arena:
        cx.new_stage()
    common_tiles(cx, A)
    load_hT(cx, A['hT'])
    for half in range(2):
        tail_body(cx, A, glu, half * 2, 2, kv_ready=(half == 1))
    store_hT(cx, A['hT_out'])
    if emit_hn:
        emit_norm(cx, A['hn_out'])
    cx.p.finalize()
    return nc, cx


def emit_norm(cx, dst):
    p = cx.p
    for tb in range(NT // 512):
        bank, bk = cx.bank()
        for c in range(NCH):
            sq, sk = cx.next_sq()
            p.op('act', I_act(sq[:], cx.hT[:, c, tb * 512:(tb + 1) * 512], AF.Square), r=['h%d_%d' % (c, tb)], w=[sk])
            p.op('pe', I_mm(bank[:], cx.ones[:], sq[:], c == 0, c == NCH - 1), r=[sk, 'ones'], w=[bk])
        rs, rk = cx.rstd_from_bank(bank, bk, 512, D)
        for c in range(NCH):
            ot, otk = cx.rot('ntmp', [128, 512], F32, n=3)
            p.op('dve', I_stt(ot[:], cx.hT[:, c, tb * 512:(tb + 1) * 512], cx.vec[:, VC['gn'] + c:VC['gn'] + c + 1], rs[:],
                              ALU.mult, ALU.mult), r=['h%d_%d' % (c, tb), 'vec', rk], w=[otk])
            drow = dst.src_rows(c * 128) if hasattr(dst, 'src_rows') else dst[c * 128:(c + 1) * 128, :]
            p.dma('sp', I_dma(drow[:, tb * 512:(tb + 1) * 512], ot[:]), r=[otk], w=['hnout'])


NPT = 16
SW = 512
NW = SEQ // SW
PI = math.pi


def s5_params(cx, A):
    p = cx.p
    NCOL = 2 * NPT
    T = {}
    for nm in ['lre', 'lim', 'ldt', 'dt', 'mag', 'ang', 'angc', 's1', 'c1', 'are', 'aim', 'nr', 'den', 't', 't2',
               'fre', 'fim', 'nfre', 'nfim']:
        T[nm] = p.sb('sp_' + nm, [128, NCOL], F32)
    k = 's5par'
    p.dma('sp', I_dma(T['lre'][:], A['lamre'][:, :]), w=[k], semkey='s5par_ld')
    p.dma('sp', I_dma(T['lim'][:], A['lamim'][:, :]), w=[k], semkey='s5par_ld')
    p.dma('sp', I_dma(T['ldt'][:], A['logdt'][:, :]), w=[k], semkey='s5par_ld')
    a = lambda n: T[n][:]
    p.op('act', I_act(a('dt'), a('ldt'), AF.Exp), r=[k], w=[k])
    p.op('dve', I_tt(a('t'), a('lre'), a('dt'), ALU.mult), r=[k], w=[k])
    p.op('act', I_act(a('mag'), a('t'), AF.Exp), r=[k], w=[k])
    p.op('dve', I_tt(a('ang'), a('lim'), a('dt'), ALU.mult), r=[k], w=[k])
    for _ in range(5):
        p.op('dve', I_ts(a('t'), a('ang'), PI, 2 * PI, ALU.is_gt, ALU.mult), r=[k], w=[k])
        p.op('dve', I_tt(a('ang'), a('ang'), a('t'), ALU.subtract), r=[k], w=[k])
    p.op('dve', I_ts(a('angc'), a('ang'), PI / 2, None, ALU.add), r=[k], w=[k])
    p.op('dve', I_ts(a('t'), a('angc'), PI, 2 * PI, ALU.is_gt, ALU.mult), r=[k], w=[k])
    p.op('dve', I_tt(a('angc'), a('angc'), a('t'), ALU.subtract), r=[k], w=[k])
    p.op('act', I_act(a('s1'), a('ang'), AF.Sin), r=[k], w=[k])
    p.op('act', I_act(a('c1'), a('angc'), AF.Sin), r=[k], w=[k])
    p.op('dve', I_tt(a('are'), a('mag'), a('c1'), ALU.mult), r=[k], w=[k])
    p.op('dve', I_tt(a('aim'), a('mag'), a('s1'), ALU.mult), r=[k], w=[k])
    p.op('dve', I_ts(a('nr'), a('are'), -1.0, None, ALU.add), r=[k], w=[k])
    p.op('dve', I_tt(a('den'), a('lre'), a('lre'), ALU.mult), r=[k], w=[k])
    p.op('dve', I_tt(a('t'), a('lim'), a('lim'), ALU.mult), r=[k], w=[k])
    p.op('dve', I_tt(a('den'), a('den'), a('t'), ALU.add), r=[k], w=[k])
    p.op('dve', I_recip(a('den'), a('den')), r=[k], w=[k])
    p.op('dve', I_tt(a('t'), a('nr'), a('lre'), ALU.mult), r=[k], w=[k])
    p.op('dve', I_tt(a('t2'), a('aim'), a('lim'), ALU.mult), r=[k], w=[k])
    p.op('dve', I_tt(a('t'), a('t'), a('t2'), ALU.add), r=[k], w=[k])
    p.op('dve', I_tt(a('fre'), a('t'), a('den'), ALU.mult), r=[k], w=[k])
    p.op('dve', I_tt(a('t'), a('aim'), a('lre'), ALU.mult), r=[k], w=[k])
    p.op('dve', I_tt(a('t2'), a('nr'), a('lim'), ALU.mult), r=[k], w=[k])
    p.op('dve', I_tt(a('t'), a('t'), a('t2'), ALU.subtract), r=[k], w=[k])
    p.op('dve', I_tt(a('fim'), a('t'), a('den'), ALU.mult), r=[k], w=[k])
    p.op('dve', I_ts(a('nfre'), a('fre'), -1.0, None, ALU.mult), r=[k], w=[k])
    p.op('dve', I_ts(a('nfim'), a('fim'), -1.0, None, ALU.mult), r=[k], w=[k])
    nlv = int(math.log2(SW))
    T['pwc'] = p.sb('sp_pwc', [128, nlv + 1, NCOL], F32)
    T['pws'] = p.sb('sp_pws', [128, nlv + 1, NCOL], F32)
    T['npws'] = p.sb('sp_npws', [128, NCOL], F32)
    p.op('dve', I_copy(T['pwc'][:, 0, :], a('c1')), r=[k], w=[k])
    p.op('dve', I_copy(T['pws'][:, 0, :], a('s1')), r=[k], w=[k])
    for lv in range(nlv):
        c_ = T['pwc'][:, lv, :]
        s_ = T['pws'][:, lv, :]
        p.op('dve', I_tt(a('t'), s_, s_, ALU.mult), r=[k], w=[k])
        p.op('dve', I_tt(a('t2'), c_, c_, ALU.mult), r=[k], w=[k])
        p.op('dve', I_tt(T['pwc'][:, lv + 1, :], a('t2'), a('t'), ALU.subtract), r=[k], w=[k])
        p.op('dve', I_stt(T['pws'][:, lv + 1, :], c_, 2.0, s_, ALU.mult, ALU.mult), r=[k], w=[k])
    p.op('dve', I_ts(T['npws'][:], T['pws'][:, nlv, :], -1.0, None, ALU.mult), r=[k], w=[k])
    return T


def s5_body(cx, A):
    p = cx.p
    T = s5_params(cx, A)
    PK = 's5par'
    if 'dbg' in A:
        for i, nm in enumerate(['dt', 'mag', 'ang', 's1', 'c1', 'fre', 'fim', 'den']):
            p.dma('sp', I_dma(A['dbg'][:, i * 2 * NPT:(i + 1) * 2 * NPT], T[nm][:]), r=[PK], w=['dbgo'])
    ub = p.sb('ub', [128, 4, SEQ], BF16)
    if 'GHN' in A:
        G = A['GHN']
        m0c = cx.vec[:, VC['m0']:VC['m0'] + 1]
        m1c = cx.vec[:, VC['m1']:VC['m1'] + 1]
        for ck in range(4):
            for w in range(NW):
                r = 0 if w < NW // 2 else 1
                if r == 0:
                    cols = slice(w * SW, (w + 1) * SW)
                else:
                    w2 = w - NW // 2
                    cols = slice(NT - (w2 + 1) * SW, NT - w2 * SW)
                X, xk = cx.rot('gx', [128, SW], F32, n=2)
                Z, zk = cx.rot('gz', [128, SW], F32, n=2)
                p.dma('sp', I_dma(X[:], G.g_rows(r, ck * 128)[:, cols]), w=[xk])
                p.dma('sp', I_dma(Z[:], G.g_rows(r, 512 + ck * 128)[:, cols]), w=[zk])
                dst = ub[:, ck, w * SW:(w + 1) * SW]
                if r == 1:
                    dst = dst[:, ::-1]
                mcombine(cx, dst, 'ub%d' % ck, X[:], xk, Z[:], zk, m0c if r == 0 else m1c, m1c if r == 0 else m0c)
    else:
        for ck in range(4):
            p.dma('pool', I_dma(ub[:, ck, :], A['uT'][ck * 128:(ck + 1) * 128, :]), w=['ub%d' % ck])
    ident = p.sb('ident', [128, 128], F32)
    p.dma('sp', I_dma(ident[:], A['ident'][:, :]), w=['ident'])
    dsk = p.sb('dskc', [128, 4], F32)
    p.dma('sp', I_dma(dsk[:], A['dsk'][:, :]), w=['dskc'])
    yacc = [p.sb('yacc%d' % i, [128, SEQ], F32) for i in range(2)]
    bb_i = [0]

    def bbank():
        i = bb_i[0]
        bb_i[0] = (i + 1) % 6
        return cx.banks[i], 'bank%d' % i
    yb_i = [0]

    def ybank():
        i = 6 + yb_i[0]
        yb_i[0] = 1 - yb_i[0]
        return cx.banks[i], 'bank%d' % i

    for ck in range(4):
        ya = yacc[ck % 2]
        yk = 'yacc%d' % (ck % 2)
        dD, dDk = cx.rot('diagD', [128, 128], BF16)
        p.op('dve', I_ts(dD[:], ident[:], dsk[:, ck:ck + 1], None, ALU.mult), r=['ident', 'dskc'], w=[dDk])
        for d in range(2):
            tabs = []
            col0 = d * NPT + ck * 4
            cos4, c4k = cx.rot('cos4', [128, 4, SW], F32, n=2)
            sin4, s4k = cx.rot('sin4', [128, 4, SW], F32, n=2)
            tk = c4k
            p.op('dve', I_memset(cos4[:, :, 0:1], 1.0), w=[tk])
            p.op('dve', I_memset(sin4[:, :, 0:1], 0.0), w=[tk])
            L = 1
            lv = 0
            while L < SW:
                pcb = T['pwc'][:, lv, col0:col0 + 4].unsqueeze(2).to_broadcast([128, 4, L])
                psb = T['pws'][:, lv, col0:col0 + 4].unsqueeze(2).to_broadcast([128, 4, L])
                ta, tak = cx.rot('tbA', [128, 4, SW // 2], F32, n=1)
                tb2, tbk = cx.rot('tbB', [128, 4, SW // 2], F32, n=1)
                p.op('dve', I_tt(ta[:, :, :L], sin4[:, :, 0:L], psb, ALU.mult), r=[tk, PK], w=[tak])
                p.op('dve', I_tt(tb2[:, :, :L], cos4[:, :, 0:L], pcb, ALU.mult), r=[tk, PK], w=[tbk])
                p.op('dve', I_tt(cos4[:, :, L:2 * L], tb2[:, :, :L], ta[:, :, :L], ALU.subtract), r=[tak, tbk], w=[tk])
                p.op('dve', I_tt(ta[:, :, :L], cos4[:, :, 0:L], psb, ALU.mult), r=[tk, PK], w=[tak])
                p.op('dve', I_tt(tb2[:, :, :L], sin4[:, :, 0:L], pcb, ALU.mult), r=[tk, PK], w=[tbk])
                p.op('dve', I_tt(sin4[:, :, L:2 * L], tb2[:, :, :L], ta[:, :, :L], ALU.add), r=[tak, tbk], w=[tk])
                L *= 2
                lv += 1
            for q in range(4):
                pt = ck * 4 + q
                col = d * NPT + pt
                cosT = cos4[:, q, :]
                sinT = sin4[:, q, :]
                cW = T['pwc'][:, lv, col:col + 1]
                sW = T['pws'][:, lv, col:col + 1]
                nsW = T['npws'][:, col:col + 1]
                braw, brk = cx.rot('bw', [128, 2, 128], BF16, n=8)
                p.dma('pool', I_dma(braw[:, 0, :], A['Bre'][d, pt]), w=[brk])
                p.dma('pool', I_dma(braw[:, 1, :], A['Bim'][d, pt]), w=[brk])
                craw, crk = cx.rot('craw', [128, 2, 128], F32, n=2)
                p.dma('sp', I_dma(craw[:, 0, :], A['CR'][d, pt]), w=[crk])
                p.dma('sp', I_dma(craw[:, 1, :], A['CI'][d, pt]), w=[crk])
                cw, cwk = cx.rot('cw', [128, 3, 128], BF16, n=8)
                ctmp, ctk = cx.rot('ctmp', [128, 128], F32, n=2)
                fre = T['fre'][:, col:col + 1]
                nfim = T['nfim'][:, col:col + 1]
                nfre = T['nfre'][:, col:col + 1]
                p.op('dve', I_ts(ctmp[:], craw[:, 1, :], nfim, None, ALU.mult), r=[crk, PK], w=[ctk])
                p.op('dve', I_stt(cw[:, 0, :], craw[:, 0, :], fre, ctmp[:], ALU.mult, ALU.add), r=[crk, PK, ctk], w=[cwk])
                ctmp2, ctk2 = cx.rot('ctmp', [128, 128], F32, n=2)
                p.op('dve', I_ts(ctmp2[:], craw[:, 0, :], nfim, None, ALU.mult), r=[crk, PK], w=[ctk2])
                p.op('dve', I_stt(cw[:, 1, :], craw[:, 1, :], nfre, ctmp2[:], ALU.mult, ALU.add), r=[crk, PK, ctk2], w=[cwk])
                ctmp3, ctk3 = cx.rot('ctmp', [128, 128], F32, n=2)
                p.op('dve', I_ts(ctmp3[:], craw[:, 1, :], T['fim'][:, col:col + 1], None, ALU.mult), r=[crk, PK], w=[ctk3])
                p.op('dve', I_stt(cw[:, 2, :], craw[:, 0, :], nfre, ctmp3[:], ALU.mult, ALU.add), r=[crk, PK, ctk3], w=[cwk])
                car, cak = cx.rot('carry', [128, 8], F32, n=8)
                tabs.append(dict(cos=cosT, sin=sinT, tk=tk, cW=cW, sW=sW, nsW=nsW, braw=braw, brk=brk, cw=cw, cwk=cwk,
                                 r=T['mag'][:, col:col + 1], car=car, cak=cak))
            worder = range(NW) if d == 0 else range(NW - 1, -1, -1)
            rv = (lambda ap: ap) if d == 0 else (lambda ap: ap[:, ::-1])
            units = [dict(wi=wi, w=w, q=q) for wi, w in enumerate(worder) for q in range(4)]
            ybs = {}

            def P01(u):
                tb_ = tabs[u['q']]
                win = slice(u['w'] * SW, (u['w'] + 1) * SW)
                bre, brek = bbank()
                bim, bimk = bbank()
                p.op('pe', I_mm(bre[:], tb_['braw'][:, 0, :], ub[:, ck, win], True, True), r=[tb_['brk'], 'ub%d' % ck], w=[brek])
                p.op('pe', I_mm(bim[:], tb_['braw'][:, 1, :], ub[:, ck, win], True, True), r=[tb_['brk'], 'ub%d' % ck], w=[bimk])
                cosT, sinT, tk = tb_['cos'], tb_['sin'], tb_['tk']
                t1, t1k = cx.rot('t1', [128, SW], F32)
                t2, t2k = cx.rot('t2', [128, SW], F32)
                t3, t3k = cx.rot('t3', [128, SW], F32)
                t4, t4k = cx.rot('t4', [128, SW], F32)
                p.op('dve', I_tt(t1[:], rv(bre[:]), cosT, ALU.mult), r=[brek, tk], w=[t1k])
                p.op('dve', I_tt(t2[:], rv(bim[:]), sinT, ALU.mult), r=[bimk, tk], w=[t2k])
                p.op('dve', I_tt(t3[:], rv(bim[:]), cosT, ALU.mult), r=[bimk, tk], w=[t3k])
                p.op('dve', I_tt(t4[:], rv(bre[:]), sinT, ALU.mult), r=[brek, tk], w=[t4k])
                u.update(t=(t1, t1k, t2, t2k, t3, t3k, t4, t4k))

            def P2(u):
                t1, t1k, t2, t2k, t3, t3k, t4, t4k = u['t']
                wre, wrk = cx.rot('wre', [128, SW], F32)
                wim, wik = cx.rot('wim', [128, SW], F32)
                p.op('pool', I_tt(wre[:], t1[:], t2[:], ALU.add), r=[t1k, t2k], w=[wrk])
                p.op('pool', I_tt(wim[:], t3[:], t4[:], ALU.subtract), r=[t3k, t4k], w=[wik])
                u.update(wv=(wre, wrk, wim, wik))

            def P3(u):
                tb_ = tabs[u['q']]
                tk = tb_['tk']
                wre, wrk, wim, wik = u['wv']
                zre, zrk = cx.rot('zre', [128, SW], F32)
                zim, zik = cx.rot('zim', [128, SW], F32)
                car, cak = tb_['car'], tb_['cak']
                rbc = tb_['r'].to_broadcast([128, SW])
                if u['wi'] == 0:
                    ire, iim = 0.0, 0.0
                else:
                    ire, iim = car[:, 2:3], car[:, 3:4]
                p.op('dve', I_scan(zre[:], rbc, wre[:], ire), r=[PK, wrk, cak], w=[zrk])
                p.op('dve', I_scan(zim[:], rbc, wim[:], iim), r=[PK, wik, cak], w=[zik])
                if u['wi'] < NW - 1:
                    p.op('act', I_act(car[:, 0:1], zim[:, SW - 1:SW], AF.Copy, scale=tb_['nsW']), r=[zik, PK], w=[cak])
                    p.op('act', I_act(car[:, 1:2], zre[:, SW - 1:SW], AF.Copy, scale=tb_['sW']), r=[zrk, PK], w=[cak])
                    p.op('act', I_act(car[:, 2:3], zre[:, SW - 1:SW], AF.Identity, scale=tb_['cW'], bias=car[:, 0:1]), r=[zrk, PK], w=[cak])
                    p.op('act', I_act(car[:, 3:4], zim[:, SW - 1:SW], AF.Identity, scale=tb_['cW'], bias=car[:, 1:2]), r=[zik, PK], w=[cak])
                u.update(z=(zre, zrk, zim, zik))

            def P4(u):
                tb_ = tabs[u['q']]
                cosT, sinT, tk = tb_['cos'], tb_['sin'], tb_['tk']
                zre, zrk, zim, zik = u['z']
                u1, u1k = cx.rot('u1', [128, SW], BF16, n=3)
                u2, u2k = cx.rot('u2', [128, SW], BF16, n=3)
                u3, u3k = cx.rot('u3', [128, SW], BF16, n=3)
                u4, u4k = cx.rot('u4', [128, SW], BF16, n=3)
                p.op('pool', I_tt(rv(u1[:]), zre[:], cosT, ALU.mult), r=[zrk, tk], w=[u1k])
                p.op('pool', I_tt(rv(u2[:]), zim[:], sinT, ALU.mult), r=[zik, tk], w=[u2k])
                p.op('pool', I_tt(rv(u3[:]), zim[:], cosT, ALU.mult), r=[zik, tk], w=[u3k])
                p.op('dve', I_tt(rv(u4[:]), zre[:], sinT, ALU.mult), r=[zrk, tk], w=[u4k])
                u.update(uu=(u1, u1k, u2, u2k, u3, u3k, u4, u4k))

            def P5(u):
                pass

            def P6(u):
                tb_ = tabs[u['q']]
                w, q = u['w'], u['q']
                win = slice(w * SW, (w + 1) * SW)
                if q == 0:
                    ybs[w] = ybank()
                yb, ybk = ybs[w]
                u1, u1k, u2, u2k, u3, u3k, u4, u4k = u['uu']
                first = (q == 0)
                last = (q == 3) and d == 1
                p.op('pe', I_mm(yb[:], tb_['cw'][:, 0, :], u1[:], first, False), r=[tb_['cwk'], u1k], w=[ybk])
                p.op('pe', I_mm(yb[:], tb_['cw'][:, 2, :], u2[:], False, False), r=[tb_['cwk'], u2k], w=[ybk])
                p.op('pe', I_mm(yb[:], tb_['cw'][:, 1, :], u3[:], False, False), r=[tb_['cwk'], u3k], w=[ybk])
                p.op('pe', I_mm(yb[:], tb_['cw'][:, 1, :], u4[:], False, last), r=[tb_['cwk'], u4k], w=[ybk])
                if q == 3:
                    if d == 0:
                        p.op('pe', I_mm(yb[:], dD[:], ub[:, ck, win], False, True), r=[dDk, 'ub%d' % ck], w=[ybk])
                        p.op('act', I_act(ya[:, win], yb[:], AF.Copy), r=[ybk], w=[yk + '_%d' % w])
                    else:
                        p.op('dve', I_tt(ya[:, win], yb[:], ya[:, win], ALU.add), r=[ybk, yk + '_%d' % w], w=[yk + '_%d' % w])
            nu = len(units)
            for step in range(nu + 2):
                if step < nu:
                    P01(units[step])
                    P2(units[step])
                if 0 <= step - 1 < nu:
                    P3(units[step - 1])
                    P4(units[step - 1])
                if 0 <= step - 2 < nu:
                    P5(units[step - 2])
                    P6(units[step - 2])
        if 'yO' in A:
            m0c = cx.vec[:, VC['m0']:VC['m0'] + 1]
            m1c = cx.vec[:, VC['m1']:VC['m1'] + 1]
            for hb in range(NT // SW):
                A1 = ya[:, hb * SW:(hb + 1) * SW]
                B1 = ya[:, SEQ - (hb + 1) * SW:SEQ - hb * SW][:, ::-1]
                ka = yk + '_%d' % hb
                kb = yk + '_%d' % (NW - 1 - hb)
                ot, otk = cx.rot('yo_t', [128, SW], F32, n=2)
                mcombine(cx, ot[:], otk, A1, ka, B1, kb, m0c, m1c)
                p.dma('sp', I_dma(A['yO'][ck * 128:(ck + 1) * 128, hb * SW:(hb + 1) * SW], ot[:]), r=[otk], w=['yout%d' % ck])
                st_, stk_ = cx.rot('yo_t', [128, SW], F32, n=2)
                mcombine(cx, st_[:], stk_, A1, ka, B1, kb, m1c, m0c)
                p.dma('sp', I_dma(A['yS'].src_rows(ck * 128)[:, hb * SW:(hb + 1) * SW], st_[:]), r=[stk_], w=['yout%d' % ck])
        else:
            p.dma('sp', I_dma(A['yT'][ck * 128:(ck + 1) * 128, :], ya[:]), r=[yk + '_%d' % w for w in range(NW)], w=['yout%d' % ck])


def build_s5(debug=False, arena=False):
    nc = bass.Bass("TRN2", target_bir_lowering=False)
    A = {}

    def inp(name, shape, dt=F32):
        A[name] = nc.dram_tensor(name, list(shape), dt, kind="ExternalInput").ap()
    inp('uT', [512, SEQ])
    inp('Bre', [2, NPT, 128, 128])
    inp('Bim', [2, NPT, 128, 128])
    inp('CR', [2, NPT, 128, 128])
    inp('CI', [2, NPT, 128, 128])
    inp('lamre', [128, 2 * NPT])
    inp('lamim', [128, 2 * NPT])
    inp('logdt', [128, 2 * NPT])
    inp('dsk', [128, 4])
    inp('ident', [128, 128])
    A['yT'] = nc.dram_tensor('yT', [512, SEQ], F32, kind="ExternalOutput").ap()
    cx = Cx(nc, arena=arena)
    if arena:
        cx.new_stage()
    if debug:
        A['dbg'] = nc.dram_tensor('dbg', [128, 8 * 2 * NPT], F32, kind="ExternalOutput").ap()
        A['dbg2'] = nc.dram_tensor('dbg2', [128, 2 * SW], F32, kind="ExternalOutput").ap()
    s5_body(cx, A)
    cx.p.finalize()
    return nc, cx


def s5_host_inputs(inp, j, half):
    g0 = 32 * half
    Bre = np.zeros((2, NPT, 128, 128), np.float32)
    Bim = np.zeros_like(Bre)
    CR = np.zeros_like(Bre)
    CI = np.zeros_like(Bre)
    lamre = np.zeros((128, 2 * NPT), np.float32)
    lamim = np.zeros_like(lamre)
    logdt = np.zeros_like(lamre)
    for d in range(2):
        for pt in range(NPT):
            for gl in range(2):
                g = g0 + 2 * pt + gl
                r0 = (pt % 4) * 32 + gl * 16
                Bre[d, pt, r0:r0 + 16, gl * 64:(gl + 1) * 64] = inp['s5_b_re'][j, d, g].T
                Bim[d, pt, r0:r0 + 16, gl * 64:(gl + 1) * 64] = inp['s5_b_im'][j, d, g].T
                CR[d, pt, gl * 64:(gl + 1) * 64, r0:r0 + 16] = inp['s5_c_re'][j, d, g].T
                CI[d, pt, gl * 64:(gl + 1) * 64, r0:r0 + 16] = inp['s5_c_im'][j, d, g].T
                lamre[gl * 64:(gl + 1) * 64, d * NPT + pt] = inp['s5_lambda_re'][j, d, g]
                lamim[gl * 64:(gl + 1) * 64, d * NPT + pt] = inp['s5_lambda_im'][j, d, g]
                logdt[gl * 64:(gl + 1) * 64, d * NPT + pt] = inp['s5_log_dt'][j, d, g]
    dsk = np.ascontiguousarray(inp['s5_d'][j, 512 * half:512 * half + 512].reshape(4, 128).T)
    return dict(Bre=Bre, Bim=Bim, CR=CR, CI=CI, lamre=lamre, lamim=lamim, logdt=logdt, dsk=dsk,
                ident=np.eye(128, dtype=np.float32))


NEXT = 3072
GRP = [(1, 2048), (4, 512), (16, 128)]
ASCALE = 128 ** -0.5


def sub_view(ap2d, d):
    if d == 1:
        return ap2d.rearrange("p (d i) -> p d i", d=1)
    return ap2d.rearrange("p (i d) -> p d i", d=d)


def attn_body(cx, A, flip=False, src=None, dst=None, gh=None):
    p = cx.p
    vec = cx.vec

    def load_blk(xt, xk, c, tb):
        if gh is None or tb < NT // 512:
            p.dma('sp', I_dma(xt[:], _src[c * 128:(c + 1) * 128, _cols(tb)]), w=[xk])
            return
        hb = tb - NT // 512
        cols = slice(1024 - 512 * (hb + 1), 1024 - 512 * hb)
        pc = (c + 4) % 8
        X, xk2 = cx.rot('gx', [128, 512], F32, n=2)
        Z, zk2 = cx.rot('gz', [128, 512], F32, n=2)
        p.dma('sp', I_dma(X[:], gh.g_rows(0, pc * 128)[:, cols]), w=[xk2])
        p.dma('sp', I_dma(Z[:], gh.g_rows(1, pc * 128)[:, cols]), w=[zk2])
        mcombine(cx, xt[:], xk, X[:], xk2, Z[:], zk2, vec[:, VC['m1']:VC['m1'] + 1], vec[:, VC['m0']:VC['m0'] + 1])

    def _cols(tb):
        if src is None or not flip:
            return slice(tb * 512, (tb + 1) * 512)
        return slice(SEQ - (tb + 1) * 512, SEQ - tb * 512)
    _src = A['hT_ext'] if src is None else src
    _dst = A['hT_out'] if dst is None else dst
    rvf = (lambda ap: ap[:, ::-1]) if (flip and src is not None) else (lambda ap: ap)
    hn = p.sb('hnx', [128, NCH, NEXT], BF16)
    mT = p.sb('mT', [128, NCH, NT], BF16)
    num = p.sb('numacc', [128, NT], F32)
    den = p.sb('denacc', [128, NT], F32)
    for tb in range(NEXT // 512):
        bank, bk = cx.bank()
        for c in range(NCH):
            xt, xk = cx.rot('xin', [128, 512], F32, n=2)
            load_blk(xt, xk, c, tb)
            sq, sk = cx.next_sq()
            p.op('act', I_act(sq[:], xt[:], AF.Square), r=[xk], w=[sk])
            p.op('pe', I_mm(bank[:], cx.ones[:], sq[:], c == 0, c == NCH - 1), r=[sk, 'ones'], w=[bk])
        rs, rk = cx.rstd_from_bank(bank, bk, 512, D)
        for c in range(NCH):
            xt, xk = cx.rot('xin', [128, 512], F32, n=2)
            load_blk(xt, xk, c, tb)
            rv_ = (lambda ap: ap[:, ::-1]) if (gh is not None and tb >= NT // 512) else rvf
            p.op('dve', I_stt(rv_(hn[:, c, tb * 512:(tb + 1) * 512]), xt[:], vec[:, VC['gn'] + c:VC['gn'] + c + 1], rs[:],
                              ALU.mult, ALU.mult), r=[xk, 'vec', rk], w=['hnx'])
    sb_i = [0]

    def sbank():
        i = sb_i[0]
        sb_i[0] = (i + 1) % 4
        return cx.banks[i], 'bank%d' % i
    ob_i = [0]

    def obanks():
        i = ob_i[0]
        ob_i[0] = 1 - i
        return cx.banks[4 + i], 'bank%d' % (4 + i), cx.banks[6 + i], 'bank%d' % (6 + i)

    def qknorm(bank, bk, n, gcol, dst, dkey):
        sq, sk = cx.next_sq()
        p.op('act', I_act(sq[:, :n], bank[:, :n], AF.Square), r=[bk], w=[sk])
        b2, b2k = sbank()
        p.op('pe', I_mm(b2[:, :n], cx.ones[:], sq[:, :n], True, True), r=[sk, 'ones'], w=[b2k])
        rs, rk = cx.rstd_from_bank(b2, b2k, n, 128)
        p.op('dve', I_stt(dst, bank[:, :n], vec[:, gcol:gcol + 1], rs[:, :n], ALU.mult, ALU.mult), r=[bk, 'vec', rk], w=[dkey])

    wq_loaded = {}
    bias_loaded = {}

    def load_w(h_, g_):
        if (h_, g_) in wq_loaded or h_ >= 8:
            return
        wsl_, wsk_ = cx.rot('wqkv', [128, NCH, 384], BF16, n=3)
        for kind in range(3):
            c0 = kind * 3072 + g_ * 1024 + h_ * 128
            p.dma('pool', I_dma(wsl_[:, :, kind * 128:(kind + 1) * 128],
                                A['wqkv'][:, c0:c0 + 128].rearrange("(k p) n -> p k n", p=128)), w=[wsk_])
        wq_loaded[(h_, g_)] = (wsl_, wsk_)

    def load_bias(h_):
        if h_ in bias_loaded or h_ >= 8:
            return
        bt_, btk_ = cx.rot('biasT', [128, 3, 256], F32, n=2)
        for g_ in range(3):
            p.dma('sp', I_dma(bt_[:, g_, :], A['biasT'][g_ * 8 + h_]), w=[btk_])
        bias_loaded[h_] = (bt_, btk_)

    for h in range(8):
        p.op('pool', I_memset(num[:], 0.0), w=['numacc'])
        p.op('pool', I_memset(den[:], 0.0), w=['denacc'])
        load_bias(h)
        bt, btk = bias_loaded[h]
        for g, (d, Lq) in enumerate(GRP):
            nto = Lq // 128
            load_w(h, g)
            wsl, wsk = wq_loaded[(h, g)]
            load_w(h + (g + 1) // 3, (g + 1) % 3)
            if g == 0:
                load_bias(h + 1)
            qT, qk_ = cx.rot('qT', [128, NT], BF16, n=2)
            kT, kk_ = cx.rot('kT', [128, NEXT], BF16, n=2)
            vt, vk_ = cx.rot('vt', [128, 32, 128], BF16, n=2)

            for kind, dstT, dk, gcol in ((0, qT, qk_, VC['aq']), (1, kT, kk_, VC['ak'])):
                for bi in range(4):
                    b, bk = sbank()
                    for kc in range(NCH):
                        if d == 1:
                            rhs, o_ap = hn[:, kc, bi * 512:(bi + 1) * 512], b[:]
                        elif d == 4:
                            rhs, o_ap = sub_view(hn[:, kc, 0:NT], 4)[:, bi, :], b[:]
                        else:
                            rhs = sub_view(hn[:, kc, 0:NT], 16)[:, 4 * bi:4 * bi + 4, :]
                            o_ap = b[:].rearrange("p (a b) -> p a b", a=4)
                        p.op('pe', I_mm(o_ap, wsl[:, kc, kind * 128:(kind + 1) * 128], rhs, kc == 0, kc == NCH - 1),
                             r=[wsk, 'hnx'], w=[bk])
                    qknorm(b, bk, 512, gcol, dstT[:, bi * 512:(bi + 1) * 512], dk)
            nh = 64 * d
            for b0 in range(0, nh, 512):
                n = min(512, nh - b0)
                b, bk = sbank()
                for kc in range(NCH):
                    if d == 1:
                        rhs = hn[:, kc, NT:NT + 64]
                        o_ap = b[:, :64]
                    else:
                        r0 = b0 // 64
                        nr = n // 64
                        rhs = sub_view(hn[:, kc, NT:NT + 64 * d], d)[:, r0:r0 + nr, :]
                        o_ap = b[:, :n].rearrange("p (a b) -> p a b", a=nr)
                    p.op('pe', I_mm(o_ap, wsl[:, kc, 128:256], rhs, kc == 0, kc == NCH - 1), r=[wsk, 'hnx'], w=[bk])
                qknorm(b, bk, n, VC['ak'], kT[:, NT + b0:NT + b0 + n], kk_)
            for t0 in range(0, 16, 4):
                b, bk = sbank()
                for tt in range(4):
                    t = t0 + tt
                    r, m = t // nto, t % nto
                    for kc in range(NCH):
                        lhsT = sub_view(hn[:, kc, 0:NT], d)[:, r, m * 128:(m + 1) * 128]
                        p.op('pe', I_mm(b[:, tt * 128:(tt + 1) * 128], lhsT, wsl[:, kc, 256:384], kc == 0, kc == NCH - 1),
                             r=[wsk, 'hnx'], w=[bk])
                p.op('act', I_act(vt[:, t0:t0 + 4, :], b[:].rearrange("p (a b) -> p a b", a=4), AF.Copy), r=[bk], w=[vk_])
            for r0 in range(0, d, 4):
                nr = min(4, d - r0)
                b, bk = sbank()
                for rr in range(nr):
                    r = r0 + rr
                    for kc in range(NCH):
                        lhsT = sub_view(hn[:, kc, NT:NT + 64 * d], d)[:, r, :]
                        p.op('pe', I_mm(b[:64, rr * 128:(rr + 1) * 128], lhsT, wsl[:, kc, 256:384], kc == 0, kc == NCH - 1),
                             r=[wsk, 'hnx'], w=[bk])
                p.op('act', I_act(vt[:64, 16 + r0:16 + r0 + nr, :], b[:64, :nr * 128].rearrange("p (a b) -> p a b", a=nr), AF.Copy),
                     r=[bk], w=[vk_])
            tiles = [(r, m) for r in range(d) for m in range(nto + 1)]
            stt_ = {'ob': None}

            def S_phase(r, m):
                qoff = r * Lq
                halo = (m == nto)
                nk = 64 if halo else 128
                b0_ = 64 if m == 0 else 0
                b1_ = 64 if halo else min(256, Lq - (128 * m - 64))
                ktile = kT[:, NT + r * 64:NT + r * 64 + 64] if halo else kT[:, qoff + m * 128:qoff + (m + 1) * 128]
                qs = qoff + 128 * m - 64 + b0_
                sbk, sbkk = sbank()
                p.op('pe', I_mm(sbk[:nk, b0_:b1_], ktile, qT[:, qs:qs + (b1_ - b0_)], True, True), r=[kk_, qk_], w=[sbkk])
                st, stk = cx.rot('stmp', [128, 256], F32, n=3)
                p.op('dve', I_stt(st[:nk, b0_:b1_], sbk[:nk, b0_:b1_], ASCALE, bt[:nk, g, b0_:b1_], ALU.mult, ALU.add),
                     r=[sbkk, btk], w=[stk])
                PT, ptk = cx.rot('PTa', [128, 256], BF16, n=4)
                p.op('act', I_act(PT[:nk, b0_:b1_], st[:nk, b0_:b1_], AF.Exp), r=[stk], w=[ptk])
                return PT, ptk

            def PV_phase(r, m, PT, ptk):
                halo = (m == nto)
                nk = 64 if halo else 128
                vtile = vt[:64, 16 + r, :] if halo else vt[:, r * nto + m, :]

                def flush(ep):
                    ob = stt_['ob']
                    qlo = max(0, 512 * ep - 64)
                    qhi = min(Lq, 512 * ep + 448)
                    c0f = qlo - (512 * ep - 64)
                    wdt = qhi - qlo
                    nv = sub_view(num[:, :], d)[:, r, qlo:qhi]
                    dv = sub_view(den[:, :], d)[:, r, qlo:qhi]
                    p.op('dve', I_tt(nv, ob[0][:, c0f:c0f + wdt], nv, ALU.add), r=[ob[1], 'numacc'], w=['numacc'])
                    p.op('dve', I_tt(dv, ob[2][:, c0f:c0f + wdt], dv, ALU.add), r=[ob[3], 'denacc'], w=['denacc'])
                if m == 0:
                    stt_['ob'] = ob = obanks()
                    p.op('pe', I_mm(ob[0][:, 64:128], vtile, PT[:nk, 64:128], True, True), r=[vk_, ptk], w=[ob[1]])
                    p.op('pe', I_mm(ob[2][:, 64:128], cx.ones[:nk, :], PT[:nk, 64:128], True, True), r=['ones', ptk], w=[ob[3]])
                else:
                    ob = stt_['ob']
                    q0 = 64 + 128 * (m - 1)
                    wq_ = min(Lq, q0 + 128) - q0
                    c0 = 128 * (m % 4)
                    p.op('pe', I_mm(ob[0][:, c0:c0 + wq_], vtile, PT[:nk, 0:wq_], False, True), r=[vk_, ptk], w=[ob[1]])
                    p.op('pe', I_mm(ob[2][:, c0:c0 + wq_], cx.ones[:nk, :], PT[:nk, 0:wq_], False, True), r=['ones', ptk], w=[ob[3]])
                    if m % 4 == 3 or halo:
                        flush(m // 4)
                if not halo:
                    q0 = 64 + 128 * m
                    wq_ = min(Lq, q0 + 128) - q0
                    if (m + 1) % 4 == 0:
                        stt_['ob'] = obanks()
                    ob = stt_['ob']
                    c0 = 128 * ((m + 1) % 4)
                    p.op('pe', I_mm(ob[0][:, c0:c0 + wq_], vtile, PT[:nk, 128:128 + wq_], True, False), r=[vk_, ptk], w=[ob[1]])
                    p.op('pe', I_mm(ob[2][:, c0:c0 + wq_], cx.ones[:nk, :], PT[:nk, 128:128 + wq_], True, False), r=['ones', ptk], w=[ob[3]])
            SKEW = 2
            pend = {}
            for i in range(len(tiles) + SKEW):
                if i < len(tiles):
                    pend[i] = S_phase(*tiles[i])
                if i - SKEW >= 0:
                    PV_phase(*tiles[i - SKEW], *pend.pop(i - SKEW))
        p.op('act', I_act(den[:], den[:], AF.Ln), r=['denacc'], w=['denacc'])
        p.op('act', I_act(den[:], den[:], AF.Exp, scale=-1.0), r=['denacc'], w=['denacc'])
        p.op('pool', I_tt(mT[:, h, :], num[:], den[:], ALU.mult), r=['numacc', 'denacc'], w=['mT'])
    for ns in range(2):
        wo, wok = wslab(cx, [(A['wo_a'][:, ns * 512:(ns + 1) * 512], 0)])
        for j in range(4):
            n = ns * 4 + j
            for tb in range(NT // 512):
                b, bk = sbank()
                for kc in range(NCH):
                    p.op('pe', I_mm(b[:], wo[:, kc * 512 + j * 128: kc * 512 + (j + 1) * 128], mT[:, kc, tb * 512:(tb + 1) * 512],
                                    kc == 0, kc == NCH - 1), r=[wok, 'mT'], w=[bk])
                xt, xk = cx.rot('xin', [128, 512], F32, n=2)
                p.dma('sp', I_dma(xt[:], _src[n * 128:(n + 1) * 128, _cols(tb)]), w=[xk])
                p.op('dve', I_tt(xt[:], rvf(b[:]), xt[:], ALU.add), r=[bk, xk], w=[xk])
                p.dma('sp', I_dma(_dst[n * 128:(n + 1) * 128, _cols(tb)], xt[:]), r=[xk], w=['hTout'])


def build_attn():
    nc = bass.Bass("TRN2", target_bir_lowering=False)
    A = {}

    def inp(name, shape, dt=F32):
        A[name] = nc.dram_tensor(name, list(shape), dt, kind="ExternalInput").ap()
    inp('hT_ext', [D, NEXT])
    inp('vecs', [128, NVEC])
    inp('wqkv', [D, 9216])
    inp('wo_a', [D, D])
    inp('biasT', [24, 128, 256])
    A['hT_out'] = nc.dram_tensor('hT_out', [D, NT], F32, kind="ExternalOutput").ap()
    cx = Cx(nc)
    cx.vec = cx.p.sb('vec', [128, NVEC], F32)
    cx.p.dma('sp', I_dma(cx.vec[:], A['vecs'][:, :]), w=['vec'])
    attn_body(cx, A)
    cx.p.finalize()
    return nc, cx


def t5_bucket(rel):
    nb = 16
    ret = (rel > 0).astype(np.int32) * nb
    n = np.abs(rel)
    max_exact = nb // 2
    large = max_exact + (np.log(np.maximum(n, 1).astype(np.float32) / max_exact)
                         / np.log(1024 / max_exact) * (nb - max_exact)).astype(np.int32)
    large = np.minimum(large, nb - 1)
    return (ret + np.where(n < max_exact, n, large)).astype(np.int32)


def host_bias(bias_table, flip):
    a = np.arange(128)[:, None]
    b = np.arange(256)[None, :]
    rel = a - b + 64
    out = np.full((24, 128, 256), -1e30, np.float32)
    band = np.abs(rel) <= 64
    for g, (dil, _) in enumerate(GRP):
        bk = t5_bucket((-rel if flip else rel) * dil)
        for h in range(8):
            out[g * 8 + h] = np.where(band, bias_table[bk, g * 8 + h], np.float32(-1e30))
    return out


def build_norm():
    nc = bass.Bass("TRN2", target_bir_lowering=False)
    A = {}
    A['hT'] = nc.dram_tensor('hT', [D, NT], F32, kind="ExternalInput").ap()
    A['vecs'] = nc.dram_tensor('vecs', [128, NVEC], F32, kind="ExternalInput").ap()
    A['hn_out'] = nc.dram_tensor('hn_out', [D, NT], F32, kind="ExternalOutput").ap()
    cx = Cx(nc)
    p = cx.p
    cx.hT = p.sb('hT', [128, NCH, NT], F32)
    cx.vec = p.sb('vec', [128, NVEC], F32)
    p.dma('sp', I_dma(cx.vec[:], A['vecs'][:, :]), w=['vec'])
    load_hT(cx, A['hT'])
    emit_norm(cx, A['hn_out'])
    p.finalize()
    return nc, cx


def build_fused(nlayers=4):
    nc = bass.Bass("TRN2", target_bir_lowering=False)
    shapes = {}

    def inp(name, shape, dt=F32):
        shapes[name] = list(shape)

    class Lazy(dict):
        def __missing__(self, name):
            ap = nc.dram_tensor(name, shapes[name], F32, kind="ExternalInput").ap()
            self[name] = ap
            return ap
    A = Lazy()

    def scratch(name):
        return nc.dram_tensor(name, [D, SEQ], F32, kind="Internal").ap()
    inp('xT', [D, SEQ])
    inp('memT', [D, MEMLEN])
    inp('ident', [128, 128])
    inp('v0', [128, NVEC])
    inp('biasT0', [24, 128, 256])
    inp('biasT1', [24, 128, 256])
    for i in range(4):
        inp('vecs%d' % i, [128, NVEC])
        inp('wq%d' % i, [D, D])
        inp('wkv%d' % i, [D, 2 * D])
        inp('wo%d' % i, [D, D])
        inp('w1_%d' % i, [D, DFF])
        inp('w2_%d' % i, [DFF, D])
    for j in range(2):
        inp('wglu%d' % j, [D, 2 * D])
        inp('wqkv%d' % j, [D, 9216])
        inp('woa%d' % j, [D, D])
        inp('avecs%d' % j, [128, NVEC])
        for c in range(2):
            sfx = '%d%d' % (j, c)
            for nm in ('Bre', 'Bim', 'CR', 'CI'):
                inp(nm + sfx, [2, NPT, 128, 128])
            for nm in ('lamre', 'lamim', 'logdt'):
                inp(nm + sfx, [128, 2 * NPT])
            inp('dsk' + sfx, [128, 4])
    xT = A['xT']
    outT = nc.dram_tensor('outT', [D, SEQ], F32, kind="ExternalOutput").ap()
    HN = scratch('HN')
    Y = scratch('Y')
    Hs = [xT, scratch('H1'), scratch('H1a'), scratch('H2'), scratch('H3'), scratch('H3a'), outT]
    cx = Cx(nc, arena=True)
    p = cx.p
    hv = lambda ap, half: ap[:, half * NT:(half + 1) * NT]

    cx.hT = p.sb('hT', [128, NCH, NT], F32)
    cx.vec = p.sb('vec', [128, NVEC], F32)
    p.dma('sp', I_dma(cx.vec[:], A['v0'][:, :]), w=['vec'])
    for half in range(2):
        load_hT(cx, hv(xT, half))
        emit_norm(cx, hv(HN, half))

    def s5_stage(j):
        for c in range(2):
            cx.new_stage()
            sfx = '%d%d' % (j, c)
            AA = {nm: A[nm + sfx] for nm in ('Bre', 'Bim', 'CR', 'CI', 'lamre', 'lamim', 'logdt', 'dsk')}
            AA['ident'] = A['ident']
            AA['uT'] = HN[512 * c:512 * c + 512, :]
            AA['yT'] = Y[512 * c:512 * c + 512, :]
            s5_body(cx, AA)

    def tail_stage(i, glu, src, dst, emit):
        cx.new_stage()
        AA = dict(memT=A['memT'], vecs=A['vecs%d' % i], wq=A['wq%d' % i], wkv=A['wkv%d' % i], wo=A['wo%d' % i],
                  w1=A['w1_%d' % i], w2=A['w2_%d' % i])
        if glu:
            AA['wglu'] = A['wglu%d' % (i // 2)]
        common_tiles(cx, AA)
        for half in range(2):
            load_hT(cx, hv(src, half))
            if glu:
                AA['yT'] = hv(Y, half)
            tail_body(cx, AA, glu, 0, 2, kv_ready=(half == 1))
            tail_body(cx, AA, glu, 2, 2, kv_ready=True)
            store_hT(cx, hv(dst, half))
            if emit:
                emit_norm(cx, hv(HN, half))

    def attn_stage(i, src, dst):
        j = i // 2
        for half in range(2):
            cx.new_stage()
            cx.vec = p.sb('vec', [128, NVEC], F32)
            p.dma('sp', I_dma(cx.vec[:], A['avecs%d' % j][:, :]), w=['vec'])
            AA = dict(wqkv=A['wqkv%d' % j], wo_a=A['woa%d' % j], biasT=A['biasT%d' % half])
            attn_body(cx, AA, flip=(half == 1), src=src, dst=dst)

    s5_stage(0)
    if nlayers == 0:
        cx.new_stage()
        cx.hT = p.sb('hT', [128, NCH, NT], F32)
        for half in range(2):
            load_hT(cx, hv(Y, half))
            store_hT(cx, hv(outT, half))
        p.finalize()
        cx.used = list(A.keys())
        return nc, cx
    tail_stage(0, True, Hs[0], Hs[1] if nlayers > 1 else outT, False)
    if nlayers > 1:
        attn_stage(1, Hs[1], Hs[2])
        tail_stage(1, False, Hs[2], Hs[3] if nlayers > 2 else outT, True)
    if nlayers > 2:
        s5_stage(1)
        tail_stage(2, True, Hs[3], Hs[4] if nlayers > 3 else outT, False)
    if nlayers > 3:
        attn_stage(3, Hs[4], Hs[5])
        tail_stage(3, False, Hs[5], Hs[6], False)
    p.finalize()
    cx.used = list(A.keys())
    return nc, cx


RG2 = [[0, 1], [2, 3], [4, 5], [6, 7]]


class GBuf:
    def __init__(self, nc, name, rows, cols, chunk_rows):
        self.cr = chunk_rows
        self.n = rows // chunk_rows
        self.src = [nc.dram_tensor('%s_s%d' % (name, q), [chunk_rows, cols], F32, kind="Internal").ap() for q in range(self.n)]
        self.dst = [nc.dram_tensor('%s_g%d' % (name, q), [2 * chunk_rows, cols], F32, kind="Internal").ap() for q in range(self.n)]

    def src_rows(self, r0, nrows=128):
        q = r0 // self.cr
        o = r0 - q * self.cr
        return self.src[q][o:o + nrows, :]

    def g_rows(self, rank, r0, nrows=128):
        q = r0 // self.cr
        o = rank * self.cr + r0 - q * self.cr
        return self.dst[q][o:o + nrows, :]


def build_fused8():
    nc = bass.Bass("TRN2", target_bir_lowering=False, num_devices=8)
    shapes = {}

    def inp(name, shape):
        shapes[name] = list(shape)

    class Lazy(dict):
        def __missing__(self, name):
            ap = nc.dram_tensor(name, shapes[name], F32, kind="ExternalInput").ap()
            self[name] = ap
            return ap
    A = Lazy()

    def scratch(name, shape):
        return nc.dram_tensor(name, list(shape), F32, kind="Internal").ap()
    inp('xT', [D, NT])
    inp('memT', [D, MEMLEN])
    inp('ident', [128, 128])
    inp('v0', [128, NVEC])
    inp('biasT', [24, 128, 256])
    for i in range(4):
        inp('vecs%d' % i, [128, NVEC])
        inp('wq%d' % i, [D, D])
        inp('wkv%d' % i, [D, 2 * D])
        inp('wo%d' % i, [D, D])
        inp('w1_%d' % i, [D, DFF])
        inp('w2_%d' % i, [DFF, D])
    for j in range(2):
        inp('wglu%d' % j, [D, 2 * D])
        inp('wqkv%d' % j, [D, 9216])
        inp('woa%d' % j, [D, D])
        inp('avecs%d' % j, [128, NVEC])
        for nm in ('Bre', 'Bim', 'CR', 'CI'):
            inp(nm + '%d' % j, [2, NPT, 128, 128])
        for nm in ('lamre', 'lamim', 'logdt'):
            inp(nm + '%d' % j, [128, 2 * NPT])
        inp('dsk%d' % j, [128, 4])
    outT = nc.dram_tensor('outT', [D, NT], F32, kind="ExternalOutput").ap()
    HNb = GBuf(nc, 'HN', D, NT, 256)
    HN = GHN = HNb
    yO = scratch('yO', [512, NT])
    ySb = GBuf(nc, 'yS', 512, NT, 256)
    yS = GS = ySb
    Hhb = GBuf(nc, 'Hh', D, 1024, 512)
    Hh = GH = Hhb
    Hs = [A['xT']] + [scratch(n, [D, NT]) for n in ('H1', 'H1a', 'H2', 'H3', 'H3a')] + [outT]
    cx = Cx(nc, arena=True)
    p = cx.p

    def allgather(gb, _unused=None):
        cx.new_stage()
        for q in range(gb.n):
            p.dma('pool', lambda e, q=q: e.collective_compute("AllGather", ALU.bypass, replica_groups=RG2,
                                                               ins=[gb.src[q][:, :]], outs=[gb.dst[q][:, :]]),
                  w=['cc'], semkey='cc', inc=1)

    cx.hT = p.sb('hT', [128, NCH, NT], F32)
    cx.vec = p.sb('vec', [128, NVEC], F32)
    p.dma('sp', I_dma(cx.vec[:], A['v0'][:, :]), w=['vec'])
    load_hT(cx, A['xT'])
    emit_norm(cx, HN)
    allgather(HN, GHN)

    def s5_stage(j):
        cx.new_stage()
        AA = {nm: A[nm + '%d' % j] for nm in ('Bre', 'Bim', 'CR', 'CI', 'lamre', 'lamim', 'logdt', 'dsk')}
        AA['ident'] = A['ident']
        AA['GHN'] = GHN
        AA['yO'] = yO
        AA['yS'] = yS
        cx.vec = p.sb('vec', [128, NVEC], F32)
        p.dma('sp', I_dma(cx.vec[:], A['v0'][:, :]), w=['vec'])
        s5_body(cx, AA)
        allgather(yS, GS)

    def tail_stage(i, glu, src, dst, emit, halo):
        cx.new_stage()
        AA = dict(memT=A['memT'], vecs=A['vecs%d' % i], wq=A['wq%d' % i], wkv=A['wkv%d' % i], wo=A['wo%d' % i],
                  w1=A['w1_%d' % i], w2=A['w2_%d' % i])
        if glu:
            AA['wglu'] = A['wglu%d' % (i // 2)]
            AA['yO'] = yO
            AA['GS'] = GS
        common_tiles(cx, AA)
        load_hT(cx, src)
        tail_body(cx, AA, glu, 0, 2, kv_ready=False)
        tail_body(cx, AA, glu, 2, 2, kv_ready=True)
        store_hT(cx, dst)
        if halo:
            for c in range(NCH):
                p.dma('sp', I_dma(Hh.src_rows(c * 128), cx.hT[:, c, 1024:2048]),
                      r=['h%d_%d' % (c, tb) for tb in (2, 3)], w=['hhout'])
            allgather(Hh, GH)
        if emit:
            emit_norm(cx, HN)
            allgather(HN, GHN)

    def attn_stage(i, src, dst):
        j = i // 2
        cx.new_stage()
        cx.vec = p.sb('vec', [128, NVEC], F32)
        p.dma('sp', I_dma(cx.vec[:], A['avecs%d' % j][:, :]), w=['vec'])
        AA = dict(wqkv=A['wqkv%d' % j], wo_a=A['woa%d' % j], biasT=A['biasT'])
        cx.ws_n = 2
        attn_body(cx, AA, flip=False, src=src, dst=dst, gh=GH)
        cx.ws_n = WS_N

    s5_stage(0)
    tail_stage(0, True, Hs[0], Hs[1], False, True)
    attn_stage(1, Hs[1], Hs[2])
    tail_stage(1, False, Hs[2], Hs[3], True, False)
    s5_stage(1)
    tail_stage(2, True, Hs[3], Hs[4], False, True)
    attn_stage(3, Hs[4], Hs[5])
    tail_stage(3, False, Hs[5], Hs[6], False, False)
    p.finalize()
    cx.used = list(A.keys())
    return nc, cx


def _pc(v, C):
    return np.ascontiguousarray(np.asarray(v, np.float32).reshape(C, 128).T)


_PROGS = {}
_NL = [4]


def _prog(name):
    if name not in _PROGS:
        if name == 'norm':
            _PROGS[name] = build_norm()[0]
        elif name == 's5':
            _PROGS[name] = build_s5()[0]
        elif name == 'tail_glu':
            _PROGS[name] = build_tail(True, True)[0]
        elif name == 'tail':
            _PROGS[name] = build_tail(False, True)[0]
        elif name == 'attn':
            _PROGS[name] = build_attn()[0]
    return _PROGS[name]


def kernel_multi(**inp):
    inp = {k: np.asarray(v) for k, v in inp.items()}
    ncore = 8
    cores = list(range(ncore))
    f32 = np.float32
    loc = [np.arange(NEXT) if (k % 2 == 0) else (SEQ - 1 - np.arange(NEXT)) for k in cores]
    H = np.array(inp['x'], dtype=f32, copy=True)
    memT = [np.ascontiguousarray(inp['mem'][k // 2].T.astype(f32)) for k in cores]
    biasT = [host_bias(inp['bias_table'].astype(f32), k % 2 == 1) for k in cores]

    def own_T(arr_bsd, k):
        return np.ascontiguousarray(arr_bsd[k // 2][loc[k][:NT]].T)

    def scatter(outs, name):
        full = np.empty((BATCH, SEQ, D), f32)
        for k in cores:
            full[k // 2][loc[k][:NT]] = np.asarray(outs[k][name], f32).T
        return full

    def tail_vecs(i):
        v = np.zeros((128, NVEC), f32)
        v[:, 0:8] = _pc(inp['norm_xattn'][i], 8)
        v[:, 8:16] = _pc(inp['norm_mem'][i], 8)
        v[:, 16:24] = _pc(inp['norm_mlp'][i], 8)
        v[:, 24:26] = _pc(inp['xattn_q_gain'][i], 2)
        v[:, 26:28] = _pc(inp['xattn_k_gain'][i], 2)
        v[:, 28:36] = _pc(inp['norm_mix'][min(i + 1, 3)], 8)
        return v

    def run_tail(i, H, Y):
        v = tail_vecs(i)
        maps = []
        for k in cores:
            m = dict(hT=own_T(H, k), memT=memT[k], vecs=v, wq=inp['xattn_w_q'][i], wkv=inp['xattn_w_kv'][i],
                     wo=inp['xattn_w_o'][i], w1=inp['mlp_w1'][i], w2=inp['mlp_w2'][i])
            if Y is not None:
                m['yT'] = own_T(Y, k)
                m['wglu'] = inp['s5_w_glu'][i // 2]
            maps.append(m)
        res = run_bass_kernel_spmd(_prog('tail_glu' if Y is not None else 'tail'), maps, core_ids=cores).results
        return scatter(res, 'hT_out'), scatter(res, 'hn_out')

    def run_s5(j, HN):
        maps = []
        for k in cores:
            b, c = k // 2, k % 2
            m = s5_host_inputs(inp, j, c)
            m['uT'] = np.ascontiguousarray(HN[b][:, 512 * c:512 * c + 512].T)
            maps.append(m)
        res = run_bass_kernel_spmd(_prog('s5'), maps, core_ids=cores).results
        Y = np.empty((BATCH, SEQ, D), f32)
        for k in cores:
            b, c = k // 2, k % 2
            Y[b][:, 512 * c:512 * c + 512] = np.asarray(res[k]['yT'], f32).T
        return Y

    def run_attn(i, H):
        j = i // 2
        v = np.zeros((128, NVEC), f32)
        v[:, 28:36] = _pc(inp['norm_mix'][i], 8)
        v[:, 36] = inp['attn_q_gain'][j]
        v[:, 37] = inp['attn_k_gain'][j]
        maps = []
        for k in cores:
            maps.append(dict(hT_ext=np.ascontiguousarray(H[k // 2][loc[k]].T), vecs=v, wqkv=inp['attn_w_qkv'][j],
                             wo_a=inp['attn_w_o'][j], biasT=biasT[k]))
        res = run_bass_kernel_spmd(_prog('attn'), maps, core_ids=cores).results
        return scatter(res, 'hT_out')

    v0 = np.zeros((128, NVEC), f32)
    v0[:, 28:36] = _pc(inp['norm_mix'][0], 8)
    res = run_bass_kernel_spmd(_prog('norm'), [dict(hT=own_T(H, k), vecs=v0) for k in cores], core_ids=cores).results
    HN = scatter(res, 'hn_out')
    for i in range(4):
        if i % 2 == 0:
            Y = run_s5(i // 2, HN)
            H, HN = run_tail(i, H, Y)
        else:
            H = run_attn(i, H)
            H, HN = run_tail(i, H, None)
    return H


def tail_vecs_host(inp, i):
    v = np.zeros((128, NVEC), np.float32)
    v[:, 0:8] = _pc(inp['norm_xattn'][i], 8)
    v[:, 8:16] = _pc(inp['norm_mem'][i], 8)
    v[:, 16:24] = _pc(inp['norm_mlp'][i], 8)
    v[:, 24:26] = _pc(inp['xattn_q_gain'][i], 2)
    v[:, 26:28] = _pc(inp['xattn_k_gain'][i], 2)
    v[:, 28:36] = _pc(inp['norm_mix'][min(i + 1, 3)], 8)
    return v


def kernel_fused4(**inp):
    inp = {k: np.asarray(v) for k, v in inp.items()}
    f32 = np.float32
    nl = _NL[0]
    if ('fused', nl) not in _PROGS:
        _PROGS[('fused', nl)] = build_fused(nl)
    nc, cxf = _PROGS[('fused', nl)]
    shared = dict(ident=np.eye(128, dtype=f32),
                  biasT0=host_bias(inp['bias_table'].astype(f32), False),
                  biasT1=host_bias(inp['bias_table'].astype(f32), True))
    v0 = np.zeros((128, NVEC), f32)
    v0[:, 28:36] = _pc(inp['norm_mix'][0], 8)
    shared['v0'] = v0
    for i in range(4):
        shared['vecs%d' % i] = tail_vecs_host(inp, i)
        shared['wq%d' % i] = inp['xattn_w_q'][i]
        shared['wkv%d' % i] = inp['xattn_w_kv'][i]
        shared['wo%d' % i] = inp['xattn_w_o'][i]
        shared['w1_%d' % i] = inp['mlp_w1'][i]
        shared['w2_%d' % i] = inp['mlp_w2'][i]
    for j in range(2):
        shared['wglu%d' % j] = inp['s5_w_glu'][j]
        shared['wqkv%d' % j] = inp['attn_w_qkv'][j]
        shared['woa%d' % j] = inp['attn_w_o'][j]
        av = np.zeros((128, NVEC), f32)
        av[:, 28:36] = _pc(inp['norm_mix'][2 * j + 1], 8)
        av[:, 36] = inp['attn_q_gain'][j]
        av[:, 37] = inp['attn_k_gain'][j]
        shared['avecs%d' % j] = av
        for c in range(2):
            for nm, arr in s5_host_inputs(inp, j, c).items():
                if nm != 'ident':
                    shared[nm + '%d%d' % (j, c)] = arr
    maps = []
    for b in range(BATCH):
        m = dict(shared)
        m['xT'] = np.ascontiguousarray(inp['x'][b].T.astype(f32))
        m['memT'] = np.ascontiguousarray(inp['mem'][b].T.astype(f32))
        maps.append({k: m[k] for k in cxf.used})
    res = run_bass_kernel_spmd(nc, maps, core_ids=list(range(BATCH))).results
    out = np.empty((BATCH, SEQ, D), f32)
    for b in range(BATCH):
        out[b] = np.asarray(res[b]['outT'], f32).T
    return out


def _sw(a, axis):
    return np.roll(a, 512, axis=axis)


def kernel(**inp):
    inp = {k: np.asarray(v, np.float32) for k, v in inp.items()}
    f32 = np.float32
    if 'fused8' not in _PROGS:
        _PROGS['fused8'] = build_fused8()
    nc, cxf = _PROGS['fused8']
    ident = np.eye(128, dtype=f32)
    per_c = []
    for c in range(2):
        sw = (lambda a, axis: _sw(a, axis)) if c == 1 else (lambda a, axis: a)
        g = {}
        gi = {k: (sw(inp[k], 1) if k in ('norm_mix', 'norm_xattn', 'norm_mem', 'norm_mlp') else inp[k]) for k in inp}
        g['ident'] = ident
        g['biasT'] = host_bias(inp['bias_table'], c == 1)
        v0 = np.zeros((128, NVEC), f32)
        v0[:, 28:36] = _pc(gi['norm_mix'][0], 8)
        v0[:, 40 + c] = 1.0
        g['v0'] = v0
        for i in range(4):
            v = tail_vecs_host(gi, i)
            v[:, 40 + c] = 1.0
            g['vecs%d' % i] = v
            g['wq%d' % i] = np.ascontiguousarray(sw(inp['xattn_w_q'][i], 0))
            g['wkv%d' % i] = np.ascontiguousarray(sw(inp['xattn_w_kv'][i], 0))
            g['wo%d' % i] = np.ascontiguousarray(sw(inp['xattn_w_o'][i], 1))
            g['w1_%d' % i] = np.ascontiguousarray(sw(inp['mlp_w1'][i], 0))
            g['w2_%d' % i] = np.ascontiguousarray(sw(inp['mlp_w2'][i], 1))
        for j in range(2):
            wg = sw(inp['s5_w_glu'][j], 0).reshape(D, 2, D)
            g['wglu%d' % j] = np.ascontiguousarray(sw(wg, 2).reshape(D, 2 * D))
            g['wqkv%d' % j] = np.ascontiguousarray(sw(inp['attn_w_qkv'][j], 0))
            g['woa%d' % j] = np.ascontiguousarray(sw(inp['attn_w_o'][j], 1))
            av = np.zeros((128, NVEC), f32)
            av[:, 28:36] = _pc(gi['norm_mix'][2 * j + 1], 8)
            av[:, 36] = inp['attn_q_gain'][j]
            av[:, 37] = inp['attn_k_gain'][j]
            av[:, 40 + c] = 1.0
            g['avecs%d' % j] = av
            for nm, arr in s5_host_inputs(inp, j, c).items():
                if nm != 'ident':
                    g[nm + '%d' % j] = arr
        per_c.append(g)
    maps = []
    for k in range(8):
        b, c = k // 2, k % 2
        m = dict(per_c[c])
        xb = inp['x'][b]
        if c == 0:
            m['xT'] = np.ascontiguousarray(xb[:NT].T)
            m['memT'] = np.ascontiguousarray(inp['mem'][b].T)
        else:
            m['xT'] = np.ascontiguousarray(_sw(xb[::-1][:NT], 1).T)
            m['memT'] = np.ascontiguousarray(_sw(inp['mem'][b], 1).T)
        maps.append({kk: m[kk] for kk in cxf.used})
    res = run_bass_kernel_spmd(nc, maps, core_ids=list(range(8))).results
    out = np.empty((BATCH, SEQ, D), f32)
    for k in range(8):
        b, c = k // 2, k % 2
        o = np.asarray(res[k]['outT'], f32).T
        if c == 0:
            out[b, :NT] = o
        else:
            out[b, NT:] = _sw(o, 1)[::-1]
    return out
```

```python
import math
import numpy as np
from contextlib import ExitStack
import concourse.bass as bass
import concourse.mybir as mybir
from concourse.bass_utils import run_bass_kernel_spmd

F32 = mybir.dt.float32
BF16 = mybir.dt.bfloat16
AF = mybir.ActivationFunctionType
ALU = mybir.AluOpType

D = 1024
NCH = 8
SEQ = 4096
BATCH = 4
NT = 2048
EPS = 1e-6
MEMLEN = 256
DFF = 4096


class Prog:
    ENGS = ('pe', 'act', 'dve', 'pool', 'sp')
    BLK = {'pe': 'tensor', 'act': 'scalar', 'dve': 'vector', 'pool': 'gpsimd', 'sp': 'sync'}

    def __init__(self, nc):
        self.nc = nc
        self.es = ExitStack()
        self.ins = {e: [] for e in self.ENGS}
        self.last_w = {}
        self.readers = {}
        self.dma_cnt = {}
        self.log = None
        self.bar_deps = {}

    ARENA_F32 = 51712

    def use_arena(self):
        self.arena = self.es.enter_context(self.nc.sbuf_tensor('arena', [128, self.ARENA_F32], F32))
        self.aoff = 0

    def sb(self, name, shape, dt):
        if getattr(self, 'arena', None) is None:
            return self.es.enter_context(self.nc.sbuf_tensor('s_' + name, list(shape), dt))
        assert shape[0] == 128, shape
        nel = 1
        for d_ in shape[1:]:
            nel *= d_
        isz = 4 if dt == F32 else 2
        nby = (nel * isz + 63) // 64 * 64
        o4 = self.aoff // 4
        self.aoff += nby
        assert self.aoff <= self.ARENA_F32 * 4, ('arena overflow', name, self.aoff)
        v = self.arena[:, o4:o4 + nby // 4]
        if dt != F32:
            v = v.bitcast(dt)
        v = v[:, :nel]
        if len(shape) == 3:
            v = v.rearrange("p (a b) -> p a b", a=shape[1])
        elif len(shape) != 2:
            raise AssertionError(shape)
        return v

    def barrier(self):
        deps = [('d', k, c) for k, c in self.dma_cnt.items()]
        for e in self.ENGS:
            n = len(self.ins[e])
            j = n - 1
            while j >= 0 and self.ins[e][j]['dma'] is not None:
                j -= 1
            if j >= 0:
                deps.append(('e', e, j))
        self.bar_deps = {e: list(deps) for e in self.ENGS}
        self.last_w.clear()
        self.readers.clear()

    def ps(self, name, shape, dt=F32):
        return self.es.enter_context(self.nc.psum_tensor(name, list(shape), dt))

    def _deps(self, r, w):
        deps = []
        for k in r:
            t = self.last_w.get(k)
            if t is not None:
                deps.append(t)
        for k in w:
            t = self.last_w.get(k)
            if t is not None:
                deps.append(t)
            deps.extend(self.readers.get(k, ()))
        return deps

    def _commit(self, tok, r, w):
        for k in r:
            lst = self.readers.setdefault(k, [])
            lst[:] = [t for t in lst if t[:2] != tok[:2]]
            lst.append(tok)
        for k in w:
            self.last_w[k] = tok
            self.readers[k] = []

    def op(self, eng, fn, r=(), w=()):
        idx = len(self.ins[eng])
        self.ins[eng].append(dict(fn=fn, deps=self._deps(r, w) + self.bar_deps.pop(eng, []), dma=None))
        self._commit(('e', eng, idx), r, w)

    def dma(self, eng, fn, r=(), w=(), semkey=None, inc=16):
        if semkey is None:
            semkey = w[0]
        c = self.dma_cnt.get(semkey, 0) + inc
        self.dma_cnt[semkey] = c
        self.ins[eng].append(dict(fn=fn, deps=self._deps(r, w) + self.bar_deps.pop(eng, []), dma=semkey, inc=inc))
        self._commit(('d', semkey, c), r, w)

    SAME_DIST = 4

    def _skip_same(self, e, i, d, rec):
        if d[1] != e or rec['dma'] is not None:
            return False
        if e == 'pe':
            return True
        return (i - d[2]) > self.SAME_DIST

    def finalize(self):
        nc = self.nc
        need = {e: set() for e in self.ENGS}
        for e in self.ENGS:
            for i, rec in enumerate(self.ins[e]):
                for d in rec['deps']:
                    if d[0] == 'e' and not self._skip_same(e, i, d, rec):
                        need[d[1]].add(d[2])
        cum = {}
        for e in self.ENGS:
            c = 0
            arr = []
            for i in range(len(self.ins[e])):
                if i in need[e]:
                    c += 1
                arr.append(c)
            cum[e] = arr
        esem = {e: self.es.enter_context(nc.semaphore('se_' + e)) for e in self.ENGS}
        dsem = {}
        for i, k in enumerate(self.dma_cnt):
            dsem[k] = self.es.enter_context(nc.semaphore('sd_%d' % i))
        self.stats = {e: (len(self.ins[e]), cum[e][-1] if cum[e] else 0) for e in self.ENGS}
        self.stats['ndsem'] = len(dsem)
        with nc.Block() as block:
            for e in self.ENGS:
                def body(eng, e=e):
                    waited = {}
                    for i, rec in enumerate(self.ins[e]):
                        req = {}
                        for d in rec['deps']:
                            if d[0] == 'e':
                                if self._skip_same(e, i, d, rec):
                                    continue
                                key = ('e', d[1])
                                val = cum[d[1]][d[2]]
                            else:
                                key = ('d', d[1])
                                val = d[2]
                            if val > req.get(key, 0):
                                req[key] = val
                        for key, val in req.items():
                            if waited.get(key, 0) < val:
                                sem = esem[key[1]] if key[0] == 'e' else dsem[key[1]]
                                eng.wait_ge(sem, val)
                                waited[key] = val
                                if self.log is not None:
                                    self.log.append((e, i, 'wait', key, val))
                        if self.log is not None:
                            self.log.append((e, i, 'inst', rec['dma'], cum[e][i] if i in need[e] else None))
                        inst = rec['fn'](eng)
                        if rec['dma'] is not None:
                            inst.then_inc(dsem[rec['dma']], rec.get('inc', 16))
                        elif i in need[e]:
                            inst.then_inc(esem[e], 1)
                    if e == 'sp':
                        for k, c in self.dma_cnt.items():
                            eng.wait_ge(dsem[k], c)
                getattr(block, self.BLK[e])(body)
        self.es.close()


def I_mm(out, lhsT, rhs, start, stop):
    return lambda e: e.matmul(out, lhsT, rhs, start=start, stop=stop)


def I_act(out, in_, func, **kw):
    return lambda e: e.activation(out=out, in_=in_, func=func, **kw)


def I_tt(out, in0, in1, op):
    return lambda e: e.tensor_tensor(out=out, in0=in0, in1=in1, op=op)


def I_ts(out, in0, s1, s2, op0, op1=None):
    if op1 is None:
        return lambda e: e.tensor_scalar(out=out, in0=in0, scalar1=s1, scalar2=None, op0=op0)
    return lambda e: e.tensor_scalar(out=out, in0=in0, scalar1=s1, scalar2=s2, op0=op0, op1=op1)


def I_stt(out, in0, scalar, in1, op0, op1):
    return lambda e: e.scalar_tensor_tensor(out=out, in0=in0, scalar=scalar, in1=in1, op0=op0, op1=op1)


def I_recip(out, in_):
    return lambda e: e.reciprocal(out=out, in_=in_)


def I_copy(out, in_):
    return lambda e: e.tensor_copy(out=out, in_=in_)


def I_memset(ap, c):
    return lambda e: e.memset(ap, c)


def I_dma(out, in_):
    return lambda e: e.dma_start(out=out, in_=in_)


def I_scan(out, d0, d1, init):
    return lambda e: e.tensor_tensor_scan(out=out, data0=d0, data1=d1, initial=init, op0=ALU.mult, op1=ALU.add)


def mcombine(cx, out, okey, X, xk, Z, zk, ma, mb, n=512):
    p = cx.p
    tmp, tk = cx.rot('mctmp', [128, 512], F32, n=2)
    p.op('act', I_act(tmp[:, :n], X, AF.Copy, scale=ma), r=[xk, 'vec'], w=[tk])
    p.op('dve', I_stt(out, Z, mb, tmp[:, :n], ALU.mult, ALU.add), r=[zk, 'vec', tk], w=[okey])


class Cx:
    def __init__(self, nc, arena=False):
        self.nc = nc
        self.p = Prog(nc)
        if arena:
            self.p.use_arena()
        self.banks = [self.p.ps('bank%d' % i, [128, 512]) for i in range(8)]
        self.bi = 0
        p = self.p
        self.ones = p.sb('ones', [128, 128], BF16)
        p.op('pool', I_memset(self.ones[:], 1.0), w=['ones'])
        self.sq = [p.sb('sq%d' % i, [128, 512], BF16) for i in range(2)]
        self.sqi = 0
        self.rstd = [p.sb('rstd%d' % i, [128, 512], F32) for i in range(2)]
        self.rsi = 0
        self._rot = {}
        self.epsc = p.sb('epsc', [128, 1], F32)
        p.op('pool', I_memset(self.epsc[:], EPS), w=['epsc'])
        self.mark = getattr(p, 'aoff', 0)

    def new_stage(self):
        self.p.barrier()
        self.p.aoff = self.mark
        self._rot = {}
        for nm in ('ws', 'wsi'):
            if hasattr(self, nm):
                delattr(self, nm)

    def bank(self):
        i = self.bi
        self.bi = (i + 1) % 8
        return self.banks[i], 'bank%d' % i

    def rot(self, name, shape, dt, n=2):
        if name not in self._rot:
            self._rot[name] = [[self.p.sb('%s_%d' % (name, i), shape, dt) for i in range(n)], 0]
        tl, i = self._rot[name]
        self._rot[name][1] = (i + 1) % n
        return tl[i], '%s_%d' % (name, i)

    def next_sq(self):
        i = self.sqi
        self.sqi = 1 - i
        return self.sq[i], 'sq%d' % i

    def next_rstd(self):
        i = self.rsi
        self.rsi = 1 - i
        return self.rstd[i], 'rstd%d' % i

    def rstd_from_bank(self, bank, bk, n, dim):
        p = self.p
        rs, rk = self.next_rstd()
        p.op('act', I_act(rs[:, :n], bank[:, :n], AF.Ln, scale=1.0 / dim, bias=self.epsc[:, 0:1]), r=[bk, 'epsc'], w=[rk])
        p.op('act', I_act(rs[:, :n], rs[:, :n], AF.Exp, scale=-0.5), r=[rk], w=[rk])
        return rs, rk


def rmsnorm(cx, src, skey, gcols, gkey, dst, dkey, C, n0, n, dim):
    p = cx.p
    bank, bk = cx.bank()
    for c in range(C):
        sq, sk = cx.next_sq()
        p.op('act', I_act(sq[:, :n], src[:, c, n0:n0 + n], AF.Square), r=[skey(c)], w=[sk])
        p.op('pe', I_mm(bank[:, :n], cx.ones[:], sq[:, :n], c == 0, c == C - 1), r=[sk, 'ones'], w=[bk])
    rs, rk = cx.rstd_from_bank(bank, bk, n, dim)
    for c in range(C):
        p.op('dve', I_stt(dst[:, c, n0:n0 + n], src[:, c, n0:n0 + n], gcols[:, c:c + 1], rs[:, :n], ALU.mult, ALU.mult),
             r=[skey(c), gkey, rk], w=[dkey(c)])


WS_N = 3


def wslab(cx, parts):
    p = cx.p
    nws = getattr(cx, 'ws_n', WS_N)
    if not hasattr(cx, 'ws'):
        cx.ws = [p.sb('ws%d' % i, [128, 4096], BF16) for i in range(nws)]
        cx.wsi = 0
    i = cx.wsi
    cx.wsi = (i + 1) % nws
    t = cx.ws[i]
    key = 'ws%d' % i
    for src, off in parts:
        K, N = src.shape
        kc = K // 128
        dst = t[:, off:off + kc * N].rearrange("p (k n) -> p k n", k=kc)
        p.dma('pool', I_dma(dst, src.rearrange("(k p) n -> p k n", p=128)), w=[key])
    return t, key


class SlabStream:
    def __init__(self, cx, specs):
        self.cx, self.specs, self.loaded, self.i = cx, specs, [], 0

    def get(self):
        while len(self.loaded) < min(len(self.specs), self.i + 2):
            self.loaded.append(wslab(self.cx, self.specs[len(self.loaded)]))
        r = self.loaded[self.i]
        self.i += 1
        return r


VC = dict(gx=0, gm=8, gl=16, gq=24, gk=26, gn=28, aq=36, ak=37, dsk=38, m0=40, m1=41)
NVEC = 48


def tail_body(cx, A, glu, tb0, ntb, kv_ready):
    p = cx.p
    hT = cx.hT
    hk = lambda c, tb: 'h%d_%d' % (c, tb)
    hn = cx.hn
    big2 = cx.big2
    vec = cx.vec
    NB = ntb
    specs = []
    if glu:
        for ns in range(2):
            specs.append([(A['wglu'][:, ns * 512:(ns + 1) * 512], 0)])
            specs.append([(A['wglu'][:, 1024 + ns * 512:1024 + (ns + 1) * 512], 0)])
    if not kv_ready:
        for hp in range(2):
            specs.append([(A['wkv'][:, hp * 512:(hp + 1) * 512], 0)])
        for vs in range(2):
            specs.append([(A['wkv'][:, 1024 + vs * 512:1024 + (vs + 1) * 512], 0)])
    for hp in range(2):
        specs.append([(A['wq'][:, hp * 512:(hp + 1) * 512], 0)])
    for ns in range(2):
        specs.append([(A['wo'][:, ns * 512:(ns + 1) * 512], 0)])
    for s in range(DFF // 256):
        specs.append([(A['w1'][:, s * 256:(s + 1) * 256], 0), (A['w2'][s * 256:(s + 1) * 256, :], 2048)])
    ss = SlabStream(cx, specs)

    if glu:
        for c in range(NCH):
            for tb in range(NB):
                yt, yk = cx.rot('ytmp', [128, 512], F32)
                col0 = (tb0 + tb) * 512
                if 'yO' in A and c >= 4:
                    X, xk = cx.rot('gx', [128, 512], F32, n=2)
                    Z, zk = cx.rot('gz', [128, 512], F32, n=2)
                    p.dma('sp', I_dma(X[:], A['GS'].g_rows(0, (c - 4) * 128)[:, col0:col0 + 512]), w=[xk])
                    p.dma('sp', I_dma(Z[:], A['GS'].g_rows(1, (c - 4) * 128)[:, col0:col0 + 512]), w=[zk])
                    mcombine(cx, yt[:], yk, X[:], xk, Z[:], zk, vec[:, VC['m1']:VC['m1'] + 1], vec[:, VC['m0']:VC['m0'] + 1])
                elif 'yO' in A:
                    p.dma('sp', I_dma(yt[:], A['yO'][c * 128:(c + 1) * 128, col0:col0 + 512]), w=[yk])
                else:
                    p.dma('sp', I_dma(yt[:], A['yT'][c * 128:(c + 1) * 128, col0:col0 + 512]), w=[yk])
                p.op('act', I_act(hn[:, c, tb * 512:(tb + 1) * 512], yt[:], AF.Gelu_apprx_tanh), r=[yk], w=['hn%d' % tb])
        for ns in range(2):
            wa, wak = ss.get()
            wb, wbk = ss.get()
            for j in range(4):
                n = ns * 4 + j
                for tb in range(NB):
                    ba, bak = cx.bank()
                    bb, bbk = cx.bank()
                    for kc in range(NCH):
                        p.op('pe', I_mm(ba[:], wa[:, kc * 512 + j * 128: kc * 512 + (j + 1) * 128],
                                        hn[:, kc, tb * 512:(tb + 1) * 512], kc == 0, kc == NCH - 1),
                             r=[wak, 'hn%d' % tb], w=[bak])
                    for kc in range(NCH):
                        p.op('pe', I_mm(bb[:], wb[:, kc * 512 + j * 128: kc * 512 + (j + 1) * 128],
                                        hn[:, kc, tb * 512:(tb + 1) * 512], kc == 0, kc == NCH - 1),
                             r=[wbk, 'hn%d' % tb], w=[bbk])
                    sg, sgk = cx.rot('sg', [128, 512], F32)
                    p.op('act', I_act(sg[:], bb[:], AF.Sigmoid), r=[bbk], w=[sgk])
                    gt, gtk = cx.rot('gtmp', [128, 512], F32)
                    p.op('dve', I_tt(gt[:], ba[:], sg[:], ALU.mult), r=[bak, sgk], w=[gtk])
                    hs = hT[:, n, (tb0 + tb) * 512:(tb0 + tb + 1) * 512]
                    p.op('pool', I_tt(hs, hs, gt[:], ALU.add), r=[gtk, hk(n, tb0 + tb)], w=[hk(n, tb0 + tb)])

    if not kv_ready:
        kraw = cx.kraw
        memn = cx.memn
        for c in range(NCH):
            p.dma('sp', I_dma(kraw[:, c, :], A['memT'][c * 128:(c + 1) * 128, :]), w=['kraw'], semkey='kraw_ld')
        rmsnorm(cx, kraw, lambda c: 'kraw', vec[:, VC['gm']:VC['gm'] + 8], 'vec', memn, lambda c: 'memn', NCH, 0, MEMLEN, D)
        for hp in range(2):
            wk, wkk = ss.get()
            for jj in range(4):
                j = hp * 4 + jj
                bk_, bkk = cx.bank()
                for kc in range(NCH):
                    p.op('pe', I_mm(bk_[:, :MEMLEN], wk[:, kc * 512 + jj * 128: kc * 512 + (jj + 1) * 128], memn[:, kc, :],
                                    kc == 0, kc == NCH - 1), r=[wkk, 'memn'], w=[bkk])
                p.op('act', I_act(kraw[:, j, :], bk_[:, :MEMLEN], AF.Copy), r=[bkk], w=['kraw'])
        for h in range(4):
            bs, bsk = cx.bank()
            for ec in range(2):
                sq, sk = cx.next_sq()
                p.op('act', I_act(sq[:, :MEMLEN], kraw[:, 2 * h + ec, :], AF.Square), r=['kraw'], w=[sk])
                p.op('pe', I_mm(bs[:, :MEMLEN], cx.ones[:], sq[:, :MEMLEN], ec == 0, ec == 1), r=[sk, 'ones'], w=[bsk])
            rs, rk = cx.rstd_from_bank(bs, bsk, MEMLEN, 256)
            for ec in range(2):
                p.op('dve', I_stt(cx.KT[:, 2 * h + ec, :], kraw[:, 2 * h + ec, :], vec[:, VC['gk'] + ec:VC['gk'] + ec + 1],
                                  rs[:, :MEMLEN], ALU.mult, ALU.mult), r=['kraw', 'vec', rk], w=['KT'])
        for vs in range(2):
            wv, wvk = ss.get()
            for mc in range(2):
                bv, bvk = cx.bank()
                for kc in range(NCH):
                    p.op('pe', I_mm(bv[:], memn[:, kc, mc * 128:(mc + 1) * 128], wv[:, kc * 512:(kc + 1) * 512],
                                    kc == 0, kc == NCH - 1), r=[wvk, 'memn'], w=[bvk])
                p.op('act', I_act(cx.V[:, mc, vs * 512:(vs + 1) * 512], bv[:], AF.Copy), r=[bvk], w=['V'])

    for tb in range(NB):
        _rmsnorm_off(cx, hT, (tb0 + tb) * 512, lambda c, tb=tb: hk(c, tb0 + tb), vec[:, VC['gx']:VC['gx'] + 8],
                     hn, tb * 512, 'hn%d' % tb)
    its = [(hp, hh, tb) for hp in range(2) for hh in range(2) for tb in range(NB)]
    BK = lambda i: (cx.banks[i], 'bank%d' % i)
    wq_cur = {}
    stx = {}

    def X_phase(i):
        hp, hh, tb = its[i]
        if hp not in wq_cur:
            wq_cur[hp] = ss.get()
        wq, wqk = wq_cur[hp]
        qb = []
        for ec in range(2):
            b, bk_ = BK((i % 2) * 2 + ec)
            cc = hh * 2 + ec
            for kc in range(NCH):
                p.op('pe', I_mm(b[:], wq[:, kc * 512 + cc * 128: kc * 512 + (cc + 1) * 128],
                                hn[:, kc, tb * 512:(tb + 1) * 512], kc == 0, kc == NCH - 1),
                     r=[wqk, 'hn%d' % tb], w=[bk_])
            qb.append((b, bk_))
        stx[i] = dict(qb=qb)

    def Y_phase(i):
        qb = stx[i]['qb']
        bs, bsk = BK(4)
        for ec in range(2):
            sq, sk = cx.next_sq()
            p.op('act', I_act(sq[:], qb[ec][0][:], AF.Square), r=[qb[ec][1]], w=[sk])
            p.op('pe', I_mm(bs[:], cx.ones[:], sq[:], ec == 0, ec == 1), r=[sk, 'ones'], w=[bsk])
        rs, rk = cx.rstd_from_bank(bs, bsk, 512, 256)
        qn, qnk = cx.rot('qn', [128, 2, 512], BF16)
        for ec in range(2):
            p.op('dve', I_stt(qn[:, ec, :], qb[ec][0][:], vec[:, VC['gq'] + ec:VC['gq'] + ec + 1], rs[:],
                              ALU.mult, ALU.mult), r=[qb[ec][1], 'vec', rk], w=[qnk])
        stx[i]['qn'] = (qn, qnk)

    def Z_phase(i):
        hp, hh, tb = its[i]
        h = 2 * hp + hh
        qn, qnk = stx[i]['qn']
        PT, ptk = cx.rot('PT', [128, 2, 512], BF16)
        for mc in range(2):
            bl, blk = BK(5 + mc)
            for ec in range(2):
                p.op('pe', I_mm(bl[:], cx.KT[:, 2 * h + ec, mc * 128:(mc + 1) * 128], qn[:, ec, :], ec == 0, ec == 1),
                     r=['KT', qnk], w=[blk])
            p.op('act', I_act(PT[:, mc, :], bl[:], AF.Exp, scale=1.0 / 16.0), r=[blk], w=[ptk])
        bd, bdk = BK(7)
        for mc in range(2):
            p.op('pe', I_mm(bd[:], cx.ones[:], PT[:, mc, :], mc == 0, mc == 1), r=['ones', ptk], w=[bdk])
        rd, rdk = cx.rot('rden', [128, 512], F32)
        p.op('act', I_act(rd[:], bd[:], AF.Ln), r=[bdk], w=[rdk])
        p.op('act', I_act(rd[:], rd[:], AF.Exp, scale=-1.0), r=[rdk], w=[rdk])
        for ec in range(2):
            bo, bok = BK(5 + ec)
            for mc in range(2):
                p.op('pe', I_mm(bo[:], cx.V[:, mc, h * 256 + ec * 128: h * 256 + (ec + 1) * 128], PT[:, mc, :],
                                mc == 0, mc == 1), r=['V', ptk], w=[bok])
            p.op('dve', I_tt(big2[:, 2 * h + ec, tb * 512:(tb + 1) * 512], bo[:], rd[:], ALU.mult),
                 r=[bok, rdk], w=['big2_%d' % tb])
        del stx[i]
    nit = len(its)
    for step in range(nit + 2):
        if step < nit:
            X_phase(step)
        if 0 <= step - 1 < nit:
            Y_phase(step - 1)
        if 0 <= step - 2 < nit:
            Z_phase(step - 2)
    for ns in range(2):
        wo, wok = ss.get()
        for j in range(4):
            n = ns * 4 + j
            for tb in range(NB):
                b, bk_ = cx.bank()
                for kc in range(NCH):
                    p.op('pe', I_mm(b[:], wo[:, kc * 512 + j * 128: kc * 512 + (j + 1) * 128],
                                    big2[:, kc, tb * 512:(tb + 1) * 512], kc == 0, kc == NCH - 1),
                         r=[wok, 'big2_%d' % tb], w=[bk_])
                hs = hT[:, n, (tb0 + tb) * 512:(tb0 + tb + 1) * 512]
                p.op('dve', I_tt(hs, b[:], hs, ALU.add), r=[bk_, hk(n, tb0 + tb)], w=[hk(n, tb0 + tb)])

    for tb in range(NB):
        _rmsnorm_off(cx, hT, (tb0 + tb) * 512, lambda c, tb=tb: hk(c, tb0 + tb), vec[:, VC['gl']:VC['gl'] + 8],
                     hn, tb * 512, 'hn%d' % tb)
    def mlp_w1(s, ws, wsk):
        hb = s % 2
        for j in range(2):
            for tb in range(NB):
                b, bk_ = cx.bank()
                for kc in range(NCH):
                    p.op('pe', I_mm(b[:], ws[:, kc * 256 + j * 128: kc * 256 + (j + 1) * 128],
                                    hn[:, kc, tb * 512:(tb + 1) * 512], kc == 0, kc == NCH - 1),
                         r=[wsk, 'hn%d' % tb], w=[bk_])
                rt, rtk = cx.rot('rtmp', [128, 512], F32)
                p.op('act', I_act(rt[:], b[:], AF.Relu), r=[bk_], w=[rtk])
                p.op('pool', I_tt(big2[:, hb * 2 + j, tb * 512:(tb + 1) * 512], rt[:], rt[:], ALU.mult),
                     r=[rtk], w=['hid%d' % hb])

    def mlp_w2(s, ws, wsk):
        hb = s % 2
        for n in range(NCH):
            for tb in range(NB):
                b, bk_ = cx.bank()
                for j in range(2):
                    p.op('pe', I_mm(b[:], ws[:, 2048 + j * 1024 + n * 128: 2048 + j * 1024 + (n + 1) * 128],
                                    big2[:, hb * 2 + j, tb * 512:(tb + 1) * 512], j == 0, j == 1),
                         r=[wsk, 'hid%d' % hb], w=[bk_])
                hs = hT[:, n, (tb0 + tb) * 512:(tb0 + tb + 1) * 512]
                p.op('dve', I_tt(hs, b[:], hs, ALU.add), r=[bk_, hk(n, tb0 + tb)], w=[hk(n, tb0 + tb)])
    nsl = DFF // 256
    prev = None
    for s in range(nsl):
        ws, wsk = ss.get()
        mlp_w1(s, ws, wsk)
        if prev is not None:
            mlp_w2(*prev)
        prev = (s, ws, wsk)
    mlp_w2(*prev)


def _rmsnorm_off(cx, src, s0, skey, gcols, dst, d0, dkey, n=512, C=NCH, dim=D):
    p = cx.p
    bank, bk = cx.bank()
    for c in range(C):
        sq, sk = cx.next_sq()
        p.op('act', I_act(sq[:, :n], src[:, c, s0:s0 + n], AF.Square), r=[skey(c)], w=[sk])
        p.op('pe', I_mm(bank[:, :n], cx.ones[:], sq[:, :n], c == 0, c == C - 1), r=[sk, 'ones'], w=[bk])
    rs, rk = cx.rstd_from_bank(bank, bk, n, dim)
    for c in range(C):
        p.op('dve', I_stt(dst[:, c, d0:d0 + n], src[:, c, s0:s0 + n], gcols[:, c:c + 1], rs[:, :n], ALU.mult, ALU.mult),
             r=[skey(c), 'vec', rk], w=[dkey])


def common_tiles(cx, A):
    p = cx.p
    cx.hT = p.sb('hT', [128, NCH, NT], F32)
    cx.hn = p.sb('hn', [128, NCH, 1024], BF16)
    cx.big2 = p.sb('big2', [128, NCH, 1024], BF16)
    cx.vec = p.sb('vec', [128, NVEC], F32)
    cx.kraw = p.sb('kraw', [128, NCH, MEMLEN], F32)
    cx.memn = p.sb('memn', [128, NCH, MEMLEN], BF16)
    cx.KT = p.sb('KT', [128, NCH, MEMLEN], BF16)
    cx.V = p.sb('V', [128, 2, D], BF16)
    p.dma('sp', I_dma(cx.vec[:], A['vecs'][:, :]), w=['vec'])


def load_hT(cx, src):
    p = cx.p
    for c in range(NCH):
        p.dma('sp', I_dma(cx.hT[:, c, 0:512], src[c * 128:(c + 1) * 128, 0:512]), w=['h%d_0' % c], semkey='hld%d' % c)
    for c in range(NCH):
        p.dma('sp', I_dma(cx.hT[:, c, 512:NT], src[c * 128:(c + 1) * 128, 512:NT]),
              w=['h%d_%d' % (c, tb) for tb in range(1, NT // 512)], semkey='hldb%d' % c)


def store_hT(cx, dst):
    p = cx.p
    for c in range(NCH):
        p.dma('sp', I_dma(dst[c * 128:(c + 1) * 128, :], cx.hT[:, c, :]),
              r=['h%d_%d' % (c, tb) for tb in range(NT // 512)], w=['hout%d' % c])


def build_tail(glu, emit_hn, arena=False):
    nc = bass.Bass("TRN2", target_bir_lowering=False)
    A = {}

    def inp(name, shape, dt=F32):
        A[name] = nc.dram_tensor(name, list(shape), dt, kind="ExternalInput").ap()

    inp('hT', [D, NT])
    inp('memT', [D, MEMLEN])
    inp('vecs', [128, NVEC])
    inp('wq', [D, D])
    inp('wkv', [D, 2 * D])
    inp('wo', [D, D])
    inp('w1', [D, DFF])
    inp('w2', [DFF, D])
    if glu:
        inp('yT', [D, NT])
        inp('wglu', [D, 2 * D])
    A['hT_out'] = nc.dram_tensor('hT_out', [D, NT], F32, kind="ExternalOutput").ap()
    if emit_hn:
        A['hn_out'] = nc.dram_tensor('hn_out', [D, NT], F32, kind="ExternalOutput").ap()
    cx = Cx(nc, arena=arena)
    if arena:
        cx.new_stage()
    common_tiles(cx, A)
    load_hT(cx, A['hT'])
    for half in range(2):
        tail_body(cx, A, glu, half * 2, 2, kv_ready=(half == 1))
    store_hT(cx, A['hT_out'])
    if emit_hn:
        emit_norm(cx, A['hn_out'])
    cx.p.finalize()
    return nc, cx


def emit_norm(cx, dst):
    p = cx.p
    for tb in range(NT // 512):
        bank, bk = cx.bank()
        for c in range(NCH):
            sq, sk = cx.next_sq()
            p.op('act', I_act(sq[:], cx.hT[:, c, tb * 512:(tb + 1) * 512], AF.Square), r=['h%d_%d' % (c, tb)], w=[sk])
            p.op('pe', I_mm(bank[:], cx.ones[:], sq[:], c == 0, c == NCH - 1), r=[sk, 'ones'], w=[bk])
        rs, rk = cx.rstd_from_bank(bank, bk, 512, D)
        for c in range(NCH):
            ot, otk = cx.rot('ntmp', [128, 512], F32, n=3)
            p.op('dve', I_stt(ot[:], cx.hT[:, c, tb * 512:(tb + 1) * 512], cx.vec[:, VC['gn'] + c:VC['gn'] + c + 1], rs[:],
                              ALU.mult, ALU.mult), r=['h%d_%d' % (c, tb), 'vec', rk], w=[otk])
            drow = dst.src_rows(c * 128) if hasattr(dst, 'src_rows') else dst[c * 128:(c + 1) * 128, :]
            p.dma('sp', I_dma(drow[:, tb * 512:(tb + 1) * 512], ot[:]), r=[otk], w=['hnout'])


NPT = 16
SW = 512
NW = SEQ // SW
PI = math.pi


def s5_params(cx, A):
    p = cx.p
    NCOL = 2 * NPT
    T = {}
    for nm in ['lre', 'lim', 'ldt', 'dt', 'mag', 'ang', 'angc', 's1', 'c1', 'are', 'aim', 'nr', 'den', 't', 't2',
               'fre', 'fim', 'nfre', 'nfim']:
        T[nm] = p.sb('sp_' + nm, [128, NCOL], F32)
    k = 's5par'
    p.dma('sp', I_dma(T['lre'][:], A['lamre'][:, :]), w=[k], semkey='s5par_ld')
    p.dma('sp', I_dma(T['lim'][:], A['lamim'][:, :]), w=[k], semkey='s5par_ld')
    p.dma('sp', I_dma(T['ldt'][:], A['logdt'][:, :]), w=[k], semkey='s5par_ld')
    a = lambda n: T[n][:]
    p.op('act', I_act(a('dt'), a('ldt'), AF.Exp), r=[k], w=[k])
    p.op('dve', I_tt(a('t'), a('lre'), a('dt'), ALU.mult), r=[k], w=[k])
    p.op('act', I_act(a('mag'), a('t'), AF.Exp), r=[k], w=[k])
    p.op('dve', I_tt(a('ang'), a('lim'), a('dt'), ALU.mult), r=[k], w=[k])
    for _ in range(5):
        p.op('dve', I_ts(a('t'), a('ang'), PI, 2 * PI, ALU.is_gt, ALU.mult), r=[k], w=[k])
        p.op('dve', I_tt(a('ang'), a('ang'), a('t'), ALU.subtract), r=[k], w=[k])
    p.op('dve', I_ts(a('angc'), a('ang'), PI / 2, None, ALU.add), r=[k], w=[k])
    p.op('dve', I_ts(a('t'), a('angc'), PI, 2 * PI, ALU.is_gt, ALU.mult), r=[k], w=[k])
    p.op('dve', I_tt(a('angc'), a('angc'), a('t'), ALU.subtract), r=[k], w=[k])
    p.op('act', I_act(a('s1'), a('ang'), AF.Sin), r=[k], w=[k])
    p.op('act', I_act(a('c1'), a('angc'), AF.Sin), r=[k], w=[k])
    p.op('dve', I_tt(a('are'), a('mag'), a('c1'), ALU.mult), r=[k], w=[k])
    p.op('dve', I_tt(a('aim'), a('mag'), a('s1'), ALU.mult), r=[k], w=[k])
    p.op('dve', I_ts(a('nr'), a('are'), -1.0, None, ALU.add), r=[k], w=[k])
    p.op('dve', I_tt(a('den'), a('lre'), a('lre'), ALU.mult), r=[k], w=[k])
    p.op('dve', I_tt(a('t'), a('lim'), a('lim'), ALU.mult), r=[k], w=[k])
    p.op('dve', I_tt(a('den'), a('den'), a('t'), ALU.add), r=[k], w=[k])
    p.op('dve', I_recip(a('den'), a('den')), r=[k], w=[k])
    p.op('dve', I_tt(a('t'), a('nr'), a('lre'), ALU.mult), r=[k], w=[k])
    p.op('dve', I_tt(a('t2'), a('aim'), a('lim'), ALU.mult), r=[k], w=[k])
    p.op('dve', I_tt(a('t'), a('t'), a('t2'), ALU.add), r=[k], w=[k])
    p.op('dve', I_tt(a('fre'), a('t'), a('den'), ALU.mult), r=[k], w=[k])
    p.op('dve', I_tt(a('t'), a('aim'), a('lre'), ALU.mult), r=[k], w=[k])
    p.op('dve', I_tt(a('t2'), a('nr'), a('lim'), ALU.mult), r=[k], w=[k])
    p.op('dve', I_tt(a('t'), a('t'), a('t2'), ALU.subtract), r=[k], w=[k])
    p.op('dve', I_tt(a('fim'), a('t'), a('den'), ALU.mult), r=[k], w=[k])
    p.op('dve', I_ts(a('nfre'), a('fre'), -1.0, None, ALU.mult), r=[k], w=[k])
    p.op('dve', I_ts(a('nfim'), a('fim'), -1.0, None, ALU.mult), r=[k], w=[k])
    nlv = int(math.log2(SW))
    T['pwc'] = p.sb('sp_pwc', [128, nlv + 1, NCOL], F32)
    T['pws'] = p.sb('sp_pws', [128, nlv + 1, NCOL], F32)
    T['npws'] = p.sb('sp_npws', [128, NCOL], F32)
    p.op('dve', I_copy(T['pwc'][:, 0, :], a('c1')), r=[k], w=[k])
    p.op('dve', I_copy(T['pws'][:, 0, :], a('s1')), r=[k], w=[k])
    for lv in range(nlv):
        c_ = T['pwc'][:, lv, :]
        s_ = T['pws'][:, lv, :]
        p.op('dve', I_tt(a('t'), s_, s_, ALU.mult), r=[k], w=[k])
        p.op('dve', I_tt(a('t2'), c_, c_, ALU.mult), r=[k], w=[k])
        p.op('dve', I_tt(T['pwc'][:, lv + 1, :], a('t2'), a('t'), ALU.subtract), r=[k], w=[k])
        p.op('dve', I_stt(T['pws'][:, lv + 1, :], c_, 2.0, s_, ALU.mult, ALU.mult), r=[k], w=[k])
    p.op('dve', I_ts(T['npws'][:], T['pws'][:, nlv, :], -1.0, None, ALU.mult), r=[k], w=[k])
    return T


def s5_body(cx, A):
    p = cx.p
    T = s5_params(cx, A)
    PK = 's5par'
    if 'dbg' in A:
        for i, nm in enumerate(['dt', 'mag', 'ang', 's1', 'c1', 'fre', 'fim', 'den']):
            p.dma('sp', I_dma(A['dbg'][:, i * 2 * NPT:(i + 1) * 2 * NPT], T[nm][:]), r=[PK], w=['dbgo'])
    ub = p.sb('ub', [128, 4, SEQ], BF16)
    if 'GHN' in A:
        G = A['GHN']
        m0c = cx.vec[:, VC['m0']:VC['m0'] + 1]
        m1c = cx.vec[:, VC['m1']:VC['m1'] + 1]
        for ck in range(4):
            for w in range(NW):
                r = 0 if w < NW // 2 else 1
                if r == 0:
                    cols = slice(w * SW, (w + 1) * SW)
                else:
                    w2 = w - NW // 2
                    cols = slice(NT - (w2 + 1) * SW, NT - w2 * SW)
                X, xk = cx.rot('gx', [128, SW], F32, n=2)
                Z, zk = cx.rot('gz', [128, SW], F32, n=2)
                p.dma('sp', I_dma(X[:], G.g_rows(r, ck * 128)[:, cols]), w=[xk])
                p.dma('sp', I_dma(Z[:], G.g_rows(r, 512 + ck * 128)[:, cols]), w=[zk])
                dst = ub[:, ck, w * SW:(w + 1) * SW]
                if r == 1:
                    dst = dst[:, ::-1]
                mcombine(cx, dst, 'ub%d' % ck, X[:], xk, Z[:], zk, m0c if r == 0 else m1c, m1c if r == 0 else m0c)
    else:
        for ck in range(4):
            p.dma('pool', I_dma(ub[:, ck, :], A['uT'][ck * 128:(ck + 1) * 128, :]), w=['ub%d' % ck])
    ident = p.sb('ident', [128, 128], F32)
    p.dma('sp', I_dma(ident[:], A['ident'][:, :]), w=['ident'])
    dsk = p.sb('dskc', [128, 4], F32)
    p.dma('sp', I_dma(dsk[:], A['dsk'][:, :]), w=['dskc'])
    yacc = [p.sb('yacc%d' % i, [128, SEQ], F32) for i in range(2)]
    bb_i = [0]

    def bbank():
        i = bb_i[0]
        bb_i[0] = (i + 1) % 6
        return cx.banks[i], 'bank%d' % i
    yb_i = [0]

    def ybank():
        i = 6 + yb_i[0]
        yb_i[0] = 1 - yb_i[0]
        return cx.banks[i], 'bank%d' % i

    for ck in range(4):
        ya = yacc[ck % 2]
        yk = 'yacc%d' % (ck % 2)
        dD, dDk = cx.rot('diagD', [128, 128], BF16)
        p.op('dve', I_ts(dD[:], ident[:], dsk[:, ck:ck + 1], None, ALU.mult), r=['ident', 'dskc'], w=[dDk])
        for d in range(2):
            tabs = []
            col0 = d * NPT + ck * 4
            cos4, c4k = cx.rot('cos4', [128, 4, SW], F32, n=2)
            sin4, s4k = cx.rot('sin4', [128, 4, SW], F32, n=2)
            tk = c4k
            p.op('dve', I_memset(cos4[:, :, 0:1], 1.0), w=[tk])
            p.op('dve', I_memset(sin4[:, :, 0:1], 0.0), w=[tk])
            L = 1
            lv = 0
            while L < SW:
                pcb = T['pwc'][:, lv, col0:col0 + 4].unsqueeze(2).to_broadcast([128, 4, L])
                psb = T['pws'][:, lv, col0:col0 + 4].unsqueeze(2).to_broadcast([128, 4, L])
                ta, tak = cx.rot('tbA', [128, 4, SW // 2], F32, n=1)
                tb2, tbk = cx.rot('tbB', [128, 4, SW // 2], F32, n=1)
                p.op('dve', I_tt(ta[:, :, :L], sin4[:, :, 0:L], psb, ALU.mult), r=[tk, PK], w=[tak])
                p.op('dve', I_tt(tb2[:, :, :L], cos4[:, :, 0:L], pcb, ALU.mult), r=[tk, PK], w=[tbk])
                p.op('dve', I_tt(cos4[:, :, L:2 * L], tb2[:, :, :L], ta[:, :, :L], ALU.subtract), r=[tak, tbk], w=[tk])
                p.op('dve', I_tt(ta[:, :, :L], cos4[:, :, 0:L], psb, ALU.mult), r=[tk, PK], w=[tak])
                p.op('dve', I_tt(tb2[:, :, :L], sin4[:, :, 0:L], pcb, ALU.mult), r=[tk, PK], w=[tbk])
                p.op('dve', I_tt(sin4[:, :, L:2 * L], tb2[:, :, :L], ta[:, :, :L], ALU.add), r=[tak, tbk], w=[tk])
                L *= 2
                lv += 1
            for q in range(4):
                pt = ck * 4 + q
                col = d * NPT + pt
                cosT = cos4[:, q, :]
                sinT = sin4[:, q, :]
                cW = T['pwc'][:, lv, col:col + 1]
                sW = T['pws'][:, lv, col:col + 1]
                nsW = T['npws'][:, col:col + 1]
                braw, brk = cx.rot('bw', [128, 2, 128], BF16, n=8)
                p.dma('pool', I_dma(braw[:, 0, :], A['Bre'][d, pt]), w=[brk])
                p.dma('pool', I_dma(braw[:, 1, :], A['Bim'][d, pt]), w=[brk])
                craw, crk = cx.rot('craw', [128, 2, 128], F32, n=2)
                p.dma('sp', I_dma(craw[:, 0, :], A['CR'][d, pt]), w=[crk])
                p.dma('sp', I_dma(craw[:, 1, :], A['CI'][d, pt]), w=[crk])
                cw, cwk = cx.rot('cw', [128, 3, 128], BF16, n=8)
                ctmp, ctk = cx.rot('ctmp', [128, 128], F32, n=2)
                fre = T['fre'][:, col:col + 1]
                nfim = T['nfim'][:, col:col + 1]
                nfre = T['nfre'][:, col:col + 1]
                p.op('dve', I_ts(ctmp[:], craw[:, 1, :], nfim, None, ALU.mult), r=[crk, PK], w=[ctk])
                p.op('dve', I_stt(cw[:, 0, :], craw[:, 0, :], fre, ctmp[:], ALU.mult, ALU.add), r=[crk, PK, ctk], w=[cwk])
                ctmp2, ctk2 = cx.rot('ctmp', [128, 128], F32, n=2)
                p.op('dve', I_ts(ctmp2[:], craw[:, 0, :], nfim, None, ALU.mult), r=[crk, PK], w=[ctk2])
                p.op('dve', I_stt(cw[:, 1, :], craw[:, 1, :], nfre, ctmp2[:], ALU.mult, ALU.add), r=[crk, PK, ctk2], w=[cwk])
                ctmp3, ctk3 = cx.rot('ctmp', [128, 128], F32, n=2)
                p.op('dve', I_ts(ctmp3[:], craw[:, 1, :], T['fim'][:, col:col + 1], None, ALU.mult), r=[crk, PK], w=[ctk3])
                p.op('dve', I_stt(cw[:, 2, :], craw[:, 0, :], nfre, ctmp3[:], ALU.mult, ALU.add), r=[crk, PK, ctk3], w=[cwk])
                car, cak = cx.rot('carry', [128, 8], F32, n=8)
                tabs.append(dict(cos=cosT, sin=sinT, tk=tk, cW=cW, sW=sW, nsW=nsW, braw=braw, brk=brk, cw=cw, cwk=cwk,
                                 r=T['mag'][:, col:col + 1], car=car, cak=cak))
            worder = range(NW) if d == 0 else range(NW - 1, -1, -1)
            rv = (lambda ap: ap) if d == 0 else (lambda ap: ap[:, ::-1])
            units = [dict(wi=wi, w=w, q=q) for wi, w in enumerate(worder) for q in range(4)]
            ybs = {}

            def P01(u):
                tb_ = tabs[u['q']]
                win = slice(u['w'] * SW, (u['w'] + 1) * SW)
                bre, brek = bbank()
                bim, bimk = bbank()
                p.op('pe', I_mm(bre[:], tb_['braw'][:, 0, :], ub[:, ck, win], True, True), r=[tb_['brk'], 'ub%d' % ck], w=[brek])
                p.op('pe', I_mm(bim[:], tb_['braw'][:, 1, :], ub[:, ck, win], True, True), r=[tb_['brk'], 'ub%d' % ck], w=[bimk])
                cosT, sinT, tk = tb_['cos'], tb_['sin'], tb_['tk']
                t1, t1k = cx.rot('t1', [128, SW], F32)
                t2, t2k = cx.rot('t2', [128, SW], F32)
                t3, t3k = cx.rot('t3', [128, SW], F32)
                t4, t4k = cx.rot('t4', [128, SW], F32)
                p.op('dve', I_tt(t1[:], rv(bre[:]), cosT, ALU.mult), r=[brek, tk], w=[t1k])
                p.op('dve', I_tt(t2[:], rv(bim[:]), sinT, ALU.mult), r=[bimk, tk], w=[t2k])
                p.op('dve', I_tt(t3[:], rv(bim[:]), cosT, ALU.mult), r=[bimk, tk], w=[t3k])
                p.op('dve', I_tt(t4[:], rv(bre[:]), sinT, ALU.mult), r=[brek, tk], w=[t4k])
                u.update(t=(t1, t1k, t2, t2k, t3, t3k, t4, t4k))

            def P2(u):
                t1, t1k, t2, t2k, t3, t3k, t4, t4k = u['t']
                wre, wrk = cx.rot('wre', [128, SW], F32)
                wim, wik = cx.rot('wim', [128, SW], F32)
                p.op('pool', I_tt(wre[:], t1[:], t2[:], ALU.add), r=[t1k, t2k], w=[wrk])
                p.op('pool', I_tt(wim[:], t3[:], t4[:], ALU.subtract), r=[t3k, t4k], w=[wik])
                u.update(wv=(wre, wrk, wim, wik))

            def P3(u):
                tb_ = tabs[u['q']]
                tk = tb_['tk']
                wre, wrk, wim, wik = u['wv']
                zre, zrk = cx.rot('zre', [128, SW], F32)
                zim, zik = cx.rot('zim', [128, SW], F32)
                car, cak = tb_['car'], tb_['cak']
                rbc = tb_['r'].to_broadcast([128, SW])
                if u['wi'] == 0:
                    ire, iim = 0.0, 0.0
                else:
                    ire, iim = car[:, 2:3], car[:, 3:4]
                p.op('dve', I_scan(zre[:], rbc, wre[:], ire), r=[PK, wrk, cak], w=[zrk])
                p.op('dve', I_scan(zim[:], rbc, wim[:], iim), r=[PK, wik, cak], w=[zik])
                if u['wi'] < NW - 1:
                    p.op('act', I_act(car[:, 0:1], zim[:, SW - 1:SW], AF.Copy, scale=tb_['nsW']), r=[zik, PK], w=[cak])
                    p.op('act', I_act(car[:, 1:2], zre[:, SW - 1:SW], AF.Copy, scale=tb_['sW']), r=[zrk, PK], w=[cak])
                    p.op('act', I_act(car[:, 2:3], zre[:, SW - 1:SW], AF.Identity, scale=tb_['cW'], bias=car[:, 0:1]), r=[zrk, PK], w=[cak])
                    p.op('act', I_act(car[:, 3:4], zim[:, SW - 1:SW], AF.Identity, scale=tb_['cW'], bias=car[:, 1:2]), r=[zik, PK], w=[cak])
                u.update(z=(zre, zrk, zim, zik))

            def P4(u):
                tb_ = tabs[u['q']]
                cosT, sinT, tk = tb_['cos'], tb_['sin'], tb_['tk']
                zre, zrk, zim, zik = u['z']
                u1, u1k = cx.rot('u1', [128, SW], BF16, n=3)
                u2, u2k = cx.rot('u2', [128, SW], BF16, n=3)
                u3, u3k = cx.rot('u3', [128, SW], BF16, n=3)
                u4, u4k = cx.rot('u4', [128, SW], BF16, n=3)
                p.op('pool', I_tt(rv(u1[:]), zre[:], cosT, ALU.mult), r=[zrk, tk], w=[u1k])
                p.op('pool', I_tt(rv(u2[:]), zim[:], sinT, ALU.mult), r=[zik, tk], w=[u2k])
                p.op('pool', I_tt(rv(u3[:]), zim[:], cosT, ALU.mult), r=[zik, tk], w=[u3k])
                p.op('dve', I_tt(rv(u4[:]), zre[:], sinT, ALU.mult), r=[zrk, tk], w=[u4k])
                u.update(uu=(u1, u1k, u2, u2k, u3, u3k, u4, u4k))

            def P5(u):
                pass

            def P6(u):
                tb_ = tabs[u['q']]
                w, q = u['w'], u['q']
                win = slice(w * SW, (w + 1) * SW)
                if q == 0:
                    ybs[w] = ybank()
                yb, ybk = ybs[w]
                u1, u1k, u2, u2k, u3, u3k, u4, u4k = u['uu']
                first = (q == 0)
                last = (q == 3) and d == 1
                p.op('pe', I_mm(yb[:], tb_['cw'][:, 0, :], u1[:], first, False), r=[tb_['cwk'], u1k], w=[ybk])
                p.op('pe', I_mm(yb[:], tb_['cw'][:, 2, :], u2[:], False, False), r=[tb_['cwk'], u2k], w=[ybk])
                p.op('pe', I_mm(yb[:], tb_['cw'][:, 1, :], u3[:], False, False), r=[tb_['cwk'], u3k], w=[ybk])
                p.op('pe', I_mm(yb[:], tb_['cw'][:, 1, :], u4[:], False, last), r=[tb_['cwk'], u4k], w=[ybk])
                if q == 3:
                    if d == 0:
                        p.op('pe', I_mm(yb[:], dD[:], ub[:, ck, win], False, True), r=[dDk, 'ub%d' % ck], w=[ybk])
                        p.op('act', I_act(ya[:, win], yb[:], AF.Copy), r=[ybk], w=[yk + '_%d' % w])
                    else:
                        p.op('dve', I_tt(ya[:, win], yb[:], ya[:, win], ALU.add), r=[ybk, yk + '_%d' % w], w=[yk + '_%d' % w])
            nu = len(units)
            for step in range(nu + 2):
                if step < nu:
                    P01(units[step])
                    P2(units[step])
                if 0 <= step - 1 < nu:
                    P3(units[step - 1])
                    P4(units[step - 1])
                if 0 <= step - 2 < nu:
                    P5(units[step - 2])
                    P6(units[step - 2])
        if 'yO' in A:
            m0c = cx.vec[:, VC['m0']:VC['m0'] + 1]
            m1c = cx.vec[:, VC['m1']:VC['m1'] + 1]
            for hb in range(NT // SW):
                A1 = ya[:, hb * SW:(hb + 1) * SW]
                B1 = ya[:, SEQ - (hb + 1) * SW:SEQ - hb * SW][:, ::-1]
                ka = yk + '_%d' % hb
                kb = yk + '_%d' % (NW - 1 - hb)
                ot, otk = cx.rot('yo_t', [128, SW], F32, n=2)
                mcombine(cx, ot[:], otk, A1, ka, B1, kb, m0c, m1c)
                p.dma('sp', I_dma(A['yO'][ck * 128:(ck + 1) * 128, hb * SW:(hb + 1) * SW], ot[:]), r=[otk], w=['yout%d' % ck])
                st_, stk_ = cx.rot('yo_t', [128, SW], F32, n=2)
                mcombine(cx, st_[:], stk_, A1, ka, B1, kb, m1c, m0c)
                p.dma('sp', I_dma(A['yS'].src_rows(ck * 128)[:, hb * SW:(hb + 1) * SW], st_[:]), r=[stk_], w=['yout%d' % ck])
        else:
            p.dma('sp', I_dma(A['yT'][ck * 128:(ck + 1) * 128, :], ya[:]), r=[yk + '_%d' % w for w in range(NW)], w=['yout%d' % ck])


def build_s5(debug=False, arena=False):
    nc = bass.Bass("TRN2", target_bir_lowering=False)
    A = {}

    def inp(name, shape, dt=F32):
        A[name] = nc.dram_tensor(name, list(shape), dt, kind="ExternalInput").ap()
    inp('uT', [512, SEQ])
    inp('Bre', [2, NPT, 128, 128])
    inp('Bim', [2, NPT, 128, 128])
    inp('CR', [2, NPT, 128, 128])
    inp('CI', [2, NPT, 128, 128])
    inp('lamre', [128, 2 * NPT])
    inp('lamim', [128, 2 * NPT])
    inp('logdt', [128, 2 * NPT])
    inp('dsk', [128, 4])
    inp('ident', [128, 128])
    A['yT'] = nc.dram_tensor('yT', [512, SEQ], F32, kind="ExternalOutput").ap()
    cx = Cx(nc, arena=arena)
    if arena:
        cx.new_stage()
    if debug:
        A['dbg'] = nc.dram_tensor('dbg', [128, 8 * 2 * NPT], F32, kind="ExternalOutput").ap()
        A['dbg2'] = nc.dram_tensor('dbg2', [128, 2 * SW], F32, kind="ExternalOutput").ap()
    s5_body(cx, A)
    cx.p.finalize()
    return nc, cx


def s5_host_inputs(inp, j, half):
    g0 = 32 * half
    Bre = np.zeros((2, NPT, 128, 128), np.float32)
    Bim = np.zeros_like(Bre)
    CR = np.zeros_like(Bre)
    CI = np.zeros_like(Bre)
    lamre = np.zeros((128, 2 * NPT), np.float32)
    lamim = np.zeros_like(lamre)
    logdt = np.zeros_like(lamre)
    for d in range(2):
        for pt in range(NPT):
            for gl in range(2):
                g = g0 + 2 * pt + gl
                r0 = (pt % 4) * 32 + gl * 16
                Bre[d, pt, r0:r0 + 16, gl * 64:(gl + 1) * 64] = inp['s5_b_re'][j, d, g].T
                Bim[d, pt, r0:r0 + 16, gl * 64:(gl + 1) * 64] = inp['s5_b_im'][j, d, g].T
                CR[d, pt, gl * 64:(gl + 1) * 64, r0:r0 + 16] = inp['s5_c_re'][j, d, g].T
                CI[d, pt, gl * 64:(gl + 1) * 64, r0:r0 + 16] = inp['s5_c_im'][j, d, g].T
                lamre[gl * 64:(gl + 1) * 64, d * NPT + pt] = inp['s5_lambda_re'][j, d, g]
                lamim[gl * 64:(gl + 1) * 64, d * NPT + pt] = inp['s5_lambda_im'][j, d, g]
                logdt[gl * 64:(gl + 1) * 64, d * NPT + pt] = inp['s5_log_dt'][j, d, g]
    dsk = np.ascontiguousarray(inp['s5_d'][j, 512 * half:512 * half + 512].reshape(4, 128).T)
    return dict(Bre=Bre, Bim=Bim, CR=CR, CI=CI, lamre=lamre, lamim=lamim, logdt=logdt, dsk=dsk,
                ident=np.eye(128, dtype=np.float32))


NEXT = 3072
GRP = [(1, 2048), (4, 512), (16, 128)]
ASCALE = 128 ** -0.5


def sub_view(ap2d, d):
    if d == 1:
        return ap2d.rearrange("p (d i) -> p d i", d=1)
    return ap2d.rearrange("p (i d) -> p d i", d=d)


def attn_body(cx, A, flip=False, src=None, dst=None, gh=None):
    p = cx.p
    vec = cx.vec

    def load_blk(xt, xk, c, tb):
        if gh is None or tb < NT // 512:
            p.dma('sp', I_dma(xt[:], _src[c * 128:(c + 1) * 128, _cols(tb)]), w=[xk])
            return
        hb = tb - NT // 512
        cols = slice(1024 - 512 * (hb + 1), 1024 - 512 * hb)
        pc = (c + 4) % 8
        X, xk2 = cx.rot('gx', [128, 512], F32, n=2)
        Z, zk2 = cx.rot('gz', [128, 512], F32, n=2)
        p.dma('sp', I_dma(X[:], gh.g_rows(0, pc * 128)[:, cols]), w=[xk2])
        p.dma('sp', I_dma(Z[:], gh.g_rows(1, pc * 128)[:, cols]), w=[zk2])
        mcombine(cx, xt[:], xk, X[:], xk2, Z[:], zk2, vec[:, VC['m1']:VC['m1'] + 1], vec[:, VC['m0']:VC['m0'] + 1])

    def _cols(tb):
        if src is None or not flip:
            return slice(tb * 512, (tb + 1) * 512)
        return slice(SEQ - (tb + 1) * 512, SEQ - tb * 512)
    _src = A['hT_ext'] if src is None else src
    _dst = A['hT_out'] if dst is None else dst
    rvf = (lambda ap: ap[:, ::-1]) if (flip and src is not None) else (lambda ap: ap)
    hn = p.sb('hnx', [128, NCH, NEXT], BF16)
    mT = p.sb('mT', [128, NCH, NT], BF16)
    num = p.sb('numacc', [128, NT], F32)
    den = p.sb('denacc', [128, NT], F32)
    m2 = getattr(p, 'aoff', None)
    NXB = 16 if m2 is not None else 2
    for tb in range(NEXT // 512):
        bank, bk = cx.bank()
        tiles_ = []
        for c in range(NCH):
            xt, xk = cx.rot('xin', [128, 512], F32, n=NXB)
            load_blk(xt, xk, c, tb)
            tiles_.append((xt, xk))
            sq, sk = cx.next_sq()
            p.op('act', I_act(sq[:], xt[:], AF.Square), r=[xk], w=[sk])
            p.op('pe', I_mm(bank[:], cx.ones[:], sq[:], c == 0, c == NCH - 1), r=[sk, 'ones'], w=[bk])
        rs, rk = cx.rstd_from_bank(bank, bk, 512, D)
        for c in range(NCH):
            if m2 is not None:
                xt, xk = tiles_[c]
            else:
                xt, xk = cx.rot('xin', [128, 512], F32, n=NXB)
                load_blk(xt, xk, c, tb)
            rv_ = (lambda ap: ap[:, ::-1]) if (gh is not None and tb >= NT // 512) else rvf
            p.op('dve', I_stt(rv_(hn[:, c, tb * 512:(tb + 1) * 512]), xt[:], vec[:, VC['gn'] + c:VC['gn'] + c + 1], rs[:],
                              ALU.mult, ALU.mult), r=[xk, 'vec', rk], w=['hnx'])
    if m2 is not None:
        p.barrier()
        p.aoff = m2
        cx._rot = {}
    sb_i = [0]

    def sbank():
        i = sb_i[0]
        sb_i[0] = (i + 1) % 4
        return cx.banks[i], 'bank%d' % i
    ob_i = [0]

    def obanks():
        i = ob_i[0]
        ob_i[0] = 1 - i
        return cx.banks[4 + i], 'bank%d' % (4 + i), cx.banks[6 + i], 'bank%d' % (6 + i)

    def qknorm(bank, bk, n, gcol, dst, dkey):
        sq, sk = cx.next_sq()
        p.op('act', I_act(sq[:, :n], bank[:, :n], AF.Square), r=[bk], w=[sk])
        b2, b2k = sbank()
        p.op('pe', I_mm(b2[:, :n], cx.ones[:], sq[:, :n], True, True), r=[sk, 'ones'], w=[b2k])
        rs, rk = cx.rstd_from_bank(b2, b2k, n, 128)
        p.op('dve', I_stt(dst, bank[:, :n], vec[:, gcol:gcol + 1], rs[:, :n], ALU.mult, ALU.mult), r=[bk, 'vec', rk], w=[dkey])

    wq_loaded = {}
    bias_loaded = {}

    def load_w(h_, g_):
        if (h_, g_) in wq_loaded or h_ >= 8:
            return
        wsl_, wsk_ = cx.rot('wqkv', [128, NCH, 384], BF16, n=3)
        for kind in range(3):
            c0 = kind * 3072 + g_ * 1024 + h_ * 128
            p.dma('pool', I_dma(wsl_[:, :, kind * 128:(kind + 1) * 128],
                                A['wqkv'][:, c0:c0 + 128].rearrange("(k p) n -> p k n", p=128)), w=[wsk_])
        wq_loaded[(h_, g_)] = (wsl_, wsk_)

    def load_bias(h_):
        if h_ in bias_loaded or h_ >= 8:
            return
        bt_, btk_ = cx.rot('biasT', [128, 3, 256], F32, n=2)
        for g_ in range(3):
            p.dma('sp', I_dma(bt_[:, g_, :], A['biasT'][g_ * 8 + h_]), w=[btk_])
        bias_loaded[h_] = (bt_, btk_)

    for h in range(8):
        p.op('pool', I_memset(num[:], 0.0), w=['numacc'])
        p.op('pool', I_memset(den[:], 0.0), w=['denacc'])
        load_bias(h)
        bt, btk = bias_loaded[h]
        for g, (d, Lq) in enumerate(GRP):
            nto = Lq // 128
            load_w(h, g)
            wsl, wsk = wq_loaded[(h, g)]
            load_w(h + (g + 1) // 3, (g + 1) % 3)
            if g == 0:
                load_bias(h + 1)
            qT, qk_ = cx.rot('qT', [128, NT], BF16, n=2)
            kT, kk_ = cx.rot('kT', [128, NEXT], BF16, n=2)
            vt, vk_ = cx.rot('vt', [128, 32, 128], BF16, n=2)

            for kind, dstT, dk, gcol in ((0, qT, qk_, VC['aq']), (1, kT, kk_, VC['ak'])):
                for bi in range(4):
                    b, bk = sbank()
                    for kc in range(NCH):
                        if d == 1:
                            rhs, o_ap = hn[:, kc, bi * 512:(bi + 1) * 512], b[:]
                        elif d == 4:
                            rhs, o_ap = sub_view(hn[:, kc, 0:NT], 4)[:, bi, :], b[:]
                        else:
                            rhs = sub_view(hn[:, kc, 0:NT], 16)[:, 4 * bi:4 * bi + 4, :]
                            o_ap = b[:].rearrange("p (a b) -> p a b", a=4)
                        p.op('pe', I_mm(o_ap, wsl[:, kc, kind * 128:(kind + 1) * 128], rhs, kc == 0, kc == NCH - 1),
                             r=[wsk, 'hnx'], w=[bk])
                    qknorm(b, bk, 512, gcol, dstT[:, bi * 512:(bi + 1) * 512], dk)
            nh = 64 * d
            for b0 in range(0, nh, 512):
                n = min(512, nh - b0)
                b, bk = sbank()
                for kc in range(NCH):
                    if d == 1:
                        rhs = hn[:, kc, NT:NT + 64]
                        o_ap = b[:, :64]
                    else:
                        r0 = b0 // 64
                        nr = n // 64
                        rhs = sub_view(hn[:, kc, NT:NT + 64 * d], d)[:, r0:r0 + nr, :]
                        o_ap = b[:, :n].rearrange("p (a b) -> p a b", a=nr)
                    p.op('pe', I_mm(o_ap, wsl[:, kc, 128:256], rhs, kc == 0, kc == NCH - 1), r=[wsk, 'hnx'], w=[bk])
                qknorm(b, bk, n, VC['ak'], kT[:, NT + b0:NT + b0 + n], kk_)
            for t0 in range(0, 16, 4):
                b, bk = sbank()
                for tt in range(4):
                    t = t0 + tt
                    r, m = t // nto, t % nto
                    for kc in range(NCH):
                        lhsT = sub_view(hn[:, kc, 0:NT], d)[:, r, m * 128:(m + 1) * 128]
                        p.op('pe', I_mm(b[:, tt * 128:(tt + 1) * 128], lhsT, wsl[:, kc, 256:384], kc == 0, kc == NCH - 1),
                             r=[wsk, 'hnx'], w=[bk])
                p.op('act', I_act(vt[:, t0:t0 + 4, :], b[:].rearrange("p (a b) -> p a b", a=4), AF.Copy), r=[bk], w=[vk_])
            for r0 in range(0, d, 4):
                nr = min(4, d - r0)
                b, bk = sbank()
                for rr in range(nr):
                    r = r0 + rr
                    for kc in range(NCH):
                        lhsT = sub_view(hn[:, kc, NT:NT + 64 * d], d)[:, r, :]
                        p.op('pe', I_mm(b[:64, rr * 128:(rr + 1) * 128], lhsT, wsl[:, kc, 256:384], kc == 0, kc == NCH - 1),
                             r=[wsk, 'hnx'], w=[bk])
                p.op('act', I_act(vt[:64, 16 + r0:16 + r0 + nr, :], b[:64, :nr * 128].rearrange("p (a b) -> p a b", a=nr), AF.Copy),
                     r=[bk], w=[vk_])
            tiles = [(r, m) for r in range(d) for m in range(nto + 1)]
            stt_ = {'ob': None}

            def S_phase(r, m):
                qoff = r * Lq
                halo = (m == nto)
                nk = 64 if halo else 128
                b0_ = 64 if m == 0 else 0
                b1_ = 64 if halo else min(256, Lq - (128 * m - 64))
                ktile = kT[:, NT + r * 64:NT + r * 64 + 64] if halo else kT[:, qoff + m * 128:qoff + (m + 1) * 128]
                qs = qoff + 128 * m - 64 + b0_
                sbk, sbkk = sbank()
                p.op('pe', I_mm(sbk[:nk, b0_:b1_], ktile, qT[:, qs:qs + (b1_ - b0_)], True, True), r=[kk_, qk_], w=[sbkk])
                st, stk = cx.rot('stmp', [128, 256], F32, n=3)
                p.op('dve', I_stt(st[:nk, b0_:b1_], sbk[:nk, b0_:b1_], ASCALE, bt[:nk, g, b0_:b1_], ALU.mult, ALU.add),
                     r=[sbkk, btk], w=[stk])
                PT, ptk = cx.rot('PTa', [128, 256], BF16, n=4)
                p.op('act', I_act(PT[:nk, b0_:b1_], st[:nk, b0_:b1_], AF.Exp), r=[stk], w=[ptk])
                return PT, ptk

            def PV_phase(r, m, PT, ptk):
                halo = (m == nto)
                nk = 64 if halo else 128
                vtile = vt[:64, 16 + r, :] if halo else vt[:, r * nto + m, :]

                def flush(ep):
                    ob = stt_['ob']
                    qlo = max(0, 512 * ep - 64)
                    qhi = min(Lq, 512 * ep + 448)
                    c0f = qlo - (512 * ep - 64)
                    wdt = qhi - qlo
                    nv = sub_view(num[:, :], d)[:, r, qlo:qhi]
                    dv = sub_view(den[:, :], d)[:, r, qlo:qhi]
                    p.op('dve', I_tt(nv, ob[0][:, c0f:c0f + wdt], nv, ALU.add), r=[ob[1], 'numacc'], w=['numacc'])
                    p.op('dve', I_tt(dv, ob[2][:, c0f:c0f + wdt], dv, ALU.add), r=[ob[3], 'denacc'], w=['denacc'])
                if m == 0:
                    stt_['ob'] = ob = obanks()
                    p.op('pe', I_mm(ob[0][:, 64:128], vtile, PT[:nk, 64:128], True, True), r=[vk_, ptk], w=[ob[1]])
                    p.op('pe', I_mm(ob[2][:, 64:128], cx.ones[:nk, :], PT[:nk, 64:128], True, True), r=['ones', ptk], w=[ob[3]])
                else:
                    ob = stt_['ob']
                    q0 = 64 + 128 * (m - 1)
                    wq_ = min(Lq, q0 + 128) - q0
                    c0 = 128 * (m % 4)
                    p.op('pe', I_mm(ob[0][:, c0:c0 + wq_], vtile, PT[:nk, 0:wq_], False, True), r=[vk_, ptk], w=[ob[1]])
                    p.op('pe', I_mm(ob[2][:, c0:c0 + wq_], cx.ones[:nk, :], PT[:nk, 0:wq_], False, True), r=['ones', ptk], w=[ob[3]])
                    if m % 4 == 3 or halo:
                        flush(m // 4)
                if not halo:
                    q0 = 64 + 128 * m
                    wq_ = min(Lq, q0 + 128) - q0
                    if (m + 1) % 4 == 0:
                        stt_['ob'] = obanks()
                    ob = stt_['ob']
                    c0 = 128 * ((m + 1) % 4)
                    p.op('pe', I_mm(ob[0][:, c0:c0 + wq_], vtile, PT[:nk, 128:128 + wq_], True, False), r=[vk_, ptk], w=[ob[1]])
                    p.op('pe', I_mm(ob[2][:, c0:c0 + wq_], cx.ones[:nk, :], PT[:nk, 128:128 + wq_], True, False), r=['ones', ptk], w=[ob[3]])
            SKEW = 2
            pend = {}
            for i in range(len(tiles) + SKEW):
                if i < len(tiles):
                    pend[i] = S_phase(*tiles[i])
                if i - SKEW >= 0:
                    PV_phase(*tiles[i - SKEW], *pend.pop(i - SKEW))
        p.op('act', I_act(den[:], den[:], AF.Ln), r=['denacc'], w=['denacc'])
        p.op('act', I_act(den[:], den[:], AF.Exp, scale=-1.0), r=['denacc'], w=['denacc'])
        p.op('pool', I_tt(mT[:, h, :], num[:], den[:], ALU.mult), r=['numacc', 'denacc'], w=['mT'])
    for ns in range(2):
        wo, wok = wslab(cx, [(A['wo_a'][:, ns * 512:(ns + 1) * 512], 0)])
        for j in range(4):
            n = ns * 4 + j
            for tb in range(NT // 512):
                b, bk = sbank()
                for kc in range(NCH):
                    p.op('pe', I_mm(b[:], wo[:, kc * 512 + j * 128: kc * 512 + (j + 1) * 128], mT[:, kc, tb * 512:(tb + 1) * 512],
                                    kc == 0, kc == NCH - 1), r=[wok, 'mT'], w=[bk])
                xt, xk = cx.rot('xin', [128, 512], F32, n=2)
                p.dma('sp', I_dma(xt[:], _src[n * 128:(n + 1) * 128, _cols(tb)]), w=[xk])
                p.op('dve', I_tt(xt[:], rvf(b[:]), xt[:], ALU.add), r=[bk, xk], w=[xk])
                p.dma('sp', I_dma(_dst[n * 128:(n + 1) * 128, _cols(tb)], xt[:]), r=[xk], w=['hTout'])


def build_attn():
    nc = bass.Bass("TRN2", target_bir_lowering=False)
    A = {}

    def inp(name, shape, dt=F32):
        A[name] = nc.dram_tensor(name, list(shape), dt, kind="ExternalInput").ap()
    inp('hT_ext', [D, NEXT])
    inp('vecs', [128, NVEC])
    inp('wqkv', [D, 9216])
    inp('wo_a', [D, D])
    inp('biasT', [24, 128, 256])
    A['hT_out'] = nc.dram_tensor('hT_out', [D, NT], F32, kind="ExternalOutput").ap()
    cx = Cx(nc, arena=True)
    cx.new_stage()
    cx.vec = cx.p.sb('vec', [128, NVEC], F32)
    cx.p.dma('sp', I_dma(cx.vec[:], A['vecs'][:, :]), w=['vec'])
    cx.ws_n = 2
    attn_body(cx, A)
    cx.p.finalize()
    return nc, cx


def t5_bucket(rel):
    nb = 16
    ret = (rel > 0).astype(np.int32) * nb
    n = np.abs(rel)
    max_exact = nb // 2
    large = max_exact + (np.log(np.maximum(n, 1).astype(np.float32) / max_exact)
                         / np.log(1024 / max_exact) * (nb - max_exact)).astype(np.int32)
    large = np.minimum(large, nb - 1)
    return (ret + np.where(n < max_exact, n, large)).astype(np.int32)


def host_bias(bias_table, flip):
    a = np.arange(128)[:, None]
    b = np.arange(256)[None, :]
    rel = a - b + 64
    out = np.full((24, 128, 256), -1e30, np.float32)
    band = np.abs(rel) <= 64
    for g, (dil, _) in enumerate(GRP):
        bk = t5_bucket((-rel if flip else rel) * dil)
        for h in range(8):
            out[g * 8 + h] = np.where(band, bias_table[bk, g * 8 + h], np.float32(-1e30))
    return out


def build_norm():
    nc = bass.Bass("TRN2", target_bir_lowering=False)
    A = {}
    A['hT'] = nc.dram_tensor('hT', [D, NT], F32, kind="ExternalInput").ap()
    A['vecs'] = nc.dram_tensor('vecs', [128, NVEC], F32, kind="ExternalInput").ap()
    A['hn_out'] = nc.dram_tensor('hn_out', [D, NT], F32, kind="ExternalOutput").ap()
    cx = Cx(nc)
    p = cx.p
    cx.hT = p.sb('hT', [128, NCH, NT], F32)
    cx.vec = p.sb('vec', [128, NVEC], F32)
    p.dma('sp', I_dma(cx.vec[:], A['vecs'][:, :]), w=['vec'])
    load_hT(cx, A['hT'])
    emit_norm(cx, A['hn_out'])
    p.finalize()
    return nc, cx


def build_fused(nlayers=4):
    nc = bass.Bass("TRN2", target_bir_lowering=False)
    shapes = {}

    def inp(name, shape, dt=F32):
        shapes[name] = list(shape)

    class Lazy(dict):
        def __missing__(self, name):
            ap = nc.dram_tensor(name, shapes[name], F32, kind="ExternalInput").ap()
            self[name] = ap
            return ap
    A = Lazy()

    def scratch(name):
        return nc.dram_tensor(name, [D, SEQ], F32, kind="Internal").ap()
    inp('xT', [D, SEQ])
    inp('memT', [D, MEMLEN])
    inp('ident', [128, 128])
    inp('v0', [128, NVEC])
    inp('biasT0', [24, 128, 256])
    inp('biasT1', [24, 128, 256])
    for i in range(4):
        inp('vecs%d' % i, [128, NVEC])
        inp('wq%d' % i, [D, D])
        inp('wkv%d' % i, [D, 2 * D])
        inp('wo%d' % i, [D, D])
        inp('w1_%d' % i, [D, DFF])
        inp('w2_%d' % i, [DFF, D])
    for j in range(2):
        inp('wglu%d' % j, [D, 2 * D])
        inp('wqkv%d' % j, [D, 9216])
        inp('woa%d' % j, [D, D])
        inp('avecs%d' % j, [128, NVEC])
        for c in range(2):
            sfx = '%d%d' % (j, c)
            for nm in ('Bre', 'Bim', 'CR', 'CI'):
                inp(nm + sfx, [2, NPT, 128, 128])
            for nm in ('lamre', 'lamim', 'logdt'):
                inp(nm + sfx, [128, 2 * NPT])
            inp('dsk' + sfx, [128, 4])
    xT = A['xT']
    outT = nc.dram_tensor('outT', [D, SEQ], F32, kind="ExternalOutput").ap()
    HN = scratch('HN')
    Y = scratch('Y')
    Hs = [xT, scratch('H1'), scratch('H1a'), scratch('H2'), scratch('H3'), scratch('H3a'), outT]
    cx = Cx(nc, arena=True)
    p = cx.p
    hv = lambda ap, half: ap[:, half * NT:(half + 1) * NT]

    cx.hT = p.sb('hT', [128, NCH, NT], F32)
    cx.vec = p.sb('vec', [128, NVEC], F32)
    p.dma('sp', I_dma(cx.vec[:], A['v0'][:, :]), w=['vec'])
    for half in range(2):
        load_hT(cx, hv(xT, half))
        emit_norm(cx, hv(HN, half))

    def s5_stage(j):
        for c in range(2):
            cx.new_stage()
            sfx = '%d%d' % (j, c)
            AA = {nm: A[nm + sfx] for nm in ('Bre', 'Bim', 'CR', 'CI', 'lamre', 'lamim', 'logdt', 'dsk')}
            AA['ident'] = A['ident']
            AA['uT'] = HN[512 * c:512 * c + 512, :]
            AA['yT'] = Y[512 * c:512 * c + 512, :]
            s5_body(cx, AA)

    def tail_stage(i, glu, src, dst, emit):
        cx.new_stage()
        AA = dict(memT=A['memT'], vecs=A['vecs%d' % i], wq=A['wq%d' % i], wkv=A['wkv%d' % i], wo=A['wo%d' % i],
                  w1=A['w1_%d' % i], w2=A['w2_%d' % i])
        if glu:
            AA['wglu'] = A['wglu%d' % (i // 2)]
        common_tiles(cx, AA)
        for half in range(2):
            load_hT(cx, hv(src, half))
            if glu:
                AA['yT'] = hv(Y, half)
            tail_body(cx, AA, glu, 0, 2, kv_ready=(half == 1))
            tail_body(cx, AA, glu, 2, 2, kv_ready=True)
            store_hT(cx, hv(dst, half))
            if emit:
                emit_norm(cx, hv(HN, half))

    def attn_stage(i, src, dst):
        j = i // 2
        for half in range(2):
            cx.new_stage()
            cx.vec = p.sb('vec', [128, NVEC], F32)
            p.dma('sp', I_dma(cx.vec[:], A['avecs%d' % j][:, :]), w=['vec'])
            AA = dict(wqkv=A['wqkv%d' % j], wo_a=A['woa%d' % j], biasT=A['biasT%d' % half])
            attn_body(cx, AA, flip=(half == 1), src=src, dst=dst)

    s5_stage(0)
    if nlayers == 0:
        cx.new_stage()
        cx.hT = p.sb('hT', [128, NCH, NT], F32)
        for half in range(2):
            load_hT(cx, hv(Y, half))
            store_hT(cx, hv(outT, half))
        p.finalize()
        cx.used = list(A.keys())
        return nc, cx
    tail_stage(0, True, Hs[0], Hs[1] if nlayers > 1 else outT, False)
    if nlayers > 1:
        attn_stage(1, Hs[1], Hs[2])
        tail_stage(1, False, Hs[2], Hs[3] if nlayers > 2 else outT, True)
    if nlayers > 2:
        s5_stage(1)
        tail_stage(2, True, Hs[3], Hs[4] if nlayers > 3 else outT, False)
    if nlayers > 3:
        attn_stage(3, Hs[4], Hs[5])
        tail_stage(3, False, Hs[5], Hs[6], False)
    p.finalize()
    cx.used = list(A.keys())
    return nc, cx


RG2 = [[0, 1], [2, 3], [4, 5], [6, 7]]


class GBuf:
    def __init__(self, nc, name, rows, cols, chunk_rows):
        self.cr = chunk_rows
        self.n = rows // chunk_rows
        self.src = [nc.dram_tensor('%s_s%d' % (name, q), [chunk_rows, cols], F32, kind="Internal").ap() for q in range(self.n)]
        self.dst = [nc.dram_tensor('%s_g%d' % (name, q), [2 * chunk_rows, cols], F32, kind="Internal").ap() for q in range(self.n)]

    def src_rows(self, r0, nrows=128):
        q = r0 // self.cr
        o = r0 - q * self.cr
        return self.src[q][o:o + nrows, :]

    def g_rows(self, rank, r0, nrows=128):
        q = r0 // self.cr
        o = rank * self.cr + r0 - q * self.cr
        return self.dst[q][o:o + nrows, :]


def build_fused8():
    nc = bass.Bass("TRN2", target_bir_lowering=False, num_devices=8)
    shapes = {}

    def inp(name, shape):
        shapes[name] = list(shape)

    class Lazy(dict):
        def __missing__(self, name):
            ap = nc.dram_tensor(name, shapes[name], F32, kind="ExternalInput").ap()
            self[name] = ap
            return ap
    A = Lazy()

    def scratch(name, shape):
        return nc.dram_tensor(name, list(shape), F32, kind="Internal").ap()
    inp('xT', [D, NT])
    inp('memT', [D, MEMLEN])
    inp('ident', [128, 128])
    inp('v0', [128, NVEC])
    inp('biasT', [24, 128, 256])
    for i in range(4):
        inp('vecs%d' % i, [128, NVEC])
        inp('wq%d' % i, [D, D])
        inp('wkv%d' % i, [D, 2 * D])
        inp('wo%d' % i, [D, D])
        inp('w1_%d' % i, [D, DFF])
        inp('w2_%d' % i, [DFF, D])
    for j in range(2):
        inp('wglu%d' % j, [D, 2 * D])
        inp('wqkv%d' % j, [D, 9216])
        inp('woa%d' % j, [D, D])
        inp('avecs%d' % j, [128, NVEC])
        for nm in ('Bre', 'Bim', 'CR', 'CI'):
            inp(nm + '%d' % j, [2, NPT, 128, 128])
        for nm in ('lamre', 'lamim', 'logdt'):
            inp(nm + '%d' % j, [128, 2 * NPT])
        inp('dsk%d' % j, [128, 4])
    outT = nc.dram_tensor('outT', [D, NT], F32, kind="ExternalOutput").ap()
    HNb = GBuf(nc, 'HN', D, NT, 256)
    HN = GHN = HNb
    yO = scratch('yO', [512, NT])
    ySb = GBuf(nc, 'yS', 512, NT, 256)
    yS = GS = ySb
    Hhb = GBuf(nc, 'Hh', D, 1024, 512)
    Hh = GH = Hhb
    Hs = [A['xT']] + [scratch(n, [D, NT]) for n in ('H1', 'H1a', 'H2', 'H3', 'H3a')] + [outT]
    cx = Cx(nc, arena=True)
    p = cx.p

    def allgather(gb, _unused=None):
        cx.new_stage()
        for q in range(gb.n):
            p.dma('pool', lambda e, q=q: e.collective_compute("AllGather", ALU.bypass, replica_groups=RG2,
                                                               ins=[gb.src[q][:, :]], outs=[gb.dst[q][:, :]]),
                  w=['cc'], semkey='cc', inc=1)

    cx.hT = p.sb('hT', [128, NCH, NT], F32)
    cx.vec = p.sb('vec', [128, NVEC], F32)
    p.dma('sp', I_dma(cx.vec[:], A['v0'][:, :]), w=['vec'])
    load_hT(cx, A['xT'])
    emit_norm(cx, HN)
    allgather(HN, GHN)

    def s5_stage(j):
        cx.new_stage()
        AA = {nm: A[nm + '%d' % j] for nm in ('Bre', 'Bim', 'CR', 'CI', 'lamre', 'lamim', 'logdt', 'dsk')}
        AA['ident'] = A['ident']
        AA['GHN'] = GHN
        AA['yO'] = yO
        AA['yS'] = yS
        cx.vec = p.sb('vec', [128, NVEC], F32)
        p.dma('sp', I_dma(cx.vec[:], A['v0'][:, :]), w=['vec'])
        s5_body(cx, AA)
        allgather(yS, GS)

    def tail_stage(i, glu, src, dst, emit, halo):
        cx.new_stage()
        AA = dict(memT=A['memT'], vecs=A['vecs%d' % i], wq=A['wq%d' % i], wkv=A['wkv%d' % i], wo=A['wo%d' % i],
                  w1=A['w1_%d' % i], w2=A['w2_%d' % i])
        if glu:
            AA['wglu'] = A['wglu%d' % (i // 2)]
            AA['yO'] = yO
            AA['GS'] = GS
        common_tiles(cx, AA)
        load_hT(cx, src)
        tail_body(cx, AA, glu, 0, 2, kv_ready=False)
        tail_body(cx, AA, glu, 2, 2, kv_ready=True)
        store_hT(cx, dst)
        if halo:
            for c in range(NCH):
                p.dma('sp', I_dma(Hh.src_rows(c * 128), cx.hT[:, c, 1024:2048]),
                      r=['h%d_%d' % (c, tb) for tb in (2, 3)], w=['hhout'])
            allgather(Hh, GH)
        if emit:
            emit_norm(cx, HN)
            allgather(HN, GHN)

    def attn_stage(i, src, dst):
        j = i // 2
        cx.new_stage()
        cx.vec = p.sb('vec', [128, NVEC], F32)
        p.dma('sp', I_dma(cx.vec[:], A['avecs%d' % j][:, :]), w=['vec'])
        AA = dict(wqkv=A['wqkv%d' % j], wo_a=A['woa%d' % j], biasT=A['biasT'])
        cx.ws_n = 2
        attn_body(cx, AA, flip=False, src=src, dst=dst, gh=GH)
        cx.ws_n = WS_N

    s5_stage(0)
    tail_stage(0, True, Hs[0], Hs[1], False, True)
    attn_stage(1, Hs[1], Hs[2])
    tail_stage(1, False, Hs[2], Hs[3], True, False)
    s5_stage(1)
    tail_stage(2, True, Hs[3], Hs[4], False, True)
    attn_stage(3, Hs[4], Hs[5])
    tail_stage(3, False, Hs[5], Hs[6], False, False)
    p.finalize()
    cx.used = list(A.keys())
    return nc, cx


def _pc(v, C):
    return np.ascontiguousarray(np.asarray(v, np.float32).reshape(C, 128).T)


_PROGS = {}
_NL = [4]


def _prog(name):
    if name not in _PROGS:
        if name == 'norm':
            _PROGS[name] = build_norm()[0]
        elif name == 's5':
            _PROGS[name] = build_s5()[0]
        elif name == 'tail_glu':
            _PROGS[name] = build_tail(True, True)[0]
        elif name == 'tail':
            _PROGS[name] = build_tail(False, True)[0]
        elif name == 'attn':
            _PROGS[name] = build_attn()[0]
    return _PROGS[name]


def kernel_multi(**inp):
    inp = {k: np.asarray(v) for k, v in inp.items()}
    ncore = 8
    cores = list(range(ncore))
    f32 = np.float32
    loc = [np.arange(NEXT) if (k % 2 == 0) else (SEQ - 1 - np.arange(NEXT)) for k in cores]
    H = np.array(inp['x'], dtype=f32, copy=True)
    memT = [np.ascontiguousarray(inp['mem'][k // 2].T.astype(f32)) for k in cores]
    biasT = [host_bias(inp['bias_table'].astype(f32), k % 2 == 1) for k in cores]

    def own_T(arr_bsd, k):
        return np.ascontiguousarray(arr_bsd[k // 2][loc[k][:NT]].T)

    def scatter(outs, name):
        full = np.empty((BATCH, SEQ, D), f32)
        for k in cores:
            full[k // 2][loc[k][:NT]] = np.asarray(outs[k][name], f32).T
        return full

    def tail_vecs(i):
        v = np.zeros((128, NVEC), f32)
        v[:, 0:8] = _pc(inp['norm_xattn'][i], 8)
        v[:, 8:16] = _pc(inp['norm_mem'][i], 8)
        v[:, 16:24] = _pc(inp['norm_mlp'][i], 8)
        v[:, 24:26] = _pc(inp['xattn_q_gain'][i], 2)
        v[:, 26:28] = _pc(inp['xattn_k_gain'][i], 2)
        v[:, 28:36] = _pc(inp['norm_mix'][min(i + 1, 3)], 8)
        return v

    def run_tail(i, H, Y):
        v = tail_vecs(i)
        maps = []
        for k in cores:
            m = dict(hT=own_T(H, k), memT=memT[k], vecs=v, wq=inp['xattn_w_q'][i], wkv=inp['xattn_w_kv'][i],
                     wo=inp['xattn_w_o'][i], w1=inp['mlp_w1'][i], w2=inp['mlp_w2'][i])
            if Y is not None:
                m['yT'] = own_T(Y, k)
                m['wglu'] = inp['s5_w_glu'][i // 2]
            maps.append(m)
        res = run_bass_kernel_spmd(_prog('tail_glu' if Y is not None else 'tail'), maps, core_ids=cores).results
        return scatter(res, 'hT_out'), scatter(res, 'hn_out')

    def run_s5(j, HN):
        maps = []
        for k in cores:
            b, c = k // 2, k % 2
            m = s5_host_inputs(inp, j, c)
            m['uT'] = np.ascontiguousarray(HN[b][:, 512 * c:512 * c + 512].T)
            maps.append(m)
        res = run_bass_kernel_spmd(_prog('s5'), maps, core_ids=cores).results
        Y = np.empty((BATCH, SEQ, D), f32)
        for k in cores:
            b, c = k // 2, k % 2
            Y[b][:, 512 * c:512 * c + 512] = np.asarray(res[k]['yT'], f32).T
        return Y

    def run_attn(i, H):
        j = i // 2
        v = np.zeros((128, NVEC), f32)
        v[:, 28:36] = _pc(inp['norm_mix'][i], 8)
        v[:, 36] = inp['attn_q_gain'][j]
        v[:, 37] = inp['attn_k_gain'][j]
        maps = []
        for k in cores:
            maps.append(dict(hT_ext=np.ascontiguousarray(H[k // 2][loc[k]].T), vecs=v, wqkv=inp['attn_w_qkv'][j],
                             wo_a=inp['attn_w_o'][j], biasT=biasT[k]))
        res = run_bass_kernel_spmd(_prog('attn'), maps, core_ids=cores).results
        return scatter(res, 'hT_out')

    v0 = np.zeros((128, NVEC), f32)
    v0[:, 28:36] = _pc(inp['norm_mix'][0], 8)
    res = run_bass_kernel_spmd(_prog('norm'), [dict(hT=own_T(H, k), vecs=v0) for k in cores], core_ids=cores).results
    HN = scatter(res, 'hn_out')
    for i in range(4):
        if i % 2 == 0:
            Y = run_s5(i // 2, HN)
            H, HN = run_tail(i, H, Y)
        else:
            H = run_attn(i, H)
            H, HN = run_tail(i, H, None)
    return H


def tail_vecs_host(inp, i):
    v = np.zeros((128, NVEC), np.float32)
    v[:, 0:8] = _pc(inp['norm_xattn'][i], 8)
    v[:, 8:16] = _pc(inp['norm_mem'][i], 8)
    v[:, 16:24] = _pc(inp['norm_mlp'][i], 8)
    v[:, 24:26] = _pc(inp['xattn_q_gain'][i], 2)
    v[:, 26:28] = _pc(inp['xattn_k_gain'][i], 2)
    v[:, 28:36] = _pc(inp['norm_mix'][min(i + 1, 3)], 8)
    return v


def kernel_fused4(**inp):
    inp = {k: np.asarray(v) for k, v in inp.items()}
    f32 = np.float32
    nl = _NL[0]
    if ('fused', nl) not in _PROGS:
        _PROGS[('fused', nl)] = build_fused(nl)
    nc, cxf = _PROGS[('fused', nl)]
    shared = dict(ident=np.eye(128, dtype=f32),
                  biasT0=host_bias(inp['bias_table'].astype(f32), False),
                  biasT1=host_bias(inp['bias_table'].astype(f32), True))
    v0 = np.zeros((128, NVEC), f32)
    v0[:, 28:36] = _pc(inp['norm_mix'][0], 8)
    shared['v0'] = v0
    for i in range(4):
        shared['vecs%d' % i] = tail_vecs_host(inp, i)
        shared['wq%d' % i] = inp['xattn_w_q'][i]
        shared['wkv%d' % i] = inp['xattn_w_kv'][i]
        shared['wo%d' % i] = inp['xattn_w_o'][i]
        shared['w1_%d' % i] = inp['mlp_w1'][i]
        shared['w2_%d' % i] = inp['mlp_w2'][i]
    for j in range(2):
        shared['wglu%d' % j] = inp['s5_w_glu'][j]
        shared['wqkv%d' % j] = inp['attn_w_qkv'][j]
        shared['woa%d' % j] = inp['attn_w_o'][j]
        av = np.zeros((128, NVEC), f32)
        av[:, 28:36] = _pc(inp['norm_mix'][2 * j + 1], 8)
        av[:, 36] = inp['attn_q_gain'][j]
        av[:, 37] = inp['attn_k_gain'][j]
        shared['avecs%d' % j] = av
        for c in range(2):
            for nm, arr in s5_host_inputs(inp, j, c).items():
                if nm != 'ident':
                    shared[nm + '%d%d' % (j, c)] = arr
    maps = []
    for b in range(BATCH):
        m = dict(shared)
        m['xT'] = np.ascontiguousarray(inp['x'][b].T.astype(f32))
        m['memT'] = np.ascontiguousarray(inp['mem'][b].T.astype(f32))
        maps.append({k: m[k] for k in cxf.used})
    res = run_bass_kernel_spmd(nc, maps, core_ids=list(range(BATCH))).results
    out = np.empty((BATCH, SEQ, D), f32)
    for b in range(BATCH):
        out[b] = np.asarray(res[b]['outT'], f32).T
    return out


def _sw(a, axis):
    return np.roll(a, 512, axis=axis)


def kernel(**inp):
    inp = {k: np.asarray(v, np.float32) for k, v in inp.items()}
    f32 = np.float32
    if 'fused8' not in _PROGS:
        _PROGS['fused8'] = build_fused8()
    nc, cxf = _PROGS['fused8']
    ident = np.eye(128, dtype=f32)
    per_c = []
    for c in range(2):
        sw = (lambda a, axis: _sw(a, axis)) if c == 1 else (lambda a, axis: a)
        g = {}
        gi = {k: (sw(inp[k], 1) if k in ('norm_mix', 'norm_xattn', 'norm_mem', 'norm_mlp') else inp[k]) for k in inp}
        g['ident'] = ident
        g['biasT'] = host_bias(inp['bias_table'], c == 1)
        v0 = np.zeros((128, NVEC), f32)
        v0[:, 28:36] = _pc(gi['norm_mix'][0], 8)
        v0[:, 40 + c] = 1.0
        g['v0'] = v0
        for i in range(4):
            v = tail_vecs_host(gi, i)
            v[:, 40 + c] = 1.0
            g['vecs%d' % i] = v
            g['wq%d' % i] = np.ascontiguousarray(sw(inp['xattn_w_q'][i], 0))
            g['wkv%d' % i] = np.ascontiguousarray(sw(inp['xattn_w_kv'][i], 0))
            g['wo%d' % i] = np.ascontiguousarray(sw(inp['xattn_w_o'][i], 1))
            g['w1_%d' % i] = np.ascontiguousarray(sw(inp['mlp_w1'][i], 0))
            g['w2_%d' % i] = np.ascontiguousarray(sw(inp['mlp_w2'][i], 1))
        for j in range(2):
            wg = sw(inp['s5_w_glu'][j], 0).reshape(D, 2, D)
            g['wglu%d' % j] = np.ascontiguousarray(sw(wg, 2).reshape(D, 2 * D))
            g['wqkv%d' % j] = np.ascontiguousarray(sw(inp['attn_w_qkv'][j], 0))
            g['woa%d' % j] = np.ascontiguousarray(sw(inp['attn_w_o'][j], 1))
            av = np.zeros((128, NVEC), f32)
            av[:, 28:36] = _pc(gi['norm_mix'][2 * j + 1], 8)
            av[:, 36] = inp['attn_q_gain'][j]
            av[:, 37] = inp['attn_k_gain'][j]
            av[:, 40 + c] = 1.0
            g['avecs%d' % j] = av
            for nm, arr in s5_host_inputs(inp, j, c).items():
                if nm != 'ident':
                    g[nm + '%d' % j] = arr
        per_c.append(g)
    maps = []
    for k in range(8):
        b, c = k // 2, k % 2
        m = dict(per_c[c])
        xb = inp['x'][b]
        if c == 0:
            m['xT'] = np.ascontiguousarray(xb[:NT].T)
            m['memT'] = np.ascontiguousarray(inp['mem'][b].T)
        else:
            m['xT'] = np.ascontiguousarray(_sw(xb[::-1][:NT], 1).T)
            m['memT'] = np.ascontiguousarray(_sw(inp['mem'][b], 1).T)
        maps.append({kk: m[kk] for kk in cxf.used})
    res = run_bass_kernel_spmd(nc, maps, core_ids=list(range(8))).results
    out = np.empty((BATCH, SEQ, D), f32)
    for k in range(8):
        b, c = k // 2, k % 2
        o = np.asarray(res[k]['outT'], f32).T
        if c == 0:
            out[b, :NT] = o
        else:
            out[b, NT:] = _sw(o, 1)[::-1]
    return out
```

```python
import math
import numpy as np
from contextlib import ExitStack
import concourse.bass as bass
import concourse.mybir as mybir
from concourse.bass_utils import run_bass_kernel_spmd

F32 = mybir.dt.float32
BF16 = mybir.dt.bfloat16
AF = mybir.ActivationFunctionType
ALU = mybir.AluOpType

D = 1024
NCH = 8
SEQ = 4096
BATCH = 4
NT = 2048
EPS = 1e-6
MEMLEN = 256
DFF = 4096


class Prog:
    ENGS = ('pe', 'act', 'dve', 'pool', 'sp')
    BLK = {'pe': 'tensor', 'act': 'scalar', 'dve': 'vector', 'pool': 'gpsimd', 'sp': 'sync'}

    def __init__(self, nc):
        self.nc = nc
        self.es = ExitStack()
        self.ins = {e: [] for e in self.ENGS}
        self.last_w = {}
        self.readers = {}
        self.dma_cnt = {}
        self.log = None
        self.bar_deps = {}

    ARENA_F32 = 51712

    def use_arena(self):
        self.arena = self.es.enter_context(self.nc.sbuf_tensor('arena', [128, self.ARENA_F32], F32))
        self.aoff = 0

    def sb(self, name, shape, dt):
        if getattr(self, 'arena', None) is None:
            return self.es.enter_context(self.nc.sbuf_tensor('s_' + name, list(shape), dt))
        assert shape[0] == 128, shape
        nel = 1
        for d_ in shape[1:]:
            nel *= d_
        isz = 4 if dt == F32 else 2
        nby = (nel * isz + 63) // 64 * 64
        o4 = self.aoff // 4
        self.aoff += nby
        assert self.aoff <= self.ARENA_F32 * 4, ('arena overflow', name, self.aoff)
        v = self.arena[:, o4:o4 + nby // 4]
        if dt != F32:
            v = v.bitcast(dt)
        v = v[:, :nel]
        if len(shape) == 3:
            v = v.rearrange("p (a b) -> p a b", a=shape[1])
        elif len(shape) != 2:
            raise AssertionError(shape)
        return v

    def barrier(self):
        deps = [('d', k, c) for k, c in self.dma_cnt.items()]
        for e in self.ENGS:
            n = len(self.ins[e])
            j = n - 1
            while j >= 0 and self.ins[e][j]['dma'] is not None:
                j -= 1
            if j >= 0:
                deps.append(('e', e, j))
        self.bar_deps = {e: list(deps) for e in self.ENGS}
        self.last_w.clear()
        self.readers.clear()

    def ps(self, name, shape, dt=F32):
        return self.es.enter_context(self.nc.psum_tensor(name, list(shape), dt))

    def _deps(self, r, w):
        deps = []
        for k in r:
            t = self.last_w.get(k)
            if t is not None:
                deps.append(t)
        for k in w:
            t = self.last_w.get(k)
            if t is not None:
                deps.append(t)
            deps.extend(self.readers.get(k, ()))
        return deps

    def _commit(self, tok, r, w):
        for k in r:
            lst = self.readers.setdefault(k, [])
            lst[:] = [t for t in lst if t[:2] != tok[:2]]
            lst.append(tok)
        for k in w:
            self.last_w[k] = tok
            self.readers[k] = []

    def op(self, eng, fn, r=(), w=()):
        idx = len(self.ins[eng])
        self.ins[eng].append(dict(fn=fn, deps=self._deps(r, w) + self.bar_deps.pop(eng, []), dma=None))
        self._commit(('e', eng, idx), r, w)

    def dma(self, eng, fn, r=(), w=(), semkey=None, inc=16):
        if semkey is None:
            semkey = w[0]
        c = self.dma_cnt.get(semkey, 0) + inc
        self.dma_cnt[semkey] = c
        self.ins[eng].append(dict(fn=fn, deps=self._deps(r, w) + self.bar_deps.pop(eng, []), dma=semkey, inc=inc))
        self._commit(('d', semkey, c), r, w)

    SAME_DIST = 4

    def _skip_same(self, e, i, d, rec):
        if d[1] != e or rec['dma'] is not None:
            return False
        if e == 'pe':
            return True
        return (i - d[2]) > self.SAME_DIST

    def finalize(self):
        nc = self.nc
        need = {e: set() for e in self.ENGS}
        for e in self.ENGS:
            for i, rec in enumerate(self.ins[e]):
                for d in rec['deps']:
                    if d[0] == 'e' and not self._skip_same(e, i, d, rec):
                        need[d[1]].add(d[2])
        cum = {}
        for e in self.ENGS:
            c = 0
            arr = []
            for i in range(len(self.ins[e])):
                if i in need[e]:
                    c += 1
                arr.append(c)
            cum[e] = arr
        esem = {e: self.es.enter_context(nc.semaphore('se_' + e)) for e in self.ENGS}
        dsem = {}
        for i, k in enumerate(self.dma_cnt):
            dsem[k] = self.es.enter_context(nc.semaphore('sd_%d' % i))
        self.stats = {e: (len(self.ins[e]), cum[e][-1] if cum[e] else 0) for e in self.ENGS}
        self.stats['ndsem'] = len(dsem)
        with nc.Block() as block:
            for e in self.ENGS:
                def body(eng, e=e):
                    waited = {}
                    for i, rec in enumerate(self.ins[e]):
                        req = {}
                        for d in rec['deps']:
                            if d[0] == 'e':
                                if self._skip_same(e, i, d, rec):
                                    continue
                                key = ('e', d[1])
                                val = cum[d[1]][d[2]]
                            else:
                                key = ('d', d[1])
                                val = d[2]
                            if val > req.get(key, 0):
                                req[key] = val
                        for key, val in req.items():
                            if waited.get(key, 0) < val:
                                sem = esem[key[1]] if key[0] == 'e' else dsem[key[1]]
                                eng.wait_ge(sem, val)
                                waited[key] = val
                                if self.log is not None:
                                    self.log.append((e, i, 'wait', key, val))
                        if self.log is not None:
                            self.log.append((e, i, 'inst', rec['dma'], cum[e][i] if i in need[e] else None))
                        inst = rec['fn'](eng)
                        if rec['dma'] is not None:
                            inst.then_inc(dsem[rec['dma']], rec.get('inc', 16))
                        elif i in need[e]:
                            inst.then_inc(esem[e], 1)
                    if e == 'sp':
                        for k, c in self.dma_cnt.items():
                            eng.wait_ge(dsem[k], c)
                getattr(block, self.BLK[e])(body)
        self.es.close()


def I_mm(out, lhsT, rhs, start, stop):
    return lambda e: e.matmul(out, lhsT, rhs, start=start, stop=stop)


def I_act(out, in_, func, **kw):
    return lambda e: e.activation(out=out, in_=in_, func=func, **kw)


def I_tt(out, in0, in1, op):
    return lambda e: e.tensor_tensor(out=out, in0=in0, in1=in1, op=op)


def I_ts(out, in0, s1, s2, op0, op1=None):
    if op1 is None:
        return lambda e: e.tensor_scalar(out=out, in0=in0, scalar1=s1, scalar2=None, op0=op0)
    return lambda e: e.tensor_scalar(out=out, in0=in0, scalar1=s1, scalar2=s2, op0=op0, op1=op1)


def I_stt(out, in0, scalar, in1, op0, op1):
    return lambda e: e.scalar_tensor_tensor(out=out, in0=in0, scalar=scalar, in1=in1, op0=op0, op1=op1)


def I_recip(out, in_):
    return lambda e: e.reciprocal(out=out, in_=in_)


def I_copy(out, in_):
    return lambda e: e.tensor_copy(out=out, in_=in_)


def I_memset(ap, c):
    return lambda e: e.memset(ap, c)


def I_dma(out, in_):
    return lambda e: e.dma_start(out=out, in_=in_)


def I_scan(out, d0, d1, init):
    return lambda e: e.tensor_tensor_scan(out=out, data0=d0, data1=d1, initial=init, op0=ALU.mult, op1=ALU.add)


def mcombine(cx, out, okey, X, xk, Z, zk, ma, mb, n=512):
    p = cx.p
    tmp, tk = cx.rot('mctmp', [128, 512], F32, n=2)
    p.op('act', I_act(tmp[:, :n], X, AF.Copy, scale=ma), r=[xk, 'vec'], w=[tk])
    p.op('dve', I_stt(out, Z, mb, tmp[:, :n], ALU.mult, ALU.add), r=[zk, 'vec', tk], w=[okey])


class Cx:
    def __init__(self, nc, arena=False):
        self.nc = nc
        self.p = Prog(nc)
        if arena:
            self.p.use_arena()
        self.banks = [self.p.ps('bank%d' % i, [128, 512]) for i in range(8)]
        self.bi = 0
        p = self.p
        self.ones = p.sb('ones', [128, 128], BF16)
        p.op('pool', I_memset(self.ones[:], 1.0), w=['ones'])
        self.sq = [p.sb('sq%d' % i, [128, 512], BF16) for i in range(2)]
        self.sqi = 0
        self.rstd = [p.sb('rstd%d' % i, [128, 512], F32) for i in range(2)]
        self.rsi = 0
        self._rot = {}
        self.epsc = p.sb('epsc', [128, 1], F32)
        p.op('pool', I_memset(self.epsc[:], EPS), w=['epsc'])
        self.mark = getattr(p, 'aoff', 0)

    def new_stage(self):
        self.p.barrier()
        self.p.aoff = self.mark
        self._rot = {}
        for nm in ('ws', 'wsi'):
            if hasattr(self, nm):
                delattr(self, nm)

    def bank(self):
        i = self.bi
        self.bi = (i + 1) % 8
        return self.banks[i], 'bank%d' % i

    def rot(self, name, shape, dt, n=2):
        if name not in self._rot:
            self._rot[name] = [[self.p.sb('%s_%d' % (name, i), shape, dt) for i in range(n)], 0]
        tl, i = self._rot[name]
        self._rot[name][1] = (i + 1) % n
        return tl[i], '%s_%d' % (name, i)

    def next_sq(self):
        i = self.sqi
        self.sqi = 1 - i
        return self.sq[i], 'sq%d' % i

    def next_rstd(self):
        i = self.rsi
        self.rsi = 1 - i
        return self.rstd[i], 'rstd%d' % i

    def rstd_from_bank(self, bank, bk, n, dim):
        p = self.p
        rs, rk = self.next_rstd()
        p.op('act', I_act(rs[:, :n], bank[:, :n], AF.Ln, scale=1.0 / dim, bias=self.epsc[:, 0:1]), r=[bk, 'epsc'], w=[rk])
        p.op('act', I_act(rs[:, :n], rs[:, :n], AF.Exp, scale=-0.5), r=[rk], w=[rk])
        return rs, rk


def rmsnorm(cx, src, skey, gcols, gkey, dst, dkey, C, n0, n, dim):
    p = cx.p
    bank, bk = cx.bank()
    for c in range(C):
        sq, sk = cx.next_sq()
        p.op('act', I_act(sq[:, :n], src[:, c, n0:n0 + n], AF.Square), r=[skey(c)], w=[sk])
        p.op('pe', I_mm(bank[:, :n], cx.ones[:], sq[:, :n], c == 0, c == C - 1), r=[sk, 'ones'], w=[bk])
    rs, rk = cx.rstd_from_bank(bank, bk, n, dim)
    for c in range(C):
        p.op('dve', I_stt(dst[:, c, n0:n0 + n], src[:, c, n0:n0 + n], gcols[:, c:c + 1], rs[:, :n], ALU.mult, ALU.mult),
             r=[skey(c), gkey, rk], w=[dkey(c)])


WS_N = 3


def wslab(cx, parts):
    p = cx.p
    nws = getattr(cx, 'ws_n', WS_N)
    if not hasattr(cx, 'ws'):
        cx.ws = [p.sb('ws%d' % i, [128, 4096], BF16) for i in range(nws)]
        cx.wsi = 0
    i = cx.wsi
    cx.wsi = (i + 1) % nws
    t = cx.ws[i]
    key = 'ws%d' % i
    for src, off in parts:
        K, N = src.shape
        kc = K // 128
        dst = t[:, off:off + kc * N].rearrange("p (k n) -> p k n", k=kc)
        p.dma('pool', I_dma(dst, src.rearrange("(k p) n -> p k n", p=128)), w=[key])
    return t, key


class SlabStream:
    def __init__(self, cx, specs):
        self.cx, self.specs, self.loaded, self.i = cx, specs, [], 0

    def get(self):
        while len(self.loaded) < min(len(self.specs), self.i + 2):
            self.loaded.append(wslab(self.cx, self.specs[len(self.loaded)]))
        r = self.loaded[self.i]
        self.i += 1
        return r


VC = dict(gx=0, gm=8, gl=16, gq=24, gk=26, gn=28, aq=36, ak=37, dsk=38, m0=40, m1=41)
NVEC = 48


def tail_body(cx, A, glu, tb0, ntb, kv_ready):
    p = cx.p
    hT = cx.hT
    hk = lambda c, tb: 'h%d_%d' % (c, tb)
    hn = cx.hn
    big2 = cx.big2
    vec = cx.vec
    NB = ntb
    specs = []
    if glu:
        for ns in range(2):
            specs.append([(A['wglu'][:, ns * 512:(ns + 1) * 512], 0)])
            specs.append([(A['wglu'][:, 1024 + ns * 512:1024 + (ns + 1) * 512], 0)])
    if not kv_ready:
        for hp in range(2):
            specs.append([(A['wkv'][:, hp * 512:(hp + 1) * 512], 0)])
        for vs in range(2):
            specs.append([(A['wkv'][:, 1024 + vs * 512:1024 + (vs + 1) * 512], 0)])
    for hp in range(2):
        specs.append([(A['wq'][:, hp * 512:(hp + 1) * 512], 0)])
    for ns in range(2):
        specs.append([(A['wo'][:, ns * 512:(ns + 1) * 512], 0)])
    for s in range(DFF // 256):
        specs.append([(A['w1'][:, s * 256:(s + 1) * 256], 0), (A['w2'][s * 256:(s + 1) * 256, :], 2048)])
    ss = SlabStream(cx, specs)

    if glu:
        for c in range(NCH):
            for tb in range(NB):
                yt, yk = cx.rot('ytmp', [128, 512], F32)
                col0 = (tb0 + tb) * 512
                if 'yO' in A and c >= 4:
                    X, xk = cx.rot('gx', [128, 512], F32, n=2)
                    Z, zk = cx.rot('gz', [128, 512], F32, n=2)
                    p.dma('sp', I_dma(X[:], A['GS'].g_rows(0, (c - 4) * 128)[:, col0:col0 + 512]), w=[xk])
                    p.dma('sp', I_dma(Z[:], A['GS'].g_rows(1, (c - 4) * 128)[:, col0:col0 + 512]), w=[zk])
                    mcombine(cx, yt[:], yk, X[:], xk, Z[:], zk, vec[:, VC['m1']:VC['m1'] + 1], vec[:, VC['m0']:VC['m0'] + 1])
                elif 'yO' in A:
                    p.dma('sp', I_dma(yt[:], A['yO'][c * 128:(c + 1) * 128, col0:col0 + 512]), w=[yk])
                else:
                    p.dma('sp', I_dma(yt[:], A['yT'][c * 128:(c + 1) * 128, col0:col0 + 512]), w=[yk])
                p.op('act', I_act(hn[:, c, tb * 512:(tb + 1) * 512], yt[:], AF.Gelu_apprx_tanh), r=[yk], w=['hn%d' % tb])
        for ns in range(2):
            wa, wak = ss.get()
            wb, wbk = ss.get()
            for j in range(4):
                n = ns * 4 + j
                for tb in range(NB):
                    ba, bak = cx.bank()
                    bb, bbk = cx.bank()
                    for kc in range(NCH):
                        p.op('pe', I_mm(ba[:], wa[:, kc * 512 + j * 128: kc * 512 + (j + 1) * 128],
                                        hn[:, kc, tb * 512:(tb + 1) * 512], kc == 0, kc == NCH - 1),
                             r=[wak, 'hn%d' % tb], w=[bak])
                    for kc in range(NCH):
                        p.op('pe', I_mm(bb[:], wb[:, kc * 512 + j * 128: kc * 512 + (j + 1) * 128],
                                        hn[:, kc, tb * 512:(tb + 1) * 512], kc == 0, kc == NCH - 1),
                             r=[wbk, 'hn%d' % tb], w=[bbk])
                    sg, sgk = cx.rot('sg', [128, 512], F32)
                    p.op('act', I_act(sg[:], bb[:], AF.Sigmoid), r=[bbk], w=[sgk])
                    gt, gtk = cx.rot('gtmp', [128, 512], F32)
                    p.op('dve', I_tt(gt[:], ba[:], sg[:], ALU.mult), r=[bak, sgk], w=[gtk])
                    hs = hT[:, n, (tb0 + tb) * 512:(tb0 + tb + 1) * 512]
                    p.op('pool', I_tt(hs, hs, gt[:], ALU.add), r=[gtk, hk(n, tb0 + tb)], w=[hk(n, tb0 + tb)])

    if not kv_ready:
        kraw = cx.kraw
        memn = cx.memn
        for c in range(NCH):
            p.dma('sp', I_dma(kraw[:, c, :], A['memT'][c * 128:(c + 1) * 128, :]), w=['kraw'], semkey='kraw_ld')
        rmsnorm(cx, kraw, lambda c: 'kraw', vec[:, VC['gm']:VC['gm'] + 8], 'vec', memn, lambda c: 'memn', NCH, 0, MEMLEN, D)
        for hp in range(2):
            wk, wkk = ss.get()
            for jj in range(4):
                j = hp * 4 + jj
                bk_, bkk = cx.bank()
                for kc in range(NCH):
                    p.op('pe', I_mm(bk_[:, :MEMLEN], wk[:, kc * 512 + jj * 128: kc * 512 + (jj + 1) * 128], memn[:, kc, :],
                                    kc == 0, kc == NCH - 1), r=[wkk, 'memn'], w=[bkk])
                p.op('act', I_act(kraw[:, j, :], bk_[:, :MEMLEN], AF.Copy), r=[bkk], w=['kraw'])
        for h in range(4):
            bs, bsk = cx.bank()
            for ec in range(2):
                sq, sk = cx.next_sq()
                p.op('act', I_act(sq[:, :MEMLEN], kraw[:, 2 * h + ec, :], AF.Square), r=['kraw'], w=[sk])
                p.op('pe', I_mm(bs[:, :MEMLEN], cx.ones[:], sq[:, :MEMLEN], ec == 0, ec == 1), r=[sk, 'ones'], w=[bsk])
            rs, rk = cx.rstd_from_bank(bs, bsk, MEMLEN, 256)
            for ec in range(2):
                p.op('dve', I_stt(cx.KT[:, 2 * h + ec, :], kraw[:, 2 * h + ec, :], vec[:, VC['gk'] + ec:VC['gk'] + ec + 1],
                                  rs[:, :MEMLEN], ALU.mult, ALU.mult), r=['kraw', 'vec', rk], w=['KT'])
        for vs in range(2):
            wv, wvk = ss.get()
            for mc in range(2):
                bv, bvk = cx.bank()
                for kc in range(NCH):
                    p.op('pe', I_mm(bv[:], memn[:, kc, mc * 128:(mc + 1) * 128], wv[:, kc * 512:(kc + 1) * 512],
                                    kc == 0, kc == NCH - 1), r=[wvk, 'memn'], w=[bvk])
                p.op('act', I_act(cx.V[:, mc, vs * 512:(vs + 1) * 512], bv[:], AF.Copy), r=[bvk], w=['V'])

    for tb in range(NB):
        _rmsnorm_off(cx, hT, (tb0 + tb) * 512, lambda c, tb=tb: hk(c, tb0 + tb), vec[:, VC['gx']:VC['gx'] + 8],
                     hn, tb * 512, 'hn%d' % tb)
    its = [(hp, hh, tb) for hp in range(2) for hh in range(2) for tb in range(NB)]
    BK = lambda i: (cx.banks[i], 'bank%d' % i)
    wq_cur = {}
    stx = {}

    def X_phase(i):
        hp, hh, tb = its[i]
        if hp not in wq_cur:
            wq_cur[hp] = ss.get()
        wq, wqk = wq_cur[hp]
        qb = []
        for ec in range(2):
            b, bk_ = BK((i % 2) * 2 + ec)
            cc = hh * 2 + ec
            for kc in range(NCH):
                p.op('pe', I_mm(b[:], wq[:, kc * 512 + cc * 128: kc * 512 + (cc + 1) * 128],
                                hn[:, kc, tb * 512:(tb + 1) * 512], kc == 0, kc == NCH - 1),
                     r=[wqk, 'hn%d' % tb], w=[bk_])
            qb.append((b, bk_))
        stx[i] = dict(qb=qb)

    def Y_phase(i):
        qb = stx[i]['qb']
        bs, bsk = BK(4)
        for ec in range(2):
            sq, sk = cx.next_sq()
            p.op('act', I_act(sq[:], qb[ec][0][:], AF.Square), r=[qb[ec][1]], w=[sk])
            p.op('pe', I_mm(bs[:], cx.ones[:], sq[:], ec == 0, ec == 1), r=[sk, 'ones'], w=[bsk])
        rs, rk = cx.rstd_from_bank(bs, bsk, 512, 256)
        qn, qnk = cx.rot('qn', [128, 2, 512], BF16)
        for ec in range(2):
            p.op('dve', I_stt(qn[:, ec, :], qb[ec][0][:], vec[:, VC['gq'] + ec:VC['gq'] + ec + 1], rs[:],
                              ALU.mult, ALU.mult), r=[qb[ec][1], 'vec', rk], w=[qnk])
        stx[i]['qn'] = (qn, qnk)

    def Z_phase(i):
        hp, hh, tb = its[i]
        h = 2 * hp + hh
        qn, qnk = stx[i]['qn']
        PT, ptk = cx.rot('PT', [128, 2, 512], BF16)
        for mc in range(2):
            bl, blk = BK(5 + mc)
            for ec in range(2):
                p.op('pe', I_mm(bl[:], cx.KT[:, 2 * h + ec, mc * 128:(mc + 1) * 128], qn[:, ec, :], ec == 0, ec == 1),
                     r=['KT', qnk], w=[blk])
            p.op('act', I_act(PT[:, mc, :], bl[:], AF.Exp, scale=1.0 / 16.0), r=[blk], w=[ptk])
        bd, bdk = BK(7)
        for mc in range(2):
            p.op('pe', I_mm(bd[:], cx.ones[:], PT[:, mc, :], mc == 0, mc == 1), r=['ones', ptk], w=[bdk])
        rd, rdk = cx.rot('rden', [128, 512], F32)
        p.op('act', I_act(rd[:], bd[:], AF.Ln), r=[bdk], w=[rdk])
        p.op('act', I_act(rd[:], rd[:], AF.Exp, scale=-1.0), r=[rdk], w=[rdk])
        for ec in range(2):
            bo, bok = BK(5 + ec)
            for mc in range(2):
                p.op('pe', I_mm(bo[:], cx.V[:, mc, h * 256 + ec * 128: h * 256 + (ec + 1) * 128], PT[:, mc, :],
                                mc == 0, mc == 1), r=['V', ptk], w=[bok])
            p.op('dve', I_tt(big2[:, 2 * h + ec, tb * 512:(tb + 1) * 512], bo[:], rd[:], ALU.mult),
                 r=[bok, rdk], w=['big2_%d' % tb])
        del stx[i]
    nit = len(its)
    for step in range(nit + 2):
        if step < nit:
            X_phase(step)
        if 0 <= step - 1 < nit:
            Y_phase(step - 1)
        if 0 <= step - 2 < nit:
            Z_phase(step - 2)
    for ns in range(2):
        wo, wok = ss.get()
        for j in range(4):
            n = ns * 4 + j
            for tb in range(NB):
                b, bk_ = cx.bank()
                for kc in range(NCH):
                    p.op('pe', I_mm(b[:], wo[:, kc * 512 + j * 128: kc * 512 + (j + 1) * 128],
                                    big2[:, kc, tb * 512:(tb + 1) * 512], kc == 0, kc == NCH - 1),
                         r=[wok, 'big2_%d' % tb], w=[bk_])
                hs = hT[:, n, (tb0 + tb) * 512:(tb0 + tb + 1) * 512]
                p.op('dve', I_tt(hs, b[:], hs, ALU.add), r=[bk_, hk(n, tb0 + tb)], w=[hk(n, tb0 + tb)])

    for tb in range(NB):
        _rmsnorm_off(cx, hT, (tb0 + tb) * 512, lambda c, tb=tb: hk(c, tb0 + tb), vec[:, VC['gl']:VC['gl'] + 8],
                     hn, tb * 512, 'hn%d' % tb)
    def mlp_w1(s, ws, wsk):
        hb = s % 2
        for j in range(2):
            for tb in range(NB):
                b, bk_ = cx.bank()
                for kc in range(NCH):
                    p.op('pe', I_mm(b[:], ws[:, kc * 256 + j * 128: kc * 256 + (j + 1) * 128],
                                    hn[:, kc, tb * 512:(tb + 1) * 512], kc == 0, kc == NCH - 1),
                         r=[wsk, 'hn%d' % tb], w=[bk_])
                rt, rtk = cx.rot('rtmp', [128, 512], F32)
                p.op('act', I_act(rt[:], b[:], AF.Relu), r=[bk_], w=[rtk])
                p.op('pool', I_tt(big2[:, hb * 2 + j, tb * 512:(tb + 1) * 512], rt[:], rt[:], ALU.mult),
                     r=[rtk], w=['hid%d' % hb])

    def mlp_w2(s, ws, wsk):
        hb = s % 2
        for n in range(NCH):
            for tb in range(NB):
                b, bk_ = cx.bank()
                for j in range(2):
                    p.op('pe', I_mm(b[:], ws[:, 2048 + j * 1024 + n * 128: 2048 + j * 1024 + (n + 1) * 128],
                                    big2[:, hb * 2 + j, tb * 512:(tb + 1) * 512], j == 0, j == 1),
                         r=[wsk, 'hid%d' % hb], w=[bk_])
                hs = hT[:, n, (tb0 + tb) * 512:(tb0 + tb + 1) * 512]
                p.op('dve', I_tt(hs, b[:], hs, ALU.add), r=[bk_, hk(n, tb0 + tb)], w=[hk(n, tb0 + tb)])
    nsl = DFF // 256
    prev = None
    for s in range(nsl):
        ws, wsk = ss.get()
        mlp_w1(s, ws, wsk)
        if prev is not None:
            mlp_w2(*prev)
        prev = (s, ws, wsk)
    mlp_w2(*prev)


def _rmsnorm_off(cx, src, s0, skey, gcols, dst, d0, dkey, n=512, C=NCH, dim=D):
    p = cx.p
    bank, bk = cx.bank()
    for c in range(C):
        sq, sk = cx.next_sq()
        p.op('act', I_act(sq[:, :n], src[:, c, s0:s0 + n], AF.Square), r=[skey(c)], w=[sk])
        p.op('pe', I_mm(bank[:, :n], cx.ones[:], sq[:, :n], c == 0, c == C - 1), r=[sk, 'ones'], w=[bk])
    rs, rk = cx.rstd_from_bank(bank, bk, n, dim)
    for c in range(C):
        p.op('dve', I_stt(dst[:, c, d0:d0 + n], src[:, c, s0:s0 + n], gcols[:, c:c + 1], rs[:, :n], ALU.mult, ALU.mult),
             r=[skey(c), 'vec', rk], w=[dkey])


def common_tiles(cx, A):
    p = cx.p
    cx.hT = p.sb('hT', [128, NCH, NT], F32)
    cx.hn = p.sb('hn', [128, NCH, 1024], BF16)
    cx.big2 = p.sb('big2', [128, NCH, 1024], BF16)
    cx.vec = p.sb('vec', [128, NVEC], F32)
    cx.kraw = p.sb('kraw', [128, NCH, MEMLEN], F32)
    cx.memn = p.sb('memn', [128, NCH, MEMLEN], BF16)
    cx.KT = p.sb('KT', [128, NCH, MEMLEN], BF16)
    cx.V = p.sb('V', [128, 2, D], BF16)
    p.dma('sp', I_dma(cx.vec[:], A['vecs'][:, :]), w=['vec'])


def load_hT(cx, src):
    p = cx.p
    for c in range(NCH):
        p.dma('sp', I_dma(cx.hT[:, c, 0:512], src[c * 128:(c + 1) * 128, 0:512]), w=['h%d_0' % c], semkey='hld%d' % c)
    for c in range(NCH):
        p.dma('sp', I_dma(cx.hT[:, c, 512:NT], src[c * 128:(c + 1) * 128, 512:NT]),
              w=['h%d_%d' % (c, tb) for tb in range(1, NT // 512)], semkey='hldb%d' % c)


def store_hT(cx, dst):
    p = cx.p
    for c in range(NCH):
        p.dma('sp', I_dma(dst[c * 128:(c + 1) * 128, :], cx.hT[:, c, :]),
              r=['h%d_%d' % (c, tb) for tb in range(NT // 512)], w=['hout%d' % c])


def build_tail(glu, emit_hn, arena=False):
    nc = bass.Bass("TRN2", target_bir_lowering=False)
    A = {}

    def inp(name, shape, dt=F32):
        A[name] = nc.dram_tensor(name, list(shape), dt, kind="ExternalInput").ap()

    inp('hT', [D, NT])
    inp('memT', [D, MEMLEN])
    inp('vecs', [128, NVEC])
    inp('wq', [D, D])
    inp('wkv', [D, 2 * D])
    inp('wo', [D, D])
    inp('w1', [D, DFF])
    inp('w2', [DFF, D])
    if glu:
        inp('yT', [D, NT])
        inp('wglu', [D, 2 * D])
    A['hT_out'] = nc.dram_tensor('hT_out', [D, NT], F32, kind="ExternalOutput").ap()
    if emit_hn:
        A['hn_out'] = nc.dram_tensor('hn_out', [D, NT], F32, kind="ExternalOutput").ap()
    cx = Cx(nc, arena=arena)
    if arena:
        cx.new_stage()
    common_tiles(cx, A)
    load_hT(cx, A['hT'])
    for half in range(2):
        tail_body(cx, A, glu, half * 2, 2, kv_ready=(half == 1))
    store_hT(cx, A['hT_out'])
    if emit_hn:
        emit_norm(cx, A['hn_out'])
    cx.p.finalize()
    return nc, cx


def emit_norm(cx, dst):
    p = cx.p
    for tb in range(NT // 512):
        bank, bk = cx.bank()
        for c in range(NCH):
            sq, sk = cx.next_sq()
            p.op('act', I_act(sq[:], cx.hT[:, c, tb * 512:(tb + 1) * 512], AF.Square), r=['h%d_%d' % (c, tb)], w=[sk])
            p.op('pe', I_mm(bank[:], cx.ones[:], sq[:], c == 0, c == NCH - 1), r=[sk, 'ones'], w=[bk])
        rs, rk = cx.rstd_from_bank(bank, bk, 512, D)
        for c in range(NCH):
            ot, otk = cx.rot('ntmp', [128, 512], F32, n=3)
            p.op('dve', I_stt(ot[:], cx.hT[:, c, tb * 512:(tb + 1) * 512], cx.vec[:, VC['gn'] + c:VC['gn'] + c + 1], rs[:],
                              ALU.mult, ALU.mult), r=['h%d_%d' % (c, tb), 'vec', rk], w=[otk])
            drow = dst.src_rows(c * 128) if hasattr(dst, 'src_rows') else dst[c * 128:(c + 1) * 128, :]
            p.dma('sp', I_dma(drow[:, tb * 512:(tb + 1) * 512], ot[:]), r=[otk], w=['hnout'])


NPT = 16
SW = 512
NW = SEQ // SW
PI = math.pi


def s5_params(cx, A):
    p = cx.p
    NCOL = 2 * NPT
    T = {}
    for nm in ['lre', 'lim', 'ldt', 'dt', 'mag', 'ang', 'angc', 's1', 'c1', 'are', 'aim', 'nr', 'den', 't', 't2',
               'fre', 'fim', 'nfre', 'nfim']:
        T[nm] = p.sb('sp_' + nm, [128, NCOL], F32)
    k = 's5par'
    p.dma('sp', I_dma(T['lre'][:], A['lamre'][:, :]), w=[k], semkey='s5par_ld')
    p.dma('sp', I_dma(T['lim'][:], A['lamim'][:, :]), w=[k], semkey='s5par_ld')
    p.dma('sp', I_dma(T['ldt'][:], A['logdt'][:, :]), w=[k], semkey='s5par_ld')
    a = lambda n: T[n][:]
    p.op('act', I_act(a('dt'), a('ldt'), AF.Exp), r=[k], w=[k])
    p.op('dve', I_tt(a('t'), a('lre'), a('dt'), ALU.mult), r=[k], w=[k])
    p.op('act', I_act(a('mag'), a('t'), AF.Exp), r=[k], w=[k])
    p.op('dve', I_tt(a('ang'), a('lim'), a('dt'), ALU.mult), r=[k], w=[k])
    for _ in range(5):
        p.op('dve', I_ts(a('t'), a('ang'), PI, 2 * PI, ALU.is_gt, ALU.mult), r=[k], w=[k])
        p.op('dve', I_tt(a('ang'), a('ang'), a('t'), ALU.subtract), r=[k], w=[k])
    p.op('dve', I_ts(a('angc'), a('ang'), PI / 2, None, ALU.add), r=[k], w=[k])
    p.op('dve', I_ts(a('t'), a('angc'), PI, 2 * PI, ALU.is_gt, ALU.mult), r=[k], w=[k])
    p.op('dve', I_tt(a('angc'), a('angc'), a('t'), ALU.subtract), r=[k], w=[k])
    p.op('act', I_act(a('s1'), a('ang'), AF.Sin), r=[k], w=[k])
    p.op('act', I_act(a('c1'), a('angc'), AF.Sin), r=[k], w=[k])
    p.op('dve', I_tt(a('are'), a('mag'), a('c1'), ALU.mult), r=[k], w=[k])
    p.op('dve', I_tt(a('aim'), a('mag'), a('s1'), ALU.mult), r=[k], w=[k])
    p.op('dve', I_ts(a('nr'), a('are'), -1.0, None, ALU.add), r=[k], w=[k])
    p.op('dve', I_tt(a('den'), a('lre'), a('lre'), ALU.mult), r=[k], w=[k])
    p.op('dve', I_tt(a('t'), a('lim'), a('lim'), ALU.mult), r=[k], w=[k])
    p.op('dve', I_tt(a('den'), a('den'), a('t'), ALU.add), r=[k], w=[k])
    p.op('dve', I_recip(a('den'), a('den')), r=[k], w=[k])
    p.op('dve', I_tt(a('t'), a('nr'), a('lre'), ALU.mult), r=[k], w=[k])
    p.op('dve', I_tt(a('t2'), a('aim'), a('lim'), ALU.mult), r=[k], w=[k])
    p.op('dve', I_tt(a('t'), a('t'), a('t2'), ALU.add), r=[k], w=[k])
    p.op('dve', I_tt(a('fre'), a('t'), a('den'), ALU.mult), r=[k], w=[k])
    p.op('dve', I_tt(a('t'), a('aim'), a('lre'), ALU.mult), r=[k], w=[k])
    p.op('dve', I_tt(a('t2'), a('nr'), a('lim'), ALU.mult), r=[k], w=[k])
    p.op('dve', I_tt(a('t'), a('t'), a('t2'), ALU.subtract), r=[k], w=[k])
    p.op('dve', I_tt(a('fim'), a('t'), a('den'), ALU.mult), r=[k], w=[k])
    p.op('dve', I_ts(a('nfre'), a('fre'), -1.0, None, ALU.mult), r=[k], w=[k])
    p.op('dve', I_ts(a('nfim'), a('fim'), -1.0, None, ALU.mult), r=[k], w=[k])
    nlv = int(math.log2(SW))
    T['pwc'] = p.sb('sp_pwc', [128, nlv + 1, NCOL], F32)
    T['pws'] = p.sb('sp_pws', [128, nlv + 1, NCOL], F32)
    T['npws'] = p.sb('sp_npws', [128, NCOL], F32)
    p.op('dve', I_copy(T['pwc'][:, 0, :], a('c1')), r=[k], w=[k])
    p.op('dve', I_copy(T['pws'][:, 0, :], a('s1')), r=[k], w=[k])
    for lv in range(nlv):
        c_ = T['pwc'][:, lv, :]
        s_ = T['pws'][:, lv, :]
        p.op('dve', I_tt(a('t'), s_, s_, ALU.mult), r=[k], w=[k])
        p.op('dve', I_tt(a('t2'), c_, c_, ALU.mult), r=[k], w=[k])
        p.op('dve', I_tt(T['pwc'][:, lv + 1, :], a('t2'), a('t'), ALU.subtract), r=[k], w=[k])
        p.op('dve', I_stt(T['pws'][:, lv + 1, :], c_, 2.0, s_, ALU.mult, ALU.mult), r=[k], w=[k])
    p.op('dve', I_ts(T['npws'][:], T['pws'][:, nlv, :], -1.0, None, ALU.mult), r=[k], w=[k])
    return T


def s5_body(cx, A):
    p = cx.p
    T = s5_params(cx, A)
    PK = 's5par'
    if 'dbg' in A:
        for i, nm in enumerate(['dt', 'mag', 'ang', 's1', 'c1', 'fre', 'fim', 'den']):
            p.dma('sp', I_dma(A['dbg'][:, i * 2 * NPT:(i + 1) * 2 * NPT], T[nm][:]), r=[PK], w=['dbgo'])
    ub = p.sb('ub', [128, 4, SEQ], BF16)
    if 'GHN' in A:
        G = A['GHN']
        m0c = cx.vec[:, VC['m0']:VC['m0'] + 1]
        m1c = cx.vec[:, VC['m1']:VC['m1'] + 1]
        for ck in range(4):
            for w in range(NW):
                r = 0 if w < NW // 2 else 1
                if r == 0:
                    cols = slice(w * SW, (w + 1) * SW)
                else:
                    w2 = w - NW // 2
                    cols = slice(NT - (w2 + 1) * SW, NT - w2 * SW)
                X, xk = cx.rot('gx', [128, SW], F32, n=2)
                Z, zk = cx.rot('gz', [128, SW], F32, n=2)
                p.dma('sp', I_dma(X[:], G.g_rows(r, ck * 128)[:, cols]), w=[xk])
                p.dma('sp', I_dma(Z[:], G.g_rows(r, 512 + ck * 128)[:, cols]), w=[zk])
                dst = ub[:, ck, w * SW:(w + 1) * SW]
                if r == 1:
                    dst = dst[:, ::-1]
                mcombine(cx, dst, 'ub%d' % ck, X[:], xk, Z[:], zk, m0c if r == 0 else m1c, m1c if r == 0 else m0c)
    else:
        for ck in range(4):
            p.dma('pool', I_dma(ub[:, ck, :], A['uT'][ck * 128:(ck + 1) * 128, :]), w=['ub%d' % ck])
    ident = p.sb('ident', [128, 128], F32)
    p.dma('sp', I_dma(ident[:], A['ident'][:, :]), w=['ident'])
    dsk = p.sb('dskc', [128, 4], F32)
    p.dma('sp', I_dma(dsk[:], A['dsk'][:, :]), w=['dskc'])
    yacc = [p.sb('yacc%d' % i, [128, SEQ], F32) for i in range(2)]
    bb_i = [0]

    def bbank():
        i = bb_i[0]
        bb_i[0] = (i + 1) % 6
        return cx.banks[i], 'bank%d' % i
    yb_i = [0]

    def ybank():
        i = 6 + yb_i[0]
        yb_i[0] = 1 - yb_i[0]
        return cx.banks[i], 'bank%d' % i

    for ck in range(4):
        ya = yacc[ck % 2]
        yk = 'yacc%d' % (ck % 2)
        dD, dDk = cx.rot('diagD', [128, 128], BF16)
        p.op('dve', I_ts(dD[:], ident[:], dsk[:, ck:ck + 1], None, ALU.mult), r=['ident', 'dskc'], w=[dDk])
        for d in range(2):
            tabs = []
            col0 = d * NPT + ck * 4
            cos4, c4k = cx.rot('cos4', [128, 4, SW], F32, n=2)
            sin4, s4k = cx.rot('sin4', [128, 4, SW], F32, n=2)
            tk = c4k
            p.op('dve', I_memset(cos4[:, :, 0:1], 1.0), w=[tk])
            p.op('dve', I_memset(sin4[:, :, 0:1], 0.0), w=[tk])
            L = 1
            lv = 0
            while L < SW:
                pcb = T['pwc'][:, lv, col0:col0 + 4].unsqueeze(2).to_broadcast([128, 4, L])
                psb = T['pws'][:, lv, col0:col0 + 4].unsqueeze(2).to_broadcast([128, 4, L])
                ta, tak = cx.rot('tbA', [128, 4, SW // 2], F32, n=1)
                tb2, tbk = cx.rot('tbB', [128, 4, SW // 2], F32, n=1)
                p.op('dve', I_tt(ta[:, :, :L], sin4[:, :, 0:L], psb, ALU.mult), r=[tk, PK], w=[tak])
                p.op('dve', I_tt(tb2[:, :, :L], cos4[:, :, 0:L], pcb, ALU.mult), r=[tk, PK], w=[tbk])
                p.op('dve', I_tt(cos4[:, :, L:2 * L], tb2[:, :, :L], ta[:, :, :L], ALU.subtract), r=[tak, tbk], w=[tk])
                p.op('dve', I_tt(ta[:, :, :L], cos4[:, :, 0:L], psb, ALU.mult), r=[tk, PK], w=[tak])
                p.op('dve', I_tt(tb2[:, :, :L], sin4[:, :, 0:L], pcb, ALU.mult), r=[tk, PK], w=[tbk])
                p.op('dve', I_tt(sin4[:, :, L:2 * L], tb2[:, :, :L], ta[:, :, :L], ALU.add), r=[tak, tbk], w=[tk])
                L *= 2
                lv += 1
            for q in range(4):
                pt = ck * 4 + q
                col = d * NPT + pt
                cosT = cos4[:, q, :]
                sinT = sin4[:, q, :]
                cW = T['pwc'][:, lv, col:col + 1]
                sW = T['pws'][:, lv, col:col + 1]
                nsW = T['npws'][:, col:col + 1]
                braw, brk = cx.rot('bw', [128, 2, 128], BF16, n=8)
                p.dma('pool', I_dma(braw[:, 0, :], A['Bre'][d, pt]), w=[brk])
                p.dma('pool', I_dma(braw[:, 1, :], A['Bim'][d, pt]), w=[brk])
                craw, crk = cx.rot('craw', [128, 2, 128], F32, n=2)
                p.dma('sp', I_dma(craw[:, 0, :], A['CR'][d, pt]), w=[crk])
                p.dma('sp', I_dma(craw[:, 1, :], A['CI'][d, pt]), w=[crk])
                cw, cwk = cx.rot('cw', [128, 3, 128], BF16, n=8)
                ctmp, ctk = cx.rot('ctmp', [128, 128], F32, n=2)
                fre = T['fre'][:, col:col + 1]
                nfim = T['nfim'][:, col:col + 1]
                nfre = T['nfre'][:, col:col + 1]
                p.op('dve', I_ts(ctmp[:], craw[:, 1, :], nfim, None, ALU.mult), r=[crk, PK], w=[ctk])
                p.op('dve', I_stt(cw[:, 0, :], craw[:, 0, :], fre, ctmp[:], ALU.mult, ALU.add), r=[crk, PK, ctk], w=[cwk])
                ctmp2, ctk2 = cx.rot('ctmp', [128, 128], F32, n=2)
                p.op('dve', I_ts(ctmp2[:], craw[:, 0, :], nfim, None, ALU.mult), r=[crk, PK], w=[ctk2])
                p.op('dve', I_stt(cw[:, 1, :], craw[:, 1, :], nfre, ctmp2[:], ALU.mult, ALU.add), r=[crk, PK, ctk2], w=[cwk])
                ctmp3, ctk3 = cx.rot('ctmp', [128, 128], F32, n=2)
                p.op('dve', I_ts(ctmp3[:], craw[:, 1, :], T['fim'][:, col:col + 1], None, ALU.mult), r=[crk, PK], w=[ctk3])
                p.op('dve', I_stt(cw[:, 2, :], craw[:, 0, :], nfre, ctmp3[:], ALU.mult, ALU.add), r=[crk, PK, ctk3], w=[cwk])
                car, cak = cx.rot('carry', [128, 8], F32, n=8)
                tabs.append(dict(cos=cosT, sin=sinT, tk=tk, cW=cW, sW=sW, nsW=nsW, braw=braw, brk=brk, cw=cw, cwk=cwk,
                                 r=T['mag'][:, col:col + 1], car=car, cak=cak))
            worder = range(NW) if d == 0 else range(NW - 1, -1, -1)
            rv = (lambda ap: ap) if d == 0 else (lambda ap: ap[:, ::-1])
            units = [dict(wi=wi, w=w, q=q) for wi, w in enumerate(worder) for q in range(4)]
            ybs = {}

            def P01(u):
                tb_ = tabs[u['q']]
                win = slice(u['w'] * SW, (u['w'] + 1) * SW)
                bre, brek = bbank()
                bim, bimk = bbank()
                p.op('pe', I_mm(bre[:], tb_['braw'][:, 0, :], ub[:, ck, win], True, True), r=[tb_['brk'], 'ub%d' % ck], w=[brek])
                p.op('pe', I_mm(bim[:], tb_['braw'][:, 1, :], ub[:, ck, win], True, True), r=[tb_['brk'], 'ub%d' % ck], w=[bimk])
                cosT, sinT, tk = tb_['cos'], tb_['sin'], tb_['tk']
                t1, t1k = cx.rot('t1', [128, SW], F32)
                t2, t2k = cx.rot('t2', [128, SW], F32)
                t3, t3k = cx.rot('t3', [128, SW], F32)
                t4, t4k = cx.rot('t4', [128, SW], F32)
                p.op('dve', I_tt(t1[:], rv(bre[:]), cosT, ALU.mult), r=[brek, tk], w=[t1k])
                p.op('dve', I_tt(t2[:], rv(bim[:]), sinT, ALU.mult), r=[bimk, tk], w=[t2k])
                p.op('dve', I_tt(t3[:], rv(bim[:]), cosT, ALU.mult), r=[bimk, tk], w=[t3k])
                p.op('dve', I_tt(t4[:], rv(bre[:]), sinT, ALU.mult), r=[brek, tk], w=[t4k])
                u.update(t=(t1, t1k, t2, t2k, t3, t3k, t4, t4k))

            def P2(u):
                t1, t1k, t2, t2k, t3, t3k, t4, t4k = u['t']
                wre, wrk = cx.rot('wre', [128, SW], F32)
                wim, wik = cx.rot('wim', [128, SW], F32)
                p.op('pool', I_tt(wre[:], t1[:], t2[:], ALU.add), r=[t1k, t2k], w=[wrk])
                p.op('pool', I_tt(wim[:], t3[:], t4[:], ALU.subtract), r=[t3k, t4k], w=[wik])
                u.update(wv=(wre, wrk, wim, wik))

            def P3(u):
                tb_ = tabs[u['q']]
                tk = tb_['tk']
                wre, wrk, wim, wik = u['wv']
                zre, zrk = cx.rot('zre', [128, SW], F32)
                zim, zik = cx.rot('zim', [128, SW], F32)
                car, cak = tb_['car'], tb_['cak']
                rbc = tb_['r'].to_broadcast([128, SW])
                if u['wi'] == 0:
                    ire, iim = 0.0, 0.0
                else:
                    ire, iim = car[:, 2:3], car[:, 3:4]
                p.op('dve', I_scan(zre[:], rbc, wre[:], ire), r=[PK, wrk, cak], w=[zrk])
                p.op('dve', I_scan(zim[:], rbc, wim[:], iim), r=[PK, wik, cak], w=[zik])
                if u['wi'] < NW - 1:
                    p.op('act', I_act(car[:, 0:1], zim[:, SW - 1:SW], AF.Copy, scale=tb_['nsW']), r=[zik, PK], w=[cak])
                    p.op('act', I_act(car[:, 1:2], zre[:, SW - 1:SW], AF.Copy, scale=tb_['sW']), r=[zrk, PK], w=[cak])
                    p.op('act', I_act(car[:, 2:3], zre[:, SW - 1:SW], AF.Identity, scale=tb_['cW'], bias=car[:, 0:1]), r=[zrk, PK], w=[cak])
                    p.op('act', I_act(car[:, 3:4], zim[:, SW - 1:SW], AF.Identity, scale=tb_['cW'], bias=car[:, 1:2]), r=[zik, PK], w=[cak])
                u.update(z=(zre, zrk, zim, zik))

            def P4(u):
                tb_ = tabs[u['q']]
                cosT, sinT, tk = tb_['cos'], tb_['sin'], tb_['tk']
                zre, zrk, zim, zik = u['z']
                u1, u1k = cx.rot('u1', [128, SW], BF16, n=3)
                u2, u2k = cx.rot('u2', [128, SW], BF16, n=3)
                u3, u3k = cx.rot('u3', [128, SW], BF16, n=3)
                u4, u4k = cx.rot('u4', [128, SW], BF16, n=3)
                p.op('pool', I_tt(rv(u1[:]), zre[:], cosT, ALU.mult), r=[zrk, tk], w=[u1k])
                p.op('pool', I_tt(rv(u2[:]), zim[:], sinT, ALU.mult), r=[zik, tk], w=[u2k])
                p.op('pool', I_tt(rv(u3[:]), zim[:], cosT, ALU.mult), r=[zik, tk], w=[u3k])
                p.op('dve', I_tt(rv(u4[:]), zre[:], sinT, ALU.mult), r=[zrk, tk], w=[u4k])
                u.update(uu=(u1, u1k, u2, u2k, u3, u3k, u4, u4k))

            def P5(u):
                pass

            def P6(u):
                tb_ = tabs[u['q']]
                w, q = u['w'], u['q']
                win = slice(w * SW, (w + 1) * SW)
                if q == 0:
                    ybs[w] = ybank()
                yb, ybk = ybs[w]
                u1, u1k, u2, u2k, u3, u3k, u4, u4k = u['uu']
                first = (q == 0)
                last = (q == 3) and d == 1
                p.op('pe', I_mm(yb[:], tb_['cw'][:, 0, :], u1[:], first, False), r=[tb_['cwk'], u1k], w=[ybk])
                p.op('pe', I_mm(yb[:], tb_['cw'][:, 2, :], u2[:], False, False), r=[tb_['cwk'], u2k], w=[ybk])
                p.op('pe', I_mm(yb[:], tb_['cw'][:, 1, :], u3[:], False, False), r=[tb_['cwk'], u3k], w=[ybk])
                p.op('pe', I_mm(yb[:], tb_['cw'][:, 1, :], u4[:], False, last), r=[tb_['cwk'], u4k], w=[ybk])
                if q == 3:
                    if d == 0:
                        p.op('pe', I_mm(yb[:], dD[:], ub[:, ck, win], False, True), r=[dDk, 'ub%d' % ck], w=[ybk])
                        p.op('act', I_act(ya[:, win], yb[:], AF.Copy), r=[ybk], w=[yk + '_%d' % w])
                    else:
                        p.op('dve', I_tt(ya[:, win], yb[:], ya[:, win], ALU.add), r=[ybk, yk + '_%d' % w], w=[yk + '_%d' % w])
            nu = len(units)
            for step in range(nu + 2):
                if step < nu:
                    P01(units[step])
                    P2(units[step])
                if 0 <= step - 1 < nu:
                    P3(units[step - 1])
                    P4(units[step - 1])
                if 0 <= step - 2 < nu:
                    P5(units[step - 2])
                    P6(units[step - 2])
        if 'yO' in A:
            m0c = cx.vec[:, VC['m0']:VC['m0'] + 1]
            m1c = cx.vec[:, VC['m1']:VC['m1'] + 1]
            for hb in range(NT // SW):
                A1 = ya[:, hb * SW:(hb + 1) * SW]
                B1 = ya[:, SEQ - (hb + 1) * SW:SEQ - hb * SW][:, ::-1]
                ka = yk + '_%d' % hb
                kb = yk + '_%d' % (NW - 1 - hb)
                ot, otk = cx.rot('yo_t', [128, SW], F32, n=2)
                mcombine(cx, ot[:], otk, A1, ka, B1, kb, m0c, m1c)
                p.dma('sp', I_dma(A['yO'][ck * 128:(ck + 1) * 128, hb * SW:(hb + 1) * SW], ot[:]), r=[otk], w=['yout%d' % ck])
                st_, stk_ = cx.rot('yo_t', [128, SW], F32, n=2)
                mcombine(cx, st_[:], stk_, A1, ka, B1, kb, m1c, m0c)
                p.dma('sp', I_dma(A['yS'].src_rows(ck * 128)[:, hb * SW:(hb + 1) * SW], st_[:]), r=[stk_], w=['yout%d' % ck])
        else:
            p.dma('sp', I_dma(A['yT'][ck * 128:(ck + 1) * 128, :], ya[:]), r=[yk + '_%d' % w for w in range(NW)], w=['yout%d' % ck])
        if 'cc_early' in A and ck == 1:
            A['cc_early']()


def build_s5(debug=False, arena=False):
    nc = bass.Bass("TRN2", target_bir_lowering=False)
    A = {}

    def inp(name, shape, dt=F32):
        A[name] = nc.dram_tensor(name, list(shape), dt, kind="ExternalInput").ap()
    inp('uT', [512, SEQ])
    inp('Bre', [2, NPT, 128, 128])
    inp('Bim', [2, NPT, 128, 128])
    inp('CR', [2, NPT, 128, 128])
    inp('CI', [2, NPT, 128, 128])
    inp('lamre', [128, 2 * NPT])
    inp('lamim', [128, 2 * NPT])
    inp('logdt', [128, 2 * NPT])
    inp('dsk', [128, 4])
    inp('ident', [128, 128])
    A['yT'] = nc.dram_tensor('yT', [512, SEQ], F32, kind="ExternalOutput").ap()
    cx = Cx(nc, arena=arena)
    if arena:
        cx.new_stage()
    if debug:
        A['dbg'] = nc.dram_tensor('dbg', [128, 8 * 2 * NPT], F32, kind="ExternalOutput").ap()
        A['dbg2'] = nc.dram_tensor('dbg2', [128, 2 * SW], F32, kind="ExternalOutput").ap()
    s5_body(cx, A)
    cx.p.finalize()
    return nc, cx


def s5_host_inputs(inp, j, half):
    g0 = 32 * half
    Bre = np.zeros((2, NPT, 128, 128), np.float32)
    Bim = np.zeros_like(Bre)
    CR = np.zeros_like(Bre)
    CI = np.zeros_like(Bre)
    lamre = np.zeros((128, 2 * NPT), np.float32)
    lamim = np.zeros_like(lamre)
    logdt = np.zeros_like(lamre)
    for d in range(2):
        for pt in range(NPT):
            for gl in range(2):
                g = g0 + 2 * pt + gl
                r0 = (pt % 4) * 32 + gl * 16
                Bre[d, pt, r0:r0 + 16, gl * 64:(gl + 1) * 64] = inp['s5_b_re'][j, d, g].T
                Bim[d, pt, r0:r0 + 16, gl * 64:(gl + 1) * 64] = inp['s5_b_im'][j, d, g].T
                CR[d, pt, gl * 64:(gl + 1) * 64, r0:r0 + 16] = inp['s5_c_re'][j, d, g].T
                CI[d, pt, gl * 64:(gl + 1) * 64, r0:r0 + 16] = inp['s5_c_im'][j, d, g].T
                lamre[gl * 64:(gl + 1) * 64, d * NPT + pt] = inp['s5_lambda_re'][j, d, g]
                lamim[gl * 64:(gl + 1) * 64, d * NPT + pt] = inp['s5_lambda_im'][j, d, g]
                logdt[gl * 64:(gl + 1) * 64, d * NPT + pt] = inp['s5_log_dt'][j, d, g]
    dsk = np.ascontiguousarray(inp['s5_d'][j, 512 * half:512 * half + 512].reshape(4, 128).T)
    return dict(Bre=Bre, Bim=Bim, CR=CR, CI=CI, lamre=lamre, lamim=lamim, logdt=logdt, dsk=dsk,
                ident=np.eye(128, dtype=np.float32))


NEXT = 3072
GRP = [(1, 2048), (4, 512), (16, 128)]
ASCALE = 128 ** -0.5


def sub_view(ap2d, d):
    if d == 1:
        return ap2d.rearrange("p (d i) -> p d i", d=1)
    return ap2d.rearrange("p (i d) -> p d i", d=d)


def attn_body(cx, A, flip=False, src=None, dst=None, gh=None):
    p = cx.p
    vec = cx.vec

    def load_blk(xt, xk, c, tb):
        if gh is None or tb < NT // 512:
            p.dma('sp', I_dma(xt[:], _src[c * 128:(c + 1) * 128, _cols(tb)]), w=[xk])
            return
        hb = tb - NT // 512
        cols = slice(1024 - 512 * (hb + 1), 1024 - 512 * hb)
        pc = (c + 4) % 8
        X, xk2 = cx.rot('gx', [128, 512], F32, n=2)
        Z, zk2 = cx.rot('gz', [128, 512], F32, n=2)
        p.dma('sp', I_dma(X[:], gh.g_rows(0, pc * 128)[:, cols]), w=[xk2])
        p.dma('sp', I_dma(Z[:], gh.g_rows(1, pc * 128)[:, cols]), w=[zk2])
        mcombine(cx, xt[:], xk, X[:], xk2, Z[:], zk2, vec[:, VC['m1']:VC['m1'] + 1], vec[:, VC['m0']:VC['m0'] + 1])

    def _cols(tb):
        if src is None or not flip:
            return slice(tb * 512, (tb + 1) * 512)
        return slice(SEQ - (tb + 1) * 512, SEQ - tb * 512)
    _src = A['hT_ext'] if src is None else src
    _dst = A['hT_out'] if dst is None else dst
    rvf = (lambda ap: ap[:, ::-1]) if (flip and src is not None) else (lambda ap: ap)
    hn = p.sb('hnx', [128, NCH, NEXT], BF16)
    mT = p.sb('mT', [128, NCH, NT], BF16)
    num = p.sb('numacc', [128, NT], F32)
    den = p.sb('denacc', [128, NT], F32)
    m2 = getattr(p, 'aoff', None)
    NXB = 16 if m2 is not None else 2
    for tb in range(NEXT // 512):
        bank, bk = cx.bank()
        tiles_ = []
        for c in range(NCH):
            xt, xk = cx.rot('xin', [128, 512], F32, n=NXB)
            load_blk(xt, xk, c, tb)
            tiles_.append((xt, xk))
            sq, sk = cx.next_sq()
            p.op('act', I_act(sq[:], xt[:], AF.Square), r=[xk], w=[sk])
            p.op('pe', I_mm(bank[:], cx.ones[:], sq[:], c == 0, c == NCH - 1), r=[sk, 'ones'], w=[bk])
        rs, rk = cx.rstd_from_bank(bank, bk, 512, D)
        for c in range(NCH):
            if m2 is not None:
                xt, xk = tiles_[c]
            else:
                xt, xk = cx.rot('xin', [128, 512], F32, n=NXB)
                load_blk(xt, xk, c, tb)
            rv_ = (lambda ap: ap[:, ::-1]) if (gh is not None and tb >= NT // 512) else rvf
            p.op('dve', I_stt(rv_(hn[:, c, tb * 512:(tb + 1) * 512]), xt[:], vec[:, VC['gn'] + c:VC['gn'] + c + 1], rs[:],
                              ALU.mult, ALU.mult), r=[xk, 'vec', rk], w=['hnx'])
    if m2 is not None:
        p.barrier()
        p.aoff = m2
        cx._rot = {}
    sb_i = [0]

    def sbank():
        i = sb_i[0]
        sb_i[0] = (i + 1) % 4
        return cx.banks[i], 'bank%d' % i
    ob_i = [0]

    def obanks():
        i = ob_i[0]
        ob_i[0] = 1 - i
        return cx.banks[4 + i], 'bank%d' % (4 + i), cx.banks[6 + i], 'bank%d' % (6 + i)

    def qknorm(bank, bk, n, gcol, dst, dkey):
        sq, sk = cx.next_sq()
        p.op('act', I_act(sq[:, :n], bank[:, :n], AF.Square), r=[bk], w=[sk])
        b2, b2k = sbank()
        p.op('pe', I_mm(b2[:, :n], cx.ones[:], sq[:, :n], True, True), r=[sk, 'ones'], w=[b2k])
        rs, rk = cx.rstd_from_bank(b2, b2k, n, 128)
        p.op('dve', I_stt(dst, bank[:, :n], vec[:, gcol:gcol + 1], rs[:, :n], ALU.mult, ALU.mult), r=[bk, 'vec', rk], w=[dkey])

    wq_loaded = {}
    bias_loaded = {}

    def load_w(h_, g_):
        if (h_, g_) in wq_loaded or h_ >= 8:
            return
        wsl_, wsk_ = cx.rot('wqkv', [128, NCH, 384], BF16, n=3)
        for kind in range(3):
            c0 = kind * 3072 + g_ * 1024 + h_ * 128
            p.dma('pool', I_dma(wsl_[:, :, kind * 128:(kind + 1) * 128],
                                A['wqkv'][:, c0:c0 + 128].rearrange("(k p) n -> p k n", p=128)), w=[wsk_])
        wq_loaded[(h_, g_)] = (wsl_, wsk_)

    def load_bias(h_):
        if h_ in bias_loaded or h_ >= 8:
            return
        bt_, btk_ = cx.rot('biasT', [128, 3, 256], F32, n=2)
        for g_ in range(3):
            p.dma('sp', I_dma(bt_[:, g_, :], A['biasT'][g_ * 8 + h_]), w=[btk_])
        bias_loaded[h_] = (bt_, btk_)

    for h in range(8):
        p.op('pool', I_memset(num[:], 0.0), w=['numacc'])
        p.op('pool', I_memset(den[:], 0.0), w=['denacc'])
        load_bias(h)
        bt, btk = bias_loaded[h]
        for g, (d, Lq) in enumerate(GRP):
            nto = Lq // 128
            load_w(h, g)
            wsl, wsk = wq_loaded[(h, g)]
            load_w(h + (g + 1) // 3, (g + 1) % 3)
            if g == 0:
                load_bias(h + 1)
            qT, qk_ = cx.rot('qT', [128, NT], BF16, n=2)
            kT, kk_ = cx.rot('kT', [128, NEXT], BF16, n=2)
            vt, vk_ = cx.rot('vt', [128, 32, 128], BF16, n=2)

            for kind, dstT, dk, gcol in ((0, qT, qk_, VC['aq']), (1, kT, kk_, VC['ak'])):
                for bi in range(4):
                    b, bk = sbank()
                    for kc in range(NCH):
                        if d == 1:
                            rhs, o_ap = hn[:, kc, bi * 512:(bi + 1) * 512], b[:]
                        elif d == 4:
                            rhs, o_ap = sub_view(hn[:, kc, 0:NT], 4)[:, bi, :], b[:]
                        else:
                            rhs = sub_view(hn[:, kc, 0:NT], 16)[:, 4 * bi:4 * bi + 4, :]
                            o_ap = b[:].rearrange("p (a b) -> p a b", a=4)
                        p.op('pe', I_mm(o_ap, wsl[:, kc, kind * 128:(kind + 1) * 128], rhs, kc == 0, kc == NCH - 1),
                             r=[wsk, 'hnx'], w=[bk])
                    qknorm(b, bk, 512, gcol, dstT[:, bi * 512:(bi + 1) * 512], dk)
            nh = 64 * d
            for b0 in range(0, nh, 512):
                n = min(512, nh - b0)
                b, bk = sbank()
                for kc in range(NCH):
                    if d == 1:
                        rhs = hn[:, kc, NT:NT + 64]
                        o_ap = b[:, :64]
                    else:
                        r0 = b0 // 64
                        nr = n // 64
                        rhs = sub_view(hn[:, kc, NT:NT + 64 * d], d)[:, r0:r0 + nr, :]
                        o_ap = b[:, :n].rearrange("p (a b) -> p a b", a=nr)
                    p.op('pe', I_mm(o_ap, wsl[:, kc, 128:256], rhs, kc == 0, kc == NCH - 1), r=[wsk, 'hnx'], w=[bk])
                qknorm(b, bk, n, VC['ak'], kT[:, NT + b0:NT + b0 + n], kk_)
            for t0 in range(0, 16, 4):
                b, bk = sbank()
                for tt in range(4):
                    t = t0 + tt
                    r, m = t // nto, t % nto
                    for kc in range(NCH):
                        lhsT = sub_view(hn[:, kc, 0:NT], d)[:, r, m * 128:(m + 1) * 128]
                        p.op('pe', I_mm(b[:, tt * 128:(tt + 1) * 128], lhsT, wsl[:, kc, 256:384], kc == 0, kc == NCH - 1),
                             r=[wsk, 'hnx'], w=[bk])
                p.op('act', I_act(vt[:, t0:t0 + 4, :], b[:].rearrange("p (a b) -> p a b", a=4), AF.Copy), r=[bk], w=[vk_])
            for r0 in range(0, d, 4):
                nr = min(4, d - r0)
                b, bk = sbank()
                for rr in range(nr):
                    r = r0 + rr
                    for kc in range(NCH):
                        lhsT = sub_view(hn[:, kc, NT:NT + 64 * d], d)[:, r, :]
                        p.op('pe', I_mm(b[:64, rr * 128:(rr + 1) * 128], lhsT, wsl[:, kc, 256:384], kc == 0, kc == NCH - 1),
                             r=[wsk, 'hnx'], w=[bk])
                p.op('act', I_act(vt[:64, 16 + r0:16 + r0 + nr, :], b[:64, :nr * 128].rearrange("p (a b) -> p a b", a=nr), AF.Copy),
                     r=[bk], w=[vk_])
            tiles = [(r, m) for r in range(d) for m in range(nto + 1)]
            stt_ = {'ob': None}

            def S_phase(r, m):
                qoff = r * Lq
                halo = (m == nto)
                nk = 64 if halo else 128
                b0_ = 64 if m == 0 else 0
                b1_ = 64 if halo else min(256, Lq - (128 * m - 64))
                ktile = kT[:, NT + r * 64:NT + r * 64 + 64] if halo else kT[:, qoff + m * 128:qoff + (m + 1) * 128]
                qs = qoff + 128 * m - 64 + b0_
                sbk, sbkk = sbank()
                p.op('pe', I_mm(sbk[:nk, b0_:b1_], ktile, qT[:, qs:qs + (b1_ - b0_)], True, True), r=[kk_, qk_], w=[sbkk])
                st, stk = cx.rot('stmp', [128, 256], F32, n=3)
                p.op('dve', I_stt(st[:nk, b0_:b1_], sbk[:nk, b0_:b1_], ASCALE, bt[:nk, g, b0_:b1_], ALU.mult, ALU.add),
                     r=[sbkk, btk], w=[stk])
                PT, ptk = cx.rot('PTa', [128, 256], BF16, n=4)
                p.op('act', I_act(PT[:nk, b0_:b1_], st[:nk, b0_:b1_], AF.Exp), r=[stk], w=[ptk])
                return PT, ptk

            def PV_phase(r, m, PT, ptk):
                halo = (m == nto)
                nk = 64 if halo else 128
                vtile = vt[:64, 16 + r, :] if halo else vt[:, r * nto + m, :]

                def flush(ep):
                    ob = stt_['ob']
                    qlo = max(0, 512 * ep - 64)
                    qhi = min(Lq, 512 * ep + 448)
                    c0f = qlo - (512 * ep - 64)
                    wdt = qhi - qlo
                    nv = sub_view(num[:, :], d)[:, r, qlo:qhi]
                    dv = sub_view(den[:, :], d)[:, r, qlo:qhi]
                    p.op('dve', I_tt(nv, ob[0][:, c0f:c0f + wdt], nv, ALU.add), r=[ob[1], 'numacc'], w=['numacc'])
                    p.op('dve', I_tt(dv, ob[2][:, c0f:c0f + wdt], dv, ALU.add), r=[ob[3], 'denacc'], w=['denacc'])
                if m == 0:
                    stt_['ob'] = ob = obanks()
                    p.op('pe', I_mm(ob[0][:, 64:128], vtile, PT[:nk, 64:128], True, True), r=[vk_, ptk], w=[ob[1]])
                    p.op('pe', I_mm(ob[2][:, 64:128], cx.ones[:nk, :], PT[:nk, 64:128], True, True), r=['ones', ptk], w=[ob[3]])
                else:
                    ob = stt_['ob']
                    q0 = 64 + 128 * (m - 1)
                    wq_ = min(Lq, q0 + 128) - q0
                    c0 = 128 * (m % 4)
                    p.op('pe', I_mm(ob[0][:, c0:c0 + wq_], vtile, PT[:nk, 0:wq_], False, True), r=[vk_, ptk], w=[ob[1]])
                    p.op('pe', I_mm(ob[2][:, c0:c0 + wq_], cx.ones[:nk, :], PT[:nk, 0:wq_], False, True), r=['ones', ptk], w=[ob[3]])
                    if m % 4 == 3 or halo:
                        flush(m // 4)
                if not halo:
                    q0 = 64 + 128 * m
                    wq_ = min(Lq, q0 + 128) - q0
                    if (m + 1) % 4 == 0:
                        stt_['ob'] = obanks()
                    ob = stt_['ob']
                    c0 = 128 * ((m + 1) % 4)
                    p.op('pe', I_mm(ob[0][:, c0:c0 + wq_], vtile, PT[:nk, 128:128 + wq_], True, False), r=[vk_, ptk], w=[ob[1]])
                    p.op('pe', I_mm(ob[2][:, c0:c0 + wq_], cx.ones[:nk, :], PT[:nk, 128:128 + wq_], True, False), r=['ones', ptk], w=[ob[3]])
            SKEW = 2
            pend = {}
            for i in range(len(tiles) + SKEW):
                if i < len(tiles):
                    pend[i] = S_phase(*tiles[i])
                if i - SKEW >= 0:
                    PV_phase(*tiles[i - SKEW], *pend.pop(i - SKEW))
        p.op('act', I_act(den[:], den[:], AF.Ln), r=['denacc'], w=['denacc'])
        p.op('act', I_act(den[:], den[:], AF.Exp, scale=-1.0), r=['denacc'], w=['denacc'])
        p.op('pool', I_tt(mT[:, h, :], num[:], den[:], ALU.mult), r=['numacc', 'denacc'], w=['mT'])
    for ns in range(2):
        wo, wok = wslab(cx, [(A['wo_a'][:, ns * 512:(ns + 1) * 512], 0)])
        for j in range(4):
            n = ns * 4 + j
            for tb in range(NT // 512):
                b, bk = sbank()
                for kc in range(NCH):
                    p.op('pe', I_mm(b[:], wo[:, kc * 512 + j * 128: kc * 512 + (j + 1) * 128], mT[:, kc, tb * 512:(tb + 1) * 512],
                                    kc == 0, kc == NCH - 1), r=[wok, 'mT'], w=[bk])
                xt, xk = cx.rot('xin', [128, 512], F32, n=2)
                p.dma('sp', I_dma(xt[:], _src[n * 128:(n + 1) * 128, _cols(tb)]), w=[xk])
                p.op('dve', I_tt(xt[:], rvf(b[:]), xt[:], ALU.add), r=[bk, xk], w=[xk])
                p.dma('sp', I_dma(_dst[n * 128:(n + 1) * 128, _cols(tb)], xt[:]), r=[xk], w=['hTout'])


def build_attn():
    nc = bass.Bass("TRN2", target_bir_lowering=False)
    A = {}

    def inp(name, shape, dt=F32):
        A[name] = nc.dram_tensor(name, list(shape), dt, kind="ExternalInput").ap()
    inp('hT_ext', [D, NEXT])
    inp('vecs', [128, NVEC])
    inp('wqkv', [D, 9216])
    inp('wo_a', [D, D])
    inp('biasT', [24, 128, 256])
    A['hT_out'] = nc.dram_tensor('hT_out', [D, NT], F32, kind="ExternalOutput").ap()
    cx = Cx(nc, arena=True)
    cx.new_stage()
    cx.vec = cx.p.sb('vec', [128, NVEC], F32)
    cx.p.dma('sp', I_dma(cx.vec[:], A['vecs'][:, :]), w=['vec'])
    cx.ws_n = 2
    attn_body(cx, A)
    cx.p.finalize()
    return nc, cx


def t5_bucket(rel):
    nb = 16
    ret = (rel > 0).astype(np.int32) * nb
    n = np.abs(rel)
    max_exact = nb // 2
    large = max_exact + (np.log(np.maximum(n, 1).astype(np.float32) / max_exact)
                         / np.log(1024 / max_exact) * (nb - max_exact)).astype(np.int32)
    large = np.minimum(large, nb - 1)
    return (ret + np.where(n < max_exact, n, large)).astype(np.int32)


def host_bias(bias_table, flip):
    a = np.arange(128)[:, None]
    b = np.arange(256)[None, :]
    rel = a - b + 64
    out = np.full((24, 128, 256), -1e30, np.float32)
    band = np.abs(rel) <= 64
    for g, (dil, _) in enumerate(GRP):
        bk = t5_bucket((-rel if flip else rel) * dil)
        for h in range(8):
            out[g * 8 + h] = np.where(band, bias_table[bk, g * 8 + h], np.float32(-1e30))
    return out


def build_norm():
    nc = bass.Bass("TRN2", target_bir_lowering=False)
    A = {}
    A['hT'] = nc.dram_tensor('hT', [D, NT], F32, kind="ExternalInput").ap()
    A['vecs'] = nc.dram_tensor('vecs', [128, NVEC], F32, kind="ExternalInput").ap()
    A['hn_out'] = nc.dram_tensor('hn_out', [D, NT], F32, kind="ExternalOutput").ap()
    cx = Cx(nc)
    p = cx.p
    cx.hT = p.sb('hT', [128, NCH, NT], F32)
    cx.vec = p.sb('vec', [128, NVEC], F32)
    p.dma('sp', I_dma(cx.vec[:], A['vecs'][:, :]), w=['vec'])
    load_hT(cx, A['hT'])
    emit_norm(cx, A['hn_out'])
    p.finalize()
    return nc, cx


def build_fused(nlayers=4):
    nc = bass.Bass("TRN2", target_bir_lowering=False)
    shapes = {}

    def inp(name, shape, dt=F32):
        shapes[name] = list(shape)

    class Lazy(dict):
        def __missing__(self, name):
            ap = nc.dram_tensor(name, shapes[name], F32, kind="ExternalInput").ap()
            self[name] = ap
            return ap
    A = Lazy()

    def scratch(name):
        return nc.dram_tensor(name, [D, SEQ], F32, kind="Internal").ap()
    inp('xT', [D, SEQ])
    inp('memT', [D, MEMLEN])
    inp('ident', [128, 128])
    inp('v0', [128, NVEC])
    inp('biasT0', [24, 128, 256])
    inp('biasT1', [24, 128, 256])
    for i in range(4):
        inp('vecs%d' % i, [128, NVEC])
        inp('wq%d' % i, [D, D])
        inp('wkv%d' % i, [D, 2 * D])
        inp('wo%d' % i, [D, D])
        inp('w1_%d' % i, [D, DFF])
        inp('w2_%d' % i, [DFF, D])
    for j in range(2):
        inp('wglu%d' % j, [D, 2 * D])
        inp('wqkv%d' % j, [D, 9216])
        inp('woa%d' % j, [D, D])
        inp('avecs%d' % j, [128, NVEC])
        for c in range(2):
            sfx = '%d%d' % (j, c)
            for nm in ('Bre', 'Bim', 'CR', 'CI'):
                inp(nm + sfx, [2, NPT, 128, 128])
            for nm in ('lamre', 'lamim', 'logdt'):
                inp(nm + sfx, [128, 2 * NPT])
            inp('dsk' + sfx, [128, 4])
    xT = A['xT']
    outT = nc.dram_tensor('outT', [D, SEQ], F32, kind="ExternalOutput").ap()
    HN = scratch('HN')
    Y = scratch('Y')
    Hs = [xT, scratch('H1'), scratch('H1a'), scratch('H2'), scratch('H3'), scratch('H3a'), outT]
    cx = Cx(nc, arena=True)
    p = cx.p
    hv = lambda ap, half: ap[:, half * NT:(half + 1) * NT]

    cx.hT = p.sb('hT', [128, NCH, NT], F32)
    cx.vec = p.sb('vec', [128, NVEC], F32)
    p.dma('sp', I_dma(cx.vec[:], A['v0'][:, :]), w=['vec'])
    for half in range(2):
        load_hT(cx, hv(xT, half))
        emit_norm(cx, hv(HN, half))

    def s5_stage(j):
        for c in range(2):
            cx.new_stage()
            sfx = '%d%d' % (j, c)
            AA = {nm: A[nm + sfx] for nm in ('Bre', 'Bim', 'CR', 'CI', 'lamre', 'lamim', 'logdt', 'dsk')}
            AA['ident'] = A['ident']
            AA['uT'] = HN[512 * c:512 * c + 512, :]
            AA['yT'] = Y[512 * c:512 * c + 512, :]
            s5_body(cx, AA)

    def tail_stage(i, glu, src, dst, emit):
        cx.new_stage()
        AA = dict(memT=A['memT'], vecs=A['vecs%d' % i], wq=A['wq%d' % i], wkv=A['wkv%d' % i], wo=A['wo%d' % i],
                  w1=A['w1_%d' % i], w2=A['w2_%d' % i])
        if glu:
            AA['wglu'] = A['wglu%d' % (i // 2)]
        common_tiles(cx, AA)
        for half in range(2):
            load_hT(cx, hv(src, half))
            if glu:
                AA['yT'] = hv(Y, half)
            tail_body(cx, AA, glu, 0, 2, kv_ready=(half == 1))
            tail_body(cx, AA, glu, 2, 2, kv_ready=True)
            store_hT(cx, hv(dst, half))
            if emit:
                emit_norm(cx, hv(HN, half))

    def attn_stage(i, src, dst):
        j = i // 2
        for half in range(2):
            cx.new_stage()
            cx.vec = p.sb('vec', [128, NVEC], F32)
            p.dma('sp', I_dma(cx.vec[:], A['avecs%d' % j][:, :]), w=['vec'])
            AA = dict(wqkv=A['wqkv%d' % j], wo_a=A['woa%d' % j], biasT=A['biasT%d' % half])
            attn_body(cx, AA, flip=(half == 1), src=src, dst=dst)

    s5_stage(0)
    if nlayers == 0:
        cx.new_stage()
        cx.hT = p.sb('hT', [128, NCH, NT], F32)
        for half in range(2):
            load_hT(cx, hv(Y, half))
            store_hT(cx, hv(outT, half))
        p.finalize()
        cx.used = list(A.keys())
        return nc, cx
    tail_stage(0, True, Hs[0], Hs[1] if nlayers > 1 else outT, False)
    if nlayers > 1:
        attn_stage(1, Hs[1], Hs[2])
        tail_stage(1, False, Hs[2], Hs[3] if nlayers > 2 else outT, True)
    if nlayers > 2:
        s5_stage(1)
        tail_stage(2, True, Hs[3], Hs[4] if nlayers > 3 else outT, False)
    if nlayers > 3:
        attn_stage(3, Hs[4], Hs[5])
        tail_stage(3, False, Hs[5], Hs[6], False)
    p.finalize()
    cx.used = list(A.keys())
    return nc, cx


RG2 = [[0, 1], [2, 3], [4, 5], [6, 7]]


class GBuf:
    def __init__(self, nc, name, rows, cols, chunk_rows):
        self.cr = chunk_rows
        self.n = rows // chunk_rows
        self.src = [nc.dram_tensor('%s_s%d' % (name, q), [chunk_rows, cols], F32, kind="Internal").ap() for q in range(self.n)]
        self.dst = [nc.dram_tensor('%s_g%d' % (name, q), [2 * chunk_rows, cols], F32, kind="Internal").ap() for q in range(self.n)]

    def src_rows(self, r0, nrows=128):
        q = r0 // self.cr
        o = r0 - q * self.cr
        return self.src[q][o:o + nrows, :]

    def g_rows(self, rank, r0, nrows=128):
        q = r0 // self.cr
        o = rank * self.cr + r0 - q * self.cr
        return self.dst[q][o:o + nrows, :]


def build_fused8():
    nc = bass.Bass("TRN2", target_bir_lowering=False, num_devices=8)
    shapes = {}

    def inp(name, shape):
        shapes[name] = list(shape)

    class Lazy(dict):
        def __missing__(self, name):
            ap = nc.dram_tensor(name, shapes[name], F32, kind="ExternalInput").ap()
            self[name] = ap
            return ap
    A = Lazy()

    def scratch(name, shape):
        return nc.dram_tensor(name, list(shape), F32, kind="Internal").ap()
    inp('xT', [D, NT])
    inp('memT', [D, MEMLEN])
    inp('ident', [128, 128])
    inp('v0', [128, NVEC])
    inp('biasT', [24, 128, 256])
    for i in range(4):
        inp('vecs%d' % i, [128, NVEC])
        inp('wq%d' % i, [D, D])
        inp('wkv%d' % i, [D, 2 * D])
        inp('wo%d' % i, [D, D])
        inp('w1_%d' % i, [D, DFF])
        inp('w2_%d' % i, [DFF, D])
    for j in range(2):
        inp('wglu%d' % j, [D, 2 * D])
        inp('wqkv%d' % j, [D, 9216])
        inp('woa%d' % j, [D, D])
        inp('avecs%d' % j, [128, NVEC])
        for nm in ('Bre', 'Bim', 'CR', 'CI'):
            inp(nm + '%d' % j, [2, NPT, 128, 128])
        for nm in ('lamre', 'lamim', 'logdt'):
            inp(nm + '%d' % j, [128, 2 * NPT])
        inp('dsk%d' % j, [128, 4])
    outT = nc.dram_tensor('outT', [D, NT], F32, kind="ExternalOutput").ap()
    HNb = GBuf(nc, 'HN', D, NT, 256)
    HN = GHN = HNb
    yO = scratch('yO', [512, NT])
    ySb = GBuf(nc, 'yS', 512, NT, 256)
    yS = GS = ySb
    Hhb = GBuf(nc, 'Hh', D, 1024, 512)
    Hh = GH = Hhb
    Hs = [A['xT']] + [scratch(n, [D, NT]) for n in ('H1', 'H1a', 'H2', 'H3', 'H3a')] + [outT]
    cx = Cx(nc, arena=True)
    p = cx.p

    def gather_chunk(gb, q, rkeys):
        p.dma('pool', lambda e, q=q: e.collective_compute("AllGather", ALU.bypass, replica_groups=RG2,
                                                           ins=[gb.src[q][:, :]], outs=[gb.dst[q][:, :]]),
              r=list(rkeys), w=['cc'], semkey='cc', inc=1)

    def allgather(gb, _unused=None, chunks=None):
        cx.new_stage()
        for q in (range(gb.n) if chunks is None else chunks):
            p.dma('pool', lambda e, q=q: e.collective_compute("AllGather", ALU.bypass, replica_groups=RG2,
                                                               ins=[gb.src[q][:, :]], outs=[gb.dst[q][:, :]]),
                  w=['cc'], semkey='cc', inc=1)

    cx.hT = p.sb('hT', [128, NCH, NT], F32)
    cx.vec = p.sb('vec', [128, NVEC], F32)
    p.dma('sp', I_dma(cx.vec[:], A['v0'][:, :]), w=['vec'])
    load_hT(cx, A['xT'])
    emit_norm(cx, HN)
    allgather(HN, GHN)

    def s5_stage(j):
        cx.new_stage()
        AA = {nm: A[nm + '%d' % j] for nm in ('Bre', 'Bim', 'CR', 'CI', 'lamre', 'lamim', 'logdt', 'dsk')}
        AA['ident'] = A['ident']
        AA['GHN'] = GHN
        AA['yO'] = yO
        AA['yS'] = yS
        cx.vec = p.sb('vec', [128, NVEC], F32)
        p.dma('sp', I_dma(cx.vec[:], A['v0'][:, :]), w=['vec'])
        AA['cc_early'] = lambda: gather_chunk(ySb, 0, ['yout0', 'yout1'])
        s5_body(cx, AA)
        allgather(yS, GS, chunks=[1])

    def tail_stage(i, glu, src, dst, emit, halo):
        cx.new_stage()
        AA = dict(memT=A['memT'], vecs=A['vecs%d' % i], wq=A['wq%d' % i], wkv=A['wkv%d' % i], wo=A['wo%d' % i],
                  w1=A['w1_%d' % i], w2=A['w2_%d' % i])
        if glu:
            AA['wglu'] = A['wglu%d' % (i // 2)]
            AA['yO'] = yO
            AA['GS'] = GS
        common_tiles(cx, AA)
        load_hT(cx, src)
        tail_body(cx, AA, glu, 2, 2, kv_ready=False)
        if halo:
            for c in range(NCH):
                p.dma('sp', I_dma(Hh.src_rows(c * 128), cx.hT[:, c, 1024:2048]),
                      r=['h%d_%d' % (c, tb) for tb in (2, 3)], w=['hh%d' % (c // 4)], semkey='hhout%d' % (c // 4))
            for q in range(Hhb.n):
                gather_chunk(Hhb, q, ['hh%d' % q])
        tail_body(cx, AA, glu, 0, 2, kv_ready=True)
        store_hT(cx, dst)
        if emit:
            emit_norm(cx, HN)
            allgather(HN, GHN)

    def attn_stage(i, src, dst):
        j = i // 2
        cx.new_stage()
        cx.vec = p.sb('vec', [128, NVEC], F32)
        p.dma('sp', I_dma(cx.vec[:], A['avecs%d' % j][:, :]), w=['vec'])
        AA = dict(wqkv=A['wqkv%d' % j], wo_a=A['woa%d' % j], biasT=A['biasT'])
        cx.ws_n = 2
        attn_body(cx, AA, flip=False, src=src, dst=dst, gh=GH)
        cx.ws_n = WS_N

    s5_stage(0)
    tail_stage(0, True, Hs[0], Hs[1], False, True)
    attn_stage(1, Hs[1], Hs[2])
    tail_stage(1, False, Hs[2], Hs[3], True, False)
    s5_stage(1)
    tail_stage(2, True, Hs[3], Hs[4], False, True)
    attn_stage(3, Hs[4], Hs[5])
    tail_stage(3, False, Hs[5], Hs[6], False, False)
    p.finalize()
    cx.used = list(A.keys())
    return nc, cx


def _pc(v, C):
    return np.ascontiguousarray(np.asarray(v, np.float32).reshape(C, 128).T)


_PROGS = {}
_NL = [4]


def _prog(name):
    if name not in _PROGS:
        if name == 'norm':
            _PROGS[name] = build_norm()[0]
        elif name == 's5':
            _PROGS[name] = build_s5()[0]
        elif name == 'tail_glu':
            _PROGS[name] = build_tail(True, True)[0]
        elif name == 'tail':
            _PROGS[name] = build_tail(False, True)[0]
        elif name == 'attn':
            _PROGS[name] = build_attn()[0]
    return _PROGS[name]


def kernel_multi(**inp):
    inp = {k: np.asarray(v) for k, v in inp.items()}
    ncore = 8
    cores = list(range(ncore))
    f32 = np.float32
    loc = [np.arange(NEXT) if (k % 2 == 0) else (SEQ - 1 - np.arange(NEXT)) for k in cores]
    H = np.array(inp['x'], dtype=f32, copy=True)
    memT = [np.ascontiguousarray(inp['mem'][k // 2].T.astype(f32)) for k in cores]
    biasT = [host_bias(inp['bias_table'].astype(f32), k % 2 == 1) for k in cores]

    def own_T(arr_bsd, k):
        return np.ascontiguousarray(arr_bsd[k // 2][loc[k][:NT]].T)

    def scatter(outs, name):
        full = np.empty((BATCH, SEQ, D), f32)
        for k in cores:
            full[k // 2][loc[k][:NT]] = np.asarray(outs[k][name], f32).T
        return full

    def tail_vecs(i):
        v = np.zeros((128, NVEC), f32)
        v[:, 0:8] = _pc(inp['norm_xattn'][i], 8)
        v[:, 8:16] = _pc(inp['norm_mem'][i], 8)
        v[:, 16:24] = _pc(inp['norm_mlp'][i], 8)
        v[:, 24:26] = _pc(inp['xattn_q_gain'][i], 2)
        v[:, 26:28] = _pc(inp['xattn_k_gain'][i], 2)
        v[:, 28:36] = _pc(inp['norm_mix'][min(i + 1, 3)], 8)
        return v

    def run_tail(i, H, Y):
        v = tail_vecs(i)
        maps = []
        for k in cores:
            m = dict(hT=own_T(H, k), memT=memT[k], vecs=v, wq=inp['xattn_w_q'][i], wkv=inp['xattn_w_kv'][i],
                     wo=inp['xattn_w_o'][i], w1=inp['mlp_w1'][i], w2=inp['mlp_w2'][i])
            if Y is not None:
                m['yT'] = own_T(Y, k)
                m['wglu'] = inp['s5_w_glu'][i // 2]
            maps.append(m)
        res = run_bass_kernel_spmd(_prog('tail_glu' if Y is not None else 'tail'), maps, core_ids=cores).results
        return scatter(res, 'hT_out'), scatter(res, 'hn_out')

    def run_s5(j, HN):
        maps = []
        for k in cores:
            b, c = k // 2, k % 2
            m = s5_host_inputs(inp, j, c)
            m['uT'] = np.ascontiguousarray(HN[b][:, 512 * c:512 * c + 512].T)
            maps.append(m)
        res = run_bass_kernel_spmd(_prog('s5'), maps, core_ids=cores).results
        Y = np.empty((BATCH, SEQ, D), f32)
        for k in cores:
            b, c = k // 2, k % 2
            Y[b][:, 512 * c:512 * c + 512] = np.asarray(res[k]['yT'], f32).T
        return Y

    def run_attn(i, H):
        j = i // 2
        v = np.zeros((128, NVEC), f32)
        v[:, 28:36] = _pc(inp['norm_mix'][i], 8)
        v[:, 36] = inp['attn_q_gain'][j]
        v[:, 37] = inp['attn_k_gain'][j]
        maps = []
        for k in cores:
            maps.append(dict(hT_ext=np.ascontiguousarray(H[k // 2][loc[k]].T), vecs=v, wqkv=inp['attn_w_qkv'][j],
                             wo_a=inp['attn_w_o'][j], biasT=biasT[k]))
        res = run_bass_kernel_spmd(_prog('attn'), maps, core_ids=cores).results
        return scatter(res, 'hT_out')

    v0 = np.zeros((128, NVEC), f32)
    v0[:, 28:36] = _pc(inp['norm_mix'][0], 8)
    res = run_bass_kernel_spmd(_prog('norm'), [dict(hT=own_T(H, k), vecs=v0) for k in cores], core_ids=cores).results
    HN = scatter(res, 'hn_out')
    for i in range(4):
        if i % 2 == 0:
            Y = run_s5(i // 2, HN)
            H, HN = run_tail(i, H, Y)
        else:
            H = run_attn(i, H)
            H, HN = run_tail(i, H, None)
    return H


def tail_vecs_host(inp, i):
    v = np.zeros((128, NVEC), np.float32)
    v[:, 0:8] = _pc(inp['norm_xattn'][i], 8)
    v[:, 8:16] = _pc(inp['norm_mem'][i], 8)
    v[:, 16:24] = _pc(inp['norm_mlp'][i], 8)
    v[:, 24:26] = _pc(inp['xattn_q_gain'][i], 2)
    v[:, 26:28] = _pc(inp['xattn_k_gain'][i], 2)
    v[:, 28:36] = _pc(inp['norm_mix'][min(i + 1, 3)], 8)
    return v


def kernel_fused4(**inp):
    inp = {k: np.asarray(v) for k, v in inp.items()}
    f32 = np.float32
    nl = _NL[0]
    if ('fused', nl) not in _PROGS:
        _PROGS[('fused', nl)] = build_fused(nl)
    nc, cxf = _PROGS[('fused', nl)]
    shared = dict(ident=np.eye(128, dtype=f32),
                  biasT0=host_bias(inp['bias_table'].astype(f32), False),
                  biasT1=host_bias(inp['bias_table'].astype(f32), True))
    v0 = np.zeros((128, NVEC), f32)
    v0[:, 28:36] = _pc(inp['norm_mix'][0], 8)
    shared['v0'] = v0
    for i in range(4):
        shared['vecs%d' % i] = tail_vecs_host(inp, i)
        shared['wq%d' % i] = inp['xattn_w_q'][i]
        shared['wkv%d' % i] = inp['xattn_w_kv'][i]
        shared['wo%d' % i] = inp['xattn_w_o'][i]
        shared['w1_%d' % i] = inp['mlp_w1'][i]
        shared['w2_%d' % i] = inp['mlp_w2'][i]
    for j in range(2):
        shared['wglu%d' % j] = inp['s5_w_glu'][j]
        shared['wqkv%d' % j] = inp['attn_w_qkv'][j]
        shared['woa%d' % j] = inp['attn_w_o'][j]
        av = np.zeros((128, NVEC), f32)
        av[:, 28:36] = _pc(inp['norm_mix'][2 * j + 1], 8)
        av[:, 36] = inp['attn_q_gain'][j]
        av[:, 37] = inp['attn_k_gain'][j]
        shared['avecs%d' % j] = av
        for c in range(2):
            for nm, arr in s5_host_inputs(inp, j, c).items():
                if nm != 'ident':
                    shared[nm + '%d%d' % (j, c)] = arr
    maps = []
    for b in range(BATCH):
        m = dict(shared)
        m['xT'] = np.ascontiguousarray(inp['x'][b].T.astype(f32))
        m['memT'] = np.ascontiguousarray(inp['mem'][b].T.astype(f32))
        maps.append({k: m[k] for k in cxf.used})
    res = run_bass_kernel_spmd(nc, maps, core_ids=list(range(BATCH))).results
    out = np.empty((BATCH, SEQ, D), f32)
    for b in range(BATCH):
        out[b] = np.asarray(res[b]['outT'], f32).T
    return out


def _sw(a, axis):
    return np.roll(a, 512, axis=axis)


def kernel(**inp):
    inp = {k: np.asarray(v, np.float32) for k, v in inp.items()}
    f32 = np.float32
    if 'fused8' not in _PROGS:
        _PROGS['fused8'] = build_fused8()
    nc, cxf = _PROGS['fused8']
    ident = np.eye(128, dtype=f32)
    per_c = []
    for c in range(2):
        sw = (lambda a, axis: _sw(a, axis)) if c == 1 else (lambda a, axis: a)
        g = {}
        gi = {k: (sw(inp[k], 1) if k in ('norm_mix', 'norm_xattn', 'norm_mem', 'norm_mlp') else inp[k]) for k in inp}
        g['ident'] = ident
        g['biasT'] = host_bias(inp['bias_table'], c == 1)
        v0 = np.zeros((128, NVEC), f32)
        v0[:, 28:36] = _pc(gi['norm_mix'][0], 8)
        v0[:, 40 + c] = 1.0
        g['v0'] = v0
        for i in range(4):
            v = tail_vecs_host(gi, i)
            v[:, 40 + c] = 1.0
            g['vecs%d' % i] = v
            g['wq%d' % i] = np.ascontiguousarray(sw(inp['xattn_w_q'][i], 0))
            g['wkv%d' % i] = np.ascontiguousarray(sw(inp['xattn_w_kv'][i], 0))
            g['wo%d' % i] = np.ascontiguousarray(sw(inp['xattn_w_o'][i], 1))
            g['w1_%d' % i] = np.ascontiguousarray(sw(inp['mlp_w1'][i], 0))
            g['w2_%d' % i] = np.ascontiguousarray(sw(inp['mlp_w2'][i], 1))
        for j in range(2):
            wg = sw(inp['s5_w_glu'][j], 0).reshape(D, 2, D)
            g['wglu%d' % j] = np.ascontiguousarray(sw(wg, 2).reshape(D, 2 * D))
            g['wqkv%d' % j] = np.ascontiguousarray(sw(inp['attn_w_qkv'][j], 0))
            g['woa%d' % j] = np.ascontiguousarray(sw(inp['attn_w_o'][j], 1))
            av = np.zeros((128, NVEC), f32)
            av[:, 28:36] = _pc(gi['norm_mix'][2 * j + 1], 8)
            av[:, 36] = inp['attn_q_gain'][j]
            av[:, 37] = inp['attn_k_gain'][j]
            av[:, 40 + c] = 1.0
            g['avecs%d' % j] = av
            for nm, arr in s5_host_inputs(inp, j, c).items():
                if nm != 'ident':
                    g[nm + '%d' % j] = arr
        per_c.append(g)
    maps = []
    for k in range(8):
        b, c = k // 2, k % 2
        m = dict(per_c[c])
        xb = inp['x'][b]
        if c == 0:
            m['xT'] = np.ascontiguousarray(xb[:NT].T)
            m['memT'] = np.ascontiguousarray(inp['mem'][b].T)
        else:
            m['xT'] = np.ascontiguousarray(_sw(xb[::-1][:NT], 1).T)
            m['memT'] = np.ascontiguousarray(_sw(inp['mem'][b], 1).T)
        maps.append({kk: m[kk] for kk in cxf.used})
    res = run_bass_kernel_spmd(nc, maps, core_ids=list(range(8))).results
    out = np.empty((BATCH, SEQ, D), f32)
    for k in range(8):
        b, c = k // 2, k % 2
        o = np.asarray(res[k]['outT'], f32).T
        if c == 0:
            out[b, :NT] = o
        else:
            out[b, NT:] = _sw(o, 1)[::-1]
    return out
```

```python
import math
import numpy as np
from contextlib import ExitStack
import concourse.bass as bass
import concourse.mybir as mybir
from concourse.bass_utils import run_bass_kernel_spmd

F32 = mybir.dt.float32
BF16 = mybir.dt.bfloat16
AF = mybir.ActivationFunctionType
ALU = mybir.AluOpType

D = 1024
NCH = 8
SEQ = 4096
BATCH = 4
NT = 2048
EPS = 1e-6
MEMLEN = 256
DFF = 4096


class Prog:
    ENGS = ('pe', 'act', 'dve', 'pool', 'sp')
    BLK = {'pe': 'tensor', 'act': 'scalar', 'dve': 'vector', 'pool': 'gpsimd', 'sp': 'sync'}

    def __init__(self, nc):
        self.nc = nc
        self.es = ExitStack()
        self.ins = {e: [] for e in self.ENGS}
        self.last_w = {}
        self.readers = {}
        self.dma_cnt = {}
        self.log = None
        self.bar_deps = {}

    ARENA_F32 = 51712

    def use_arena(self):
        self.arena = self.es.enter_context(self.nc.sbuf_tensor('arena', [128, self.ARENA_F32], F32))
        self.aoff = 0

    def sb(self, name, shape, dt):
        if getattr(self, 'arena', None) is None:
            return self.es.enter_context(self.nc.sbuf_tensor('s_' + name, list(shape), dt))
        assert shape[0] == 128, shape
        nel = 1
        for d_ in shape[1:]:
            nel *= d_
        isz = 4 if dt == F32 else 2
        nby = (nel * isz + 63) // 64 * 64
        o4 = self.aoff // 4
        self.aoff += nby
        assert self.aoff <= self.ARENA_F32 * 4, ('arena overflow', name, self.aoff)
        v = self.arena[:, o4:o4 + nby // 4]
        if dt != F32:
            v = v.bitcast(dt)
        v = v[:, :nel]
        if len(shape) == 3:
            v = v.rearrange("p (a b) -> p a b", a=shape[1])
        elif len(shape) != 2:
            raise AssertionError(shape)
        return v

    def barrier(self):
        deps = [('d', k, c) for k, c in self.dma_cnt.items()]
        for e in self.ENGS:
            n = len(self.ins[e])
            j = n - 1
            while j >= 0 and self.ins[e][j]['dma'] is not None:
                j -= 1
            if j >= 0:
                deps.append(('e', e, j))
        self.bar_deps = {e: list(deps) for e in self.ENGS}
        self.last_w.clear()
        self.readers.clear()

    def ps(self, name, shape, dt=F32):
        return self.es.enter_context(self.nc.psum_tensor(name, list(shape), dt))

    def _deps(self, r, w):
        deps = []
        for k in r:
            t = self.last_w.get(k)
            if t is not None:
                deps.append(t)
        for k in w:
            t = self.last_w.get(k)
            if t is not None:
                deps.append(t)
            deps.extend(self.readers.get(k, ()))
        return deps

    def _commit(self, tok, r, w):
        for k in r:
            lst = self.readers.setdefault(k, [])
            lst[:] = [t for t in lst if t[:2] != tok[:2]]
            lst.append(tok)
        for k in w:
            self.last_w[k] = tok
            self.readers[k] = []

    def op(self, eng, fn, r=(), w=()):
        idx = len(self.ins[eng])
        self.ins[eng].append(dict(fn=fn, deps=self._deps(r, w) + self.bar_deps.pop(eng, []), dma=None))
        self._commit(('e', eng, idx), r, w)

    def dma(self, eng, fn, r=(), w=(), semkey=None, inc=16):
        if semkey is None:
            semkey = w[0]
        c = self.dma_cnt.get(semkey, 0) + inc
        self.dma_cnt[semkey] = c
        self.ins[eng].append(dict(fn=fn, deps=self._deps(r, w) + self.bar_deps.pop(eng, []), dma=semkey, inc=inc))
        self._commit(('d', semkey, c), r, w)

    SAME_DIST = 4

    def _skip_same(self, e, i, d, rec):
        if d[1] != e or rec['dma'] is not None:
            return False
        if e == 'pe':
            return True
        return (i - d[2]) > self.SAME_DIST

    def finalize(self):
        nc = self.nc
        need = {e: set() for e in self.ENGS}
        for e in self.ENGS:
            for i, rec in enumerate(self.ins[e]):
                for d in rec['deps']:
                    if d[0] == 'e' and not self._skip_same(e, i, d, rec):
                        need[d[1]].add(d[2])
        cum = {}
        for e in self.ENGS:
            c = 0
            arr = []
            for i in range(len(self.ins[e])):
                if i in need[e]:
                    c += 1
                arr.append(c)
            cum[e] = arr
        esem = {e: self.es.enter_context(nc.semaphore('se_' + e)) for e in self.ENGS}
        dsem = {}
        for i, k in enumerate(self.dma_cnt):
            dsem[k] = self.es.enter_context(nc.semaphore('sd_%d' % i))
        self.stats = {e: (len(self.ins[e]), cum[e][-1] if cum[e] else 0) for e in self.ENGS}
        self.stats['ndsem'] = len(dsem)
        with nc.Block() as block:
            for e in self.ENGS:
                def body(eng, e=e):
                    waited = {}
                    for i, rec in enumerate(self.ins[e]):
                        req = {}
                        for d in rec['deps']:
                            if d[0] == 'e':
                                if self._skip_same(e, i, d, rec):
                                    continue
                                key = ('e', d[1])
                                val = cum[d[1]][d[2]]
                            else:
                                key = ('d', d[1])
                                val = d[2]
                            if val > req.get(key, 0):
                                req[key] = val
                        for key, val in req.items():
                            if waited.get(key, 0) < val:
                                sem = esem[key[1]] if key[0] == 'e' else dsem[key[1]]
                                eng.wait_ge(sem, val)
                                waited[key] = val
                                if self.log is not None:
                                    self.log.append((e, i, 'wait', key, val))
                        if self.log is not None:
                            self.log.append((e, i, 'inst', rec['dma'], cum[e][i] if i in need[e] else None))
                        inst = rec['fn'](eng)
                        if rec['dma'] is not None:
                            inst.then_inc(dsem[rec['dma']], rec.get('inc', 16))
                        elif i in need[e]:
                            inst.then_inc(esem[e], 1)
                    if e == 'sp':
                        for k, c in self.dma_cnt.items():
                            eng.wait_ge(dsem[k], c)
                getattr(block, self.BLK[e])(body)
        self.es.close()


def I_mm(out, lhsT, rhs, start, stop):
    return lambda e: e.matmul(out, lhsT, rhs, start=start, stop=stop)


def I_act(out, in_, func, **kw):
    return lambda e: e.activation(out=out, in_=in_, func=func, **kw)


def I_tt(out, in0, in1, op):
    return lambda e: e.tensor_tensor(out=out, in0=in0, in1=in1, op=op)


def I_ts(out, in0, s1, s2, op0, op1=None):
    if op1 is None:
        return lambda e: e.tensor_scalar(out=out, in0=in0, scalar1=s1, scalar2=None, op0=op0)
    return lambda e: e.tensor_scalar(out=out, in0=in0, scalar1=s1, scalar2=s2, op0=op0, op1=op1)


def I_stt(out, in0, scalar, in1, op0, op1):
    return lambda e: e.scalar_tensor_tensor(out=out, in0=in0, scalar=scalar, in1=in1, op0=op0, op1=op1)


def I_recip(out, in_):
    return lambda e: e.reciprocal(out=out, in_=in_)


def I_copy(out, in_):
    return lambda e: e.tensor_copy(out=out, in_=in_)


def I_memset(ap, c):
    return lambda e: e.memset(ap, c)


def I_dma(out, in_):
    return lambda e: e.dma_start(out=out, in_=in_)


def I_scan(out, d0, d1, init):
    return lambda e: e.tensor_tensor_scan(out=out, data0=d0, data1=d1, initial=init, op0=ALU.mult, op1=ALU.add)


def mcombine(cx, out, okey, X, xk, Z, zk, ma, mb, n=512):
    p = cx.p
    tmp, tk = cx.rot('mctmp', [128, 512], F32, n=2)
    p.op('act', I_act(tmp[:, :n], X, AF.Copy, scale=ma), r=[xk, 'vec'], w=[tk])
    p.op('dve', I_stt(out, Z, mb, tmp[:, :n], ALU.mult, ALU.add), r=[zk, 'vec', tk], w=[okey])


class Cx:
    def __init__(self, nc, arena=False):
        self.nc = nc
        self.p = Prog(nc)
        if arena:
            self.p.use_arena()
        self.banks = [self.p.ps('bank%d' % i, [128, 512]) for i in range(8)]
        self.bi = 0
        p = self.p
        self.ones = p.sb('ones', [128, 128], BF16)
        p.op('pool', I_memset(self.ones[:], 1.0), w=['ones'])
        self.sq = [p.sb('sq%d' % i, [128, 512], BF16) for i in range(2)]
        self.sqi = 0
        self.rstd = [p.sb('rstd%d' % i, [128, 512], F32) for i in range(2)]
        self.rsi = 0
        self._rot = {}
        self.epsc = p.sb('epsc', [128, 1], F32)
        p.op('pool', I_memset(self.epsc[:], EPS), w=['epsc'])
        self.mark = getattr(p, 'aoff', 0)

    def new_stage(self):
        self.p.barrier()
        self.p.aoff = self.mark
        self._rot = {}
        for nm in ('ws', 'wsi'):
            if hasattr(self, nm):
                delattr(self, nm)

    def bank(self):
        i = self.bi
        self.bi = (i + 1) % 8
        return self.banks[i], 'bank%d' % i

    def rot(self, name, shape, dt, n=2):
        if name not in self._rot:
            self._rot[name] = [[self.p.sb('%s_%d' % (name, i), shape, dt) for i in range(n)], 0]
        tl, i = self._rot[name]
        self._rot[name][1] = (i + 1) % n
        return tl[i], '%s_%d' % (name, i)

    def next_sq(self):
        i = self.sqi
        self.sqi = 1 - i
        return self.sq[i], 'sq%d' % i

    def next_rstd(self):
        i = self.rsi
        self.rsi = 1 - i
        return self.rstd[i], 'rstd%d' % i

    def rstd_from_bank(self, bank, bk, n, dim):
        p = self.p
        rs, rk = self.next_rstd()
        p.op('act', I_act(rs[:, :n], bank[:, :n], AF.Ln, scale=1.0 / dim, bias=self.epsc[:, 0:1]), r=[bk, 'epsc'], w=[rk])
        p.op('act', I_act(rs[:, :n], rs[:, :n], AF.Exp, scale=-0.5), r=[rk], w=[rk])
        return rs, rk


def rmsnorm(cx, src, skey, gcols, gkey, dst, dkey, C, n0, n, dim):
    p = cx.p
    bank, bk = cx.bank()
    for c in range(C):
        sq, sk = cx.next_sq()
        p.op('act', I_act(sq[:, :n], src[:, c, n0:n0 + n], AF.Square), r=[skey(c)], w=[sk])
        p.op('pe', I_mm(bank[:, :n], cx.ones[:], sq[:, :n], c == 0, c == C - 1), r=[sk, 'ones'], w=[bk])
    rs, rk = cx.rstd_from_bank(bank, bk, n, dim)
    for c in range(C):
        p.op('dve', I_stt(dst[:, c, n0:n0 + n], src[:, c, n0:n0 + n], gcols[:, c:c + 1], rs[:, :n], ALU.mult, ALU.mult),
             r=[skey(c), gkey, rk], w=[dkey(c)])


WS_N = 3


def wslab(cx, parts):
    p = cx.p
    nws = getattr(cx, 'ws_n', WS_N)
    if not hasattr(cx, 'ws'):
        cx.ws = [p.sb('ws%d' % i, [128, 4096], BF16) for i in range(nws)]
        cx.wsi = 0
    i = cx.wsi
    cx.wsi = (i + 1) % nws
    t = cx.ws[i]
    key = 'ws%d' % i
    for src, off in parts:
        K, N = src.shape
        kc = K // 128
        dst = t[:, off:off + kc * N].rearrange("p (k n) -> p k n", k=kc)
        p.dma('pool', I_dma(dst, src.rearrange("(k p) n -> p k n", p=128)), w=[key])
    return t, key


class SlabStream:
    def __init__(self, cx, specs):
        self.cx, self.specs, self.loaded, self.i = cx, specs, [], 0

    def get(self):
        while len(self.loaded) < min(len(self.specs), self.i + 2):
            self.loaded.append(wslab(self.cx, self.specs[len(self.loaded)]))
        r = self.loaded[self.i]
        self.i += 1
        return r


VC = dict(gx=0, gm=8, gl=16, gq=24, gk=26, gn=28, aq=36, ak=37, dsk=38, m0=40, m1=41)
NVEC = 48


def tail_body(cx, A, glu, tb0, ntb, kv_ready):
    p = cx.p
    hT = cx.hT
    hk = lambda c, tb: 'h%d_%d' % (c, tb)
    hn = cx.hn
    big2 = cx.big2
    vec = cx.vec
    NB = ntb
    specs = []
    if glu:
        for ns in range(2):
            specs.append([(A['wglu'][:, ns * 512:(ns + 1) * 512], 0)])
            specs.append([(A['wglu'][:, 1024 + ns * 512:1024 + (ns + 1) * 512], 0)])
    if not kv_ready:
        for hp in range(2):
            specs.append([(A['wkv'][:, hp * 512:(hp + 1) * 512], 0)])
        for vs in range(2):
            specs.append([(A['wkv'][:, 1024 + vs * 512:1024 + (vs + 1) * 512], 0)])
    for hp in range(2):
        specs.append([(A['wq'][:, hp * 512:(hp + 1) * 512], 0)])
    for ns in range(2):
        specs.append([(A['wo'][:, ns * 512:(ns + 1) * 512], 0)])
    for s in range(DFF // 256):
        specs.append([(A['w1'][:, s * 256:(s + 1) * 256], 0), (A['w2'][s * 256:(s + 1) * 256, :], 2048)])
    ss = SlabStream(cx, specs)

    if glu:
        for c in range(NCH):
            for tb in range(NB):
                yt, yk = cx.rot('ytmp', [128, 512], F32)
                col0 = (tb0 + tb) * 512
                if 'yO' in A and c >= 4:
                    X, xk = cx.rot('gx', [128, 512], F32, n=2)
                    Z, zk = cx.rot('gz', [128, 512], F32, n=2)
                    p.dma('sp', I_dma(X[:], A['GS'].g_rows(0, (c - 4) * 128)[:, col0:col0 + 512]), w=[xk])
                    p.dma('sp', I_dma(Z[:], A['GS'].g_rows(1, (c - 4) * 128)[:, col0:col0 + 512]), w=[zk])
                    mcombine(cx, yt[:], yk, X[:], xk, Z[:], zk, vec[:, VC['m1']:VC['m1'] + 1], vec[:, VC['m0']:VC['m0'] + 1])
                elif 'yO' in A:
                    p.dma('sp', I_dma(yt[:], A['yO'][c * 128:(c + 1) * 128, col0:col0 + 512]), w=[yk])
                else:
                    p.dma('sp', I_dma(yt[:], A['yT'][c * 128:(c + 1) * 128, col0:col0 + 512]), w=[yk])
                p.op('act', I_act(hn[:, c, tb * 512:(tb + 1) * 512], yt[:], AF.Gelu_apprx_tanh), r=[yk], w=['hn%d' % tb])
        for ns in range(2):
            wa, wak = ss.get()
            wb, wbk = ss.get()
            for j in range(4):
                n = ns * 4 + j
                for tb in range(NB):
                    ba, bak = cx.bank()
                    bb, bbk = cx.bank()
                    for kc in range(NCH):
                        p.op('pe', I_mm(ba[:], wa[:, kc * 512 + j * 128: kc * 512 + (j + 1) * 128],
                                        hn[:, kc, tb * 512:(tb + 1) * 512], kc == 0, kc == NCH - 1),
                             r=[wak, 'hn%d' % tb], w=[bak])
                    for kc in range(NCH):
                        p.op('pe', I_mm(bb[:], wb[:, kc * 512 + j * 128: kc * 512 + (j + 1) * 128],
                                        hn[:, kc, tb * 512:(tb + 1) * 512], kc == 0, kc == NCH - 1),
                             r=[wbk, 'hn%d' % tb], w=[bbk])
                    sg, sgk = cx.rot('sg', [128, 512], F32)
                    p.op('act', I_act(sg[:], bb[:], AF.Sigmoid), r=[bbk], w=[sgk])
                    gt, gtk = cx.rot('gtmp', [128, 512], F32)
                    p.op('dve', I_tt(gt[:], ba[:], sg[:], ALU.mult), r=[bak, sgk], w=[gtk])
                    hs = hT[:, n, (tb0 + tb) * 512:(tb0 + tb + 1) * 512]
                    p.op('pool', I_tt(hs, hs, gt[:], ALU.add), r=[gtk, hk(n, tb0 + tb)], w=[hk(n, tb0 + tb)])

    if not kv_ready:
        kraw = cx.kraw
        memn = cx.memn
        for c in range(NCH):
            p.dma('sp', I_dma(kraw[:, c, :], A['memT'][c * 128:(c + 1) * 128, :]), w=['kraw'], semkey='kraw_ld')
        rmsnorm(cx, kraw, lambda c: 'kraw', vec[:, VC['gm']:VC['gm'] + 8], 'vec', memn, lambda c: 'memn', NCH, 0, MEMLEN, D)
        for hp in range(2):
            wk, wkk = ss.get()
            for jj in range(4):
                j = hp * 4 + jj
                bk_, bkk = cx.bank()
                for kc in range(NCH):
                    p.op('pe', I_mm(bk_[:, :MEMLEN], wk[:, kc * 512 + jj * 128: kc * 512 + (jj + 1) * 128], memn[:, kc, :],
                                    kc == 0, kc == NCH - 1), r=[wkk, 'memn'], w=[bkk])
                p.op('act', I_act(kraw[:, j, :], bk_[:, :MEMLEN], AF.Copy), r=[bkk], w=['kraw'])
        for h in range(4):
            bs, bsk = cx.bank()
            for ec in range(2):
                sq, sk = cx.next_sq()
                p.op('act', I_act(sq[:, :MEMLEN], kraw[:, 2 * h + ec, :], AF.Square), r=['kraw'], w=[sk])
                p.op('pe', I_mm(bs[:, :MEMLEN], cx.ones[:], sq[:, :MEMLEN], ec == 0, ec == 1), r=[sk, 'ones'], w=[bsk])
            rs, rk = cx.rstd_from_bank(bs, bsk, MEMLEN, 256)
            for ec in range(2):
                p.op('dve', I_stt(cx.KT[:, 2 * h + ec, :], kraw[:, 2 * h + ec, :], vec[:, VC['gk'] + ec:VC['gk'] + ec + 1],
                                  rs[:, :MEMLEN], ALU.mult, ALU.mult), r=['kraw', 'vec', rk], w=['KT'])
        for vs in range(2):
            wv, wvk = ss.get()
            for mc in range(2):
                bv, bvk = cx.bank()
                for kc in range(NCH):
                    p.op('pe', I_mm(bv[:], memn[:, kc, mc * 128:(mc + 1) * 128], wv[:, kc * 512:(kc + 1) * 512],
                                    kc == 0, kc == NCH - 1), r=[wvk, 'memn'], w=[bvk])
                p.op('act', I_act(cx.V[:, mc, vs * 512:(vs + 1) * 512], bv[:], AF.Copy), r=[bvk], w=['V'])

    for tb in range(NB):
        _rmsnorm_off(cx, hT, (tb0 + tb) * 512, lambda c, tb=tb: hk(c, tb0 + tb), vec[:, VC['gx']:VC['gx'] + 8],
                     hn, tb * 512, 'hn%d' % tb)
    its = [(hp, hh, tb) for hp in range(2) for hh in range(2) for tb in range(NB)]
    BK = lambda i: (cx.banks[i], 'bank%d' % i)
    wq_cur = {}
    stx = {}

    def X_phase(i):
        hp, hh, tb = its[i]
        if hp not in wq_cur:
            wq_cur[hp] = ss.get()
        wq, wqk = wq_cur[hp]
        qb = []
        for ec in range(2):
            b, bk_ = BK((i % 2) * 2 + ec)
            cc = hh * 2 + ec
            for kc in range(NCH):
                p.op('pe', I_mm(b[:], wq[:, kc * 512 + cc * 128: kc * 512 + (cc + 1) * 128],
                                hn[:, kc, tb * 512:(tb + 1) * 512], kc == 0, kc == NCH - 1),
                     r=[wqk, 'hn%d' % tb], w=[bk_])
            qb.append((b, bk_))
        stx[i] = dict(qb=qb)

    def Y_phase(i):
        qb = stx[i]['qb']
        bs, bsk = BK(4)
        for ec in range(2):
            sq, sk = cx.next_sq()
            p.op('act', I_act(sq[:], qb[ec][0][:], AF.Square), r=[qb[ec][1]], w=[sk])
            p.op('pe', I_mm(bs[:], cx.ones[:], sq[:], ec == 0, ec == 1), r=[sk, 'ones'], w=[bsk])
        rs, rk = cx.rstd_from_bank(bs, bsk, 512, 256)
        qn, qnk = cx.rot('qn', [128, 2, 512], BF16)
        for ec in range(2):
            p.op('dve', I_stt(qn[:, ec, :], qb[ec][0][:], vec[:, VC['gq'] + ec:VC['gq'] + ec + 1], rs[:],
                              ALU.mult, ALU.mult), r=[qb[ec][1], 'vec', rk], w=[qnk])
        stx[i]['qn'] = (qn, qnk)

    def Z_phase(i):
        hp, hh, tb = its[i]
        h = 2 * hp + hh
        qn, qnk = stx[i]['qn']
        PT, ptk = cx.rot('PT', [128, 2, 512], BF16)
        for mc in range(2):
            bl, blk = BK(5 + mc)
            for ec in range(2):
                p.op('pe', I_mm(bl[:], cx.KT[:, 2 * h + ec, mc * 128:(mc + 1) * 128], qn[:, ec, :], ec == 0, ec == 1),
                     r=['KT', qnk], w=[blk])
            p.op('act', I_act(PT[:, mc, :], bl[:], AF.Exp, scale=1.0 / 16.0), r=[blk], w=[ptk])
        bd, bdk = BK(7)
        for mc in range(2):
            p.op('pe', I_mm(bd[:], cx.ones[:], PT[:, mc, :], mc == 0, mc == 1), r=['ones', ptk], w=[bdk])
        rd, rdk = cx.rot('rden', [128, 512], F32)
        p.op('act', I_act(rd[:], bd[:], AF.Ln), r=[bdk], w=[rdk])
        p.op('act', I_act(rd[:], rd[:], AF.Exp, scale=-1.0), r=[rdk], w=[rdk])
        for ec in range(2):
            bo, bok = BK(5 + ec)
            for mc in range(2):
                p.op('pe', I_mm(bo[:], cx.V[:, mc, h * 256 + ec * 128: h * 256 + (ec + 1) * 128], PT[:, mc, :],
                                mc == 0, mc == 1), r=['V', ptk], w=[bok])
            p.op('dve', I_tt(big2[:, 2 * h + ec, tb * 512:(tb + 1) * 512], bo[:], rd[:], ALU.mult),
                 r=[bok, rdk], w=['big2_%d' % tb])
        del stx[i]
    nit = len(its)
    for step in range(nit + 2):
        if step < nit:
            X_phase(step)
        if 0 <= step - 1 < nit:
            Y_phase(step - 1)
        if 0 <= step - 2 < nit:
            Z_phase(step - 2)
    for ns in range(2):
        wo, wok = ss.get()
        for j in range(4):
            n = ns * 4 + j
            for tb in range(NB):
                b, bk_ = cx.bank()
                for kc in range(NCH):
                    p.op('pe', I_mm(b[:], wo[:, kc * 512 + j * 128: kc * 512 + (j + 1) * 128],
                                    big2[:, kc, tb * 512:(tb + 1) * 512], kc == 0, kc == NCH - 1),
                         r=[wok, 'big2_%d' % tb], w=[bk_])
                hs = hT[:, n, (tb0 + tb) * 512:(tb0 + tb + 1) * 512]
                p.op('dve', I_tt(hs, b[:], hs, ALU.add), r=[bk_, hk(n, tb0 + tb)], w=[hk(n, tb0 + tb)])

    for tb in range(NB):
        _rmsnorm_off(cx, hT, (tb0 + tb) * 512, lambda c, tb=tb: hk(c, tb0 + tb), vec[:, VC['gl']:VC['gl'] + 8],
                     hn, tb * 512, 'hn%d' % tb)
    def mlp_w1(s, ws, wsk):
        hb = s % 2
        for j in range(2):
            for tb in range(NB):
                b, bk_ = cx.bank()
                for kc in range(NCH):
                    p.op('pe', I_mm(b[:], ws[:, kc * 256 + j * 128: kc * 256 + (j + 1) * 128],
                                    hn[:, kc, tb * 512:(tb + 1) * 512], kc == 0, kc == NCH - 1),
                         r=[wsk, 'hn%d' % tb], w=[bk_])
                rt, rtk = cx.rot('rtmp', [128, 512], F32)
                p.op('act', I_act(rt[:], b[:], AF.Relu), r=[bk_], w=[rtk])
                p.op('pool', I_tt(big2[:, hb * 2 + j, tb * 512:(tb + 1) * 512], rt[:], rt[:], ALU.mult),
                     r=[rtk], w=['hid%d' % hb])

    def mlp_w2(s, ws, wsk):
        hb = s % 2
        for n in range(NCH):
            for tb in range(NB):
                b, bk_ = cx.bank()
                for j in range(2):
                    p.op('pe', I_mm(b[:], ws[:, 2048 + j * 1024 + n * 128: 2048 + j * 1024 + (n + 1) * 128],
                                    big2[:, hb * 2 + j, tb * 512:(tb + 1) * 512], j == 0, j == 1),
                         r=[wsk, 'hid%d' % hb], w=[bk_])
                hs = hT[:, n, (tb0 + tb) * 512:(tb0 + tb + 1) * 512]
                p.op('dve', I_tt(hs, b[:], hs, ALU.add), r=[bk_, hk(n, tb0 + tb)], w=[hk(n, tb0 + tb)])
    nsl = DFF // 256
    prev = None
    for s in range(nsl):
        ws, wsk = ss.get()
        mlp_w1(s, ws, wsk)
        if prev is not None:
            mlp_w2(*prev)
        prev = (s, ws, wsk)
    mlp_w2(*prev)


def _rmsnorm_off(cx, src, s0, skey, gcols, dst, d0, dkey, n=512, C=NCH, dim=D):
    p = cx.p
    bank, bk = cx.bank()
    for c in range(C):
        sq, sk = cx.next_sq()
        p.op('act', I_act(sq[:, :n], src[:, c, s0:s0 + n], AF.Square), r=[skey(c)], w=[sk])
        p.op('pe', I_mm(bank[:, :n], cx.ones[:], sq[:, :n], c == 0, c == C - 1), r=[sk, 'ones'], w=[bk])
    rs, rk = cx.rstd_from_bank(bank, bk, n, dim)
    for c in range(C):
        p.op('dve', I_stt(dst[:, c, d0:d0 + n], src[:, c, s0:s0 + n], gcols[:, c:c + 1], rs[:, :n], ALU.mult, ALU.mult),
             r=[skey(c), 'vec', rk], w=[dkey])


def common_tiles(cx, A):
    p = cx.p
    cx.hT = p.sb('hT', [128, NCH, NT], F32)
    cx.hn = p.sb('hn', [128, NCH, 1024], BF16)
    cx.big2 = p.sb('big2', [128, NCH, 1024], BF16)
    cx.vec = p.sb('vec', [128, NVEC], F32)
    cx.kraw = p.sb('kraw', [128, NCH, MEMLEN], F32)
    cx.memn = p.sb('memn', [128, NCH, MEMLEN], BF16)
    cx.KT = p.sb('KT', [128, NCH, MEMLEN], BF16)
    cx.V = p.sb('V', [128, 2, D], BF16)
    p.dma('sp', I_dma(cx.vec[:], A['vecs'][:, :]), w=['vec'])


def load_hT(cx, src):
    p = cx.p
    for c in range(NCH):
        p.dma('sp', I_dma(cx.hT[:, c, 0:512], src[c * 128:(c + 1) * 128, 0:512]), w=['h%d_0' % c], semkey='hld%d' % c)
    for c in range(NCH):
        p.dma('sp', I_dma(cx.hT[:, c, 512:NT], src[c * 128:(c + 1) * 128, 512:NT]),
              w=['h%d_%d' % (c, tb) for tb in range(1, NT // 512)], semkey='hldb%d' % c)


def store_hT(cx, dst):
    p = cx.p
    for c in range(NCH):
        p.dma('sp', I_dma(dst[c * 128:(c + 1) * 128, :], cx.hT[:, c, :]),
              r=['h%d_%d' % (c, tb) for tb in range(NT // 512)], w=['hout%d' % c])


def build_tail(glu, emit_hn, arena=False):
    nc = bass.Bass("TRN2", target_bir_lowering=False)
    A = {}

    def inp(name, shape, dt=F32):
        A[name] = nc.dram_tensor(name, list(shape), dt, kind="ExternalInput").ap()

    inp('hT', [D, NT])
    inp('memT', [D, MEMLEN])
    inp('vecs', [128, NVEC])
    inp('wq', [D, D])
    inp('wkv', [D, 2 * D])
    inp('wo', [D, D])
    inp('w1', [D, DFF])
    inp('w2', [DFF, D])
    if glu:
        inp('yT', [D, NT])
        inp('wglu', [D, 2 * D])
    A['hT_out'] = nc.dram_tensor('hT_out', [D, NT], F32, kind="ExternalOutput").ap()
    if emit_hn:
        A['hn_out'] = nc.dram_tensor('hn_out', [D, NT], F32, kind="ExternalOutput").ap()
    cx = Cx(nc, arena=arena)
    if arena:
        cx.new_stage()
    common_tiles(cx, A)
    load_hT(cx, A['hT'])
    for half in range(2):
        tail_body(cx, A, glu, half * 2, 2, kv_ready=(half == 1))
    store_hT(cx, A['hT_out'])
    if emit_hn:
        emit_norm(cx, A['hn_out'])
    cx.p.finalize()
    return nc, cx


def emit_norm(cx, dst):
    p = cx.p
    for tb in range(NT // 512):
        bank, bk = cx.bank()
        for c in range(NCH):
            sq, sk = cx.next_sq()
            p.op('act', I_act(sq[:], cx.hT[:, c, tb * 512:(tb + 1) * 512], AF.Square), r=['h%d_%d' % (c, tb)], w=[sk])
            p.op('pe', I_mm(bank[:], cx.ones[:], sq[:], c == 0, c == NCH - 1), r=[sk, 'ones'], w=[bk])
        rs, rk = cx.rstd_from_bank(bank, bk, 512, D)
        for c in range(NCH):
            ot, otk = cx.rot('ntmp', [128, 512], F32, n=3)
            p.op('dve', I_stt(ot[:], cx.hT[:, c, tb * 512:(tb + 1) * 512], cx.vec[:, VC['gn'] + c:VC['gn'] + c + 1], rs[:],
                              ALU.mult, ALU.mult), r=['h%d_%d' % (c, tb), 'vec', rk], w=[otk])
            drow = dst.src_rows(c * 128) if hasattr(dst, 'src_rows') else dst[c * 128:(c + 1) * 128, :]
            p.dma('sp', I_dma(drow[:, tb * 512:(tb + 1) * 512], ot[:]), r=[otk], w=['hnout'])


NPT = 16
SW = 512
NW = SEQ // SW
PI = math.pi


def s5_params(cx, A):
    p = cx.p
    NCOL = 2 * NPT
    T = {}
    for nm in ['lre', 'lim', 'ldt', 'dt', 'mag', 'ang', 'angc', 's1', 'c1', 'are', 'aim', 'nr', 'den', 't', 't2',
               'fre', 'fim', 'nfre', 'nfim']:
        T[nm] = p.sb('sp_' + nm, [128, NCOL], F32)
    k = 's5par'
    p.dma('sp', I_dma(T['lre'][:], A['lamre'][:, :]), w=[k], semkey='s5par_ld')
    p.dma('sp', I_dma(T['lim'][:], A['lamim'][:, :]), w=[k], semkey='s5par_ld')
    p.dma('sp', I_dma(T['ldt'][:], A['logdt'][:, :]), w=[k], semkey='s5par_ld')
    a = lambda n: T[n][:]
    p.op('act', I_act(a('dt'), a('ldt'), AF.Exp), r=[k], w=[k])
    p.op('dve', I_tt(a('t'), a('lre'), a('dt'), ALU.mult), r=[k], w=[k])
    p.op('act', I_act(a('mag'), a('t'), AF.Exp), r=[k], w=[k])
    p.op('dve', I_tt(a('ang'), a('lim'), a('dt'), ALU.mult), r=[k], w=[k])
    for _ in range(5):
        p.op('dve', I_ts(a('t'), a('ang'), PI, 2 * PI, ALU.is_gt, ALU.mult), r=[k], w=[k])
        p.op('dve', I_tt(a('ang'), a('ang'), a('t'), ALU.subtract), r=[k], w=[k])
    p.op('dve', I_ts(a('angc'), a('ang'), PI / 2, None, ALU.add), r=[k], w=[k])
    p.op('dve', I_ts(a('t'), a('angc'), PI, 2 * PI, ALU.is_gt, ALU.mult), r=[k], w=[k])
    p.op('dve', I_tt(a('angc'), a('angc'), a('t'), ALU.subtract), r=[k], w=[k])
    p.op('act', I_act(a('s1'), a('ang'), AF.Sin), r=[k], w=[k])
    p.op('act', I_act(a('c1'), a('angc'), AF.Sin), r=[k], w=[k])
    p.op('dve', I_tt(a('are'), a('mag'), a('c1'), ALU.mult), r=[k], w=[k])
    p.op('dve', I_tt(a('aim'), a('mag'), a('s1'), ALU.mult), r=[k], w=[k])
    p.op('dve', I_ts(a('nr'), a('are'), -1.0, None, ALU.add), r=[k], w=[k])
    p.op('dve', I_tt(a('den'), a('lre'), a('lre'), ALU.mult), r=[k], w=[k])
    p.op('dve', I_tt(a('t'), a('lim'), a('lim'), ALU.mult), r=[k], w=[k])
    p.op('dve', I_tt(a('den'), a('den'), a('t'), ALU.add), r=[k], w=[k])
    p.op('dve', I_recip(a('den'), a('den')), r=[k], w=[k])
    p.op('dve', I_tt(a('t'), a('nr'), a('lre'), ALU.mult), r=[k], w=[k])
    p.op('dve', I_tt(a('t2'), a('aim'), a('lim'), ALU.mult), r=[k], w=[k])
    p.op('dve', I_tt(a('t'), a('t'), a('t2'), ALU.add), r=[k], w=[k])
    p.op('dve', I_tt(a('fre'), a('t'), a('den'), ALU.mult), r=[k], w=[k])
    p.op('dve', I_tt(a('t'), a('aim'), a('lre'), ALU.mult), r=[k], w=[k])
    p.op('dve', I_tt(a('t2'), a('nr'), a('lim'), ALU.mult), r=[k], w=[k])
    p.op('dve', I_tt(a('t'), a('t'), a('t2'), ALU.subtract), r=[k], w=[k])
    p.op('dve', I_tt(a('fim'), a('t'), a('den'), ALU.mult), r=[k], w=[k])
    p.op('dve', I_ts(a('nfre'), a('fre'), -1.0, None, ALU.mult), r=[k], w=[k])
    p.op('dve', I_ts(a('nfim'), a('fim'), -1.0, None, ALU.mult), r=[k], w=[k])
    nlv = int(math.log2(SW))
    T['pwc'] = p.sb('sp_pwc', [128, nlv + 1, NCOL], F32)
    T['pws'] = p.sb('sp_pws', [128, nlv + 1, NCOL], F32)
    T['npws'] = p.sb('sp_npws', [128, NCOL], F32)
    p.op('dve', I_copy(T['pwc'][:, 0, :], a('c1')), r=[k], w=[k])
    p.op('dve', I_copy(T['pws'][:, 0, :], a('s1')), r=[k], w=[k])
    for lv in range(nlv):
        c_ = T['pwc'][:, lv, :]
        s_ = T['pws'][:, lv, :]
        p.op('dve', I_tt(a('t'), s_, s_, ALU.mult), r=[k], w=[k])
        p.op('dve', I_tt(a('t2'), c_, c_, ALU.mult), r=[k], w=[k])
        p.op('dve', I_tt(T['pwc'][:, lv + 1, :], a('t2'), a('t'), ALU.subtract), r=[k], w=[k])
        p.op('dve', I_stt(T['pws'][:, lv + 1, :], c_, 2.0, s_, ALU.mult, ALU.mult), r=[k], w=[k])
    p.op('dve', I_ts(T['npws'][:], T['pws'][:, nlv, :], -1.0, None, ALU.mult), r=[k], w=[k])
    return T


def s5_body(cx, A):
    p = cx.p
    T = s5_params(cx, A)
    PK = 's5par'
    if 'dbg' in A:
        for i, nm in enumerate(['dt', 'mag', 'ang', 's1', 'c1', 'fre', 'fim', 'den']):
            p.dma('sp', I_dma(A['dbg'][:, i * 2 * NPT:(i + 1) * 2 * NPT], T[nm][:]), r=[PK], w=['dbgo'])
    ub = p.sb('ub', [128, 4, SEQ], BF16)
    if 'GHN' in A:
        G = A['GHN']
        m0c = cx.vec[:, VC['m0']:VC['m0'] + 1]
        m1c = cx.vec[:, VC['m1']:VC['m1'] + 1]
        for ck in range(4):
            for w in range(NW):
                r = 0 if w < NW // 2 else 1
                if r == 0:
                    cols = slice(w * SW, (w + 1) * SW)
                else:
                    w2 = w - NW // 2
                    cols = slice(NT - (w2 + 1) * SW, NT - w2 * SW)
                X, xk = cx.rot('gx', [128, SW], F32, n=2)
                Z, zk = cx.rot('gz', [128, SW], F32, n=2)
                p.dma('sp', I_dma(X[:], G.g_rows(r, ck * 128)[:, cols]), w=[xk])
                p.dma('sp', I_dma(Z[:], G.g_rows(r, 512 + ck * 128)[:, cols]), w=[zk])
                dst = ub[:, ck, w * SW:(w + 1) * SW]
                if r == 1:
                    dst = dst[:, ::-1]
                mcombine(cx, dst, 'ub%d' % ck, X[:], xk, Z[:], zk, m0c if r == 0 else m1c, m1c if r == 0 else m0c)
    else:
        for ck in range(4):
            p.dma('pool', I_dma(ub[:, ck, :], A['uT'][ck * 128:(ck + 1) * 128, :]), w=['ub%d' % ck])
    ident = p.sb('ident', [128, 128], F32)
    p.dma('sp', I_dma(ident[:], A['ident'][:, :]), w=['ident'])
    dsk = p.sb('dskc', [128, 4], F32)
    p.dma('sp', I_dma(dsk[:], A['dsk'][:, :]), w=['dskc'])
    yacc = [p.sb('yacc%d' % i, [128, SEQ], F32) for i in range(2)]
    bb_i = [0]

    def bbank():
        i = bb_i[0]
        bb_i[0] = (i + 1) % 6
        return cx.banks[i], 'bank%d' % i
    yb_i = [0]

    def ybank():
        i = 6 + yb_i[0]
        yb_i[0] = 1 - yb_i[0]
        return cx.banks[i], 'bank%d' % i

    for ck in range(4):
        ya = yacc[ck % 2]
        yk = 'yacc%d' % (ck % 2)
        dD, dDk = cx.rot('diagD', [128, 128], BF16)
        p.op('dve', I_ts(dD[:], ident[:], dsk[:, ck:ck + 1], None, ALU.mult), r=['ident', 'dskc'], w=[dDk])
        for d in range(2):
            tabs = []
            col0 = d * NPT + ck * 4
            cos4, c4k = cx.rot('cos4', [128, 4, SW], F32, n=2)
            sin4, s4k = cx.rot('sin4', [128, 4, SW], F32, n=2)
            tk = c4k
            p.op('dve', I_memset(cos4[:, :, 0:1], 1.0), w=[tk])
            p.op('dve', I_memset(sin4[:, :, 0:1], 0.0), w=[tk])
            L = 1
            lv = 0
            while L < SW:
                pcb = T['pwc'][:, lv, col0:col0 + 4].unsqueeze(2).to_broadcast([128, 4, L])
                psb = T['pws'][:, lv, col0:col0 + 4].unsqueeze(2).to_broadcast([128, 4, L])
                ta, tak = cx.rot('tbA', [128, 4, SW // 2], F32, n=1)
                tb2, tbk = cx.rot('tbB', [128, 4, SW // 2], F32, n=1)
                p.op('dve', I_tt(ta[:, :, :L], sin4[:, :, 0:L], psb, ALU.mult), r=[tk, PK], w=[tak])
                p.op('dve', I_tt(tb2[:, :, :L], cos4[:, :, 0:L], pcb, ALU.mult), r=[tk, PK], w=[tbk])
                p.op('dve', I_tt(cos4[:, :, L:2 * L], tb2[:, :, :L], ta[:, :, :L], ALU.subtract), r=[tak, tbk], w=[tk])
                p.op('dve', I_tt(ta[:, :, :L], cos4[:, :, 0:L], psb, ALU.mult), r=[tk, PK], w=[tak])
                p.op('dve', I_tt(tb2[:, :, :L], sin4[:, :, 0:L], pcb, ALU.mult), r=[tk, PK], w=[tbk])
                p.op('dve', I_tt(sin4[:, :, L:2 * L], tb2[:, :, :L], ta[:, :, :L], ALU.add), r=[tak, tbk], w=[tk])
                L *= 2
                lv += 1
            for q in range(4):
                pt = ck * 4 + q
                col = d * NPT + pt
                cosT = cos4[:, q, :]
                sinT = sin4[:, q, :]
                cW = T['pwc'][:, lv, col:col + 1]
                sW = T['pws'][:, lv, col:col + 1]
                nsW = T['npws'][:, col:col + 1]
                braw, brk = cx.rot('bw', [128, 2, 128], BF16, n=8)
                p.dma('pool', I_dma(braw[:, 0, :], A['Bre'][d, pt]), w=[brk])
                p.dma('pool', I_dma(braw[:, 1, :], A['Bim'][d, pt]), w=[brk])
                craw, crk = cx.rot('craw', [128, 2, 128], F32, n=2)
                p.dma('sp', I_dma(craw[:, 0, :], A['CR'][d, pt]), w=[crk])
                p.dma('sp', I_dma(craw[:, 1, :], A['CI'][d, pt]), w=[crk])
                cw, cwk = cx.rot('cw', [128, 3, 128], BF16, n=8)
                ctmp, ctk = cx.rot('ctmp', [128, 128], F32, n=2)
                fre = T['fre'][:, col:col + 1]
                nfim = T['nfim'][:, col:col + 1]
                nfre = T['nfre'][:, col:col + 1]
                p.op('dve', I_ts(ctmp[:], craw[:, 1, :], nfim, None, ALU.mult), r=[crk, PK], w=[ctk])
                p.op('dve', I_stt(cw[:, 0, :], craw[:, 0, :], fre, ctmp[:], ALU.mult, ALU.add), r=[crk, PK, ctk], w=[cwk])
                ctmp2, ctk2 = cx.rot('ctmp', [128, 128], F32, n=2)
                p.op('dve', I_ts(ctmp2[:], craw[:, 0, :], nfim, None, ALU.mult), r=[crk, PK], w=[ctk2])
                p.op('dve', I_stt(cw[:, 1, :], craw[:, 1, :], nfre, ctmp2[:], ALU.mult, ALU.add), r=[crk, PK, ctk2], w=[cwk])
                ctmp3, ctk3 = cx.rot('ctmp', [128, 128], F32, n=2)
                p.op('dve', I_ts(ctmp3[:], craw[:, 1, :], T['fim'][:, col:col + 1], None, ALU.mult), r=[crk, PK], w=[ctk3])
                p.op('dve', I_stt(cw[:, 2, :], craw[:, 0, :], nfre, ctmp3[:], ALU.mult, ALU.add), r=[crk, PK, ctk3], w=[cwk])
                car, cak = cx.rot('carry', [128, 8], F32, n=8)
                tabs.append(dict(cos=cosT, sin=sinT, tk=tk, cW=cW, sW=sW, nsW=nsW, braw=braw, brk=brk, cw=cw, cwk=cwk,
                                 r=T['mag'][:, col:col + 1], car=car, cak=cak))
            worder = range(NW) if d == 0 else range(NW - 1, -1, -1)
            rv = (lambda ap: ap) if d == 0 else (lambda ap: ap[:, ::-1])
            units = [dict(wi=wi, w=w, q=q) for wi, w in enumerate(worder) for q in range(4)]
            ybs = {}

            def P01(u):
                tb_ = tabs[u['q']]
                win = slice(u['w'] * SW, (u['w'] + 1) * SW)
                bre, brek = bbank()
                bim, bimk = bbank()
                p.op('pe', I_mm(bre[:], tb_['braw'][:, 0, :], ub[:, ck, win], True, True), r=[tb_['brk'], 'ub%d' % ck], w=[brek])
                p.op('pe', I_mm(bim[:], tb_['braw'][:, 1, :], ub[:, ck, win], True, True), r=[tb_['brk'], 'ub%d' % ck], w=[bimk])
                cosT, sinT, tk = tb_['cos'], tb_['sin'], tb_['tk']
                t1, t1k = cx.rot('t1', [128, SW], F32)
                t2, t2k = cx.rot('t2', [128, SW], F32)
                t3, t3k = cx.rot('t3', [128, SW], F32)
                t4, t4k = cx.rot('t4', [128, SW], F32)
                p.op('dve', I_tt(t1[:], rv(bre[:]), cosT, ALU.mult), r=[brek, tk], w=[t1k])
                p.op('dve', I_tt(t2[:], rv(bim[:]), sinT, ALU.mult), r=[bimk, tk], w=[t2k])
                p.op('dve', I_tt(t3[:], rv(bim[:]), cosT, ALU.mult), r=[bimk, tk], w=[t3k])
                p.op('dve', I_tt(t4[:], rv(bre[:]), sinT, ALU.mult), r=[brek, tk], w=[t4k])
                u.update(t=(t1, t1k, t2, t2k, t3, t3k, t4, t4k))

            def P2(u):
                t1, t1k, t2, t2k, t3, t3k, t4, t4k = u['t']
                wre, wrk = cx.rot('wre', [128, SW], F32)
                wim, wik = cx.rot('wim', [128, SW], F32)
                p.op('pool', I_tt(wre[:], t1[:], t2[:], ALU.add), r=[t1k, t2k], w=[wrk])
                p.op('pool', I_tt(wim[:], t3[:], t4[:], ALU.subtract), r=[t3k, t4k], w=[wik])
                u.update(wv=(wre, wrk, wim, wik))

            def P3(u):
                tb_ = tabs[u['q']]
                tk = tb_['tk']
                wre, wrk, wim, wik = u['wv']
                zre, zrk = cx.rot('zre', [128, SW], F32)
                zim, zik = cx.rot('zim', [128, SW], F32)
                car, cak = tb_['car'], tb_['cak']
                rbc = tb_['r'].to_broadcast([128, SW])
                if u['wi'] == 0:
                    ire, iim = 0.0, 0.0
                else:
                    ire, iim = car[:, 2:3], car[:, 3:4]
                p.op('dve', I_scan(zre[:], rbc, wre[:], ire), r=[PK, wrk, cak], w=[zrk])
                p.op('dve', I_scan(zim[:], rbc, wim[:], iim), r=[PK, wik, cak], w=[zik])
                if u['wi'] < NW - 1:
                    p.op('act', I_act(car[:, 0:1], zim[:, SW - 1:SW], AF.Copy, scale=tb_['nsW']), r=[zik, PK], w=[cak])
                    p.op('act', I_act(car[:, 1:2], zre[:, SW - 1:SW], AF.Copy, scale=tb_['sW']), r=[zrk, PK], w=[cak])
                    p.op('act', I_act(car[:, 2:3], zre[:, SW - 1:SW], AF.Identity, scale=tb_['cW'], bias=car[:, 0:1]), r=[zrk, PK], w=[cak])
                    p.op('act', I_act(car[:, 3:4], zim[:, SW - 1:SW], AF.Identity, scale=tb_['cW'], bias=car[:, 1:2]), r=[zik, PK], w=[cak])
                u.update(z=(zre, zrk, zim, zik))

            def P4(u):
                tb_ = tabs[u['q']]
                cosT, sinT, tk = tb_['cos'], tb_['sin'], tb_['tk']
                zre, zrk, zim, zik = u['z']
                u1, u1k = cx.rot('u1', [128, SW], BF16, n=3)
                u2, u2k = cx.rot('u2', [128, SW], BF16, n=3)
                u3, u3k = cx.rot('u3', [128, SW], BF16, n=3)
                u4, u4k = cx.rot('u4', [128, SW], BF16, n=3)
                p.op('pool', I_tt(rv(u1[:]), zre[:], cosT, ALU.mult), r=[zrk, tk], w=[u1k])
                p.op('pool', I_tt(rv(u2[:]), zim[:], sinT, ALU.mult), r=[zik, tk], w=[u2k])
                p.op('pool', I_tt(rv(u3[:]), zim[:], cosT, ALU.mult), r=[zik, tk], w=[u3k])
                p.op('dve', I_tt(rv(u4[:]), zre[:], sinT, ALU.mult), r=[zrk, tk], w=[u4k])
                u.update(uu=(u1, u1k, u2, u2k, u3, u3k, u4, u4k))

            def P5(u):
                pass

            def P6(u):
                tb_ = tabs[u['q']]
                w, q = u['w'], u['q']
                win = slice(w * SW, (w + 1) * SW)
                if q == 0:
                    ybs[w] = ybank()
                yb, ybk = ybs[w]
                u1, u1k, u2, u2k, u3, u3k, u4, u4k = u['uu']
                first = (q == 0)
                last = (q == 3) and d == 1
                p.op('pe', I_mm(yb[:], tb_['cw'][:, 0, :], u1[:], first, False), r=[tb_['cwk'], u1k], w=[ybk])
                p.op('pe', I_mm(yb[:], tb_['cw'][:, 2, :], u2[:], False, False), r=[tb_['cwk'], u2k], w=[ybk])
                p.op('pe', I_mm(yb[:], tb_['cw'][:, 1, :], u3[:], False, False), r=[tb_['cwk'], u3k], w=[ybk])
                p.op('pe', I_mm(yb[:], tb_['cw'][:, 1, :], u4[:], False, last), r=[tb_['cwk'], u4k], w=[ybk])
                if q == 3:
                    if d == 0:
                        p.op('pe', I_mm(yb[:], dD[:], ub[:, ck, win], False, True), r=[dDk, 'ub%d' % ck], w=[ybk])
                        p.op('act', I_act(ya[:, win], yb[:], AF.Copy), r=[ybk], w=[yk + '_%d' % w])
                    else:
                        p.op('dve', I_tt(ya[:, win], yb[:], ya[:, win], ALU.add), r=[ybk, yk + '_%d' % w], w=[yk + '_%d' % w])
            nu = len(units)
            for step in range(nu + 2):
                if step < nu:
                    P01(units[step])
                    P2(units[step])
                if 0 <= step - 1 < nu:
                    P3(units[step - 1])
                    P4(units[step - 1])
                if 0 <= step - 2 < nu:
                    P5(units[step - 2])
                    P6(units[step - 2])
        if 'yO' in A:
            m0c = cx.vec[:, VC['m0']:VC['m0'] + 1]
            m1c = cx.vec[:, VC['m1']:VC['m1'] + 1]
            for hb in range(NT // SW):
                A1 = ya[:, hb * SW:(hb + 1) * SW]
                B1 = ya[:, SEQ - (hb + 1) * SW:SEQ - hb * SW][:, ::-1]
                ka = yk + '_%d' % hb
                kb = yk + '_%d' % (NW - 1 - hb)
                ot, otk = cx.rot('yo_t', [128, SW], F32, n=2)
                mcombine(cx, ot[:], otk, A1, ka, B1, kb, m0c, m1c)
                p.dma('sp', I_dma(A['yO'][ck * 128:(ck + 1) * 128, hb * SW:(hb + 1) * SW], ot[:]), r=[otk], w=['yout%d' % ck])
                st_, stk_ = cx.rot('yo_t', [128, SW], F32, n=2)
                mcombine(cx, st_[:], stk_, A1, ka, B1, kb, m1c, m0c)
                p.dma('sp', I_dma(A['yS'].src_rows(ck * 128)[:, hb * SW:(hb + 1) * SW], st_[:]), r=[stk_], w=['yout%d' % ck])
        else:
            p.dma('sp', I_dma(A['yT'][ck * 128:(ck + 1) * 128, :], ya[:]), r=[yk + '_%d' % w for w in range(NW)], w=['yout%d' % ck])
        if 'cc_early' in A and ck == 1:
            A['cc_early']()


def build_s5(debug=False, arena=False):
    nc = bass.Bass("TRN2", target_bir_lowering=False)
    A = {}

    def inp(name, shape, dt=F32):
        A[name] = nc.dram_tensor(name, list(shape), dt, kind="ExternalInput").ap()
    inp('uT', [512, SEQ])
    inp('Bre', [2, NPT, 128, 128])
    inp('Bim', [2, NPT, 128, 128])
    inp('CR', [2, NPT, 128, 128])
    inp('CI', [2, NPT, 128, 128])
    inp('lamre', [128, 2 * NPT])
    inp('lamim', [128, 2 * NPT])
    inp('logdt', [128, 2 * NPT])
    inp('dsk', [128, 4])
    inp('ident', [128, 128])
    A['yT'] = nc.dram_tensor('yT', [512, SEQ], F32, kind="ExternalOutput").ap()
    cx = Cx(nc, arena=arena)
    if arena:
        cx.new_stage()
    if debug:
        A['dbg'] = nc.dram_tensor('dbg', [128, 8 * 2 * NPT], F32, kind="ExternalOutput").ap()
        A['dbg2'] = nc.dram_tensor('dbg2', [128, 2 * SW], F32, kind="ExternalOutput").ap()
    s5_body(cx, A)
    cx.p.finalize()
    return nc, cx


def s5_host_inputs(inp, j, half):
    g0 = 32 * half
    Bre = np.zeros((2, NPT, 128, 128), np.float32)
    Bim = np.zeros_like(Bre)
    CR = np.zeros_like(Bre)
    CI = np.zeros_like(Bre)
    lamre = np.zeros((128, 2 * NPT), np.float32)
    lamim = np.zeros_like(lamre)
    logdt = np.zeros_like(lamre)
    for d in range(2):
        for pt in range(NPT):
            for gl in range(2):
                g = g0 + 2 * pt + gl
                r0 = (pt % 4) * 32 + gl * 16
                Bre[d, pt, r0:r0 + 16, gl * 64:(gl + 1) * 64] = inp['s5_b_re'][j, d, g].T
                Bim[d, pt, r0:r0 + 16, gl * 64:(gl + 1) * 64] = inp['s5_b_im'][j, d, g].T
                CR[d, pt, gl * 64:(gl + 1) * 64, r0:r0 + 16] = inp['s5_c_re'][j, d, g].T
                CI[d, pt, gl * 64:(gl + 1) * 64, r0:r0 + 16] = inp['s5_c_im'][j, d, g].T
                lamre[gl * 64:(gl + 1) * 64, d * NPT + pt] = inp['s5_lambda_re'][j, d, g]
                lamim[gl * 64:(gl + 1) * 64, d * NPT + pt] = inp['s5_lambda_im'][j, d, g]
                logdt[gl * 64:(gl + 1) * 64, d * NPT + pt] = inp['s5_log_dt'][j, d, g]
    dsk = np.ascontiguousarray(inp['s5_d'][j, 512 * half:512 * half + 512].reshape(4, 128).T)
    return dict(Bre=Bre, Bim=Bim, CR=CR, CI=CI, lamre=lamre, lamim=lamim, logdt=logdt, dsk=dsk,
                ident=np.eye(128, dtype=np.float32))


NEXT = 3072
GRP = [(1, 2048), (4, 512), (16, 128)]
ASCALE = 128 ** -0.5


def sub_view(ap2d, d):
    if d == 1:
        return ap2d.rearrange("p (d i) -> p d i", d=1)
    return ap2d.rearrange("p (i d) -> p d i", d=d)


def attn_body(cx, A, flip=False, src=None, dst=None, gh=None):
    p = cx.p
    vec = cx.vec

    def load_blk(xt, xk, c, tb):
        if gh is None or tb < NT // 512:
            p.dma('sp', I_dma(xt[:], _src[c * 128:(c + 1) * 128, _cols(tb)]), w=[xk])
            return
        hb = tb - NT // 512
        cols = slice(1024 - 512 * (hb + 1), 1024 - 512 * hb)
        pc = (c + 4) % 8
        X, xk2 = cx.rot('gx', [128, 512], F32, n=2)
        Z, zk2 = cx.rot('gz', [128, 512], F32, n=2)
        p.dma('sp', I_dma(X[:], gh.g_rows(0, pc * 128)[:, cols]), w=[xk2])
        p.dma('sp', I_dma(Z[:], gh.g_rows(1, pc * 128)[:, cols]), w=[zk2])
        mcombine(cx, xt[:], xk, X[:], xk2, Z[:], zk2, vec[:, VC['m1']:VC['m1'] + 1], vec[:, VC['m0']:VC['m0'] + 1])

    def _cols(tb):
        if src is None or not flip:
            return slice(tb * 512, (tb + 1) * 512)
        return slice(SEQ - (tb + 1) * 512, SEQ - tb * 512)
    _src = A['hT_ext'] if src is None else src
    _dst = A['hT_out'] if dst is None else dst
    rvf = (lambda ap: ap[:, ::-1]) if (flip and src is not None) else (lambda ap: ap)
    hn = p.sb('hnx', [128, NCH, NEXT], BF16)
    mT = p.sb('mT', [128, NCH, NT], BF16)
    num = p.sb('numacc', [128, NT], F32)
    den = p.sb('denacc', [128, NT], F32)
    m2 = getattr(p, 'aoff', None)
    NXB = 16 if m2 is not None else 2
    for tb in range(NEXT // 512):
        bank, bk = cx.bank()
        tiles_ = []
        for c in range(NCH):
            xt, xk = cx.rot('xin', [128, 512], F32, n=NXB)
            load_blk(xt, xk, c, tb)
            tiles_.append((xt, xk))
            sq, sk = cx.next_sq()
            p.op('act', I_act(sq[:], xt[:], AF.Square), r=[xk], w=[sk])
            p.op('pe', I_mm(bank[:], cx.ones[:], sq[:], c == 0, c == NCH - 1), r=[sk, 'ones'], w=[bk])
        rs, rk = cx.rstd_from_bank(bank, bk, 512, D)
        for c in range(NCH):
            if m2 is not None:
                xt, xk = tiles_[c]
            else:
                xt, xk = cx.rot('xin', [128, 512], F32, n=NXB)
                load_blk(xt, xk, c, tb)
            rv_ = (lambda ap: ap[:, ::-1]) if (gh is not None and tb >= NT // 512) else rvf
            p.op('dve', I_stt(rv_(hn[:, c, tb * 512:(tb + 1) * 512]), xt[:], vec[:, VC['gn'] + c:VC['gn'] + c + 1], rs[:],
                              ALU.mult, ALU.mult), r=[xk, 'vec', rk], w=['hnx'])
    if m2 is not None:
        p.barrier()
        p.aoff = m2
        cx._rot = {}
    sb_i = [0]

    def sbank():
        i = sb_i[0]
        sb_i[0] = (i + 1) % 4
        return cx.banks[i], 'bank%d' % i
    ob_i = [0]

    def obanks():
        i = ob_i[0]
        ob_i[0] = 1 - i
        return cx.banks[4 + i], 'bank%d' % (4 + i), cx.banks[6 + i], 'bank%d' % (6 + i)

    def qknorm(bank, bk, n, gcol, dst, dkey):
        sq, sk = cx.next_sq()
        p.op('act', I_act(sq[:, :n], bank[:, :n], AF.Square), r=[bk], w=[sk])
        b2, b2k = sbank()
        p.op('pe', I_mm(b2[:, :n], cx.ones[:], sq[:, :n], True, True), r=[sk, 'ones'], w=[b2k])
        rs, rk = cx.rstd_from_bank(b2, b2k, n, 128)
        p.op('dve', I_stt(dst, bank[:, :n], vec[:, gcol:gcol + 1], rs[:, :n], ALU.mult, ALU.mult), r=[bk, 'vec', rk], w=[dkey])

    wq_loaded = {}
    bias_loaded = {}

    def load_w(h_, g_):
        if (h_, g_) in wq_loaded or h_ >= 8:
            return
        wsl_, wsk_ = cx.rot('wqkv', [128, NCH, 384], BF16, n=3)
        for kind in range(3):
            c0 = kind * 3072 + g_ * 1024 + h_ * 128
            p.dma('pool', I_dma(wsl_[:, :, kind * 128:(kind + 1) * 128],
                                A['wqkv'][:, c0:c0 + 128].rearrange("(k p) n -> p k n", p=128)), w=[wsk_])
        wq_loaded[(h_, g_)] = (wsl_, wsk_)

    def load_bias(h_):
        if h_ in bias_loaded or h_ >= 8:
            return
        bt_, btk_ = cx.rot('biasT', [128, 3, 256], F32, n=2)
        for g_ in range(3):
            p.dma('sp', I_dma(bt_[:, g_, :], A['biasT'][g_ * 8 + h_]), w=[btk_])
        bias_loaded[h_] = (bt_, btk_)

    for h in range(8):
        p.op('pool', I_memset(num[:], 0.0), w=['numacc'])
        p.op('pool', I_memset(den[:], 0.0), w=['denacc'])
        load_bias(h)
        bt, btk = bias_loaded[h]
        for g, (d, Lq) in enumerate(GRP):
            nto = Lq // 128
            load_w(h, g)
            wsl, wsk = wq_loaded[(h, g)]
            load_w(h + (g + 1) // 3, (g + 1) % 3)
            if g == 0:
                load_bias(h + 1)
            qT, qk_ = cx.rot('qT', [128, NT], BF16, n=2)
            kT, kk_ = cx.rot('kT', [128, NEXT], BF16, n=2)
            vt, vk_ = cx.rot('vt', [128, 32, 128], BF16, n=2)

            for kind, dstT, dk, gcol in ((0, qT, qk_, VC['aq']), (1, kT, kk_, VC['ak'])):
                for bi in range(4):
                    b, bk = sbank()
                    for kc in range(NCH):
                        if d == 1:
                            rhs, o_ap = hn[:, kc, bi * 512:(bi + 1) * 512], b[:]
                        elif d == 4:
                            rhs, o_ap = sub_view(hn[:, kc, 0:NT], 4)[:, bi, :], b[:]
                        else:
                            rhs = sub_view(hn[:, kc, 0:NT], 16)[:, 4 * bi:4 * bi + 4, :]
                            o_ap = b[:].rearrange("p (a b) -> p a b", a=4)
                        p.op('pe', I_mm(o_ap, wsl[:, kc, kind * 128:(kind + 1) * 128], rhs, kc == 0, kc == NCH - 1),
                             r=[wsk, 'hnx'], w=[bk])
                    qknorm(b, bk, 512, gcol, dstT[:, bi * 512:(bi + 1) * 512], dk)
            nh = 64 * d
            for b0 in range(0, nh, 512):
                n = min(512, nh - b0)
                b, bk = sbank()
                for kc in range(NCH):
                    if d == 1:
                        rhs = hn[:, kc, NT:NT + 64]
                        o_ap = b[:, :64]
                    else:
                        r0 = b0 // 64
                        nr = n // 64
                        rhs = sub_view(hn[:, kc, NT:NT + 64 * d], d)[:, r0:r0 + nr, :]
                        o_ap = b[:, :n].rearrange("p (a b) -> p a b", a=nr)
                    p.op('pe', I_mm(o_ap, wsl[:, kc, 128:256], rhs, kc == 0, kc == NCH - 1), r=[wsk, 'hnx'], w=[bk])
                qknorm(b, bk, n, VC['ak'], kT[:, NT + b0:NT + b0 + n], kk_)
            for t0 in range(0, 16, 4):
                b, bk = sbank()
                for tt in range(4):
                    t = t0 + tt
                    r, m = t // nto, t % nto
                    for kc in range(NCH):
                        lhsT = sub_view(hn[:, kc, 0:NT], d)[:, r, m * 128:(m + 1) * 128]
                        p.op('pe', I_mm(b[:, tt * 128:(tt + 1) * 128], lhsT, wsl[:, kc, 256:384], kc == 0, kc == NCH - 1),
                             r=[wsk, 'hnx'], w=[bk])
                p.op('act', I_act(vt[:, t0:t0 + 4, :], b[:].rearrange("p (a b) -> p a b", a=4), AF.Copy), r=[bk], w=[vk_])
            for r0 in range(0, d, 4):
                nr = min(4, d - r0)
                b, bk = sbank()
                for rr in range(nr):
                    r = r0 + rr
                    for kc in range(NCH):
                        lhsT = sub_view(hn[:, kc, NT:NT + 64 * d], d)[:, r, :]
                        p.op('pe', I_mm(b[:64, rr * 128:(rr + 1) * 128], lhsT, wsl[:, kc, 256:384], kc == 0, kc == NCH - 1),
                             r=[wsk, 'hnx'], w=[bk])
                p.op('act', I_act(vt[:64, 16 + r0:16 + r0 + nr, :], b[:64, :nr * 128].rearrange("p (a b) -> p a b", a=nr), AF.Copy),
                     r=[bk], w=[vk_])
            tiles = [(r, m) for r in range(d) for m in range(nto + 1)]
            stt_ = {'ob': None}

            def S_phase(r, m):
                qoff = r * Lq
                halo = (m == nto)
                nk = 64 if halo else 128
                b0_ = 64 if m == 0 else 0
                b1_ = 64 if halo else min(256, Lq - (128 * m - 64))
                ktile = kT[:, NT + r * 64:NT + r * 64 + 64] if halo else kT[:, qoff + m * 128:qoff + (m + 1) * 128]
                qs = qoff + 128 * m - 64 + b0_
                sbk, sbkk = sbank()
                p.op('pe', I_mm(sbk[:nk, b0_:b1_], ktile, qT[:, qs:qs + (b1_ - b0_)], True, True), r=[kk_, qk_], w=[sbkk])
                st, stk = cx.rot('stmp', [128, 256], F32, n=3)
                p.op('dve', I_stt(st[:nk, b0_:b1_], sbk[:nk, b0_:b1_], ASCALE, bt[:nk, g, b0_:b1_], ALU.mult, ALU.add),
                     r=[sbkk, btk], w=[stk])
                PT, ptk = cx.rot('PTa', [128, 256], BF16, n=4)
                p.op('act', I_act(PT[:nk, b0_:b1_], st[:nk, b0_:b1_], AF.Exp), r=[stk], w=[ptk])
                return PT, ptk

            def PV_phase(r, m, PT, ptk):
                halo = (m == nto)
                nk = 64 if halo else 128
                vtile = vt[:64, 16 + r, :] if halo else vt[:, r * nto + m, :]

                def flush(ep):
                    ob = stt_['ob']
                    qlo = max(0, 512 * ep - 64)
                    qhi = min(Lq, 512 * ep + 448)
                    c0f = qlo - (512 * ep - 64)
                    wdt = qhi - qlo
                    nv = sub_view(num[:, :], d)[:, r, qlo:qhi]
                    dv = sub_view(den[:, :], d)[:, r, qlo:qhi]
                    p.op('dve', I_tt(nv, ob[0][:, c0f:c0f + wdt], nv, ALU.add), r=[ob[1], 'numacc'], w=['numacc'])
                    p.op('dve', I_tt(dv, ob[2][:, c0f:c0f + wdt], dv, ALU.add), r=[ob[3], 'denacc'], w=['denacc'])
                if m == 0:
                    stt_['ob'] = ob = obanks()
                    p.op('pe', I_mm(ob[0][:, 64:128], vtile, PT[:nk, 64:128], True, True), r=[vk_, ptk], w=[ob[1]])
                    p.op('pe', I_mm(ob[2][:, 64:128], cx.ones[:nk, :], PT[:nk, 64:128], True, True), r=['ones', ptk], w=[ob[3]])
                else:
                    ob = stt_['ob']
                    q0 = 64 + 128 * (m - 1)
                    wq_ = min(Lq, q0 + 128) - q0
                    c0 = 128 * (m % 4)
                    p.op('pe', I_mm(ob[0][:, c0:c0 + wq_], vtile, PT[:nk, 0:wq_], False, True), r=[vk_, ptk], w=[ob[1]])
                    p.op('pe', I_mm(ob[2][:, c0:c0 + wq_], cx.ones[:nk, :], PT[:nk, 0:wq_], False, True), r=['ones', ptk], w=[ob[3]])
                    if m % 4 == 3 or halo:
                        flush(m // 4)
                if not halo:
                    q0 = 64 + 128 * m
                    wq_ = min(Lq, q0 + 128) - q0
                    if (m + 1) % 4 == 0:
                        stt_['ob'] = obanks()
                    ob = stt_['ob']
                    c0 = 128 * ((m + 1) % 4)
                    p.op('pe', I_mm(ob[0][:, c0:c0 + wq_], vtile, PT[:nk, 128:128 + wq_], True, False), r=[vk_, ptk], w=[ob[1]])
                    p.op('pe', I_mm(ob[2][:, c0:c0 + wq_], cx.ones[:nk, :], PT[:nk, 128:128 + wq_], True, False), r=['ones', ptk], w=[ob[3]])
            SKEW = 3
            pend = {}
            for i in range(len(tiles) + SKEW):
                if i < len(tiles):
                    pend[i] = S_phase(*tiles[i])
                if i - SKEW >= 0:
                    PV_phase(*tiles[i - SKEW], *pend.pop(i - SKEW))
        p.op('act', I_act(den[:], den[:], AF.Ln), r=['denacc'], w=['denacc'])
        p.op('act', I_act(den[:], den[:], AF.Exp, scale=-1.0), r=['denacc'], w=['denacc'])
        p.op('pool', I_tt(mT[:, h, :], num[:], den[:], ALU.mult), r=['numacc', 'denacc'], w=['mT'])
    for ns in range(2):
        wo, wok = wslab(cx, [(A['wo_a'][:, ns * 512:(ns + 1) * 512], 0)])
        for j in range(4):
            n = ns * 4 + j
            for tb in range(NT // 512):
                b, bk = sbank()
                for kc in range(NCH):
                    p.op('pe', I_mm(b[:], wo[:, kc * 512 + j * 128: kc * 512 + (j + 1) * 128], mT[:, kc, tb * 512:(tb + 1) * 512],
                                    kc == 0, kc == NCH - 1), r=[wok, 'mT'], w=[bk])
                xt, xk = cx.rot('xin', [128, 512], F32, n=2)
                p.dma('sp', I_dma(xt[:], _src[n * 128:(n + 1) * 128, _cols(tb)]), w=[xk])
                p.op('dve', I_tt(xt[:], rvf(b[:]), xt[:], ALU.add), r=[bk, xk], w=[xk])
                p.dma('sp', I_dma(_dst[n * 128:(n + 1) * 128, _cols(tb)], xt[:]), r=[xk], w=['hTout'])


def build_attn():
    nc = bass.Bass("TRN2", target_bir_lowering=False)
    A = {}

    def inp(name, shape, dt=F32):
        A[name] = nc.dram_tensor(name, list(shape), dt, kind="ExternalInput").ap()
    inp('hT_ext', [D, NEXT])
    inp('vecs', [128, NVEC])
    inp('wqkv', [D, 9216])
    inp('wo_a', [D, D])
    inp('biasT', [24, 128, 256])
    A['hT_out'] = nc.dram_tensor('hT_out', [D, NT], F32, kind="ExternalOutput").ap()
    cx = Cx(nc, arena=True)
    cx.new_stage()
    cx.vec = cx.p.sb('vec', [128, NVEC], F32)
    cx.p.dma('sp', I_dma(cx.vec[:], A['vecs'][:, :]), w=['vec'])
    cx.ws_n = 2
    attn_body(cx, A)
    cx.p.finalize()
    return nc, cx


def t5_bucket(rel):
    nb = 16
    ret = (rel > 0).astype(np.int32) * nb
    n = np.abs(rel)
    max_exact = nb // 2
    large = max_exact + (np.log(np.maximum(n, 1).astype(np.float32) / max_exact)
                         / np.log(1024 / max_exact) * (nb - max_exact)).astype(np.int32)
    large = np.minimum(large, nb - 1)
    return (ret + np.where(n < max_exact, n, large)).astype(np.int32)


def host_bias(bias_table, flip):
    a = np.arange(128)[:, None]
    b = np.arange(256)[None, :]
    rel = a - b + 64
    out = np.full((24, 128, 256), -1e30, np.float32)
    band = np.abs(rel) <= 64
    for g, (dil, _) in enumerate(GRP):
        bk = t5_bucket((-rel if flip else rel) * dil)
        for h in range(8):
            out[g * 8 + h] = np.where(band, bias_table[bk, g * 8 + h], np.float32(-1e30))
    return out


def build_norm():
    nc = bass.Bass("TRN2", target_bir_lowering=False)
    A = {}
    A['hT'] = nc.dram_tensor('hT', [D, NT], F32, kind="ExternalInput").ap()
    A['vecs'] = nc.dram_tensor('vecs', [128, NVEC], F32, kind="ExternalInput").ap()
    A['hn_out'] = nc.dram_tensor('hn_out', [D, NT], F32, kind="ExternalOutput").ap()
    cx = Cx(nc)
    p = cx.p
    cx.hT = p.sb('hT', [128, NCH, NT], F32)
    cx.vec = p.sb('vec', [128, NVEC], F32)
    p.dma('sp', I_dma(cx.vec[:], A['vecs'][:, :]), w=['vec'])
    load_hT(cx, A['hT'])
    emit_norm(cx, A['hn_out'])
    p.finalize()
    return nc, cx


def build_fused(nlayers=4):
    nc = bass.Bass("TRN2", target_bir_lowering=False)
    shapes = {}

    def inp(name, shape, dt=F32):
        shapes[name] = list(shape)

    class Lazy(dict):
        def __missing__(self, name):
            ap = nc.dram_tensor(name, shapes[name], F32, kind="ExternalInput").ap()
            self[name] = ap
            return ap
    A = Lazy()

    def scratch(name):
        return nc.dram_tensor(name, [D, SEQ], F32, kind="Internal").ap()
    inp('xT', [D, SEQ])
    inp('memT', [D, MEMLEN])
    inp('ident', [128, 128])
    inp('v0', [128, NVEC])
    inp('biasT0', [24, 128, 256])
    inp('biasT1', [24, 128, 256])
    for i in range(4):
        inp('vecs%d' % i, [128, NVEC])
        inp('wq%d' % i, [D, D])
        inp('wkv%d' % i, [D, 2 * D])
        inp('wo%d' % i, [D, D])
        inp('w1_%d' % i, [D, DFF])
        inp('w2_%d' % i, [DFF, D])
    for j in range(2):
        inp('wglu%d' % j, [D, 2 * D])
        inp('wqkv%d' % j, [D, 9216])
        inp('woa%d' % j, [D, D])
        inp('avecs%d' % j, [128, NVEC])
        for c in range(2):
            sfx = '%d%d' % (j, c)
            for nm in ('Bre', 'Bim', 'CR', 'CI'):
                inp(nm + sfx, [2, NPT, 128, 128])
            for nm in ('lamre', 'lamim', 'logdt'):
                inp(nm + sfx, [128, 2 * NPT])
            inp('dsk' + sfx, [128, 4])
    xT = A['xT']
    outT = nc.dram_tensor('outT', [D, SEQ], F32, kind="ExternalOutput").ap()
    HN = scratch('HN')
    Y = scratch('Y')
    Hs = [xT, scratch('H1'), scratch('H1a'), scratch('H2'), scratch('H3'), scratch('H3a'), outT]
    cx = Cx(nc, arena=True)
    p = cx.p
    hv = lambda ap, half: ap[:, half * NT:(half + 1) * NT]

    cx.hT = p.sb('hT', [128, NCH, NT], F32)
    cx.vec = p.sb('vec', [128, NVEC], F32)
    p.dma('sp', I_dma(cx.vec[:], A['v0'][:, :]), w=['vec'])
    for half in range(2):
        load_hT(cx, hv(xT, half))
        emit_norm(cx, hv(HN, half))

    def s5_stage(j):
        for c in range(2):
            cx.new_stage()
            sfx = '%d%d' % (j, c)
            AA = {nm: A[nm + sfx] for nm in ('Bre', 'Bim', 'CR', 'CI', 'lamre', 'lamim', 'logdt', 'dsk')}
            AA['ident'] = A['ident']
            AA['uT'] = HN[512 * c:512 * c + 512, :]
            AA['yT'] = Y[512 * c:512 * c + 512, :]
            s5_body(cx, AA)

    def tail_stage(i, glu, src, dst, emit):
        cx.new_stage()
        AA = dict(memT=A['memT'], vecs=A['vecs%d' % i], wq=A['wq%d' % i], wkv=A['wkv%d' % i], wo=A['wo%d' % i],
                  w1=A['w1_%d' % i], w2=A['w2_%d' % i])
        if glu:
            AA['wglu'] = A['wglu%d' % (i // 2)]
        common_tiles(cx, AA)
        for half in range(2):
            load_hT(cx, hv(src, half))
            if glu:
                AA['yT'] = hv(Y, half)
            tail_body(cx, AA, glu, 0, 2, kv_ready=(half == 1))
            tail_body(cx, AA, glu, 2, 2, kv_ready=True)
            store_hT(cx, hv(dst, half))
            if emit:
                emit_norm(cx, hv(HN, half))

    def attn_stage(i, src, dst):
        j = i // 2
        for half in range(2):
            cx.new_stage()
            cx.vec = p.sb('vec', [128, NVEC], F32)
            p.dma('sp', I_dma(cx.vec[:], A['avecs%d' % j][:, :]), w=['vec'])
            AA = dict(wqkv=A['wqkv%d' % j], wo_a=A['woa%d' % j], biasT=A['biasT%d' % half])
            attn_body(cx, AA, flip=(half == 1), src=src, dst=dst)

    s5_stage(0)
    if nlayers == 0:
        cx.new_stage()
        cx.hT = p.sb('hT', [128, NCH, NT], F32)
        for half in range(2):
            load_hT(cx, hv(Y, half))
            store_hT(cx, hv(outT, half))
        p.finalize()
        cx.used = list(A.keys())
        return nc, cx
    tail_stage(0, True, Hs[0], Hs[1] if nlayers > 1 else outT, False)
    if nlayers > 1:
        attn_stage(1, Hs[1], Hs[2])
        tail_stage(1, False, Hs[2], Hs[3] if nlayers > 2 else outT, True)
    if nlayers > 2:
        s5_stage(1)
        tail_stage(2, True, Hs[3], Hs[4] if nlayers > 3 else outT, False)
    if nlayers > 3:
        attn_stage(3, Hs[4], Hs[5])
        tail_stage(3, False, Hs[5], Hs[6], False)
    p.finalize()
    cx.used = list(A.keys())
    return nc, cx


RG2 = [[0, 1], [2, 3], [4, 5], [6, 7]]


class GBuf:
    def __init__(self, nc, name, rows, cols, chunk_rows):
        self.cr = chunk_rows
        self.n = rows // chunk_rows
        self.src = [nc.dram_tensor('%s_s%d' % (name, q), [chunk_rows, cols], F32, kind="Internal").ap() for q in range(self.n)]
        self.dst = [nc.dram_tensor('%s_g%d' % (name, q), [2 * chunk_rows, cols], F32, kind="Internal").ap() for q in range(self.n)]

    def src_rows(self, r0, nrows=128):
        q = r0 // self.cr
        o = r0 - q * self.cr
        return self.src[q][o:o + nrows, :]

    def g_rows(self, rank, r0, nrows=128):
        q = r0 // self.cr
        o = rank * self.cr + r0 - q * self.cr
        return self.dst[q][o:o + nrows, :]


def build_fused8():
    nc = bass.Bass("TRN2", target_bir_lowering=False, num_devices=8)
    shapes = {}

    def inp(name, shape):
        shapes[name] = list(shape)

    class Lazy(dict):
        def __missing__(self, name):
            ap = nc.dram_tensor(name, shapes[name], F32, kind="ExternalInput").ap()
            self[name] = ap
            return ap
    A = Lazy()

    def scratch(name, shape):
        return nc.dram_tensor(name, list(shape), F32, kind="Internal").ap()
    inp('xT', [D, NT])
    inp('memT', [D, MEMLEN])
    inp('ident', [128, 128])
    inp('v0', [128, NVEC])
    inp('biasT', [24, 128, 256])
    for i in range(4):
        inp('vecs%d' % i, [128, NVEC])
        inp('wq%d' % i, [D, D])
        inp('wkv%d' % i, [D, 2 * D])
        inp('wo%d' % i, [D, D])
        inp('w1_%d' % i, [D, DFF])
        inp('w2_%d' % i, [DFF, D])
    for j in range(2):
        inp('wglu%d' % j, [D, 2 * D])
        inp('wqkv%d' % j, [D, 9216])
        inp('woa%d' % j, [D, D])
        inp('avecs%d' % j, [128, NVEC])
        for nm in ('Bre', 'Bim', 'CR', 'CI'):
            inp(nm + '%d' % j, [2, NPT, 128, 128])
        for nm in ('lamre', 'lamim', 'logdt'):
            inp(nm + '%d' % j, [128, 2 * NPT])
        inp('dsk%d' % j, [128, 4])
    outT = nc.dram_tensor('outT', [D, NT], F32, kind="ExternalOutput").ap()
    HNb = GBuf(nc, 'HN', D, NT, 256)
    HN = GHN = HNb
    yO = scratch('yO', [512, NT])
    ySb = GBuf(nc, 'yS', 512, NT, 256)
    yS = GS = ySb
    Hhb = GBuf(nc, 'Hh', D, 1024, 512)
    Hh = GH = Hhb
    Hs = [A['xT']] + [scratch(n, [D, NT]) for n in ('H1', 'H1a', 'H2', 'H3', 'H3a')] + [outT]
    cx = Cx(nc, arena=True)
    p = cx.p

    def gather_chunk(gb, q, rkeys):
        p.dma('pool', lambda e, q=q: e.collective_compute("AllGather", ALU.bypass, replica_groups=RG2,
                                                           ins=[gb.src[q][:, :]], outs=[gb.dst[q][:, :]]),
              r=list(rkeys), w=['cc'], semkey='cc', inc=1)

    def allgather(gb, _unused=None, chunks=None):
        cx.new_stage()
        for q in (range(gb.n) if chunks is None else chunks):
            p.dma('pool', lambda e, q=q: e.collective_compute("AllGather", ALU.bypass, replica_groups=RG2,
                                                               ins=[gb.src[q][:, :]], outs=[gb.dst[q][:, :]]),
                  w=['cc'], semkey='cc', inc=1)

    cx.hT = p.sb('hT', [128, NCH, NT], F32)
    cx.vec = p.sb('vec', [128, NVEC], F32)
    p.dma('sp', I_dma(cx.vec[:], A['v0'][:, :]), w=['vec'])
    load_hT(cx, A['xT'])
    emit_norm(cx, HN)
    allgather(HN, GHN)

    def s5_stage(j):
        cx.new_stage()
        AA = {nm: A[nm + '%d' % j] for nm in ('Bre', 'Bim', 'CR', 'CI', 'lamre', 'lamim', 'logdt', 'dsk')}
        AA['ident'] = A['ident']
        AA['GHN'] = GHN
        AA['yO'] = yO
        AA['yS'] = yS
        cx.vec = p.sb('vec', [128, NVEC], F32)
        p.dma('sp', I_dma(cx.vec[:], A['v0'][:, :]), w=['vec'])
        AA['cc_early'] = lambda: gather_chunk(ySb, 0, ['yout0', 'yout1'])
        s5_body(cx, AA)
        allgather(yS, GS, chunks=[1])

    def tail_stage(i, glu, src, dst, emit, halo):
        cx.new_stage()
        AA = dict(memT=A['memT'], vecs=A['vecs%d' % i], wq=A['wq%d' % i], wkv=A['wkv%d' % i], wo=A['wo%d' % i],
                  w1=A['w1_%d' % i], w2=A['w2_%d' % i])
        if glu:
            AA['wglu'] = A['wglu%d' % (i // 2)]
            AA['yO'] = yO
            AA['GS'] = GS
        common_tiles(cx, AA)
        load_hT(cx, src)
        tail_body(cx, AA, glu, 2, 2, kv_ready=False)
        if halo:
            for c in range(NCH):
                p.dma('sp', I_dma(Hh.src_rows(c * 128), cx.hT[:, c, 1024:2048]),
                      r=['h%d_%d' % (c, tb) for tb in (2, 3)], w=['hh%d' % (c // 4)], semkey='hhout%d' % (c // 4))
            for q in range(Hhb.n):
                gather_chunk(Hhb, q, ['hh%d' % q])
        tail_body(cx, AA, glu, 0, 2, kv_ready=True)
        store_hT(cx, dst)
        if emit:
            emit_norm(cx, HN)
            allgather(HN, GHN)

    def attn_stage(i, src, dst):
        j = i // 2
        cx.new_stage()
        cx.vec = p.sb('vec', [128, NVEC], F32)
        p.dma('sp', I_dma(cx.vec[:], A['avecs%d' % j][:, :]), w=['vec'])
        AA = dict(wqkv=A['wqkv%d' % j], wo_a=A['woa%d' % j], biasT=A['biasT'])
        cx.ws_n = 2
        attn_body(cx, AA, flip=False, src=src, dst=dst, gh=GH)
        cx.ws_n = WS_N

    s5_stage(0)
    tail_stage(0, True, Hs[0], Hs[1], False, True)
    attn_stage(1, Hs[1], Hs[2])
    tail_stage(1, False, Hs[2], Hs[3], True, False)
    s5_stage(1)
    tail_stage(2, True, Hs[3], Hs[4], False, True)
    attn_stage(3, Hs[4], Hs[5])
    tail_stage(3, False, Hs[5], Hs[6], False, False)
    p.finalize()
    cx.used = list(A.keys())
    return nc, cx


def _pc(v, C):
    return np.ascontiguousarray(np.asarray(v, np.float32).reshape(C, 128).T)


_PROGS = {}
_NL = [4]


def _prog(name):
    if name not in _PROGS:
        if name == 'norm':
            _PROGS[name] = build_norm()[0]
        elif name == 's5':
            _PROGS[name] = build_s5()[0]
        elif name == 'tail_glu':
            _PROGS[name] = build_tail(True, True)[0]
        elif name == 'tail':
            _PROGS[name] = build_tail(False, True)[0]
        elif name == 'attn':
            _PROGS[name] = build_attn()[0]
    return _PROGS[name]


def kernel_multi(**inp):
    inp = {k: np.asarray(v) for k, v in inp.items()}
    ncore = 8
    cores = list(range(ncore))
    f32 = np.float32
    loc = [np.arange(NEXT) if (k % 2 == 0) else (SEQ - 1 - np.arange(NEXT)) for k in cores]
    H = np.array(inp['x'], dtype=f32, copy=True)
    memT = [np.ascontiguousarray(inp['mem'][k // 2].T.astype(f32)) for k in cores]
    biasT = [host_bias(inp['bias_table'].astype(f32), k % 2 == 1) for k in cores]

    def own_T(arr_bsd, k):
        return np.ascontiguousarray(arr_bsd[k // 2][loc[k][:NT]].T)

    def scatter(outs, name):
        full = np.empty((BATCH, SEQ, D), f32)
        for k in cores:
            full[k // 2][loc[k][:NT]] = np.asarray(outs[k][name], f32).T
        return full

    def tail_vecs(i):
        v = np.zeros((128, NVEC), f32)
        v[:, 0:8] = _pc(inp['norm_xattn'][i], 8)
        v[:, 8:16] = _pc(inp['norm_mem'][i], 8)
        v[:, 16:24] = _pc(inp['norm_mlp'][i], 8)
        v[:, 24:26] = _pc(inp['xattn_q_gain'][i], 2)
        v[:, 26:28] = _pc(inp['xattn_k_gain'][i], 2)
        v[:, 28:36] = _pc(inp['norm_mix'][min(i + 1, 3)], 8)
        return v

    def run_tail(i, H, Y):
        v = tail_vecs(i)
        maps = []
        for k in cores:
            m = dict(hT=own_T(H, k), memT=memT[k], vecs=v, wq=inp['xattn_w_q'][i], wkv=inp['xattn_w_kv'][i],
                     wo=inp['xattn_w_o'][i], w1=inp['mlp_w1'][i], w2=inp['mlp_w2'][i])
            if Y is not None:
                m['yT'] = own_T(Y, k)
                m['wglu'] = inp['s5_w_glu'][i // 2]
            maps.append(m)
        res = run_bass_kernel_spmd(_prog('tail_glu' if Y is not None else 'tail'), maps, core_ids=cores).results
        return scatter(res, 'hT_out'), scatter(res, 'hn_out')

    def run_s5(j, HN):
        maps = []
        for k in cores:
            b, c = k // 2, k % 2
            m = s5_host_inputs(inp, j, c)
            m['uT'] = np.ascontiguousarray(HN[b][:, 512 * c:512 * c + 512].T)
            maps.append(m)
        res = run_bass_kernel_spmd(_prog('s5'), maps, core_ids=cores).results
        Y = np.empty((BATCH, SEQ, D), f32)
        for k in cores:
            b, c = k // 2, k % 2
            Y[b][:, 512 * c:512 * c + 512] = np.asarray(res[k]['yT'], f32).T
        return Y

    def run_attn(i, H):
        j = i // 2
        v = np.zeros((128, NVEC), f32)
        v[:, 28:36] = _pc(inp['norm_mix'][i], 8)
        v[:, 36] = inp['attn_q_gain'][j]
        v[:, 37] = inp['attn_k_gain'][j]
        maps = []
        for k in cores:
            maps.append(dict(hT_ext=np.ascontiguousarray(H[k // 2][loc[k]].T), vecs=v, wqkv=inp['attn_w_qkv'][j],
                             wo_a=inp['attn_w_o'][j], biasT=biasT[k]))
        res = run_bass_kernel_spmd(_prog('attn'), maps, core_ids=cores).results
        return scatter(res, 'hT_out')

    v0 = np.zeros((128, NVEC), f32)
    v0[:, 28:36] = _pc(inp['norm_mix'][0], 8)
    res = run_bass_kernel_spmd(_prog('norm'), [dict(hT=own_T(H, k), vecs=v0) for k in cores], core_ids=cores).results
    HN = scatter(res, 'hn_out')
    for i in range(4):
        if i % 2 == 0:
            Y = run_s5(i // 2, HN)
            H, HN = run_tail(i, H, Y)
        else:
            H = run_attn(i, H)
            H, HN = run_tail(i, H, None)
    return H


def tail_vecs_host(inp, i):
    v = np.zeros((128, NVEC), np.float32)
    v[:, 0:8] = _pc(inp['norm_xattn'][i], 8)
    v[:, 8:16] = _pc(inp['norm_mem'][i], 8)
    v[:, 16:24] = _pc(inp['norm_mlp'][i], 8)
    v[:, 24:26] = _pc(inp['xattn_q_gain'][i], 2)
    v[:, 26:28] = _pc(inp['xattn_k_gain'][i], 2)
    v[:, 28:36] = _pc(inp['norm_mix'][min(i + 1, 3)], 8)
    return v


def kernel_fused4(**inp):
    inp = {k: np.asarray(v) for k, v in inp.items()}
    f32 = np.float32
    nl = _NL[0]
    if ('fused', nl) not in _PROGS:
        _PROGS[('fused', nl)] = build_fused(nl)
    nc, cxf = _PROGS[('fused', nl)]
    shared = dict(ident=np.eye(128, dtype=f32),
                  biasT0=host_bias(inp['bias_table'].astype(f32), False),
                  biasT1=host_bias(inp['bias_table'].astype(f32), True))
    v0 = np.zeros((128, NVEC), f32)
    v0[:, 28:36] = _pc(inp['norm_mix'][0], 8)
    shared['v0'] = v0
    for i in range(4):
        shared['vecs%d' % i] = tail_vecs_host(inp, i)
        shared['wq%d' % i] = inp['xattn_w_q'][i]
        shared['wkv%d' % i] = inp['xattn_w_kv'][i]
        shared['wo%d' % i] = inp['xattn_w_o'][i]
        shared['w1_%d' % i] = inp['mlp_w1'][i]
        shared['w2_%d' % i] = inp['mlp_w2'][i]
    for j in range(2):
        shared['wglu%d' % j] = inp['s5_w_glu'][j]
        shared['wqkv%d' % j] = inp['attn_w_qkv'][j]
        shared['woa%d' % j] = inp['attn_w_o'][j]
        av = np.zeros((128, NVEC), f32)
        av[:, 28:36] = _pc(inp['norm_mix'][2 * j + 1], 8)
        av[:, 36] = inp['attn_q_gain'][j]
        av[:, 37] = inp['attn_k_gain'][j]
        shared['avecs%d' % j] = av
        for c in range(2):
            for nm, arr in s5_host_inputs(inp, j, c).items():
                if nm != 'ident':
                    shared[nm + '%d%d' % (j, c)] = arr
    maps = []
    for b in range(BATCH):
        m = dict(shared)
        m['xT'] = np.ascontiguousarray(inp['x'][b].T.astype(f32))
        m['memT'] = np.ascontiguousarray(inp['mem'][b].T.astype(f32))
        maps.append({k: m[k] for k in cxf.used})
    res = run_bass_kernel_spmd(nc, maps, core_ids=list(range(BATCH))).results
    out = np.empty((BATCH, SEQ, D), f32)
    for b in range(BATCH):
        out[b] = np.asarray(res[b]['outT'], f32).T
    return out


def _sw(a, axis):
    return np.roll(a, 512, axis=axis)


def kernel(**inp):
    inp = {k: np.asarray(v, np.float32) for k, v in inp.items()}
    f32 = np.float32
    if 'fused8' not in _PROGS:
        _PROGS['fused8'] = build_fused8()
    nc, cxf = _PROGS['fused8']
    ident = np.eye(128, dtype=f32)
    per_c = []
    for c in range(2):
        sw = (lambda a, axis: _sw(a, axis)) if c == 1 else (lambda a, axis: a)
        g = {}
        gi = {k: (sw(inp[k], 1) if k in ('norm_mix', 'norm_xattn', 'norm_mem', 'norm_mlp') else inp[k]) for k in inp}
        g['ident'] = ident
        g['biasT'] = host_bias(inp['bias_table'], c == 1)
        v0 = np.zeros((128, NVEC), f32)
        v0[:, 28:36] = _pc(gi['norm_mix'][0], 8)
        v0[:, 40 + c] = 1.0
        g['v0'] = v0
        for i in range(4):
            v = tail_vecs_host(gi, i)
            v[:, 40 + c] = 1.0
            g['vecs%d' % i] = v
            g['wq%d' % i] = np.ascontiguousarray(sw(inp['xattn_w_q'][i], 0))
            g['wkv%d' % i] = np.ascontiguousarray(sw(inp['xattn_w_kv'][i], 0))
            g['wo%d' % i] = np.ascontiguousarray(sw(inp['xattn_w_o'][i], 1))
            g['w1_%d' % i] = np.ascontiguousarray(sw(inp['mlp_w1'][i], 0))
            g['w2_%d' % i] = np.ascontiguousarray(sw(inp['mlp_w2'][i], 1))
        for j in range(2):
            wg = sw(inp['s5_w_glu'][j], 0).reshape(D, 2, D)
            g['wglu%d' % j] = np.ascontiguousarray(sw(wg, 2).reshape(D, 2 * D))
            g['wqkv%d' % j] = np.ascontiguousarray(sw(inp['attn_w_qkv'][j], 0))
            g['woa%d' % j] = np.ascontiguousarray(sw(inp['attn_w_o'][j], 1))
            av = np.zeros((128, NVEC), f32)
            av[:, 28:36] = _pc(gi['norm_mix'][2 * j + 1], 8)
            av[:, 36] = inp['attn_q_gain'][j]
            av[:, 37] = inp['attn_k_gain'][j]
            av[:, 40 + c] = 1.0
            g['avecs%d' % j] = av
            for nm, arr in s5_host_inputs(inp, j, c).items():
                if nm != 'ident':
                    g[nm + '%d' % j] = arr
        per_c.append(g)
    maps = []
    for k in range(8):
        b, c = k // 2, k % 2
        m = dict(per_c[c])
        xb = inp['x'][b]
        if c == 0:
            m['xT'] = np.ascontiguousarray(xb[:NT].T)
            m['memT'] = np.ascontiguousarray(inp['mem'][b].T)
        else:
            m['xT'] = np.ascontiguousarray(_sw(xb[::-1][:NT], 1).T)
            m['memT'] = np.ascontiguousarray(_sw(inp['mem'][b], 1).T)
        maps.append({kk: m[kk] for kk in cxf.used})
    res = run_bass_kernel_spmd(nc, maps, core_ids=list(range(8))).results
    out = np.empty((BATCH, SEQ, D), f32)
    for k in range(8):
        b, c = k // 2, k % 2
        o = np.asarray(res[k]['outT'], f32).T
        if c == 0:
            out[b, :NT] = o
        else:
            out[b, NT:] = _sw(o, 1)[::-1]
    return out
```
